# Optimizing a Trainium2 kernel written in Bass

```python
import math
import jax, jax.numpy as jnp
from jax import lax
import numpy as np

D_MODEL = 1024
BATCH = 32
SEQ = 256
DEPTH = 2
DEC_BATCH = 2
DEC_SEQ = 2048
PAST_LEN = 512

GRID_W = 64
N_MIXERS = 4
D_BRANCH = D_MODEL // N_MIXERS
D_MIX = N_MIXERS * D_BRANCH
EPS = 1e-6

MLA_HEADS = 4
MLA_Q_RANK = 3 * D_MODEL // 16
MLA_KV_RANK = D_MODEL // 8
MLA_NOPE = 64
MLA_ROPE = 32
MLA_QK = MLA_NOPE + MLA_ROPE
MLA_V = D_BRANCH // MLA_HEADS
ROPE_BASE = 10000.0
Q_BLOCK = 128

HY_ORDER = 2
HY_SHORT = 3
HY_BANDS = 16
HY_FEAT = 1 + 2 * HY_BANDS
HY_HIDDEN = 64
HY_SHIFT = 0.05
HY_FAST_DECAY = 0.3
HY_SLOW_DECAY = 1.5
HY_TARGET = 1e-2

S5_GROUP = 16
S5_GROUPS = D_BRANCH // S5_GROUP
S5_STATE = 64
S5_DT_MIN = 1e-3
S5_DT_MAX = 1e-1

GLA_HEADS = 4
GLA_DK = D_BRANCH // 2 // GLA_HEADS
GLA_DV = D_BRANCH // GLA_HEADS
GLA_GATE_RANK = 16
GLA_TAU = 16.0
GLA_CHUNK = 64

IN_SPLITS = (MLA_Q_RANK, MLA_KV_RANK, MLA_ROPE, D_BRANCH,
             3 * D_BRANCH, D_BRANCH,
             D_BRANCH, D_BRANCH,
             GLA_HEADS * GLA_DK, GLA_HEADS * GLA_DK, GLA_HEADS * GLA_DV,
             2 * GLA_GATE_RANK, D_BRANCH)
N_IN = sum(IN_SPLITS)

kernel_name = 'hybrid_prefix_diffusion_step'


def split_points():
    return np.cumsum(IN_SPLITS)[:-1].tolist()


def rmsnorm(x, w):
    xf = x.astype(jnp.float32)
    y = xf * lax.rsqrt(jnp.mean(xf * xf, axis=-1, keepdims=True) + EPS)
    return (y * w.astype(jnp.float32)).astype(x.dtype)


def rope_1d(x, pos):
    d = x.shape[-1]
    inv = ROPE_BASE ** (-jnp.arange(0, d, 2, dtype=jnp.float32) / d)
    ang = pos.astype(jnp.float32)[:, None] * inv[None, :]
    ang = jnp.concatenate([ang, ang], axis=-1)[None, :, None, :]
    x1, x2 = jnp.split(x, 2, axis=-1)
    rot = jnp.concatenate([-x2, x1], axis=-1)
    return (x * jnp.cos(ang) + rot * jnp.sin(ang)).astype(x.dtype)


def rope_tail(x, row, col):
    r = x[..., MLA_NOPE:]
    half = MLA_ROPE // 2
    r = jnp.concatenate([rope_1d(r[..., :half], row), rope_1d(r[..., half:], col)], axis=-1)
    return jnp.concatenate([x[..., :MLA_NOPE], r], axis=-1)


def block_attention(q, k, v):
    b, lq, h, dq = q.shape
    nb = lq // Q_BLOCK
    qb = jnp.moveaxis(q.reshape(b, nb, Q_BLOCK, h, dq), 1, 0)

    def one(qi):
        s = jnp.einsum('bqhd,bkhd->bhqk', qi, k).astype(jnp.float32) * (MLA_QK ** -0.5)
        pr = jax.nn.softmax(s, axis=-1).astype(v.dtype)
        return jnp.einsum('bhqk,bkhd->bqhd', pr, v)

    o = lax.map(one, qb)
    return jnp.moveaxis(o, 0, 1).reshape(b, lq, h, v.shape[-1])


def mla_keys_values(c_kv, k_rope, p):
    b, L, _ = c_kv.shape
    kv = (c_kv @ p['mla_w_ukv']).reshape(b, L, MLA_HEADS, MLA_NOPE + MLA_V)
    kr = jnp.broadcast_to(k_rope[:, :, None, :], (b, L, MLA_HEADS, MLA_ROPE)).astype(kv.dtype)
    k = rmsnorm(jnp.concatenate([kv[..., :MLA_NOPE], kr], axis=-1), p['mla_k_norm'])
    return k, kv[..., MLA_NOPE:]


def hyena_filters(L, p):
    pos = jnp.arange(L, dtype=jnp.float32)
    t = pos / L
    w = 2.0 * math.pi * pos / L
    bands = jnp.linspace(1e-4, HY_BANDS - 1, HY_BANDS, dtype=jnp.float32)
    feat = jnp.concatenate([t[:, None], jnp.cos(w[:, None] * bands), jnp.sin(w[:, None] * bands)], axis=-1)
    hid = jnp.sin(p['hy_freq1'] * (feat @ p['hy_w1'] + p['hy_b1']))
    hid = jnp.sin(p['hy_freq2'] * (hid @ p['hy_w2'] + p['hy_b2']))
    filt = (hid @ p['hy_w3']).astype(jnp.float32).reshape(L, 2, HY_ORDER, D_BRANCH)
    deltas = jnp.linspace(math.log(1.0 / HY_TARGET) / HY_FAST_DECAY,
                          math.log(1.0 / HY_TARGET) / HY_SLOW_DECAY, D_BRANCH, dtype=jnp.float32)
    window = jnp.exp(-t[:, None] * deltas[None, :]) + HY_SHIFT
    filt = filt * window[:, None, None, :]
    kern = jnp.concatenate([filt[:, 0], jnp.zeros((1, HY_ORDER, D_BRANCH), jnp.float32),
                            filt[1:, 1][::-1]], axis=0)
    kern = kern / jnp.sum(jnp.abs(kern), axis=0, keepdims=True)
    return jnp.fft.rfft(kern, axis=0)


def fft_conv(u, kf):
    L = u.shape[1]
    uf = jnp.fft.rfft(u, n=2 * L, axis=1)
    return jnp.fft.irfft(uf * kf[None], n=2 * L, axis=1)[:, :L]


def hyena_mixer(z, p):
    L = z.shape[1]
    z = lax.conv_general_dilated(z, p['hy_conv_w'][:, None, :].astype(z.dtype), (1,),
                                 ((HY_SHORT // 2, HY_SHORT // 2),),
                                 dimension_numbers=('NWC', 'WIO', 'NWC'),
                                 feature_group_count=z.shape[-1]) + p['hy_conv_b']
    v, x1, x2 = jnp.split(z.astype(jnp.float32), 3, axis=-1)
    kf = hyena_filters(L, p)
    bias = p['hy_bias'].astype(jnp.float32)
    y = x1 * (fft_conv(v, kf[:, 0]) + bias[0] * v)
    y = x2 * (fft_conv(y, kf[:, 1]) + bias[1] * y)
    return y


def s5_discretise(p, d):
    f32 = jnp.float32
    a = lax.complex(jnp.minimum(p['s5_a_re'][d].astype(f32), -1e-4), p['s5_a_im'][d].astype(f32))
    dt = jnp.exp(p['s5_log_dt'][d].astype(f32))[:, None]
    a_bar = jnp.exp(a * dt)
    bmat = lax.complex(p['s5_b_re'][d].astype(f32), p['s5_b_im'][d].astype(f32))
    b_bar = ((a_bar - 1.0) / a)[..., None] * bmat
    cmat = lax.complex(p['s5_c_re'][d].astype(f32), p['s5_c_im'][d].astype(f32))
    return a_bar, b_bar, cmat


def s5_scan(ug, a_bar, b_bar, h0, reverse):
    bu = jnp.einsum('blgi,gpi->blgp', ug.astype(jnp.complex64), b_bar)
    a = jnp.broadcast_to(a_bar, bu.shape)

    def combine(e1, e2):
        a1, b1 = e1
        a2, b2 = e2
        return a1 * a2, a2 * b1 + b2

    a_cum, h = lax.associative_scan(combine, (a, bu), reverse=reverse, axis=1)
    return h + a_cum * h0[:, None]


def s5_mixer(u, p, h0):
    b, L, _ = u.shape
    uf = u.astype(jnp.float32)
    ug = uf.reshape(b, L, S5_GROUPS, S5_GROUP)
    y = p['s5_d'].astype(jnp.float32) * uf
    finals = []
    for d in range(2):
        a_bar, b_bar, cmat = s5_discretise(p, d)
        h = s5_scan(ug, a_bar, b_bar, h0[:, d], reverse=(d == 1))
        y = y + jnp.real(jnp.einsum('blgp,gip->blgi', h, cmat)).reshape(b, L, D_BRANCH)
        finals.append(h[:, -1] if d == 0 else h[:, 0])
    g = jax.nn.gelu(y)
    out = g * jax.nn.sigmoid(g @ p['s5_glu_w'] + p['s5_glu_b'])
    return out, jnp.stack(finals, axis=1)


def gla_chunked(q, k, v, log_a, s0):
    b, L, h, _ = q.shape
    n = L // GLA_CHUNK

    def to_chunks(x):
        return jnp.moveaxis(x.reshape(b, n, GLA_CHUNK, h, x.shape[-1]), 1, 0)

    mask = jnp.tril(jnp.ones((GLA_CHUNK, GLA_CHUNK), dtype=bool))[None, :, :, None, None]

    def step(s, inp):
        qc, kc, vc, gc = inp
        bc = jnp.cumsum(gc, axis=1)
        diff = bc[:, :, None] - bc[:, None, :]
        decay = jnp.exp(jnp.where(mask, diff, -jnp.inf))
        att = jnp.einsum('btshd,bthd,bshd->bhts', decay, qc, kc)
        o = (jnp.einsum('bhts,bshe->bthe', att, vc)
             + jnp.einsum('bthd,bhde->bthe', qc * jnp.exp(bc), s))
        blast = bc[:, -1]
        s_new = (jnp.exp(blast)[..., None] * s
                 + jnp.einsum('bshd,bshe->bhde', kc * jnp.exp(blast[:, None] - bc), vc))
        return s_new, o

    s_fin, o = lax.scan(step, s0, (to_chunks(q), to_chunks(k), to_chunks(v), to_chunks(log_a)))
    return jnp.moveaxis(o, 0, 1).reshape(b, L, h, v.shape[-1]), s_fin


def gla_mixer(q, k, v, g_lr, p, s0):
    b, L, _ = q.shape
    f32 = jnp.float32
    qh = q.astype(f32).reshape(b, L, GLA_HEADS, GLA_DK) * (GLA_DK ** -0.5)
    kh = k.astype(f32).reshape(b, L, GLA_HEADS, GLA_DK)
    vh = v.astype(f32).reshape(b, L, GLA_HEADS, GLA_DV)
    g_lr = g_lr.astype(f32).reshape(b, L, 2, GLA_GATE_RANK)
    flip = lambda t: t[:, ::-1]
    outs, finals = [], []
    for d in range(2):
        log_a = jax.nn.log_sigmoid(g_lr[:, :, d] @ p['gla_gw'][d] + p['gla_gb'][d]) / GLA_TAU
        log_a = log_a.astype(f32).reshape(b, L, GLA_HEADS, GLA_DK)
        if d == 0:
            od, sf = gla_chunked(qh, kh, vh, log_a, s0[:, d])
        else:
            od, sf = gla_chunked(flip(qh), flip(kh), flip(vh), flip(log_a), s0[:, d])
            od = flip(od)
        outs.append(od)
        finals.append(sf)
    o = rmsnorm(outs[0] + outs[1], p['gla_norm']).reshape(b, L, D_BRANCH)
    return o, jnp.stack(finals, axis=1)


def trunk_layer(x, cond, p, pos=None, ctx=None):
    b, L, _ = x.shape
    f32 = jnp.float32
    mod = jax.nn.silu(cond) @ p['ada_w'] + p['ada_b']
    shift, scale, gate = jnp.split(mod[:, None, :], 3, axis=-1)
    h = rmsnorm(x, p['norm_w']) * (1.0 + scale) + shift
    (c_q, c_kv, k_rope, g_mla, hy_in, g_hy, s5_in, g_s5,
     gla_q, gla_k, gla_v, gla_g, g_gla) = jnp.split(h @ p['w_in'], split_points(), axis=-1)

    c_q = rmsnorm(c_q, p['mla_qa_norm'])
    c_kv = rmsnorm(c_kv, p['mla_kva_norm'])
    q = rmsnorm((c_q @ p['mla_w_uq']).reshape(b, L, MLA_HEADS, MLA_QK), p['mla_q_norm'])
    k, v = mla_keys_values(c_kv, k_rope, p)
    if ctx is None:
        o_mla = block_attention(q, k, v)
        s5_h0 = jnp.zeros((b, 2, S5_GROUPS, S5_STATE), jnp.complex64)
        gla_s0 = jnp.zeros((b, 2, GLA_HEADS, GLA_DK, GLA_DV), f32)
    else:
        ctx_ckv, ctx_krope, ctx_s5, ctx_gla = ctx
        row, col = pos
        k_ctx, v_ctx = mla_keys_values(ctx_ckv, ctx_krope, p)
        q = rope_tail(q, row, col)
        k = rope_tail(k, row, col)
        o_mla = block_attention(q, jnp.concatenate([k_ctx.astype(k.dtype), k], axis=1),
                                jnp.concatenate([v_ctx.astype(v.dtype), v], axis=1))
        s5_h0 = lax.complex(ctx_s5[..., 0].astype(f32), ctx_s5[..., 1].astype(f32))
        gla_s0 = ctx_gla.astype(f32)

    o_hy = hyena_mixer(hy_in, p)
    o_s5, s5_fin = s5_mixer(s5_in, p, s5_h0)
    o_gla, gla_fin = gla_mixer(gla_q, gla_k, gla_v, gla_g, p, gla_s0)

    branches = jnp.concatenate([
        o_mla.reshape(b, L, D_BRANCH) * jax.nn.silu(g_mla),
        o_hy * jax.nn.silu(g_hy),
        o_s5 * jax.nn.silu(g_s5),
        o_gla * jax.nn.silu(g_gla)], axis=-1).astype(x.dtype)
    x = (x + gate * (branches @ p['w_out'])).astype(x.dtype)
    if ctx is None:
        s5_state = jnp.stack([jnp.real(s5_fin), jnp.imag(s5_fin)], axis=-1)
        return x, (c_kv, k_rope, s5_state, gla_fin)
    return x, None


def setup_inputs(seed: int = 0) -> dict:
    key = jax.random.key(seed)
    ks = iter(jax.random.split(key, 48))
    f32 = jnp.float32

    def nrm(shape, scale):
        return jax.random.normal(next(ks), shape, f32) * scale

    def gain(shape):
        return 1.0 + nrm(shape, 0.02)

    a_im = jnp.broadcast_to(math.pi * jnp.arange(S5_STATE, dtype=f32), (DEPTH, 2, S5_GROUPS, S5_STATE))
    return {
        'x_prompt': nrm((BATCH, SEQ, D_MODEL), 1.0),
        'x_sample': nrm((DEC_BATCH, DEC_SEQ, D_MODEL), 1.0),
        'c': nrm((DEC_BATCH, D_MODEL), 1.0),
        'cache_mla_ckv': nrm((DEC_BATCH, DEPTH, PAST_LEN, MLA_KV_RANK), 1.0),
        'cache_mla_krope': nrm((DEC_BATCH, DEPTH, PAST_LEN, MLA_ROPE), 1.0),
        'state_s5': nrm((DEC_BATCH, DEPTH, 2, S5_GROUPS, S5_STATE, 2), 1.0),
        'state_gla': nrm((DEC_BATCH, DEPTH, 2, GLA_HEADS, GLA_DK, GLA_DV), 1.0),
        'c_ctx': nrm((D_MODEL,), 1.0),
        'norm_w': gain((DEPTH, D_MODEL)),
        'ada_w': nrm((DEPTH, D_MODEL, 3 * D_MODEL), 0.5 * D_MODEL ** -0.5),
        'ada_b': nrm((DEPTH, 3 * D_MODEL), 0.02),
        'w_in': nrm((DEPTH, D_MODEL, N_IN), D_MODEL ** -0.5),
        'w_out': nrm((DEPTH, D_MIX, D_MODEL), D_MIX ** -0.5),
        'mla_qa_norm': gain((DEPTH, MLA_Q_RANK)),
        'mla_kva_norm': gain((DEPTH, MLA_KV_RANK)),
        'mla_w_uq': nrm((DEPTH, MLA_Q_RANK, MLA_HEADS * MLA_QK), MLA_Q_RANK ** -0.5),
        'mla_w_ukv': nrm((DEPTH, MLA_KV_RANK, MLA_HEADS * (MLA_NOPE + MLA_V)), MLA_KV_RANK ** -0.5),
        'mla_q_norm': gain((DEPTH, MLA_QK)),
        'mla_k_norm': gain((DEPTH, MLA_QK)),
        'hy_conv_w': nrm((DEPTH, HY_SHORT, 3 * D_BRANCH), HY_SHORT ** -0.5),
        'hy_conv_b': nrm((DEPTH, 3 * D_BRANCH), 0.02),
        'hy_w1': nrm((DEPTH, HY_FEAT, HY_HIDDEN), HY_FEAT ** -0.5),
        'hy_b1': nrm((DEPTH, HY_HIDDEN), 0.02),
        'hy_freq1': gain((DEPTH, HY_HIDDEN)),
        'hy_w2': nrm((DEPTH, HY_HIDDEN, HY_HIDDEN), HY_HIDDEN ** -0.5),
        'hy_b2': nrm((DEPTH, HY_HIDDEN), 0.02),
        'hy_freq2': gain((DEPTH, HY_HIDDEN)),
        'hy_w3': nrm((DEPTH, HY_HIDDEN, 2 * HY_ORDER * D_BRANCH), HY_HIDDEN ** -0.5),
        'hy_bias': nrm((DEPTH, HY_ORDER, D_BRANCH), 1.0),
        's5_a_re': -0.5 + nrm((DEPTH, 2, S5_GROUPS, S5_STATE), 0.01),
        's5_a_im': a_im + nrm((DEPTH, 2, S5_GROUPS, S5_STATE), 0.01),
        's5_log_dt': jax.random.uniform(next(ks), (DEPTH, 2, S5_GROUPS), f32,
                                        math.log(S5_DT_MIN), math.log(S5_DT_MAX)),
        's5_b_re': nrm((DEPTH, 2, S5_GROUPS, S5_STATE, S5_GROUP), (2 * S5_GROUP) ** -0.5),
        's5_b_im': nrm((DEPTH, 2, S5_GROUPS, S5_STATE, S5_GROUP), (2 * S5_GROUP) ** -0.5),
        's5_c_re': nrm((DEPTH, 2, S5_GROUPS, S5_GROUP, S5_STATE), (2 * S5_STATE) ** -0.5),
        's5_c_im': nrm((DEPTH, 2, S5_GROUPS, S5_GROUP, S5_STATE), (2 * S5_STATE) ** -0.5),
        's5_d': nrm((DEPTH, D_BRANCH), 1.0),
        's5_glu_w': nrm((DEPTH, D_BRANCH, D_BRANCH), D_BRANCH ** -0.5),
        's5_glu_b': nrm((DEPTH, D_BRANCH), 0.02),
        'gla_gw': nrm((DEPTH, 2, GLA_GATE_RANK, GLA_HEADS * GLA_DK), GLA_GATE_RANK ** -0.5),
        'gla_gb': nrm((DEPTH, 2, GLA_HEADS * GLA_DK), 0.02),
        'gla_norm': gain((DEPTH, GLA_DV)),
    }


def reference(x_prompt, x_sample, c, cache_mla_ckv, cache_mla_krope, state_s5, state_gla, c_ctx,
              norm_w, ada_w, ada_b, w_in, w_out,
              mla_qa_norm, mla_kva_norm, mla_w_uq, mla_w_ukv, mla_q_norm, mla_k_norm,
              hy_conv_w, hy_conv_b, hy_w1, hy_b1, hy_freq1, hy_w2, hy_b2, hy_freq2, hy_w3, hy_bias,
              s5_a_re, s5_a_im, s5_log_dt, s5_b_re, s5_b_im, s5_c_re, s5_c_im, s5_d, s5_glu_w, s5_glu_b,
              gla_gw, gla_gb, gla_norm):
    weights = {
        'norm_w': norm_w, 'ada_w': ada_w, 'ada_b': ada_b, 'w_in': w_in, 'w_out': w_out,
        'mla_qa_norm': mla_qa_norm, 'mla_kva_norm': mla_kva_norm, 'mla_w_uq': mla_w_uq,
        'mla_w_ukv': mla_w_ukv, 'mla_q_norm': mla_q_norm, 'mla_k_norm': mla_k_norm,
        'hy_conv_w': hy_conv_w, 'hy_conv_b': hy_conv_b, 'hy_w1': hy_w1, 'hy_b1': hy_b1,
        'hy_freq1': hy_freq1, 'hy_w2': hy_w2, 'hy_b2': hy_b2, 'hy_freq2': hy_freq2,
        'hy_w3': hy_w3, 'hy_bias': hy_bias,
        's5_a_re': s5_a_re, 's5_a_im': s5_a_im, 's5_log_dt': s5_log_dt, 's5_b_re': s5_b_re,
        's5_b_im': s5_b_im, 's5_c_re': s5_c_re, 's5_c_im': s5_c_im, 's5_d': s5_d,
        's5_glu_w': s5_glu_w, 's5_glu_b': s5_glu_b,
        'gla_gw': gla_gw, 'gla_gb': gla_gb, 'gla_norm': gla_norm,
    }
    n_rows = x_sample.shape[1] // GRID_W
    row = jnp.repeat(jnp.arange(n_rows), GRID_W)
    col = jnp.tile(jnp.arange(GRID_W), n_rows)

    y_prompt, y_sample = x_prompt, x_sample
    ckv_l, krope_l, s5_l, gla_l = [], [], [], []
    for l in range(DEPTH):
        p = {name: w[l] for name, w in weights.items()}
        y_prompt, (ckv, krope, s5s, glas) = trunk_layer(y_prompt, c_ctx[None, :], p)
        ckv_l.append(ckv)
        krope_l.append(krope)
        s5_l.append(s5s)
        gla_l.append(glas)
        y_sample, _ = trunk_layer(y_sample, c, p, pos=(row, col),
                                  ctx=(cache_mla_ckv[:, l], cache_mla_krope[:, l],
                                       state_s5[:, l], state_gla[:, l]))
    new_mla_ckv = jnp.stack(ckv_l, axis=1)
    new_mla_krope = jnp.stack(krope_l, axis=1)
    new_s5 = jnp.stack(s5_l, axis=1)
    new_gla = jnp.stack(gla_l, axis=1)
    return (y_prompt, y_sample, new_mla_ckv, new_mla_krope, new_s5, new_gla)
```

```python
import contextlib
import math

import numpy as np
import ml_dtypes

import concourse.bass as bass
import concourse.mybir as mybir
from concourse.bass_utils import run_bass_kernel_spmd

F32 = mybir.dt.float32
BF16 = mybir.dt.bfloat16
AF = mybir.ActivationFunctionType
ALU = mybir.AluOpType
AX = mybir.AxisListType

D = 1024
DEPTH = 2
NIN = 2944
NINU = 608
EPS = 1e-6
SEQ = 256
NPS = 4
TP = NPS * SEQ
LS = 2048
PAST = 512
H = 4
QK = 96
GLA_DK = 32
GC = 646
SC = 1025
HC = 192
HY0 = 608
S50 = 1632
GLA0 = 2144
SAFE_SAME_ENGINE = True


class Buf:
    __slots__ = ("w", "r", "x")

    def __init__(self, exclusive=False):
        self.w = None
        self.r = {}
        self.x = exclusive


class Tile:
    def __init__(self, t, n=1):
        self.t = t
        self.b = Buf()
        self.bs = [Buf() for _ in range(n)]

    def __getitem__(self, k):
        return self.t[k]


class KB:
    def __init__(self, nc, es):
        self.nc, self.es = nc, es
        self.E = {"pe": nc.tensor, "act": nc.scalar, "dve": nc.vector, "pool": nc.gpsimd, "sp": nc.sync}
        self.sem = {k: es.enter_context(nc.semaphore("s_" + k)) for k in ("pe", "act", "dve", "pool")}
        self.NDS = 16
        for i in range(self.NDS):
            self.sem["d%d" % i] = es.enter_context(nc.semaphore("s_d%d" % i))
        self.cnt = dict.fromkeys(self.sem, 0)
        self.ndma = 0
        self.ndma_pool = 0
        self.seen = {}
        self.n = 0

    def sb(self, shape, dt, n=1):
        self.n += 1
        if getattr(self, "arena", None) is not None and self.in_arena:
            return self._carve(list(shape), dt)
        return Tile(self.es.enter_context(self.nc.sbuf_tensor("sb%d" % self.n, list(shape), dt)), n)

    def make_arena(self, ncol_f32):
        self.arena = self.es.enter_context(self.nc.sbuf_tensor("arena", [128, ncol_f32], F32))
        self.arena_cols = ncol_f32
        self.in_arena = False
        self.aoff = 0

    def phase(self, name):
        self.in_arena = name is not None
        self.aoff = 0

    def _carve(self, shape, dt):
        nel = 1
        for d_ in shape[1:]:
            nel *= d_
        ncol = nel if dt == F32 else (nel + 1) // 2
        ncol = (ncol + 7) // 8 * 8
        assert self.aoff + ncol <= self.arena_cols, ("arena overflow", self.aoff, ncol, shape)
        ap = self.arena[0:shape[0], self.aoff:self.aoff + ncol]
        self.aoff += ncol
        if dt != F32:
            ap = ap.bitcast(dt)
        ap = ap[:, 0:nel]
        if len(shape) == 3:
            ap = ap.rearrange("p (a b) -> p a b", a=shape[1])
        elif len(shape) == 4:
            ap = ap.rearrange("p (a b c) -> p a b c", a=shape[1], b=shape[2])
        return Tile(ap)

    def barrier(self):
        for eng, E in self.E.items():
            for k, c in self.cnt.items():
                if c and k != eng and self.seen.get((eng, k), 0) < c:
                    E.wait_ge(self.sem[k], c)
                    self.seen[(eng, k)] = c

    def ps(self, shape, dt):
        self.n += 1
        t = Tile(self.es.enter_context(self.nc.psum_tensor("ps%d" % self.n, list(shape), dt)))
        t.b.x = True
        return t

    def op(self, eng, fn, reads=(), writes=(), dma=False):
        deps = {}
        for b in reads:
            if b.w:
                deps[b.w[0]] = max(deps.get(b.w[0], 0), b.w[1])
            if b.x:
                for k, c in b.r.items():
                    if k != eng:
                        deps[k] = max(deps.get(k, 0), c)
        for b in writes:
            if b.w:
                deps[b.w[0]] = max(deps.get(b.w[0], 0), b.w[1])
            for k, c in b.r.items():
                deps[k] = max(deps.get(k, 0), c)
        E = self.E[eng]
        if dma:
            if eng == "pool":
                key = "d%d" % (12 + self.ndma_pool % 4)
                self.ndma_pool += 1
            else:
                key = "d%d" % (self.ndma % 12)
                self.ndma += 1
            inc = 16
            if self.cnt[key] and self.seen.get((eng, key), 0) < self.cnt[key]:
                E.wait_ge(self.sem[key], self.cnt[key])
                self.seen[(eng, key)] = self.cnt[key]
        else:
            key, inc = eng, 1
        for pk, c in deps.items():
            if pk == key and (eng == "pe" or not SAFE_SAME_ENGINE):
                continue
            if self.seen.get((eng, pk), 0) >= c:
                continue
            E.wait_ge(self.sem[pk], c)
            self.seen[(eng, pk)] = c
        inst = fn(E)
        self.cnt[key] += inc
        inst.then_inc(self.sem[key], inc)
        c = self.cnt[key]
        for b in reads:
            b.r[key] = c
        for b in writes:
            b.w = (key, c)
            b.r = {}

    def dma(self, out, in_, reads=(), writes=(), q="sp", **kw):
        self.op(q, lambda e: e.dma_start(out=out, in_=in_, **kw), reads, writes, dma=True)

    def finish(self):
        sp = self.E["sp"]
        for k in self.sem:
            if self.cnt[k]:
                sp.wait_ge(self.sem[k], self.cnt[k])


def _b(x):
    return x.b if isinstance(x, Tile) else x


class _Stop(Exception):
    pass


def build_program(stop=None):
    def chk(name):
        if stop == name:
            raise _Stop(name)

    nc = bass.Bass("TRN2", target_bir_lowering=False)
    es = contextlib.ExitStack()

    def din(name, shape, dt=F32):
        return nc.dram_tensor(name, list(shape), dt, kind="ExternalInput").ap()

    def dout(name, shape):
        return nc.dram_tensor(name, list(shape), F32, kind="ExternalOutput").ap()

    xp = din("xp", [TP, D])
    xs = din("xs", [LS, D])
    cond = din("cond", [2, D])
    cckv = din("cckv", [DEPTH, PAST, 128])
    ckro = din("ckro", [DEPTH, PAST, 32])
    ropec = din("ropec", [LS, 32])
    ropes = din("ropes", [LS, 32])
    norm_w = din("norm_w", [DEPTH, D])
    ada_w = din("ada_w", [DEPTH, D, 3 * D])
    ada_b = din("ada_b", [DEPTH, 3 * D])
    w_in = din("w_in", [DEPTH, D, NIN])
    w_out = din("w_out", [DEPTH, D, D])
    qa_n = din("mla_qa_norm", [DEPTH, 192])
    kva_n = din("mla_kva_norm", [DEPTH, 128])
    w_uq = din("mla_w_uq", [DEPTH, 192, 384])
    w_ukv = din("mla_w_ukv", [DEPTH, 128, 512])
    q_n = din("mla_q_norm", [DEPTH, 96])
    k_n = din("mla_k_norm", [DEPTH, 96])
    gla_gw = din("gla_gw", [DEPTH, 2, 16, 128])
    gla_gb = din("gla_gb", [DEPTH, 2, 128])
    gla_nw = din("gla_norm", [DEPTH, 64])
    sgla = din("sgla", [DEPTH, 2, 128, 64])
    gcon = din("gcon", [128, GC])
    s5_are = din("s5_a_re", [DEPTH, 2, 16, 64])
    s5_aim = din("s5_a_im", [DEPTH, 2, 16, 64])
    s5_ldt = din("s5_log_dt", [DEPTH, 2, 16])
    s5_bre = din("s5_b_re", [DEPTH, 2, 16, 64, 16])
    s5_bim = din("s5_b_im", [DEPTH, 2, 16, 64, 16])
    s5_cre = din("s5_c_re", [DEPTH, 2, 16, 16, 64])
    s5_cim = din("s5_c_im", [DEPTH, 2, 16, 16, 64])
    s5_dd = din("s5_d", [DEPTH, 256])
    s5_gw = din("s5_glu_w", [DEPTH, 256, 256])
    s5_gb = din("s5_glu_b", [DEPTH, 256])
    ss5 = din("ss5", [DEPTH, 2, 1024, 2])
    scon = din("scon", [128, SC])
    hy_cw = din("hy_conv_w", [DEPTH, 3, 768])
    hy_cb = din("hy_conv_b", [DEPTH, 768])
    hy_w1 = din("hy_w1", [DEPTH, 33, 64])
    hy_b1 = din("hy_b1", [DEPTH, 64])
    hy_f1 = din("hy_freq1", [DEPTH, 64])
    hy_w2 = din("hy_w2", [DEPTH, 64, 64])
    hy_b2 = din("hy_b2", [DEPTH, 64])
    hy_f2 = din("hy_freq2", [DEPTH, 64])
    hy_w3 = din("hy_w3", [DEPTH, 64, 1024])
    hy_bi = din("hy_bias", [DEPTH, 2, 256])
    htab = {}
    for nm_, L_ in (("p", SEQ), ("s", LS)):
        htab[nm_] = dict(cos=din("hcos_" + nm_, [L_, L_], BF16), sin=din("hsin_" + nm_, [L_, L_], BF16),
                         feat=din("hfeat_" + nm_, [33, L_], BF16), win=din("hwin_" + nm_, [L_, 256]))
    hcon = din("hcon", [128, HC])
    haltr = din("haltr", [1, 512], BF16)

    yp = dout("yp", [TP, D])
    ys = dout("ys", [LS, D])
    o_ckv = dout("o_ckv", [NPS, DEPTH, SEQ, 128])
    o_kro = dout("o_kro", [NPS, DEPTH, SEQ, 32])
    o_s5 = dout("o_s5", [NPS, DEPTH, 2, 16, 64, 2])
    o_gla = dout("o_gla", [NPS, DEPTH, 2, 4, 32, 64])
    dbg = dout("dbg", [128, 2048]) if stop else None

    xmid_p = nc.dram_tensor("xmid_p", [TP, D], F32, kind="Internal").ap()
    xmid_s = nc.dram_tensor("xmid_s", [LS, D], F32, kind="Internal").ap()
    xmid_p_b, xmid_s_b = Buf(), Buf()
    s5w = nc.dram_tensor("s5w", [16, 128, 34 * 128], BF16, kind="Internal").ap()
    s5k = nc.dram_tensor("s5k", [2, 128, 16 * 128], BF16, kind="Internal").ap()
    s5w_b, s5k_b = Buf(), Buf()
    wc = {n_: (nc.dram_tensor("wc_" + n_, sh_, BF16, kind="Internal").ap(), Buf()) for n_, sh_ in (
        ("in", [128, 8 * NINU]), ("gla", [128, 8 * 800]), ("s5", [128, 8 * 512]), ("hy0", [128, 8 * 512]), ("hy1", [128, 8 * 512]),
        ("w30", [64, 512]), ("w31", [64, 512]), ("out", [128, 8 * D]))}

    with es:
        kb = KB(nc, es)
        sb, ps, op, dma = kb.sb, kb.ps, kb.op, kb.dma

        ident_f = sb([128, 128], F32)
        ident_b = sb([128, 128], BF16)
        ones_f = sb([1, 128], F32)
        eps_c = sb([128, 1], F32)
        zero_f = sb([128, 512], BF16)
        op("pool", lambda e: e.memset(ident_f[:], 1.0), writes=[ident_f.b])
        op("pool", lambda e: e.affine_select(out=ident_f[:], in_=ident_f[:], pattern=[[-1, 128]],
                                             compare_op=ALU.is_equal, fill=0.0, base=0,
                                             channel_multiplier=1), reads=[ident_f.b], writes=[ident_f.b])
        op("pool", lambda e: e.tensor_copy(out=ident_b[:], in_=ident_f[:]), reads=[ident_f.b], writes=[ident_b.b])
        op("pool", lambda e: e.memset(ones_f[:], 1.0), writes=[ones_f.b])
        op("pool", lambda e: e.memset(eps_c[:], EPS), writes=[eps_c.b])
        op("pool", lambda e: e.memset(zero_f[:], 0.0), writes=[zero_f.b])

        pf = [ps([128, 512], F32) for _ in range(6)]
        pb = [ps([128, 1024], BF16) for _ in range(2)]
        rr = {"f": 0, "b": 0}

        def psf():
            rr["f"] = (rr["f"] + 1) % 5
            return pf[rr["f"]]

        p_acc = pf[5]

        def psb():
            rr["b"] = (rr["b"] + 1) % len(pb)
            return pb[rr["b"]]

        qT = sb([128, H, LS], BF16)
        kT = sb([128, H, LS + PAST], BF16)
        vaug = sb([128, (LS + PAST) // 128, H, 66], BF16)
        hT = sb([128, 8, LS], BF16)
        brT = sb([128, 8, LS], BF16)
        w_uq_a = sb([128, 384], BF16)
        w_uq_c = sb([64, 384], BF16)
        w_ukv_b = sb([128, 512], BF16)
        bc_qa = sb([128, 192], F32)
        bc_kva = sb([128, 128], F32)
        bc_qn = sb([128, 96], F32)
        bc_kn = sb([128, 96], F32)
        gate_bc = [sb([128, D], F32) for _ in range(2)]
        effsc = sb([128, 2, 8], F32)
        shift = sb([128, 2, 8], F32)
        rowst = sb([1, D], F32)
        scond = sb([8, 2, 128], F32)
        scondT = sb([128, 2, 8], F32)
        screp = sb([128, 128], F32)
        colst = sb([24, 128], F32)
        colsT = sb([128, 3, 8], F32)
        nwst = sb([8, 128], F32)
        nwT = sb([128, 8], F32)
        modacc = sb([128, 32], F32)

        class _Alias:
            def __init__(self, ap, owner):
                self.ap, self.b = ap, owner.b

            def __getitem__(self, k):
                return self.ap[k]

        wstage = [_Alias(qT.t[:].bitcast(F32).rearrange("p a c -> p (a c)")[:, 0:3 * D], qT)] * 2
        adast = wstage
        w_out_b = _Alias(kT.t[:].rearrange("p h t -> p (h t)")[:, 0:8 * D].rearrange("p (k c) -> p k c", k=8), kT)
        wmix = _Alias(kT.t[:].rearrange("p h t -> p (h t)")[:, 0:6400].rearrange("p (k c) -> p k c", k=8), kT)
        gcon_t = sb([128, GC], F32)
        dma(gcon_t[:, :], gcon[:, :], writes=[gcon_t.b])
        maskF, maskB_ = gcon_t[:, 0:128], gcon_t[:, 128:256]
        headm, halfm, blockm, ones64 = gcon_t[:, 256:260], gcon_t[:, 260:262], gcon_t[:, 262:518], gcon_t[:, 518:582]
        gwp_f = sb([32, 2, 128], F32)
        gwp = sb([32, 2, 128], BF16)
        gbst = sb([2, 128], F32)
        ngb = sb([128, 2], F32)
        bc_gn = sb([128, 64], F32)
        kb.make_arena(10496)
        kb.phase("GLA")
        o_bwd = _Alias(qT.t[:].bitcast(F32).rearrange("p a (b c) -> p (a b) c", c=256), qT)
        g_qf, g_kf, g_sp, g_bcp, g_e1, g_e2, g_e3 = [sb([128, 128], F32) for _ in range(7)]
        g_glr = sb([32, 128], BF16)
        g_vt = sb([128, 256], BF16)
        g_qt, g_kt, g_khT, g_khA, g_khB, g_attm = [sb([128, 128], BF16) for _ in range(6)]
        g_km = [sb([128, 128], BF16) for _ in range(4)]
        g_QA = [sb([128, 128], BF16) for _ in range(4)]
        g_QB = [sb([128, 128], BF16) for _ in range(4)]
        g_nbl = sb([128, 2], F32)
        g_tot = sb([128, 2], F32)
        g_eb = sb([128, 2], F32)
        g_dS = sb([128, 256], F32)
        g_dSr = [sb([128, 64], F32) for _ in range(2)]
        g_S = [sb([128, 64], F32) for _ in range(3)]
        g_Sb = [sb([128, 64], BF16) for _ in range(2)]
        g_os = sb([128, 256], F32)
        g_sq = sb([128, 256], F32)
        g_ss = sb([128, 4], F32)
        g_rs = sb([128, 4], F32)
        g_sil = sb([128, 256], F32)
        g_og = sb([128, 256], BF16)

        kflat = kT.t[:].rearrange("p h t -> p (h t)")
        vflat = vaug.t[:].rearrange("p a h d -> p (a h d)")
        s_wmix = _Alias(kflat[:, 0:4096].rearrange("p (k c) -> p k c", k=8), kT)
        s_uT = _Alias(kflat[:, 4096:8192].rearrange("p (h t) -> p h t", h=2), kT)
        s_Kbd = _Alias(kflat[:, 8192:10240].rearrange("p (a c) -> p a c", a=16), kT)
        s_gT = _Alias(vflat[:, 0:4096].rearrange("p (h t) -> p h t", h=2), vaug)
        s_Kacc = _Alias(qT.t[:].bitcast(F32).rearrange("p a c -> p (a c)")[:, 0:2048].rearrange("p (a c) -> p a c", a=16), qT)
        kb.phase(None)
        scon_t = sb([128, SC], F32)
        dma(scon_t[:, :], scon[:, :], writes=[scon_t.b])
        s_prst = sb([16, 128], F32)
        s_ldst = sb([16, 2], F32)
        s_par = {n_: sb([128, 16], F32) for n_ in ("are", "aim", "dt", "lr", "th", "er", "ei", "t1", "t2", "ar1", "ai1", "nai1",
                                                    "kr", "ki", "nki", "dr", "den")}
        s_pwr = sb([128, 8, 16], F32)
        s_pwi = sb([128, 8, 16], F32)
        s_npwi = sb([128, 8, 16], F32)
        s_dcol = sb([128, 2], F32)
        s_gbcol = sb([128, 2], F32)
        s_st2 = sb([2, 128], F32)
        s_Dd = [sb([128, 128], BF16) for _ in range(2)]
        s_glu = sb([128, 2, 256], BF16)
        kb.phase("S5")
        s_B = [sb([128, 16], F32) for _ in range(2)]
        s_bb = [sb([128, 16], F32) for _ in range(2)]
        s_Xm = [[sb([128, 128], F32) for _ in range(2)] for _ in range(2)]
        s_Xb = [[sb([128, 128], BF16) for _ in range(2)] for _ in range(8)]
        s_W = sb([128, 34, 128], BF16)

        class _View:
            def __init__(self, ap, b):
                self.ap, self.b = ap, b

            def __getitem__(self, k):
                return self.ap[k]

        s_Wb = [[_View(s_W[:, 2 * k_ + r_, :], s_W.b) for r_ in range(2)] for k_ in range(8)]
        s_C = [sb([128, 64], F32) for _ in range(2)]
        s_Cp = sb([128, 128], F32)
        s_Cm = [[sb([128, 128], F32) for _ in range(2)] for _ in range(2)]
        s_Wc = [[_View(s_W[:, 16 + 2 * k_ + r_, :], s_W.b) for r_ in range(2)] for k_ in range(9)]
        s_XA = [sb([128, 256], F32) for _ in range(2)]
        s_XB = [sb([128, 256], F32) for _ in range(2)]
        s_Hs = [sb([128, 4 * 33 + 257], BF16) for _ in range(2)]
        s_h0 = sb([128, 2], F32)
        s_fin = sb([128, 2], F32)
        s_tmp = sb([128, 4], F32)
        s_ya = sb([128, 512], F32)
        s_yb = sb([128, 512], F32)
        kb.phase(None)

        def bcast_row(dst, src_row_ap, n):
            dma(rowst[0:1, 0:n], src_row_ap, writes=[rowst.b])
            for c0 in range(0, n, 512):
                w = min(512, n - c0)
                p = psf()
                op("pe", lambda e: e.matmul(p[:, 0:w], lhsT=ones_f[0:1, :], rhs=rowst[0:1, c0:c0 + w],
                                            start=True, stop=True),
                   reads=[ones_f.b, rowst.b], writes=[p.b])
                op("dve", lambda e: e.tensor_copy(out=dst[:, c0:c0 + w], in_=p[:, 0:w]),
                   reads=[p.b], writes=[dst.b])

        def rstd_from_ss(dst, ss, n):
            op("act", lambda e: e.activation(out=dst[:], in_=ss[:], func=AF.Sqrt, scale=1.0 / n,
                                             bias=eps_c[:, 0:1]), reads=[ss.b, eps_c.b], writes=[dst.b])
            op("dve", lambda e: e.reciprocal(out=dst[:], in_=dst[:]), reads=[dst.b], writes=[dst.b])

        hcon_t = sb([128, HC], F32)
        dma(hcon_t[:, :], hcon[:, :], writes=[hcon_t.b])
        haltr_t = sb([1, 512], BF16)
        dma(haltr_t[:, :], haltr[:, :], writes=[haltr_t.b])
        h_alt = sb([128, 128], BF16)
        op("dve", lambda e: e.tensor_copy(out=h_alt[:], in_=hcon_t[:, 0:128]), reads=[hcon_t.b], writes=[h_alt.b])
        ones_sq = sb([128, 128], F32)
        op("pool", lambda e: e.memset(ones_sq[:], 1.0), writes=[ones_sq.b])
        hw1_b = sb([33, 64], BF16)
        hw2_b = sb([64, 64], BF16)
        h_mrow = sb([4, 64], F32)
        h_mcol = sb([64, 4], F32)
        h_mlp = sb([64, 8], F32)
        h_rows = sb([14, 128], F32)
        h_colp = sb([128, 14], F32)
        h_w3 = sb([64, 4, 128], BF16)
        h_win = sb([128, 128], F32)
        h_feat = sb([33, 512], BF16)
        h_cr = [sb([128, 512], BF16) for _ in range(2)]
        h_sr = [sb([128, 512], BF16) for _ in range(2)]
        h_sig = _Alias(qT.t[:], qT)
        class _Sub(_Alias):
            def __init__(self, ap):
                self.ap, self.b = ap, Buf()

        h_tc = [_Sub(kflat[:, 2048 * i_:2048 * i_ + 2048].rearrange("p (a j) -> p a j", a=16)) for i_ in (0, 1)]
        h_ts = [_Sub(kflat[:, 2048 * i_:2048 * i_ + 2048].rearrange("p (a j) -> p a j", a=16)) for i_ in (2, 3)]
        h_utm = _Sub(kflat[:, 8192:10240].rearrange("p (a j) -> p a j", a=16))
        h_wmix = _Alias(vflat[:, 0:4096].rearrange("p (k c) -> p k c", k=8), vaug)
        kb.phase("HY")
        h_hid = [sb([64, LS], BF16) for _ in range(2)]
        h_ksd = sb([128, 16, 4, 128], BF16)
        h_Y = sb([128, 17, 2, 128], BF16)
        h_k = [sb([128, 128], F32) for _ in range(2)]
        h_t = [sb([128, 128], F32) for _ in range(4)]
        h_rn = sb([128, 2, 128], F32)
        kb.phase(None)
        h_fw = sb([128, 512], F32)
        h_abs = sb([128, 512], F32)
        h_s2 = sb([64, 512], F32)
        h_s4 = sb([64, 512], F32)

        kb.phase("PA")
        w_in_b = sb([128, 8, NINU], BF16)
        xt = sb([128, D], F32)
        xn = sb([128, D], BF16)
        junk = xn
        ss1 = sb([128, 1], F32)
        rs1 = sb([128, 1], F32)
        lat = sb([128, 352], F32)
        junk2 = sb([128, 192], BF16)
        gm = sb([128, 256], F32)
        ssq = sb([128, 2], F32)
        rsq = sb([128, 2], F32)
        cqn = sb([128, 192], BF16)
        ckvn = sb([128, 128], F32)
        ckvn_b = sb([128, 128], BF16)
        kro = sb([128, 32], F32)
        cqT = sb([128, 2, 128], BF16)
        ckvT = sb([128, 128], BF16)
        qpre = sb([128, H, QK], F32)
        kcat = sb([128, H, QK], F32)
        sqh = sb([128, H, QK], F32)
        ssh = sb([128, H], F32)
        rsh = sb([128, H], F32)
        qf = sb([128, H, QK], F32)
        qb = sb([128, H, QK], BF16)
        rtmp = sb([128, H, 32], F32)
        rtmp2 = sb([128, H, 32], F32)
        cosT = sb([128, 32], F32)
        sinT = sb([128, 32], F32)
        pT2 = [sb([128, 512], BF16) for _ in range(2)]
        omla4 = sb([128, 4, H, 64], F32)
        den = sb([128, H], F32)
        omla = sb([128, H, 64], F32)
        omg = sb([128, 256], BF16)
        silg = sb([128, 256], F32)
        ot = sb([128, D], F32)
        kb.phase(None)

        try:
          for l in range(DEPTH):
              st = wstage[0]
              dma(st[:, 0:384], w_uq[l, 0:128, :], writes=[st.b], q="pool")
              op("dve", lambda e: e.tensor_copy(out=w_uq_a[:], in_=st[:, 0:384]), reads=[st.b], writes=[w_uq_a.b])
              st = wstage[1]
              dma(st[0:64, 0:384], w_uq[l, 128:192, :], writes=[st.b], q="pool")
              op("dve", lambda e: e.tensor_copy(out=w_uq_c[:], in_=st[0:64, 0:384]), reads=[st.b], writes=[w_uq_c.b])
              st = wstage[0]
              dma(st[:, 0:512], w_ukv[l, :, :], writes=[st.b], q="pool")
              op("dve", lambda e: e.tensor_copy(out=w_ukv_b[:], in_=st[:, 0:512]), reads=[st.b], writes=[w_ukv_b.b])
              bcast_row(bc_qa, qa_n[l:l + 1, :], 192)
              bcast_row(bc_kva, kva_n[l:l + 1, :], 128)
              bcast_row(bc_qn, q_n[l:l + 1, :], 96)
              bcast_row(bc_kn, k_n[l:l + 1, :], 96)
              bcast_row(bc_gn, gla_nw[l:l + 1, :], 64)
              op("dve", lambda e: e.memset(gwp_f[:], 0.0), writes=[gwp_f.b])
              dma(gwp_f[0:16, 0, :], gla_gw[l, 0, :, :], reads=[gwp_f.b], writes=[gwp_f.b])
              dma(gwp_f[16:32, 1, :], gla_gw[l, 1, :, :], reads=[gwp_f.b], writes=[gwp_f.b])
              op("dve", lambda e: e.tensor_copy(out=gwp[:], in_=gwp_f[:]), reads=[gwp_f.b], writes=[gwp.b])
              dma(gbst[:, :], gla_gb[l, :, :], writes=[gbst.b])
              p = psf()
              op("pe", lambda e: e.transpose(p[:, 0:2], gbst[:, :], ident_f[0:2, 0:2]), reads=[gbst.b, ident_f.b], writes=[p.b])
              op("dve", lambda e: e.tensor_scalar(out=ngb[:], in0=p[:, 0:2], scalar1=-1.0, scalar2=None, op0=ALU.mult),
                 reads=[p.b], writes=[ngb.b])

              P_ = s_par

              def colparam(dst, rows_ap, nrow=16):
                  dma(s_prst[0:nrow, :], rows_ap, writes=[s_prst.b])
                  p_ = psf()
                  op("pe", lambda e: e.transpose(p_[:, 0:nrow], s_prst[0:nrow, :], ident_f[0:nrow, 0:nrow]),
                     reads=[s_prst.b, ident_f.b], writes=[p_.b])
                  op("dve", lambda e: e.tensor_copy(out=dst, in_=p_[:, 0:nrow]), reads=[p_.b], writes=[s_par_b])

              s_par_b = P_["are"].b
              for t_ in P_.values():
                  t_.b = s_par_b
              s_pwr.b = s_pwi.b = s_npwi.b = s_par_b
              colparam(P_["are"][:], s5_are[l].rearrange("d (st gl) p -> (d st) (gl p)", gl=2))
              colparam(P_["aim"][:], s5_aim[l].rearrange("d (st gl) p -> (d st) (gl p)", gl=2))
              dma(s_ldst[:, :], s5_ldt[l].rearrange("d (st gl) -> (d st) gl", gl=2), writes=[s_ldst.b])
              op("dve", lambda e: e.tensor_copy(out=s_prst[:, :].rearrange("r (gl p) -> r gl p", gl=2),
                                                in_=s_ldst[:, :].unsqueeze(2).to_broadcast([16, 2, 64])),
                 reads=[s_ldst.b], writes=[s_prst.b])
              p_ = psf()
              op("pe", lambda e: e.transpose(p_[:, 0:16], s_prst[:, :], ident_f[0:16, 0:16]), reads=[s_prst.b, ident_f.b], writes=[p_.b])
              op("act", lambda e: e.activation(out=P_["dt"][:], in_=p_[:, 0:16], func=AF.Exp), reads=[p_.b], writes=[s_par_b])

              def pp(eng, fn):
                  op(eng, fn, reads=[s_par_b, scon_t.b], writes=[s_par_b])

              are, aim, dt_, lr, th, er, ei, t1, t2 = (P_[n_] for n_ in ("are", "aim", "dt", "lr", "th", "er", "ei", "t1", "t2"))
              pp("dve", lambda e: e.tensor_scalar(out=are[:], in0=are[:], scalar1=-1e-4, scalar2=None, op0=ALU.min))
              pp("dve", lambda e: e.tensor_tensor(out=lr[:], in0=are[:], in1=dt_[:], op=ALU.mult))
              pp("dve", lambda e: e.tensor_tensor(out=th[:], in0=aim[:], in1=dt_[:], op=ALU.mult))
              pp("act", lambda e: e.activation(out=t1[:], in_=lr[:], func=AF.Exp, scale=1.0 / 16))
              pp("act", lambda e: e.activation(out=er[:], in_=th[:], func=AF.Sin, scale=1.0 / 16, bias=scon_t[:, 1024:1025]))
              pp("act", lambda e: e.activation(out=ei[:], in_=th[:], func=AF.Sin, scale=1.0 / 16))
              pp("dve", lambda e: e.tensor_tensor(out=er[:], in0=er[:], in1=t1[:], op=ALU.mult))
              pp("dve", lambda e: e.tensor_tensor(out=ei[:], in0=ei[:], in1=t1[:], op=ALU.mult))

              def csq():
                  pp("dve", lambda e: e.tensor_tensor(out=t1[:], in0=er[:], in1=er[:], op=ALU.mult))
                  pp("dve", lambda e: e.tensor_tensor(out=t2[:], in0=ei[:], in1=ei[:], op=ALU.mult))
                  pp("dve", lambda e: e.scalar_tensor_tensor(out=ei[:], in0=er[:], scalar=2.0, in1=ei[:], op0=ALU.mult, op1=ALU.mult))
                  pp("dve", lambda e: e.tensor_tensor(out=er[:], in0=t1[:], in1=t2[:], op=ALU.subtract))

              for _ in range(4):
                  csq()
              ar1, ai1, nai1, kr, ki, nki, dr, dn_ = (P_[n_] for n_ in ("ar1", "ai1", "nai1", "kr", "ki", "nki", "dr", "den"))
              pp("dve", lambda e: e.tensor_copy(out=ar1[:], in_=er[:]))
              pp("dve", lambda e: e.tensor_copy(out=ai1[:], in_=ei[:]))
              pp("dve", lambda e: e.tensor_scalar(out=nai1[:], in0=ei[:], scalar1=-1.0, scalar2=None, op0=ALU.mult))
              pp("dve", lambda e: e.tensor_scalar(out=dr[:], in0=ar1[:], scalar1=-1.0, scalar2=None, op0=ALU.add))
              pp("dve", lambda e: e.tensor_tensor(out=t1[:], in0=are[:], in1=are[:], op=ALU.mult))
              pp("dve", lambda e: e.tensor_tensor(out=t2[:], in0=aim[:], in1=aim[:], op=ALU.mult))
              pp("dve", lambda e: e.tensor_tensor(out=dn_[:], in0=t1[:], in1=t2[:], op=ALU.add))
              pp("dve", lambda e: e.reciprocal(out=dn_[:], in_=dn_[:]))
              pp("dve", lambda e: e.tensor_tensor(out=t1[:], in0=dr[:], in1=are[:], op=ALU.mult))
              pp("dve", lambda e: e.tensor_tensor(out=t2[:], in0=ai1[:], in1=aim[:], op=ALU.mult))
              pp("dve", lambda e: e.tensor_tensor(out=t1[:], in0=t1[:], in1=t2[:], op=ALU.add))
              pp("dve", lambda e: e.tensor_tensor(out=kr[:], in0=t1[:], in1=dn_[:], op=ALU.mult))
              pp("dve", lambda e: e.tensor_tensor(out=t1[:], in0=ai1[:], in1=are[:], op=ALU.mult))
              pp("dve", lambda e: e.tensor_tensor(out=t2[:], in0=dr[:], in1=aim[:], op=ALU.mult))
              pp("dve", lambda e: e.tensor_tensor(out=t1[:], in0=t1[:], in1=t2[:], op=ALU.subtract))
              pp("dve", lambda e: e.tensor_tensor(out=ki[:], in0=t1[:], in1=dn_[:], op=ALU.mult))
              pp("dve", lambda e: e.tensor_scalar(out=nki[:], in0=ki[:], scalar1=-1.0, scalar2=None, op0=ALU.mult))
              for _ in range(3):
                  csq()
              for k in range(8):
                  pp("dve", lambda e: e.tensor_copy(out=s_pwr[:, k, :], in_=er[:]))
                  pp("dve", lambda e: e.tensor_copy(out=s_pwi[:, k, :], in_=ei[:]))
                  pp("dve", lambda e: e.tensor_scalar(out=s_npwi[:, k, :], in0=ei[:], scalar1=-1.0, scalar2=None, op0=ALU.mult))
                  if k < 7:
                      csq()
              for dst_, src_ in ((s_dcol, s5_dd), (s_gbcol, s5_gb)):
                  dma(s_st2[:, :], src_[l].rearrange("(h p) -> h p", p=128), writes=[s_st2.b])
                  p_ = psf()
                  op("pe", lambda e: e.transpose(p_[:, 0:2], s_st2[:, :], ident_f[0:2, 0:2]), reads=[s_st2.b, ident_f.b], writes=[p_.b])
                  op("dve", lambda e: e.tensor_copy(out=dst_[:], in_=p_[:, 0:2]), reads=[p_.b], writes=[dst_.b])
              for hf in range(2):
                  op("dve", lambda e: e.tensor_scalar(out=s_Dd[hf][:], in0=ident_f[:], scalar1=s_dcol[:, hf:hf + 1], scalar2=None, op0=ALU.mult),
                     reads=[ident_f.b, s_dcol.b], writes=[s_Dd[hf].b])
                  st = wstage[0]
                  dma(st[:, 0:256], s5_gw[l, hf * 128:(hf + 1) * 128, :], writes=[st.b], q="pool")
                  op("dve", lambda e: e.tensor_copy(out=s_glu[:, hf, :], in_=st[:, 0:256]), reads=[st.b], writes=[s_glu.b])
              for r_, src_ in enumerate((hy_f1, hy_b1, hy_f2, hy_b2)):
                  dma(h_mrow[r_:r_ + 1, :], src_[l:l + 1, :], reads=[h_mrow.b], writes=[h_mrow.b])
              p_ = psf()
              op("pe", lambda e: e.transpose(p_[0:64, 0:4], h_mrow[:, :], ident_f[0:4, 0:4]), reads=[h_mrow.b, ident_f.b], writes=[p_.b])
              op("dve", lambda e: e.tensor_copy(out=h_mcol[:], in_=p_[0:64, 0:4]), reads=[p_.b], writes=[h_mcol.b])
              for li in range(2):
                  fcol, bcol = h_mcol[:, 2 * li:2 * li + 1], h_mcol[:, 2 * li + 1:2 * li + 2]
                  o4 = 4 * li
                  op("dve", lambda e: e.tensor_scalar(out=h_mlp[:, o4:o4 + 1], in0=fcol, scalar1=0.5, scalar2=None, op0=ALU.mult),
                     reads=[h_mcol.b, h_mlp.b], writes=[h_mlp.b])
                  op("dve", lambda e: e.scalar_tensor_tensor(out=h_mlp[:, o4 + 1:o4 + 2], in0=fcol, scalar=0.5, in1=bcol, op0=ALU.mult, op1=ALU.mult),
                     reads=[h_mcol.b, h_mlp.b], writes=[h_mlp.b])
                  op("dve", lambda e: e.tensor_scalar(out=h_mlp[:, o4 + 2:o4 + 3], in0=fcol, scalar1=0.25, scalar2=None, op0=ALU.mult),
                     reads=[h_mcol.b, h_mlp.b], writes=[h_mlp.b])
                  op("dve", lambda e: e.scalar_tensor_tensor(out=h_mlp[:, o4 + 3:o4 + 4], in0=fcol, scalar=0.25, in1=bcol, op0=ALU.mult, op1=ALU.mult),
                     reads=[h_mcol.b, h_mlp.b], writes=[h_mlp.b])
              st = wstage[0]
              dma(st[0:33, 0:64], hy_w1[l, :, :], writes=[st.b], q="pool")
              op("dve", lambda e: e.tensor_copy(out=hw1_b[:], in_=st[0:33, 0:64]), reads=[st.b], writes=[hw1_b.b])
              dma(st[0:64, 0:64], hy_w2[l, :, :], writes=[st.b], q="pool")
              op("dve", lambda e: e.tensor_copy(out=hw2_b[:], in_=st[0:64, 0:64]), reads=[st.b], writes=[hw2_b.b])
              chk('weights')
              if l == 0:
                  for c in range(2):
                      dma(scond[:, c, :], cond[c].rearrange("(k p) -> k p", p=128), writes=[scond.b])
                  op("act", lambda e: e.activation(out=scond[:], in_=scond[:], func=AF.Silu),
                     reads=[scond.b], writes=[scond.b])
                  for c in range(2):
                      p = psf()
                      op("pe", lambda e: e.transpose(p[:, 0:8], scond[:, c, :], ident_f[0:8, 0:8]),
                         reads=[scond.b, ident_f.b], writes=[p.b])
                      op("dve", lambda e: e.tensor_copy(out=scondT[:, c, :], in_=p[:, 0:8]),
                         reads=[p.b], writes=[scondT.b])
              dma(colst[:, :], ada_b[l].rearrange("(j p) -> j p", p=128), writes=[colst.b])
              p = psf()
              op("pe", lambda e: e.transpose(p[:, 0:24], colst[:, :], ident_f[0:24, 0:24]),
                 reads=[colst.b, ident_f.b], writes=[p.b])
              op("dve", lambda e: e.tensor_copy(out=colsT[:].rearrange("p a k -> p (a k)"), in_=p[:, 0:24]),
                 reads=[p.b], writes=[colsT.b])
              dma(nwst[:, :], norm_w[l].rearrange("(k p) -> k p", p=128), writes=[nwst.b])
              p = psf()
              op("pe", lambda e: e.transpose(p[:, 0:8], nwst[:, :], ident_f[0:8, 0:8]),
                 reads=[nwst.b, ident_f.b], writes=[p.b])
              op("dve", lambda e: e.tensor_copy(out=nwT[:], in_=p[:, 0:8]), reads=[p.b], writes=[nwT.b])

              pmod = modacc
              op("dve", lambda e: e.memset(modacc[:], 0.0), writes=[modacc.b])
              pk = psf()
              pg = [psf() for _ in range(4)]
              for k in range(8):
                  a = adast[k % 2]
                  dma(a[:, :], ada_w[l, k * 128:(k + 1) * 128, :], writes=[a.b])
                  for j in range(16):
                      op("pe", lambda e: e.matmul(pk[:, 2 * j:2 * j + 2], lhsT=a[:, j * 128:(j + 1) * 128],
                                                  rhs=scondT[:, :, k], start=True, stop=True),
                         reads=[a.b, scondT.b], writes=[pk.b])
                  op("dve", lambda e: e.tensor_tensor(out=modacc[:, 0:32], in0=modacc[:, 0:32], in1=pk[:, 0:32], op=ALU.add),
                     reads=[pk.b, modacc.b], writes=[modacc.b])
                  for c in range(2):
                      op("dve", lambda e: e.tensor_copy(out=screp[:], in_=scondT[:, c, k:k + 1].to_broadcast([128, 128])),
                         reads=[scondT.b], writes=[screp.b])
                      for hf in range(2):
                          pgt = pg[2 * c + hf]
                          op("pe", lambda e: e.matmul(pgt[:, :], lhsT=screp[:, :],
                                                      rhs=a[:, 2048 + hf * 512:2048 + (hf + 1) * 512],
                                                      start=(k == 0), stop=False),
                             reads=[a.b, screp.b], writes=[pgt.b])
              dma(rowst[0:1, 0:D], ada_b[l:l + 1, 2048:3072], writes=[rowst.b])
              for c in range(2):
                  for hf in range(2):
                      pgt = pg[2 * c + hf]
                      op("pe", lambda e: e.matmul(pgt[:, :], lhsT=ones_f[0:1, :], rhs=rowst[0:1, hf * 512:(hf + 1) * 512],
                                                  start=False, stop=True),
                         reads=[ones_f.b, rowst.b], writes=[pgt.b])
                      op("dve", lambda e: e.tensor_copy(out=gate_bc[c][:, hf * 512:(hf + 1) * 512], in_=pgt[:, :]),
                         reads=[pgt.b], writes=[gate_bc[c].b])
              pm = pmod[:, 0:32].rearrange("p (j c) -> p c j", c=2)
              for c in range(2):
                  op("dve", lambda e: e.tensor_tensor(out=shift[:, c, :], in0=pm[:, c, 0:8], in1=colsT[:, 0, :], op=ALU.add),
                     reads=[pmod.b, colsT.b], writes=[shift.b])
                  op("dve", lambda e: e.tensor_tensor(out=effsc[:, c, :], in0=pm[:, c, 8:16], in1=colsT[:, 1, :], op=ALU.add),
                     reads=[pmod.b, colsT.b], writes=[effsc.b])
                  op("dve", lambda e: e.scalar_tensor_tensor(out=effsc[:, c, :], in0=effsc[:, c, :], scalar=1.0,
                                                             in1=nwT[:, :], op0=ALU.add, op1=ALU.mult),
                     reads=[effsc.b, nwT.b], writes=[effsc.b])

              chk('ada')
              def gla_group(l, g):
                  c, T, L, sample = g["c"], g["T"], g["L"], g["sample"]
                  for t_ in g_QA + g_QB:
                      op("pool", lambda e: e.memset(t_[:], 0.0), writes=[t_.b])
                  tps = L // 128
                  nseq = T // L
                  if sample:
                      dma(wmix[:, :, :].rearrange("p k c -> p (k c)"), wc["gla"][0][:, :], reads=[wc["gla"][1]], writes=[wmix.b], q="pool")
                  else:
                      for k in range(8):
                          st = wstage[0]
                          dma(st[:, 0:800], w_in[l, k * 128:(k + 1) * 128, GLA0:GLA0 + 800], writes=[st.b], q="pool")
                          op("dve" if k % 2 == 0 else "pool", lambda e: e.tensor_copy(out=wmix[:, k, :], in_=st[:, 0:800]),
                             reads=[st.b], writes=[wmix.b])
                      dma(wc["gla"][0][:, :], wmix[:, :, :].rearrange("p k c -> p (k c)"), reads=[wmix.b], writes=[wc["gla"][1]])
                  for d in (1, 0):
                      mask_d = maskF if d == 0 else maskB_
                      for s_ in range(nseq):
                          S0, S1, S2 = g_S
                          if sample:
                              dma(S0[:, :], sgla[l, d, :, :], writes=[S0.b])
                          else:
                              op("dve", lambda e: e.memset(S0[:], 0.0), writes=[S0.b])
                          order = range(tps) if d == 0 else range(tps - 1, -1, -1)
                          for tt in order:
                              ti = s_ * tps + tt
                              r0 = ti * 128
                              pq_, pk_, pg_, pv_ = psf(), psf(), psf(), psf()
                              for k in range(8):
                                  op("pe", lambda e: e.matmul(pq_[:, 0:128], lhsT=wmix[:, k, 0:128], rhs=hT[:, k, r0:r0 + 128],
                                                              start=(k == 0), stop=(k == 7)), reads=[wmix.b, hT.b], writes=[pq_.b])
                              for k in range(8):
                                  op("pe", lambda e: e.matmul(pk_[:, 0:128], lhsT=wmix[:, k, 128:256], rhs=hT[:, k, r0:r0 + 128],
                                                              start=(k == 0), stop=(k == 7)), reads=[wmix.b, hT.b], writes=[pk_.b])
                              for k in range(8):
                                  op("pe", lambda e: e.matmul(pg_[0:32, 0:128], lhsT=wmix[:, k, 512:544], rhs=hT[:, k, r0:r0 + 128],
                                                              start=(k == 0), stop=(k == 7)), reads=[wmix.b, hT.b], writes=[pg_.b])
                              for k in range(8):
                                  op("pe", lambda e: e.matmul(pv_[:, 0:256], lhsT=hT[:, k, r0:r0 + 128], rhs=wmix[:, k, 256:512],
                                                              start=(k == 0), stop=(k == 7)), reads=[wmix.b, hT.b], writes=[pv_.b])
                              op("act", lambda e: e.activation(out=g_qf[:], in_=pq_[:, 0:128], func=AF.Copy), reads=[pq_.b], writes=[g_qf.b])
                              op("act", lambda e: e.activation(out=g_kf[:], in_=pk_[:, 0:128], func=AF.Copy), reads=[pk_.b], writes=[g_kf.b])
                              op("act", lambda e: e.activation(out=g_glr[:], in_=pg_[0:32, 0:128], func=AF.Copy), reads=[pg_.b], writes=[g_glr.b])
                              op("dve", lambda e: e.tensor_copy(out=g_vt[:], in_=pv_[:, 0:256]), reads=[pv_.b], writes=[g_vt.b])
                              pl_ = psf()
                              op("pe", lambda e: e.matmul(pl_[:, 0:128], lhsT=gwp[:, d, :], rhs=g_glr[:, :], start=True, stop=True),
                                 reads=[gwp.b, g_glr.b], writes=[pl_.b])
                              op("act", lambda e: e.activation(out=g_sp[:], in_=pl_[:, 0:128], func=AF.Exp, scale=-1.0, bias=ngb[:, d:d + 1]),
                                 reads=[pl_.b, ngb.b], writes=[g_sp.b])
                              op("act", lambda e: e.activation(out=g_sp[:], in_=g_sp[:], func=AF.Ln, bias=ones64[:, 0:1]),
                                 reads=[g_sp.b, gcon_t.b], writes=[g_sp.b])
                              for ch in range(2):
                                  c0 = 64 * ch
                                  op("dve", lambda e: e.tensor_tensor_scan(out=g_bcp[:, c0:c0 + 64], data0=ones64, data1=g_sp[:, c0:c0 + 64],
                                                                           initial=0.0, op0=ALU.mult, op1=ALU.add),
                                     reads=[g_sp.b, gcon_t.b, g_bcp.b], writes=[g_bcp.b])
                              if d == 1:
                                  for ch in range(2):
                                      c0 = 64 * ch
                                      op("dve", lambda e: e.tensor_copy(out=g_tot[:, ch:ch + 1], in_=g_bcp[:, c0 + 63:c0 + 64]),
                                         reads=[g_bcp.b, g_tot.b], writes=[g_tot.b])
                                  op("dve", lambda e: e.tensor_tensor(out=g_bcp[:], in0=g_sp[:], in1=g_bcp[:], op=ALU.subtract),
                                     reads=[g_sp.b, g_bcp.b], writes=[g_bcp.b])
                                  for ch in range(2):
                                      c0 = 64 * ch
                                      op("dve", lambda e: e.tensor_scalar(out=g_bcp[:, c0:c0 + 64], in0=g_bcp[:, c0:c0 + 64],
                                                                          scalar1=g_tot[:, ch:ch + 1], scalar2=None, op0=ALU.add),
                                         reads=[g_bcp.b, g_tot.b], writes=[g_bcp.b])
                              for ch in range(2):
                                  col = 64 * ch + (63 if d == 0 else 0)
                                  op("dve", lambda e: e.tensor_scalar(out=g_nbl[:, ch:ch + 1], in0=g_bcp[:, col:col + 1], scalar1=-1.0 / 16,
                                                                      scalar2=None, op0=ALU.mult), reads=[g_bcp.b, g_nbl.b], writes=[g_nbl.b])
                              op("act", lambda e: e.activation(out=g_eb[:], in_=g_nbl[:], func=AF.Exp), reads=[g_nbl.b], writes=[g_eb.b])
                              op("act", lambda e: e.activation(out=g_e1[:], in_=g_bcp[:], func=AF.Exp, scale=-1.0 / 16), reads=[g_bcp.b], writes=[g_e1.b])
                              op("act", lambda e: e.activation(out=g_e2[:], in_=g_bcp[:], func=AF.Exp, scale=1.0 / 16), reads=[g_bcp.b], writes=[g_e2.b])
                              for ch in range(2):
                                  c0 = 64 * ch
                                  op("act", lambda e: e.activation(out=g_e3[:, c0:c0 + 64], in_=g_bcp[:, c0:c0 + 64], func=AF.Exp, scale=1.0 / 16,
                                                                   bias=g_nbl[:, ch:ch + 1]), reads=[g_bcp.b, g_nbl.b, g_e3.b], writes=[g_e3.b])
                              op("dve", lambda e: e.scalar_tensor_tensor(out=g_qt[:], in0=g_qf[:], scalar=GLA_DK ** -0.5, in1=g_e1[:],
                                                                         op0=ALU.mult, op1=ALU.mult), reads=[g_qf.b, g_e1.b], writes=[g_qt.b])
                              op("dve", lambda e: e.tensor_tensor(out=g_kt[:], in0=g_kf[:], in1=g_e2[:], op=ALU.mult),
                                 reads=[g_kf.b, g_e2.b], writes=[g_kt.b])
                              op("dve", lambda e: e.tensor_tensor(out=g_khT[:], in0=g_kf[:], in1=g_e3[:], op=ALU.mult),
                                 reads=[g_kf.b, g_e3.b], writes=[g_khT.b])
                              pt_ = psb()
                              op("pe", lambda e: e.transpose(pt_[:, 0:128], g_khT[:, :], ident_b[:, :]), reads=[g_khT.b, ident_b.b], writes=[pt_.b])
                              op("dve", lambda e: e.tensor_scalar(out=g_khA[:], in0=pt_[:, 0:128], scalar1=halfm[:, 0:1], scalar2=None, op0=ALU.mult),
                                 reads=[pt_.b, gcon_t.b], writes=[g_khA.b])
                              op("dve", lambda e: e.tensor_scalar(out=g_khB[:], in0=pt_[:, 0:128], scalar1=halfm[:, 1:2], scalar2=None, op0=ALU.mult),
                                 reads=[pt_.b, gcon_t.b], writes=[g_khB.b])
                              for h in range(4):
                                  op("pool", lambda e: e.tensor_scalar(out=g_km[h][:], in0=g_kt[:], scalar1=headm[:, h:h + 1], scalar2=None, op0=ALU.mult),
                                     reads=[g_kt.b, gcon_t.b], writes=[g_km[h].b])
                                  op("pool", lambda e: e.tensor_scalar(out=g_QA[h][:, 0:64], in0=g_qt[:, 0:64], scalar1=headm[:, h:h + 1], scalar2=None,
                                                                       op0=ALU.mult), reads=[g_qt.b, gcon_t.b], writes=[g_QA[h].b])
                                  op("pool", lambda e: e.tensor_scalar(out=g_QB[h][:, 64:128], in0=g_qt[:, 64:128], scalar1=headm[:, h:h + 1], scalar2=None,
                                                                       op0=ALU.mult), reads=[g_qt.b, gcon_t.b], writes=[g_QB[h].b])
                              for ch, kh in ((0, g_khA), (1, g_khB)):
                                  pd_ = psf()
                                  op("pe", lambda e: e.matmul(pd_[:, 0:256], lhsT=kh[:, :], rhs=g_vt[:, :], start=True, stop=True),
                                     reads=[kh.b, g_vt.b], writes=[pd_.b])
                                  op("dve", lambda e: e.tensor_tensor(out=g_dS[:], in0=pd_[:, 0:256], in1=blockm, op=ALU.mult),
                                     reads=[pd_.b, gcon_t.b], writes=[g_dS.b])
                                  op("dve", lambda e: e.tensor_reduce(out=g_dSr[ch][:], in_=g_dS[:].rearrange("p (h v) -> p v h", h=4),
                                                                      axis=AX.X, op=ALU.add), reads=[g_dS.b], writes=[g_dSr[ch].b])
                              first, second = (0, 1) if d == 0 else (1, 0)
                              op("dve", lambda e: e.tensor_copy(out=g_Sb[0][:], in_=S0[:]), reads=[S0.b], writes=[g_Sb[0].b])
                              op("dve", lambda e: e.scalar_tensor_tensor(out=S1[:], in0=S0[:], scalar=g_eb[:, first:first + 1], in1=g_dSr[first][:],
                                                                         op0=ALU.mult, op1=ALU.add), reads=[S0.b, g_eb.b, g_dSr[first].b], writes=[S1.b])
                              op("dve", lambda e: e.tensor_copy(out=g_Sb[1][:], in_=S1[:]), reads=[S1.b], writes=[g_Sb[1].b])
                              op("dve", lambda e: e.scalar_tensor_tensor(out=S2[:], in0=S1[:], scalar=g_eb[:, second:second + 1], in1=g_dSr[second][:],
                                                                         op0=ALU.mult, op1=ALU.add), reads=[S1.b, g_eb.b, g_dSr[second].b], writes=[S2.b])
                              SbA, SbB = (g_Sb[0], g_Sb[1]) if d == 0 else (g_Sb[1], g_Sb[0])
                              po = p_acc
                              for h in range(4):
                                  pa_ = psf()
                                  op("pe", lambda e: e.matmul(pa_[:, 0:128], lhsT=g_km[h][:, :], rhs=g_qt[:, :], start=True, stop=True),
                                     reads=[g_km[h].b, g_qt.b], writes=[pa_.b])
                                  op("dve", lambda e: e.tensor_tensor(out=g_attm[:], in0=pa_[:, 0:128], in1=mask_d, op=ALU.mult),
                                     reads=[pa_.b, gcon_t.b], writes=[g_attm.b])
                                  op("pe", lambda e: e.matmul(po[:, 64 * h:64 * h + 64], lhsT=g_attm[:, :], rhs=g_vt[:, 64 * h:64 * h + 64],
                                                              start=True, stop=False), reads=[g_attm.b, g_vt.b], writes=[po.b])
                                  op("pe", lambda e: e.matmul(po[:, 64 * h:64 * h + 64], lhsT=g_QA[h][:, :], rhs=SbA[:, :], start=False, stop=False),
                                     reads=[g_QA[h].b, SbA.b], writes=[po.b])
                                  op("pe", lambda e: e.matmul(po[:, 64 * h:64 * h + 64], lhsT=g_QB[h][:, :], rhs=SbB[:, :], start=False, stop=True),
                                     reads=[g_QB[h].b, SbB.b], writes=[po.b])
                              if d == 1:
                                  op("act", lambda e: e.activation(out=o_bwd[:, ti, :], in_=po[:, 0:256], func=AF.Copy), reads=[po.b], writes=[o_bwd.b])
                              else:
                                  op("dve", lambda e: e.tensor_tensor(out=g_os[:], in0=po[:, 0:256], in1=o_bwd[:, ti, :], op=ALU.add),
                                     reads=[po.b, o_bwd.b], writes=[g_os.b])
                                  op("act", lambda e: e.activation(out=g_sq[:], in_=g_os[:], func=AF.Square), reads=[g_os.b], writes=[g_sq.b])
                                  op("dve", lambda e: e.tensor_reduce(out=g_ss[:], in_=g_sq[:].rearrange("p (h v) -> p h v", h=4), axis=AX.X, op=ALU.add),
                                     reads=[g_sq.b], writes=[g_ss.b])
                                  rstd_from_ss(g_rs, g_ss, 64)
                                  os3 = g_os[:].rearrange("p (h v) -> p h v", h=4)
                                  op("dve", lambda e: e.tensor_tensor(out=os3, in0=os3, in1=g_rs[:, :].unsqueeze(2).to_broadcast([128, 4, 64]), op=ALU.mult),
                                     reads=[g_os.b, g_rs.b], writes=[g_os.b])
                                  op("dve", lambda e: e.tensor_tensor(out=os3, in0=os3, in1=bc_gn[:, :].unsqueeze(1).to_broadcast([128, 4, 64]), op=ALU.mult),
                                     reads=[g_os.b, bc_gn.b], writes=[g_os.b])
                                  pgg = psf()
                                  for k in range(8):
                                      op("pe", lambda e: e.matmul(pgg[:, 0:256], lhsT=hT[:, k, r0:r0 + 128], rhs=wmix[:, k, 544:800],
                                                                  start=(k == 0), stop=(k == 7)), reads=[hT.b, wmix.b], writes=[pgg.b])
                                  op("act", lambda e: e.activation(out=g_sil[:], in_=pgg[:, 0:256], func=AF.Silu), reads=[pgg.b], writes=[g_sil.b])
                                  op("dve", lambda e: e.tensor_tensor(out=g_og[:], in0=g_os[:], in1=g_sil[:], op=ALU.mult),
                                     reads=[g_os.b, g_sil.b], writes=[g_og.b])
                                  pbr_ = psb()
                                  for k2 in range(2):
                                      op("pe", lambda e: e.transpose(pbr_[:, k2 * 128:(k2 + 1) * 128], g_og[:, k2 * 128:(k2 + 1) * 128], ident_b[:, :]),
                                         reads=[g_og.b, ident_b.b], writes=[pbr_.b])
                                  op("act", lambda e: e.activation(out=brT[:, 6:8, r0:r0 + 128], in_=pbr_[:, 0:256].rearrange("p (k t) -> p k t", k=2),
                                                                   func=AF.Copy), reads=[pbr_.b], writes=[brT.b])
                              S0, S1, S2 = S2, S0, S1
                          if not sample:
                              dma(o_gla[s_, l, d].rearrange("h k v -> (h k) v"), S0[:, :], reads=[S0.b])

              def s5_group(l, g):
                  c, T, L, sample = g["c"], g["T"], g["L"], g["sample"]
                  nseq, Mseq, M, nb = T // L, L // 8, T // 8, T // 512
                  nsteps = Mseq.bit_length() - 1
                  reuse = sample
                  P_ = s_par
                  pb_ = s_par["are"].b
                  ybank = pf[1:1 + nb]
                  ptmp = pf[0]
                  if sample:
                      dma(s_wmix[:, :, :].rearrange("p k c -> p (k c)"), wc["s5"][0][:, :], reads=[wc["s5"][1]], writes=[s_wmix.b], q="pool")
                  else:
                      for k in range(8):
                          st_ = wstage[0]
                          dma(st_[:, 0:512], w_in[l, k * 128:(k + 1) * 128, S50:S50 + 512], writes=[st_.b], q="pool")
                          op("dve" if k % 2 == 0 else "pool", lambda e: e.tensor_copy(out=s_wmix[:, k, :], in_=st_[:, 0:512]),
                             reads=[st_.b], writes=[s_wmix.b])
                      dma(wc["s5"][0][:, :], s_wmix[:, :, :].rearrange("p k c -> p (k c)"), reads=[s_wmix.b], writes=[wc["s5"][1]])
                  for hf in range(2):
                      for bk in range(nb):
                          for k in range(8):
                              op("pe", lambda e: e.matmul(ptmp[:, :], lhsT=s_wmix[:, k, hf * 128:(hf + 1) * 128], rhs=hT[:, k, bk * 512:(bk + 1) * 512],
                                                          start=(k == 0), stop=(k == 7)), reads=[s_wmix.b, hT.b], writes=[ptmp.b])
                          op("act", lambda e: e.activation(out=s_uT[:, hf, bk * 512:(bk + 1) * 512], in_=ptmp[:, :], func=AF.Copy),
                             reads=[ptmp.b], writes=[s_uT.b])
                  for hf in range(2):
                      for bk in range(nb):
                          op("pe", lambda e: e.matmul(ybank[bk][:, :], lhsT=s_Dd[hf][:, :], rhs=s_uT[:, hf, bk * 512:(bk + 1) * 512],
                                                      start=True, stop=False), reads=[s_Dd[hf].b, s_uT.b], writes=[ybank[bk].b])
                      if not reuse:
                          op("dve", lambda e: e.memset(s_Kacc[:], 0.0), writes=[s_Kacc.b])
                      for d in range(2):
                          for ri, src_ in ((0, s5_cre), (1, s5_cim)):
                              if not reuse:
                                  dma(s_C[ri][:, :], src_[l, d, 8 * hf:8 * hf + 8].rearrange("g i p -> (g i) p"), writes=[s_C[ri].b], q="pool")
                          for k4 in range(4):
                              st = 4 * hf + k4
                              td = d * 8 + st
                              ar, ai, nai = P_["ar1"][:, td:td + 1], P_["ai1"][:, td:td + 1], P_["nai1"][:, td:td + 1]
                              kr_, ki_, nki_ = P_["kr"][:, td:td + 1], P_["ki"][:, td:td + 1], P_["nki"][:, td:td + 1]
                              if reuse:
                                  dma(s_W[:, :, :].rearrange("p a c -> p (a c)"), s5w[td], reads=[s5w_b], writes=[s_W.b], q="pool")
                              else:
                                  for ri, src_ in ((0, s5_bre), (1, s5_bim)):
                                      dma(s_B[ri][:, :], src_[l, d, 2 * st:2 * st + 2].rearrange("g p i -> (g p) i"), writes=[s_B[ri].b], q="pool")
                                  op("dve", lambda e: e.tensor_scalar(out=s_bb[0][:], in0=s_B[0][:], scalar1=kr_, scalar2=None, op0=ALU.mult),
                                     reads=[s_B[0].b, pb_], writes=[s_bb[0].b])
                                  op("dve", lambda e: e.tensor_scalar(out=s_bb[1][:], in0=s_B[0][:], scalar1=ki_, scalar2=None, op0=ALU.mult),
                                     reads=[s_B[0].b, pb_], writes=[s_bb[1].b])
                                  op("dve", lambda e: e.scalar_tensor_tensor(out=s_bb[0][:], in0=s_B[1][:], scalar=nki_, in1=s_bb[0][:], op0=ALU.mult, op1=ALU.add),
                                     reads=[s_B[1].b, pb_, s_bb[0].b], writes=[s_bb[0].b])
                                  op("dve", lambda e: e.scalar_tensor_tensor(out=s_bb[1][:], in0=s_B[1][:], scalar=kr_, in1=s_bb[1][:], op0=ALU.mult, op1=ALU.add),
                                     reads=[s_B[1].b, pb_, s_bb[1].b], writes=[s_bb[1].b])
                                  mB3 = scon_t[:, 128 * k4:128 * k4 + 128].rearrange("p (g i) -> p g i", g=8)
                                  for ri in range(2):
                                      op("dve", lambda e: e.tensor_tensor(out=s_Xm[0][ri][:].rearrange("p (g i) -> p g i", g=8), in0=mB3,
                                                                          in1=s_bb[ri][:, :].unsqueeze(1).to_broadcast([128, 8, 16]), op=ALU.mult),
                                         reads=[scon_t.b, s_bb[ri].b], writes=[s_Xm[0][ri].b])
                                  mC3 = scon_t[:, 512 + 128 * k4:512 + 128 * k4 + 128].rearrange("p (a q) -> p a q", a=2)
                                  for ri in range(2):
                                      op("dve", lambda e: e.tensor_tensor(out=s_Cp[:].rearrange("p (a q) -> p a q", a=2), in0=mC3,
                                                                          in1=s_C[ri][:, :].unsqueeze(1).to_broadcast([128, 2, 64]), op=ALU.mult),
                                         reads=[scon_t.b, s_C[ri].b], writes=[s_Cp.b])
                                      op("pe", lambda e: e.transpose(ptmp[:, 0:128], s_Cp[:, :], ident_f[:, :]), reads=[s_Cp.b, ident_f.b], writes=[ptmp.b])
                                      op("dve", lambda e: e.tensor_copy(out=s_Cm[0][ri][:], in_=ptmp[:, 0:128]), reads=[ptmp.b], writes=[s_Cm[0][ri].b])

                                  def half1(cur, nxt):
                                      op("dve", lambda e: e.tensor_scalar(out=nxt[0][:], in0=cur[0][:], scalar1=ar, scalar2=None, op0=ALU.mult),
                                         reads=[cur[0].b, pb_], writes=[nxt[0].b])
                                      op("dve", lambda e: e.tensor_scalar(out=nxt[1][:], in0=cur[0][:], scalar1=ai, scalar2=None, op0=ALU.mult),
                                         reads=[cur[0].b, pb_], writes=[nxt[1].b])

                                  def half2(cur, nxt):
                                      op("dve", lambda e: e.scalar_tensor_tensor(out=nxt[0][:], in0=cur[1][:], scalar=nai, in1=nxt[0][:], op0=ALU.mult, op1=ALU.add),
                                         reads=[cur[1].b, pb_, nxt[0].b], writes=[nxt[0].b])
                                      op("dve", lambda e: e.scalar_tensor_tensor(out=nxt[1][:], in0=cur[1][:], scalar=ar, in1=nxt[1][:], op0=ALU.mult, op1=ALU.add),
                                         reads=[cur[1].b, pb_, nxt[1].b], writes=[nxt[1].b])

                                  for k in range(9):
                                      Xc, Xn = s_Xm[k % 2], s_Xm[(k + 1) % 2]
                                      Cc, Cn = s_Cm[k % 2], s_Cm[(k + 1) % 2]
                                      if k < 8:
                                          for ri in range(2):
                                              op("act", lambda e: e.activation(out=s_Xb[k][ri][:], in_=Xc[ri][:], func=AF.Copy),
                                                 reads=[Xc[ri].b], writes=[s_Xb[k][ri].b])
                                      op("act", lambda e: e.activation(out=s_Wc[k][0][:], in_=Cc[0][:], func=AF.Copy), reads=[Cc[0].b], writes=[s_Wc[k][0].b])
                                      op("act", lambda e: e.activation(out=s_Wc[k][1][:], in_=Cc[1][:], func=AF.Copy, scale=-1.0),
                                         reads=[Cc[1].b], writes=[s_Wc[k][1].b])
                                      if k < 7:
                                          half1(Xc, Xn)
                                      if k < 8:
                                          half1(Cc, Cn)
                                      if k < 7:
                                          half2(Xc, Xn)
                                      if k < 8:
                                          half2(Cc, Cn)
                                      if k < 8:
                                          pt_ = psb()
                                          for ri in range(2):
                                              op("pe", lambda e: e.transpose(pt_[:, ri * 128:(ri + 1) * 128], s_Xb[k][ri][:, :], ident_b[:, :]),
                                                 reads=[s_Xb[k][ri].b, ident_b.b], writes=[pt_.b])
                                          for ri in range(2):
                                              op("act", lambda e: e.activation(out=s_Wb[k][ri][:], in_=pt_[:, ri * 128:(ri + 1) * 128], func=AF.Copy),
                                                 reads=[pt_.b], writes=[s_Wb[k][ri].b])
                                  for q4 in range(2):
                                      for dd in range(4):
                                          dl = 4 * q4 + dd
                                          op("pe", lambda e: e.matmul(ptmp[:, dd * 128:(dd + 1) * 128], lhsT=s_Xb[dl][0][:, :], rhs=s_Wc[0][0][:, :],
                                                                      start=True, stop=False), reads=[s_Xb[dl][0].b, s_Wc[0][0].b], writes=[ptmp.b])
                                          op("pe", lambda e: e.matmul(ptmp[:, dd * 128:(dd + 1) * 128], lhsT=s_Xb[dl][1][:, :], rhs=s_Wc[0][1][:, :],
                                                                      start=False, stop=True), reads=[s_Xb[dl][1].b, s_Wc[0][1].b], writes=[ptmp.b])
                                      ka = s_Kacc[:, d * 8 + 4 * q4:d * 8 + 4 * q4 + 4, :]
                                      op("dve", lambda e: e.tensor_tensor(out=ka, in0=ka, in1=ptmp[:, :].rearrange("p (a c) -> p a c", a=4), op=ALU.add),
                                         reads=[ptmp.b, s_Kacc.b], writes=[s_Kacc.b])
                                  dma(s5w[td], s_W[:, :, :].rearrange("p a c -> p (a c)"), reads=[s_W.b], writes=[s5w_b])
                              ub = s_uT[:, hf, 0:T].rearrange("p (m j) -> p m j", j=8)
                              for ri in range(2):
                                  for s_ in range(8):
                                      kk = 7 - s_ if d == 0 else s_
                                      op("pe", lambda e: e.matmul(p_acc[:, ri * 256:ri * 256 + M], lhsT=s_Wb[kk][ri][:, :], rhs=ub[:, :, s_],
                                                                  start=(s_ == 0), stop=(s_ == 7)), reads=[s_Wb[kk][ri].b, s_uT.b], writes=[p_acc.b])
                              for ri in range(2):
                                  op("dve", lambda e: e.tensor_copy(out=s_XA[ri][:, 0:M], in_=p_acc[:, ri * 256:ri * 256 + M]),
                                     reads=[p_acc.b], writes=[s_XA[ri].b])
                              if sample:
                                  dma(s_h0[:, :], ss5[l, d, 128 * st:128 * st + 128, :], writes=[s_h0.b], q="pool")
                                  a8r, a8i, na8i = s_pwr[:, 0, td:td + 1], s_pwi[:, 0, td:td + 1], s_npwi[:, 0, td:td + 1]
                                  cc = 0 if d == 0 else M - 1
                                  for ri, (sa, sb_) in enumerate(((a8r, na8i), (a8i, a8r))):
                                      xc = s_XA[ri][:, cc:cc + 1]
                                      op("dve", lambda e: e.scalar_tensor_tensor(out=xc, in0=s_h0[:, 0:1], scalar=sa, in1=xc, op0=ALU.mult, op1=ALU.add),
                                         reads=[s_h0.b, pb_, s_XA[ri].b], writes=[s_XA[ri].b])
                                      op("dve", lambda e: e.scalar_tensor_tensor(out=xc, in0=s_h0[:, 1:2], scalar=sb_, in1=xc, op0=ALU.mult, op1=ALU.add),
                                         reads=[s_h0.b, pb_, s_XA[ri].b], writes=[s_XA[ri].b])
                              src, dst = s_XA, s_XB
                              v3 = lambda t_: t_[:, 0:M].rearrange("p (s m) -> p s m", s=nseq)
                              for k in range(nsteps):
                                  sh = 1 << k
                                  pr, pi, npi = s_pwr[:, k, td:td + 1], s_pwi[:, k, td:td + 1], s_npwi[:, k, td:td + 1]
                                  if d == 0:
                                      lo, ls, kp = slice(sh, Mseq), slice(0, Mseq - sh), slice(0, sh)
                                  else:
                                      lo, ls, kp = slice(0, Mseq - sh), slice(sh, Mseq), slice(Mseq - sh, Mseq)
                                  for ri in range(2):
                                      op("act", lambda e: e.activation(out=v3(dst[ri])[:, :, kp], in_=v3(src[ri])[:, :, kp], func=AF.Copy),
                                         reads=[src[ri].b, dst[ri].b], writes=[dst[ri].b])
                                  for ri, m1 in enumerate((pr, pr)):
                                      op("dve", lambda e: e.scalar_tensor_tensor(out=v3(dst[ri])[:, :, lo], in0=v3(src[ri])[:, :, ls], scalar=m1,
                                                                                 in1=v3(src[ri])[:, :, lo], op0=ALU.mult, op1=ALU.add),
                                         reads=[src[ri].b, pb_, dst[ri].b], writes=[dst[ri].b])
                                  for ri, m2 in enumerate((npi, pi)):
                                      other = src[1 - ri]
                                      op("dve", lambda e: e.scalar_tensor_tensor(out=v3(dst[ri])[:, :, lo], in0=v3(other)[:, :, ls], scalar=m2,
                                                                                 in1=v3(dst[ri])[:, :, lo], op0=ALU.mult, op1=ALU.add),
                                         reads=[other.b, pb_, dst[ri].b], writes=[dst[ri].b])
                                  src, dst = dst, src
                              Hh = src
                              if not sample:
                                  for s_ in range(nseq):
                                      cc = s_ * Mseq + (Mseq - 1 if d == 0 else 0)
                                      for ri in range(2):
                                          op("dve", lambda e: e.tensor_copy(out=s_fin[:, ri:ri + 1], in_=Hh[ri][:, cc:cc + 1]),
                                             reads=[Hh[ri].b, s_fin.b], writes=[s_fin.b])
                                      dma(o_s5[s_, l, d].rearrange("g p r -> (g p) r")[128 * st:128 * st + 128, :], s_fin[:, :], reads=[s_fin.b])
                              W1 = Mseq + 1
                              for ri in range(2):
                                  hv = s_Hs[ri][:, 0:nseq * W1].rearrange("p (s m) -> p s m", s=nseq)
                                  body = hv[:, :, 1:W1] if d == 0 else hv[:, :, 0:Mseq]
                                  edge = hv[:, :, 0:1] if d == 0 else hv[:, :, Mseq:W1]
                                  op("act", lambda e: e.activation(out=body, in_=v3(Hh[ri]), func=AF.Copy), reads=[Hh[ri].b, s_Hs[ri].b], writes=[s_Hs[ri].b])
                                  if sample:
                                      op("dve", lambda e: e.tensor_copy(out=edge, in_=s_h0[:, ri:ri + 1].unsqueeze(1)), reads=[s_h0.b, s_Hs[ri].b], writes=[s_Hs[ri].b])
                                  else:
                                      op("dve", lambda e: e.memset(edge, 0.0), reads=[s_Hs[ri].b], writes=[s_Hs[ri].b])
                              off = 0 if d == 0 else 1
                              for bk in range(nb):
                                  for j in range(8):
                                      kk = j + 1 if d == 0 else 8 - j
                                      for ri in range(2):
                                          hv = s_Hs[ri][:, 0:nseq * W1].rearrange("p (s m) -> p s m", s=nseq)
                                          if nseq == 1:
                                              rhs_ = hv[:, 0, off + 64 * bk:off + 64 * bk + 64]
                                              out_ = ybank[bk][:, :].rearrange("p (m j) -> p m j", j=8)[:, :, j]
                                          else:
                                              rhs_ = hv[:, 2 * bk:2 * bk + 2, off:off + Mseq]
                                              out_ = ybank[bk][:, :].rearrange("p (s m j) -> p s m j", s=2, j=8)[:, :, :, j]
                                          op("pe", lambda e: e.matmul(out_, lhsT=s_Wc[kk][ri][:, :], rhs=rhs_, start=False, stop=False),
                                             reads=[s_Wc[kk][ri].b, s_Hs[ri].b], writes=[ybank[bk].b])
                      if reuse:
                          dma(s_Kbd[:, :, :].rearrange("p a c -> p (a c)"), s5k[hf], reads=[s5k_b], writes=[s_Kbd.b], q="pool")
                      else:
                          for a_ in range(16):
                              op("act", lambda e: e.activation(out=s_Kbd[:, a_, :], in_=s_Kacc[:, a_, :], func=AF.Copy), reads=[s_Kacc.b], writes=[s_Kbd.b])
                          dma(s5k[hf], s_Kbd[:, :, :].rearrange("p a c -> p (a c)"), reads=[s_Kbd.b], writes=[s5k_b])
                      for bk in range(nb):
                          u3 = s_uT[:, hf, bk * 512:(bk + 1) * 512].rearrange("p (m j) -> p m j", j=8)
                          y3 = ybank[bk][:, :].rearrange("p (m j) -> p m j", j=8)
                          for dl in range(8):
                              op("pe", lambda e: e.matmul(y3[:, :, dl:8], lhsT=s_Kbd[:, dl, :], rhs=u3[:, :, 0:8 - dl], start=False, stop=False),
                                 reads=[s_Kbd.b, s_uT.b], writes=[ybank[bk].b])
                              op("pe", lambda e: e.matmul(y3[:, :, 0:8 - dl], lhsT=s_Kbd[:, 8 + dl, :], rhs=u3[:, :, dl:8], start=False, stop=(dl == 7)),
                                 reads=[s_Kbd.b, s_uT.b], writes=[ybank[bk].b])
                      for bk in range(nb):
                          yb_ = ybank[bk]
                          op("act", lambda e: e.activation(out=s_ya[:], in_=yb_[:, :], func=AF.Square), reads=[yb_.b], writes=[s_ya.b])
                          op("dve", lambda e: e.tensor_scalar(out=s_ya[:], in0=s_ya[:], scalar1=0.044715, scalar2=1.0, op0=ALU.mult, op1=ALU.add),
                             reads=[s_ya.b], writes=[s_ya.b])
                          op("dve", lambda e: e.tensor_tensor(out=s_ya[:], in0=s_ya[:], in1=yb_[:, :], op=ALU.mult), reads=[s_ya.b, yb_.b], writes=[s_ya.b])
                          op("act", lambda e: e.activation(out=s_yb[:], in_=s_ya[:], func=AF.Sigmoid, scale=1.5957691216057308),
                             reads=[s_ya.b], writes=[s_yb.b])
                          op("dve", lambda e: e.tensor_tensor(out=s_gT[:, hf, bk * 512:(bk + 1) * 512], in0=s_yb[:], in1=yb_[:, :], op=ALU.mult),
                             reads=[s_yb.b, yb_.b], writes=[s_gT.b])
                  for oh in range(2):
                      for bk in range(nb):
                          tok = slice(bk * 512, (bk + 1) * 512)
                          for ih in range(2):
                              op("pe", lambda e: e.matmul(ptmp[:, :], lhsT=s_glu[:, ih, oh * 128:(oh + 1) * 128], rhs=s_gT[:, ih, tok],
                                                          start=(ih == 0), stop=(ih == 1)), reads=[s_glu.b, s_gT.b], writes=[ptmp.b])
                          op("act", lambda e: e.activation(out=s_ya[:], in_=ptmp[:, :], func=AF.Sigmoid, bias=s_gbcol[:, oh:oh + 1]),
                             reads=[ptmp.b, s_gbcol.b], writes=[s_ya.b])
                          for k in range(8):
                              op("pe", lambda e: e.matmul(p_acc[:, :], lhsT=s_wmix[:, k, 256 + oh * 128:256 + (oh + 1) * 128], rhs=hT[:, k, tok],
                                                          start=(k == 0), stop=(k == 7)), reads=[s_wmix.b, hT.b], writes=[p_acc.b])
                          op("act", lambda e: e.activation(out=s_yb[:], in_=p_acc[:, :], func=AF.Silu), reads=[p_acc.b], writes=[s_yb.b])
                          op("dve", lambda e: e.tensor_tensor(out=s_ya[:], in0=s_ya[:], in1=s_gT[:, oh, tok], op=ALU.mult),
                             reads=[s_ya.b, s_gT.b], writes=[s_ya.b])
                          op("dve", lambda e: e.tensor_tensor(out=brT[:, 4 + oh, tok], in0=s_ya[:], in1=s_yb[:], op=ALU.mult),
                             reads=[s_ya.b, s_yb.b], writes=[brT.b])

              def hy_group(l, g):
                  c, T, L, sample = g["c"], g["T"], g["L"], g["sample"]
                  tb = htab["s" if sample else "p"]
                  nseq, ntt = T // L, L // 128
                  nfb = ntt + 1
                  wbase = 136 if sample else 128
                  NB = min(512, L)
                  nbank = L // NB
                  cos_cb = tb["cos"].rearrange("(a p) j -> p a j", p=128)
                  sin_cb = tb["sin"].rearrange("(a p) j -> p a j", p=128)

                  def sin_layer(pp_, li, dst, w):
                      o4 = 4 * li
                      op("act", lambda e: e.activation(out=h_s2[:, 0:w], in_=pp_[0:64, 0:w], func=AF.Sin, scale=h_mlp[:, o4:o4 + 1],
                                                       bias=h_mlp[:, o4 + 1:o4 + 2]), reads=[pp_.b, h_mlp.b], writes=[h_s2.b])
                      op("act", lambda e: e.activation(out=h_s4[:, 0:w], in_=pp_[0:64, 0:w], func=AF.Sin, scale=h_mlp[:, o4 + 2:o4 + 3],
                                                       bias=h_mlp[:, o4 + 3:o4 + 4]), reads=[pp_.b, h_mlp.b], writes=[h_s4.b])
                      op("dve", lambda e: e.tensor_tensor(out=h_s4[:, 0:w], in0=h_s4[:, 0:w], in1=h_s4[:, 0:w], op=ALU.mult),
                         reads=[h_s4.b], writes=[h_s4.b])
                      op("dve", lambda e: e.tensor_scalar(out=h_s4[:, 0:w], in0=h_s4[:, 0:w], scalar1=-2.0, scalar2=1.0, op0=ALU.mult, op1=ALU.add),
                         reads=[h_s4.b], writes=[h_s4.b])
                      op("dve", lambda e: e.scalar_tensor_tensor(out=dst, in0=h_s2[:, 0:w], scalar=2.0, in1=h_s4[:, 0:w], op0=ALU.mult, op1=ALU.mult),
                         reads=[h_s2.b, h_s4.b], writes=[h_hid[li].b])

                  for c0 in range(0, L, 512):
                      w = min(512, L - c0)
                      dma(h_feat[:, 0:w], tb["feat"][:, c0:c0 + w], writes=[h_feat.b], q="pool")
                      pp_ = psf()
                      op("pe", lambda e: e.matmul(pp_[0:64, 0:w], lhsT=hw1_b[:, :], rhs=h_feat[:, 0:w], start=True, stop=True),
                         reads=[hw1_b.b, h_feat.b], writes=[pp_.b])
                      sin_layer(pp_, 0, h_hid[0][:, c0:c0 + w], w)
                  for c0 in range(0, L, 512):
                      w = min(512, L - c0)
                      pp_ = psf()
                      op("pe", lambda e: e.matmul(pp_[0:64, 0:w], lhsT=hw2_b[:, :], rhs=h_hid[0][:, c0:c0 + w], start=True, stop=True),
                         reads=[hw2_b.b, h_hid[0].b], writes=[pp_.b])
                      sin_layer(pp_, 1, h_hid[1][:, c0:c0 + w], w)

                  if not sample:
                      for b in range(ntt):
                          dma(h_tc[b][:, 0:ntt, :], cos_cb[:, :, b * 128:(b + 1) * 128], writes=[h_tc[b].b], q="pool")
                          dma(h_ts[b][:, 0:ntt, :], sin_cb[:, :, b * 128:(b + 1) * 128], writes=[h_ts[b].b], q="pool")
                          dma(h_cr[b][:, 0:NB], tb["cos"][b * 128:(b + 1) * 128, 0:NB], writes=[h_cr[b].b], q="pool")
                          dma(h_sr[b][:, 0:NB], tb["sin"][b * 128:(b + 1) * 128, 0:NB], writes=[h_sr[b].b], q="pool")
                  for hc in range(2):
                      r_ = 0
                      for k in range(3):
                          for wh in range(3):
                              dma(h_rows[r_:r_ + 1, :], hy_cw[l, k:k + 1, wh * 256 + hc * 128:wh * 256 + hc * 128 + 128], reads=[h_rows.b], writes=[h_rows.b])
                              r_ += 1
                      for wh in range(3):
                          dma(h_rows[r_:r_ + 1, :], hy_cb[l:l + 1, wh * 256 + hc * 128:wh * 256 + hc * 128 + 128], reads=[h_rows.b], writes=[h_rows.b])
                          r_ += 1
                      for o in range(2):
                          dma(h_rows[r_:r_ + 1, :], hy_bi[l, o:o + 1, hc * 128:hc * 128 + 128], reads=[h_rows.b], writes=[h_rows.b])
                          r_ += 1
                      pp_ = psf()
                      op("pe", lambda e: e.transpose(pp_[:, 0:14], h_rows[:, :], ident_f[0:14, 0:14]), reads=[h_rows.b, ident_f.b], writes=[pp_.b])
                      op("dve", lambda e: e.tensor_copy(out=h_colp[:], in_=pp_[:, 0:14]), reads=[pp_.b], writes=[h_colp.b])
                      if sample:
                          dma(h_wmix[:, :, :].rearrange("p k c -> p (k c)"), wc["hy%d" % hc][0][:, :], reads=[wc["hy%d" % hc][1]], writes=[h_wmix.b], q="pool")
                      else:
                          for k in range(8):
                              st_ = wstage[0]
                              for j in range(4):
                                  dma(st_[:, j * 128:(j + 1) * 128], w_in[l, k * 128:(k + 1) * 128, HY0 + j * 256 + hc * 128:HY0 + j * 256 + hc * 128 + 128],
                                      reads=[st_.b], writes=[st_.b])
                              op("dve" if k % 2 == 0 else "pool", lambda e: e.tensor_copy(out=h_wmix[:, k, :], in_=st_[:, 0:512]),
                                 reads=[st_.b], writes=[h_wmix.b])
                          dma(wc["hy%d" % hc][0][:, :], h_wmix[:, :, :].rearrange("p k c -> p (k c)"), reads=[h_wmix.b], writes=[wc["hy%d" % hc][1]])
                      if sample:
                          dma(h_w3[:].rearrange("p j c -> p (j c)"), wc["w3%d" % hc][0][:, :], reads=[wc["w3%d" % hc][1]], writes=[h_w3.b], q="pool")
                      else:
                          st_ = wstage[0]
                          for j in range(4):
                              dma(st_[0:64, j * 128:(j + 1) * 128], hy_w3[l, :, j * 256 + hc * 128:j * 256 + hc * 128 + 128], reads=[st_.b], writes=[st_.b], q="pool")
                          op("dve", lambda e: e.tensor_copy(out=h_w3[:].rearrange("p j c -> p (j c)"), in_=st_[0:64, 0:512]), reads=[st_.b], writes=[h_w3.b])
                          dma(wc["w3%d" % hc][0][:, :], h_w3[:].rearrange("p j c -> p (j c)"), reads=[h_w3.b], writes=[wc["w3%d" % hc][1]])
                      zT = h_sig[:, 3, 0:T]
                      z3 = zT.rearrange("p (s t) -> p s t", s=nseq)
                      for wh in range(3):
                          for c0 in range(0, T, 512):
                              pp_ = psf()
                              for k in range(8):
                                  op("pe", lambda e: e.matmul(pp_[:, :], lhsT=h_wmix[:, k, wh * 128:(wh + 1) * 128], rhs=hT[:, k, c0:c0 + 512],
                                                              start=(k == 0), stop=(k == 7)), reads=[h_wmix.b, hT.b], writes=[pp_.b])
                              op("act", lambda e: e.activation(out=h_sig[:, 3, c0:c0 + 512], in_=pp_[:, :], func=AF.Copy), reads=[pp_.b], writes=[h_sig.b])
                          dst = h_sig[:, wh, 0:T]
                          d3 = dst.rearrange("p (s t) -> p s t", s=nseq)
                          op("dve", lambda e: e.tensor_scalar(out=dst, in0=zT, scalar1=h_colp[:, 3 + wh:4 + wh], scalar2=h_colp[:, 9 + wh:10 + wh],
                                                              op0=ALU.mult, op1=ALU.add), reads=[h_sig.b, h_colp.b], writes=[h_sig.b])
                          op("dve", lambda e: e.scalar_tensor_tensor(out=d3[:, :, 1:L], in0=z3[:, :, 0:L - 1], scalar=h_colp[:, wh:wh + 1], in1=d3[:, :, 1:L],
                                                                     op0=ALU.mult, op1=ALU.add), reads=[h_sig.b, h_colp.b], writes=[h_sig.b])
                          op("dve", lambda e: e.scalar_tensor_tensor(out=d3[:, :, 0:L - 1], in0=z3[:, :, 1:L], scalar=h_colp[:, 6 + wh:7 + wh], in1=d3[:, :, 0:L - 1],
                                                                     op0=ALU.mult, op1=ALU.add), reads=[h_sig.b, h_colp.b], writes=[h_sig.b])
                      for lt in range(ntt):
                          pp_ = psf()
                          op("pe", lambda e: e.matmul(pp_[:, :], lhsT=h_hid[1][:, lt * 128:(lt + 1) * 128], rhs=h_w3[:].rearrange("p j c -> p (j c)"),
                                                      start=True, stop=True), reads=[h_hid[1].b, h_w3.b], writes=[pp_.b])
                          dma(h_win[:, :], tb["win"][lt * 128:(lt + 1) * 128, hc * 128:(hc + 1) * 128], writes=[h_win.b], q="pool")
                          f4 = h_fw[:].rearrange("p (j c) -> p j c", j=4)
                          op("dve", lambda e: e.tensor_tensor(out=f4, in0=pp_[:, :].rearrange("p (j c) -> p j c", j=4),
                                                              in1=h_win[:, :].unsqueeze(1).to_broadcast([128, 4, 128]), op=ALU.mult),
                             reads=[pp_.b, h_win.b], writes=[h_fw.b])
                          if lt == 0:
                              op("dve", lambda e: e.tensor_scalar(out=h_fw[:, 256:512], in0=h_fw[:, 256:512], scalar1=hcon_t[:, 160:161], scalar2=None,
                                                                  op0=ALU.mult), reads=[h_fw.b, hcon_t.b], writes=[h_fw.b])
                          op("act", lambda e: e.activation(out=h_abs[:], in_=h_fw[:], func=AF.Abs), reads=[h_fw.b], writes=[h_abs.b])
                          op("pe", lambda e: e.matmul(p_acc[:, :], lhsT=ones_sq[:, :], rhs=h_abs[:, :], start=(lt == 0), stop=(lt == ntt - 1)),
                             reads=[ones_sq.b, h_abs.b], writes=[p_acc.b])
                          op("dve", lambda e: e.tensor_tensor(out=h_ksd[:, lt, 0:2, :], in0=f4[:, 0:2, :], in1=f4[:, 2:4, :], op=ALU.add),
                             reads=[h_fw.b], writes=[h_ksd.b])
                          op("dve", lambda e: e.tensor_tensor(out=h_ksd[:, lt, 2:4, :], in0=f4[:, 2:4, :], in1=f4[:, 0:2, :], op=ALU.subtract),
                             reads=[h_fw.b, h_ksd.b], writes=[h_ksd.b])
                      n4 = p_acc[:, :].rearrange("p (j c) -> p j c", j=4)
                      op("dve", lambda e: e.tensor_copy(out=h_rn[:], in_=n4[:, 0:2, :]), reads=[p_acc.b], writes=[h_rn.b])
                      op("dve", lambda e: e.tensor_tensor(out=h_rn[:], in0=h_rn[:], in1=n4[:, 2:4, :], op=ALU.add), reads=[p_acc.b, h_rn.b], writes=[h_rn.b])
                      op("dve", lambda e: e.reciprocal(out=h_rn[:], in_=h_rn[:]), reads=[h_rn.b], writes=[h_rn.b])

                      def long_conv(s_, o, src_idx, combine):
                          t0 = s_ * L
                          for a in range(ntt):
                              pt_ = psb()
                              op("pe", lambda e: e.transpose(pt_[:, 0:128], h_sig[:, src_idx, t0 + a * 128:t0 + (a + 1) * 128], ident_b[:, :]),
                                 reads=[h_sig.b, ident_b.b], writes=[pt_.b])
                              op("act", lambda e: e.activation(out=h_utm[:, a, :], in_=pt_[:, 0:128], func=AF.Copy), reads=[pt_.b], writes=[h_utm.b])
                          for b in range(nfb):
                              nyq = (b == ntt)
                              tcb, tsb = h_tc[b % 2], h_ts[b % 2]
                              if not nyq and sample:
                                  dma(tcb[:, 0:ntt, :], cos_cb[:, :, b * 128:(b + 1) * 128], writes=[tcb.b], q="pool")
                                  dma(tsb[:, 0:ntt, :], sin_cb[:, :, b * 128:(b + 1) * 128], writes=[tsb.b], q="pool")
                              pu, pk = psf(), psf()
                              for dst_, col, tab, rhs_of in ((pu, 0, "c", lambda a: h_utm[:, a, :]), (pu, 128, "s", lambda a: h_utm[:, a, :]),
                                                             (pk, 0, "c", lambda a: h_ksd[:, a, o, :]), (pk, 128, "s", lambda a: h_ksd[:, a, 2 + o, :])):
                                  if nyq and tab == "s":
                                      continue
                                  for a in range(ntt):
                                      lt_ = h_alt[:, :] if nyq else (tcb if tab == "c" else tsb)[:, a, :]
                                      rd_ = [h_alt.b] if nyq else [(tcb if tab == "c" else tsb).b]
                                      op("pe", lambda e: e.matmul(dst_[:, col:col + 128], lhsT=lt_, rhs=rhs_of(a), start=(a == 0), stop=(a == ntt - 1)),
                                         reads=rd_ + [h_utm.b, h_ksd.b], writes=[dst_.b])
                              wc = hcon_t[:, wbase + b:wbase + b + 1]
                              op("dve", lambda e: e.scalar_tensor_tensor(out=h_k[0][:], in0=pk[:, 0:128], scalar=wc, in1=h_rn[:, o, :], op0=ALU.mult, op1=ALU.mult),
                                 reads=[pk.b, hcon_t.b, h_rn.b], writes=[h_k[0].b])
                              if nyq:
                                  op("dve", lambda e: e.tensor_tensor(out=h_Y[:, b, 0, :], in0=pu[:, 0:128], in1=h_k[0][:], op=ALU.mult),
                                     reads=[pu.b, h_k[0].b], writes=[h_Y.b])
                                  continue
                              op("dve", lambda e: e.scalar_tensor_tensor(out=h_k[1][:], in0=pk[:, 128:256], scalar=wc, in1=h_rn[:, o, :], op0=ALU.mult, op1=ALU.mult),
                                 reads=[pk.b, hcon_t.b, h_rn.b], writes=[h_k[1].b])
                              op("dve", lambda e: e.tensor_tensor(out=h_t[0][:], in0=pu[:, 0:128], in1=h_k[0][:], op=ALU.mult), reads=[pu.b, h_k[0].b], writes=[h_t[0].b])
                              op("dve", lambda e: e.tensor_tensor(out=h_t[1][:], in0=pu[:, 128:256], in1=h_k[1][:], op=ALU.mult), reads=[pu.b, h_k[1].b], writes=[h_t[1].b])
                              op("dve", lambda e: e.tensor_tensor(out=h_t[2][:], in0=pu[:, 128:256], in1=h_k[0][:], op=ALU.mult), reads=[pu.b, h_k[0].b], writes=[h_t[2].b])
                              op("dve", lambda e: e.tensor_tensor(out=h_t[3][:], in0=pu[:, 0:128], in1=h_k[1][:], op=ALU.mult), reads=[pu.b, h_k[1].b], writes=[h_t[3].b])
                              op("dve", lambda e: e.tensor_tensor(out=h_Y[:, b, 0, :], in0=h_t[0][:], in1=h_t[1][:], op=ALU.add),
                                 reads=[h_t[0].b, h_t[1].b], writes=[h_Y.b])
                              op("dve", lambda e: e.tensor_tensor(out=h_Y[:, b, 1, :], in0=h_t[2][:], in1=h_t[3][:], op=ALU.subtract),
                                 reads=[h_t[2].b, h_t[3].b, h_Y.b], writes=[h_Y.b])
                          for bank in range(nbank):
                              c0 = bank * NB
                              for b in range(ntt):
                                  crb, srb = h_cr[b % 2], h_sr[b % 2]
                                  if sample:
                                      dma(crb[:, 0:NB], tb["cos"][b * 128:(b + 1) * 128, c0:c0 + NB], writes=[crb.b], q="pool")
                                      dma(srb[:, 0:NB], tb["sin"][b * 128:(b + 1) * 128, c0:c0 + NB], writes=[srb.b], q="pool")
                                  op("pe", lambda e: e.matmul(p_acc[:, 0:NB], lhsT=h_Y[:, b, 0, :], rhs=crb[:, 0:NB], start=(b == 0), stop=False),
                                     reads=[h_Y.b, crb.b], writes=[p_acc.b])
                                  op("pe", lambda e: e.matmul(p_acc[:, 0:NB], lhsT=h_Y[:, b, 1, :], rhs=srb[:, 0:NB], start=False, stop=False),
                                     reads=[h_Y.b, srb.b], writes=[p_acc.b])
                              op("pe", lambda e: e.matmul(p_acc[:, 0:NB], lhsT=h_Y[0:1, ntt, 0, :], rhs=haltr_t[0:1, 0:NB], start=False, stop=True),
                                 reads=[h_Y.b, haltr_t.b], writes=[p_acc.b])
                              combine(t0 + c0)

                      def comb1(tk):
                          op("dve", lambda e: e.scalar_tensor_tensor(out=h_fw[:, 0:NB], in0=h_sig[:, 0, tk:tk + NB], scalar=h_colp[:, 12:13], in1=p_acc[:, 0:NB],
                                                                     op0=ALU.mult, op1=ALU.add), reads=[h_sig.b, h_colp.b, p_acc.b], writes=[h_fw.b])
                          op("dve", lambda e: e.tensor_tensor(out=h_sig[:, 3, tk:tk + NB], in0=h_fw[:, 0:NB], in1=h_sig[:, 1, tk:tk + NB], op=ALU.mult),
                             reads=[h_fw.b, h_sig.b], writes=[h_sig.b])

                      def comb2(tk):
                          op("dve", lambda e: e.scalar_tensor_tensor(out=h_fw[:, 0:NB], in0=h_sig[:, 3, tk:tk + NB], scalar=h_colp[:, 13:14], in1=p_acc[:, 0:NB],
                                                                     op0=ALU.mult, op1=ALU.add), reads=[h_sig.b, h_colp.b, p_acc.b], writes=[h_fw.b])
                          op("dve", lambda e: e.tensor_tensor(out=h_fw[:, 0:NB], in0=h_fw[:, 0:NB], in1=h_sig[:, 2, tk:tk + NB], op=ALU.mult),
                             reads=[h_fw.b, h_sig.b], writes=[h_fw.b])
                          pg_ = psf()
                          for k in range(8):
                              op("pe", lambda e: e.matmul(pg_[:, 0:NB], lhsT=h_wmix[:, k, 384:512], rhs=hT[:, k, tk:tk + NB], start=(k == 0), stop=(k == 7)),
                                 reads=[h_wmix.b, hT.b], writes=[pg_.b])
                          op("act", lambda e: e.activation(out=h_abs[:, 0:NB], in_=pg_[:, 0:NB], func=AF.Silu), reads=[pg_.b], writes=[h_abs.b])
                          op("dve", lambda e: e.tensor_tensor(out=brT[:, 2 + hc, tk:tk + NB], in0=h_fw[:, 0:NB], in1=h_abs[:, 0:NB], op=ALU.mult),
                             reads=[h_fw.b, h_abs.b], writes=[brT.b])

                      for s_ in range(nseq):
                          long_conv(s_, 0, 0, comb1)
                      for s_ in range(nseq):
                          long_conv(s_, 1, 3, comb2)

              groups = [
                  dict(c=0, src=(xp if l == 0 else xmid_p), srcb=xmid_p_b, T=TP, L=SEQ, sample=False),
                  dict(c=1, src=(xs if l == 0 else xmid_s), srcb=xmid_s_b, T=LS, L=LS, sample=True),
              ]
              for g in groups:
                  c, T, L, sample = g["c"], g["T"], g["L"], g["sample"]
                  ntile = T // 128
                  nseq = T // L
                  kb.barrier()
                  if sample:
                      dma(w_in_b[:, :, :].rearrange("p k c -> p (k c)"), wc["in"][0][:, :], reads=[wc["in"][1]], writes=[w_in_b.b], q="pool")
                  else:
                      for k in range(8):
                          st = wstage[k % 2]
                          dma(st[:, 0:NINU], w_in[l, k * 128:(k + 1) * 128, 0:NINU], writes=[st.b], q="pool")
                          eng = "dve" if k % 2 == 0 else "pool"
                          op(eng, lambda e: e.tensor_copy(out=w_in_b[:, k, :], in_=st[:, 0:NINU]), reads=[st.b], writes=[w_in_b.b])
                      dma(wc["in"][0][:, :], w_in_b[:, :, :].rearrange("p k c -> p (k c)"), reads=[w_in_b.b], writes=[wc["in"][1]])

                  def k_path(src_ckv_f32, src_kro_f32, rope_tile, kcol, vt, rd):
                      op("dve", lambda e: e.tensor_copy(out=ckvn_b[:], in_=src_ckv_f32), reads=rd, writes=[ckvn_b.b])
                      p = psb()
                      op("pe", lambda e: e.transpose(p[:, 0:128], ckvn_b[:, :], ident_b[:, :]),
                         reads=[ckvn_b.b, ident_b.b], writes=[p.b])
                      op("act", lambda e: e.activation(out=ckvT[:], in_=p[:, 0:128], func=AF.Copy),
                         reads=[p.b], writes=[ckvT.b])
                      pkv = psf()
                      op("pe", lambda e: e.matmul(pkv[:, :], lhsT=ckvT[:, :], rhs=w_ukv_b[:, :], start=True, stop=True),
                         reads=[ckvT.b, w_ukv_b.b], writes=[pkv.b])
                      chk('k1')
                      kv3 = pkv[:, :].rearrange("p (h d) -> p h d", h=H)
                      op("dve", lambda e: e.tensor_copy(out=kcat[:, :, 0:64], in_=kv3[:, :, 0:64]),
                         reads=[pkv.b], writes=[kcat.b])
                      op("pool", lambda e: e.tensor_copy(out=kcat[:, :, 64:96],
                                                         in_=src_kro_f32.unsqueeze(1).to_broadcast([128, H, 32])),
                         reads=rd + [kcat.b], writes=[kcat.b])
                      chk('k2')
                      op("act", lambda e: e.activation(out=vaug[:, vt, :, 0:64], in_=kv3[:, :, 64:128], func=AF.Copy),
                         reads=[pkv.b], writes=[vaug.b])
                      op("pool", lambda e: e.memset(vaug[:, vt, :, 64:65], 1.0), reads=[vaug.b], writes=[vaug.b])
                      chk('k3')
                      head_norm(kcat, bc_kn, rope_tile)
                      chk('k4')
                      pq = psb()
                      for h in range(H):
                          op("pe", lambda e: e.transpose(pq[0:QK, h * 128:(h + 1) * 128], qb[:, h, :], ident_b[:, :]),
                             reads=[qb.b, ident_b.b], writes=[pq.b])
                      op("act", lambda e: e.activation(out=kT[0:QK, :, kcol:kcol + 128],
                                                       in_=pq[0:QK, 0:512].rearrange("p (h t) -> p h t", h=H),
                                                       func=AF.Copy), reads=[pq.b], writes=[kT.b])

                  def head_norm(src, wbc, rope_tile):
                      op("act", lambda e: e.activation(out=sqh[:], in_=src[:], func=AF.Square),
                         reads=[src.b], writes=[sqh.b])
                      op("dve", lambda e: e.tensor_reduce(out=ssh[:], in_=sqh[:], axis=AX.X, op=ALU.add),
                         reads=[sqh.b], writes=[ssh.b])
                      rstd_from_ss(rsh, ssh, QK)
                      op("dve", lambda e: e.tensor_tensor(out=qf[:], in0=src[:],
                                                          in1=rsh[:, :].unsqueeze(2).to_broadcast([128, H, QK]),
                                                          op=ALU.mult), reads=[src.b, rsh.b], writes=[qf.b])
                      op("dve", lambda e: e.tensor_tensor(out=qf[:], in0=qf[:],
                                                          in1=wbc[:, :].unsqueeze(1).to_broadcast([128, H, QK]),
                                                          op=ALU.mult), reads=[qf.b, wbc.b], writes=[qf.b])
                      if rope_tile is not None:
                          r5 = qf[:, :, 64:96].rearrange("p h (a f j) -> p h a f j", a=2, f=2)
                          t5 = rtmp[:].rearrange("p h (a f j) -> p h a f j", a=2, f=2)
                          op("dve", lambda e: e.tensor_copy(out=t5[:, :, :, 0, :], in_=r5[:, :, :, 1, :]),
                             reads=[qf.b], writes=[rtmp.b])
                          op("dve", lambda e: e.tensor_copy(out=t5[:, :, :, 1, :], in_=r5[:, :, :, 0, :]),
                             reads=[qf.b, rtmp.b], writes=[rtmp.b])
                          op("dve", lambda e: e.tensor_tensor(out=rtmp[:], in0=rtmp[:],
                                                              in1=sinT[:, :].unsqueeze(1).to_broadcast([128, H, 32]),
                                                              op=ALU.mult), reads=[rtmp.b, sinT.b], writes=[rtmp.b])
                          op("dve", lambda e: e.tensor_tensor(out=rtmp2[:], in0=qf[:, :, 64:96],
                                                              in1=cosT[:, :].unsqueeze(1).to_broadcast([128, H, 32]),
                                                              op=ALU.mult), reads=[qf.b, cosT.b], writes=[rtmp2.b])
                          op("dve", lambda e: e.tensor_tensor(out=qf[:, :, 64:96], in0=rtmp2[:], in1=rtmp[:], op=ALU.add),
                             reads=[rtmp.b, rtmp2.b, qf.b], writes=[qf.b])
                      op("dve", lambda e: e.tensor_copy(out=qb[:], in_=qf[:]), reads=[qf.b], writes=[qb.b])

                  for s in range(nseq):
                      tps = L // 128
                      nctx = 0
                      if sample:
                          nctx = PAST // 128
                          for t in range(nctx):
                              dma(ckvn[:, :], cckv[l, t * 128:(t + 1) * 128, :], writes=[ckvn.b], q="pool")
                              dma(kro[:, :], ckro[l, t * 128:(t + 1) * 128, :], writes=[kro.b], q="pool")
                              k_path(ckvn[:, :], kro[:, :], None, t * 128, t, [ckvn.b, kro.b])
                      for tt in range(tps):
                          ti = s * tps + tt
                          r0 = ti * 128
                          dma(xt[:, :], g["src"][r0:r0 + 128, :], reads=[g["srcb"]], writes=[xt.b], q="pool")
                          op("act", lambda e: e.activation(out=junk[:], in_=xt[:], func=AF.Square, accum_out=ss1[:, 0:1]),
                             reads=[xt.b], writes=[xn.b, ss1.b])
                          rstd_from_ss(rs1, ss1, D)
                          op("dve", lambda e: e.tensor_scalar(out=xn[:], in0=xt[:], scalar1=rs1[:, 0:1], scalar2=None,
                                                              op0=ALU.mult), reads=[xt.b, rs1.b], writes=[xn.b])
                          pt = psb()
                          for k in range(8):
                              op("pe", lambda e: e.transpose(pt[:, k * 128:(k + 1) * 128], xn[:, k * 128:(k + 1) * 128],
                                                             ident_b[:, :]), reads=[xn.b, ident_b.b], writes=[pt.b])
                          for k in range(8):
                              op("act", lambda e: e.activation(out=hT[:, k, r0:r0 + 128], in_=pt[:, k * 128:(k + 1) * 128],
                                                               func=AF.Identity, scale=effsc[:, c, k:k + 1],
                                                               bias=shift[:, c, k:k + 1]),
                                 reads=[pt.b, effsc.b, shift.b], writes=[hT.b])
                          chk('normT')
                          pl = psf()
                          for k in range(8):
                              op("pe", lambda e: e.matmul(pl[:, 0:352], lhsT=hT[:, k, r0:r0 + 128], rhs=w_in_b[:, k, 0:352],
                                                          start=(k == 0), stop=(k == 7)),
                                 reads=[hT.b, w_in_b.b], writes=[pl.b])
                          op("act", lambda e: e.activation(out=lat[:], in_=pl[:, 0:352], func=AF.Copy),
                             reads=[pl.b], writes=[lat.b])
                          op("act", lambda e: e.activation(out=junk2[:, 0:192], in_=lat[:, 0:192], func=AF.Square,
                                                           accum_out=ssq[:, 0:1]), reads=[lat.b], writes=[junk2.b, ssq.b])
                          op("act", lambda e: e.activation(out=junk2[:, 0:128], in_=lat[:, 192:320], func=AF.Square,
                                                           accum_out=ssq[:, 1:2]), reads=[lat.b, ssq.b], writes=[junk2.b, ssq.b])
                          op("act", lambda e: e.activation(out=rsq[:, 0:1], in_=ssq[:, 0:1], func=AF.Sqrt, scale=1.0 / 192,
                                                           bias=eps_c[:, 0:1]), reads=[ssq.b, eps_c.b], writes=[rsq.b])
                          op("act", lambda e: e.activation(out=rsq[:, 1:2], in_=ssq[:, 1:2], func=AF.Sqrt, scale=1.0 / 128,
                                                           bias=eps_c[:, 0:1]), reads=[ssq.b, eps_c.b, rsq.b], writes=[rsq.b])
                          op("dve", lambda e: e.reciprocal(out=rsq[:], in_=rsq[:]), reads=[rsq.b], writes=[rsq.b])
                          op("dve", lambda e: e.scalar_tensor_tensor(out=cqn[:], in0=lat[:, 0:192], scalar=rsq[:, 0:1],
                                                                     in1=bc_qa[:, :], op0=ALU.mult, op1=ALU.mult),
                             reads=[lat.b, rsq.b, bc_qa.b], writes=[cqn.b])
                          op("dve", lambda e: e.scalar_tensor_tensor(out=ckvn[:], in0=lat[:, 192:320], scalar=rsq[:, 1:2],
                                                                     in1=bc_kva[:, :], op0=ALU.mult, op1=ALU.mult),
                             reads=[lat.b, rsq.b, bc_kva.b], writes=[ckvn.b])
                          if not sample:
                              dma(o_ckv[s, l, tt * 128:(tt + 1) * 128, :], ckvn[:, :], reads=[ckvn.b])
                              dma(o_kro[s, l, tt * 128:(tt + 1) * 128, :], lat[:, 320:352], reads=[lat.b])
                          rope_tile = None
                          if sample:
                              dma(cosT[:, :], ropec[r0:r0 + 128, :], writes=[cosT.b], q="pool")
                              dma(sinT[:, :], ropes[r0:r0 + 128, :], writes=[sinT.b], q="pool")
                              rope_tile = True
                          chk('latent')
                          pc = psb()
                          op("pe", lambda e: e.transpose(pc[:, 0:128], cqn[:, 0:128], ident_b[:, :]),
                             reads=[cqn.b, ident_b.b], writes=[pc.b])
                          op("pe", lambda e: e.transpose(pc[0:64, 128:256], cqn[:, 128:192], ident_b[:, :]),
                             reads=[cqn.b, ident_b.b], writes=[pc.b])
                          op("act", lambda e: e.activation(out=cqT[:, 0, :], in_=pc[:, 0:128], func=AF.Copy),
                             reads=[pc.b], writes=[cqT.b])
                          op("act", lambda e: e.activation(out=cqT[0:64, 1, :], in_=pc[0:64, 128:256], func=AF.Copy),
                             reads=[pc.b, cqT.b], writes=[cqT.b])
                          pqp = psf()
                          op("pe", lambda e: e.matmul(pqp[:, 0:384], lhsT=cqT[:, 0, :], rhs=w_uq_a[:, :], start=True, stop=False),
                             reads=[cqT.b, w_uq_a.b], writes=[pqp.b])
                          op("pe", lambda e: e.matmul(pqp[:, 0:384], lhsT=cqT[0:64, 1, :], rhs=w_uq_c[:, :], start=False, stop=True),
                             reads=[cqT.b, w_uq_c.b], writes=[pqp.b])
                          op("act", lambda e: e.activation(out=qpre[:].rearrange("p h d -> p (h d)"), in_=pqp[:, 0:384],
                                                           func=AF.Copy), reads=[pqp.b], writes=[qpre.b])
                          head_norm(qpre, bc_qn, rope_tile)
                          pq = psb()
                          for h in range(H):
                              op("pe", lambda e: e.transpose(pq[0:QK, h * 128:(h + 1) * 128], qb[:, h, :], ident_b[:, :]),
                                 reads=[qb.b, ident_b.b], writes=[pq.b])
                          op("act", lambda e: e.activation(out=qT[0:QK, :, tt * 128:(tt + 1) * 128],
                                                           in_=pq[0:QK, 0:512].rearrange("p (h t) -> p h t", h=H),
                                                           func=AF.Copy), reads=[pq.b], writes=[qT.b])
                          chk('queries')
                          k_path(ckvn[:, :], lat[:, 320:352], rope_tile, (nctx + tt) * 128, nctx + tt, [ckvn.b, lat.b])

                      chk('tile0') if False else None
                      chk('phaseA')
                      nk = nctx + tps
                      QB = min(512, L)
                      nqs = QB // 128
                      po_b = [pf[5], pf[4], pf[3], pf[2]][:nqs]
                      for qb_ in range(L // QB):
                          q0 = qb_ * QB
                          for h in range(H):
                              for kt in range(nk):
                                  pss = pf[kt % 2]
                                  pTb = pT2[kt % 2]
                                  op("pe", lambda e: e.matmul(pss[:, 0:QB], lhsT=kT[0:QK, h, kt * 128:(kt + 1) * 128],
                                                              rhs=qT[0:QK, h, q0:q0 + QB], start=True, stop=True),
                                     reads=[kT.b, qT.b], writes=[pss.b])
                                  op("act", lambda e: e.activation(out=pTb[:, 0:QB], in_=pss[:, 0:QB], func=AF.Exp, scale=QK ** -0.5),
                                     reads=[pss.b], writes=[pTb.b])
                                  for qs in range(nqs):
                                      op("pe", lambda e: e.matmul(po_b[qs][:, 0:65], lhsT=pTb[:, qs * 128:(qs + 1) * 128], rhs=vaug[:, kt, h, 0:65],
                                                                  start=(kt == 0), stop=(kt == nk - 1)),
                                         reads=[pTb.b, vaug.b], writes=[po_b[qs].b])
                              for qs in range(nqs):
                                  op("dve", lambda e: e.reciprocal(out=den[:, 0:1], in_=po_b[qs][:, 64:65]), reads=[po_b[qs].b], writes=[den.b])
                                  op("dve", lambda e: e.tensor_scalar(out=omla4[:, qs, h, :], in0=po_b[qs][:, 0:64], scalar1=den[:, 0:1], scalar2=None,
                                                                      op0=ALU.mult), reads=[po_b[qs].b, den.b], writes=[omla4.b])
                          for qs in range(nqs):
                              r0 = s * L + q0 + qs * 128
                              pgm = pf[0]
                              for k in range(8):
                                  op("pe", lambda e: e.matmul(pgm[:, 0:256], lhsT=hT[:, k, r0:r0 + 128], rhs=w_in_b[:, k, 352:608],
                                                              start=(k == 0), stop=(k == 7)),
                                     reads=[hT.b, w_in_b.b], writes=[pgm.b])
                              op("act", lambda e: e.activation(out=silg[:], in_=pgm[:, 0:256], func=AF.Silu),
                                 reads=[pgm.b], writes=[silg.b])
                              op("dve", lambda e: e.tensor_tensor(out=omg[:], in0=omla4[:, qs, :, :].rearrange("p h d -> p (h d)"),
                                                                  in1=silg[:], op=ALU.mult), reads=[omla4.b, silg.b], writes=[omg.b])
                              pbr = psb()
                              for k2 in range(2):
                                  op("pe", lambda e: e.transpose(pbr[:, k2 * 128:(k2 + 1) * 128], omg[:, k2 * 128:(k2 + 1) * 128],
                                                                 ident_b[:, :]), reads=[omg.b, ident_b.b], writes=[pbr.b])
                              op("act", lambda e: e.activation(out=brT[:, 0:2, r0:r0 + 128],
                                                               in_=pbr[:, 0:256].rearrange("p (k t) -> p k t", k=2),
                                                               func=AF.Copy), reads=[pbr.b], writes=[brT.b])

                  chk('attn%d' % c)
                  kb.barrier()
                  gla_group(l, g)
                  chk('gla%d' % c)
                  kb.barrier()
                  s5_group(l, g)
                  chk('s5%d' % c)
                  kb.barrier()
                  hy_group(l, g)
                  chk('hy%d' % c)
                  kb.barrier()
                  if sample:
                      dma(w_out_b[:, :, :].rearrange("p k c -> p (k c)"), wc["out"][0][:, :], reads=[wc["out"][1]], writes=[w_out_b.b], q="pool")
                  else:
                      for k in range(8):
                          st = wstage[k % 2]
                          dma(st[:, 0:D], w_out[l, k * 128:(k + 1) * 128, :], writes=[st.b], q="pool")
                          eng = "dve" if k % 2 == 0 else "pool"
                          op(eng, lambda e: e.tensor_copy(out=w_out_b[:, k, :], in_=st[:, 0:D]), reads=[st.b], writes=[w_out_b.b])
                      dma(wc["out"][0][:, :], w_out_b[:, :, :].rearrange("p k c -> p (k c)"), reads=[w_out_b.b], writes=[wc["out"][1]])
                  for ti in range(ntile):
                      r0 = ti * 128
                      last = (l == DEPTH - 1)
                      dma(xt[:, :], g["src"][r0:r0 + 128, :], reads=[g["srcb"]], writes=[xt.b], q="pool")
                      for hf in range(2):
                          p = psf()
                          for k in range(8):
                              op("pe", lambda e: e.matmul(p[:, :], lhsT=brT[:, k, r0:r0 + 128],
                                                          rhs=w_out_b[:, k, hf * 512:(hf + 1) * 512],
                                                          start=(k == 0), stop=(k == 7)),
                                 reads=[brT.b, w_out_b.b], writes=[p.b])
                          op("dve", lambda e: e.tensor_tensor(out=ot[:, hf * 512:(hf + 1) * 512], in0=p[:, :],
                                                              in1=gate_bc[c][:, hf * 512:(hf + 1) * 512], op=ALU.mult),
                             reads=[p.b, gate_bc[c].b], writes=[ot.b])
                      op("dve", lambda e: e.tensor_tensor(out=ot[:], in0=ot[:], in1=xt[:], op=ALU.add),
                         reads=[ot.b, xt.b], writes=[ot.b])
                      if not last:
                          dst = (xmid_s if sample else xmid_p)
                          dma(dst[r0:r0 + 128, :], ot[:, :], reads=[ot.b], writes=[xmid_s_b if sample else xmid_p_b])
                      elif not sample:
                          dma(yp[r0:r0 + 128, :], ot[:, :], reads=[ot.b])
                      else:
                          dma(ys[r0:r0 + 128, :], ot[:, :], reads=[ot.b])

              chk('layer0')
        except _Stop:
            if stop == 'layer0':
                dma(yp[:, :], xmid_p[:, :], reads=[xmid_p_b])
                dma(ys[:, :], xmid_s[:, :], reads=[xmid_s_b])
            kb.barrier()
            kb.phase("DBG")
            dbt = sb([128, 2048], F32)
            kb.phase(None)
            op("dve", lambda e: e.memset(dbt[:], 0.0), writes=[dbt.b])
            op("dve", lambda e: e.tensor_copy(out=dbt[:, 0:16], in_=effsc[:].rearrange("p c k -> p (c k)")), reads=[effsc.b, dbt.b], writes=[dbt.b])
            op("dve", lambda e: e.tensor_copy(out=dbt[:, 16:32], in_=shift[:].rearrange("p c k -> p (c k)")), reads=[shift.b, dbt.b], writes=[dbt.b])
            op("dve", lambda e: e.tensor_copy(out=dbt[:, 32:33], in_=rs1[:, 0:1]), reads=[rs1.b, dbt.b], writes=[dbt.b])
            op("dve", lambda e: e.tensor_copy(out=dbt[:, 64:416], in_=lat[:, :]), reads=[lat.b, dbt.b], writes=[dbt.b])
            op("dve", lambda e: e.tensor_copy(out=dbt[:, 512:1536].rearrange("p (k t) -> p k t", k=8), in_=hT[:, :, 0:128]), reads=[hT.b, dbt.b], writes=[dbt.b])
            op("dve", lambda e: e.tensor_copy(out=dbt[:, 1536:2048], in_=gate_bc[0][:, 0:512]), reads=[gate_bc[0].b, dbt.b], writes=[dbt.b])
            if stop == 'weights':
                for i_, t_ in enumerate((s_par["ar1"][:], s_par["ai1"][:], s_par["kr"][:], s_par["ki"][:], s_pwr[:, 0, :], s_pwi[:, 0, :],
                                         s_pwr[:, 7, :], s_pwi[:, 7, :])):
                    op("dve", lambda e: e.tensor_copy(out=dbt[:, 512 + 16 * i_:528 + 16 * i_], in_=t_), reads=[s_par["are"].b, dbt.b], writes=[dbt.b])
            dma(dbg[:, :], dbt[:, :], reads=[dbt.b])
        kb.finish()
    return nc


_CACHE = {}


def _rope_tables():
    half = 16
    inv = (10000.0 ** (-np.arange(0, half, 2, dtype=np.float32) / half)).astype(np.float32)
    t = np.arange(LS)
    row = (t // 64).astype(np.float32)
    col = (t % 64).astype(np.float32)
    cos = np.zeros((LS, 32), np.float32)
    sin = np.zeros((LS, 32), np.float32)
    for a, pos in enumerate((row, col)):
        ang = pos[:, None] * inv[None, :]
        ang = np.concatenate([ang, ang], axis=-1)
        cos[:, 16 * a:16 * a + 16] = np.cos(ang)
        s = np.sin(ang)
        s[:, 0:8] *= -1.0
        sin[:, 16 * a:16 * a + 16] = s
    return cos, sin


def _gla_consts():
    p = np.arange(128)[:, None]
    q = np.arange(128)[None, :]
    same = (p // 64) == (q // 64)
    c = np.zeros((128, GC), np.float32)
    c[:, 0:128] = same & (p <= q)
    c[:, 128:256] = same & (p >= q)
    c[:, 256:260] = (p // 32) == np.arange(4)[None, :]
    c[:, 260] = (p[:, 0] < 64)
    c[:, 261] = (p[:, 0] >= 64)
    c[:, 262:518] = (p // 32) == (np.arange(256)[None, :] // 64)
    c[:, 518:646] = 1.0
    return c


def _s5_consts():
    c = np.zeros((128, SC), np.float32)
    r = np.arange(128)[:, None]
    q = np.arange(128)[None, :]
    for k in range(4):
        c[:, 128 * k:128 * k + 128] = (q // 16) == (2 * k + r // 64)
        c[:, 512 + 128 * k:512 + 128 * k + 128] = (r // 16) == (2 * k + q // 64)
    c[:, 1024] = math.pi / 2
    return c


def _hy_tables(L):
    i = np.arange(L, dtype=np.float64)
    ang = (2.0 * np.pi / (2 * L)) * np.outer(i, i)
    cos = np.cos(ang).astype(ml_dtypes.bfloat16)
    sin = np.sin(ang).astype(ml_dtypes.bfloat16)
    pos = np.arange(L, dtype=np.float32)
    t = pos / L
    w = (2.0 * math.pi * pos / L).astype(np.float32)
    bands = np.linspace(1e-4, 15, 16, dtype=np.float32)
    feat = np.concatenate([t[:, None], np.cos(w[:, None] * bands), np.sin(w[:, None] * bands)], axis=-1)
    deltas = np.linspace(math.log(100.0) / 0.3, math.log(100.0) / 1.5, 256, dtype=np.float32)
    win = (np.exp(-t[:, None] * deltas[None, :]) + 0.05).astype(np.float32)
    return cos, sin, np.ascontiguousarray(feat.T).astype(ml_dtypes.bfloat16), win


def _hy_consts():
    c = np.zeros((128, HC), np.float32)
    p = np.arange(128)
    c[:, 0] = 1.0 - 2.0 * (p % 2)
    for base, L in ((128, SEQ), (136, LS)):
        n = 2 * L
        nt = L // 128
        c[:, base:base + nt] = 2.0 / n
        c[0, base] = 1.0 / n
        c[:, base + nt] = 1.0 / n
    c[:, 160] = 1.0
    c[0, 160] = 0.0
    altr = (1.0 - 2.0 * (np.arange(512) % 2)).astype(ml_dtypes.bfloat16)[None, :]
    return c, altr


W_NAMES = ["norm_w", "ada_w", "ada_b", "w_in", "w_out", "mla_qa_norm", "mla_kva_norm", "mla_w_uq",
           "mla_w_ukv", "mla_q_norm", "mla_k_norm", "gla_gw", "gla_gb", "gla_norm",
           "s5_a_re", "s5_a_im", "s5_log_dt", "s5_b_re", "s5_b_im", "s5_c_re", "s5_c_im", "s5_d", "s5_glu_w", "s5_glu_b",
           "hy_conv_w", "hy_conv_b", "hy_w1", "hy_b1", "hy_freq1", "hy_w2", "hy_b2", "hy_freq2", "hy_w3", "hy_bias"]


def _f32(a):
    return np.ascontiguousarray(np.asarray(a, dtype=np.float32))


def core_inputs(inp, i):
    b = i // 4
    cos, sin = _rope_tables()
    m = {
        "xp": _f32(inp["x_prompt"])[4 * i:4 * i + 4].reshape(TP, D),
        "xs": _f32(inp["x_sample"])[b],
        "cond": np.stack([_f32(inp["c_ctx"]), _f32(inp["c"])[b]]),
        "cckv": _f32(inp["cache_mla_ckv"])[b],
        "ckro": _f32(inp["cache_mla_krope"])[b],
        "ropec": cos, "ropes": sin,
        "sgla": _f32(inp["state_gla"])[b].reshape(DEPTH, 2, 128, 64),
        "gcon": _gla_consts(),
        "ss5": _f32(inp["state_s5"])[b].reshape(DEPTH, 2, 1024, 2),
        "scon": _s5_consts(),
    }
    if "hy" not in _CACHE:
        _CACHE["hy"] = {"p": _hy_tables(SEQ), "s": _hy_tables(LS), "c": _hy_consts()}
    for nm_ in ("p", "s"):
        cos_, sin_, feat_, win_ = _CACHE["hy"][nm_]
        m["hcos_" + nm_], m["hsin_" + nm_], m["hfeat_" + nm_], m["hwin_" + nm_] = cos_, sin_, feat_, win_
    m["hcon"], m["haltr"] = _CACHE["hy"]["c"]
    for n in W_NAMES:
        m[n] = _f32(inp[n])
    return m


def kernel(**inp):
    if "nc" not in _CACHE:
        _CACHE["nc"] = build_program()
    nc = _CACHE["nc"]
    in_maps = [core_inputs(inp, i) for i in range(8)]
    res = run_bass_kernel_spmd(nc, in_maps, core_ids=list(range(8))).results
    y_prompt = np.concatenate([r["yp"].reshape(NPS, SEQ, D) for r in res], axis=0)
    y_sample = np.stack([np.concatenate([res[4 * b + q]["ys"][512 * q:512 * (q + 1)] for q in range(4)], axis=0)
                         for b in range(2)], axis=0)
    ckv = np.concatenate([r["o_ckv"] for r in res], axis=0)
    kro = np.concatenate([r["o_kro"] for r in res], axis=0)
    s5 = np.concatenate([r["o_s5"] for r in res], axis=0)
    gla = np.concatenate([r["o_gla"] for r in res], axis=0)
    return (y_prompt.astype(np.float32), y_sample.astype(np.float32), ckv.astype(np.float32),
            kro.astype(np.float32), s5.astype(np.float32), gla.astype(np.float32))
```

```python
import contextlib
import math

import numpy as np
import ml_dtypes

import concourse.bass as bass
import concourse.mybir as mybir
from concourse.bass_utils import run_bass_kernel_spmd

F32 = mybir.dt.float32
BF16 = mybir.dt.bfloat16
AF = mybir.ActivationFunctionType
ALU = mybir.AluOpType
AX = mybir.AxisListType

D = 1024
DEPTH = 2
NIN = 2944
NINU = 608
EPS = 1e-6
SEQ = 256
NPS = 4
TP = NPS * SEQ
LS = 2048
PAST = 512
H = 4
QK = 96
GLA_DK = 32
GC = 646
SC = 1025
HC = 192
HY0 = 608
S50 = 1632
GLA0 = 2144
SAFE_SAME_ENGINE = True


class Buf:
    __slots__ = ("w", "r", "x")

    def __init__(self, exclusive=False):
        self.w = None
        self.r = {}
        self.x = exclusive


class Tile:
    def __init__(self, t, n=1):
        self.t = t
        self.b = Buf()
        self.bs = [Buf() for _ in range(n)]

    def __getitem__(self, k):
        return self.t[k]


class KB:
    def __init__(self, nc, es):
        self.nc, self.es = nc, es
        self.E = {"pe": nc.tensor, "act": nc.scalar, "dve": nc.vector, "pool": nc.gpsimd, "sp": nc.sync}
        self.sem = {k: es.enter_context(nc.semaphore("s_" + k)) for k in ("pe", "act", "dve", "pool")}
        self.NDS = 16
        for i in range(self.NDS):
            self.sem["d%d" % i] = es.enter_context(nc.semaphore("s_d%d" % i))
        self.cnt = dict.fromkeys(self.sem, 0)
        self.ndma = 0
        self.ndma_pool = 0
        self.seen = {}
        self.n = 0

    def sb(self, shape, dt, n=1):
        self.n += 1
        if getattr(self, "arena", None) is not None and self.in_arena:
            return self._carve(list(shape), dt)
        return Tile(self.es.enter_context(self.nc.sbuf_tensor("sb%d" % self.n, list(shape), dt)), n)

    def make_arena(self, ncol_f32):
        self.arena = self.es.enter_context(self.nc.sbuf_tensor("arena", [128, ncol_f32], F32))
        self.arena_cols = ncol_f32
        self.in_arena = False
        self.aoff = 0

    def phase(self, name):
        self.in_arena = name is not None
        self.aoff = 0

    def _carve(self, shape, dt):
        nel = 1
        for d_ in shape[1:]:
            nel *= d_
        ncol = nel if dt == F32 else (nel + 1) // 2
        ncol = (ncol + 7) // 8 * 8
        assert self.aoff + ncol <= self.arena_cols, ("arena overflow", self.aoff, ncol, shape)
        ap = self.arena[0:shape[0], self.aoff:self.aoff + ncol]
        self.aoff += ncol
        if dt != F32:
            ap = ap.bitcast(dt)
        ap = ap[:, 0:nel]
        if len(shape) == 3:
            ap = ap.rearrange("p (a b) -> p a b", a=shape[1])
        elif len(shape) == 4:
            ap = ap.rearrange("p (a b c) -> p a b c", a=shape[1], b=shape[2])
        return Tile(ap)

    def barrier(self):
        for eng, E in self.E.items():
            for k, c in self.cnt.items():
                if c and k != eng and self.seen.get((eng, k), 0) < c:
                    E.wait_ge(self.sem[k], c)
                    self.seen[(eng, k)] = c

    def ps(self, shape, dt):
        self.n += 1
        t = Tile(self.es.enter_context(self.nc.psum_tensor("ps%d" % self.n, list(shape), dt)))
        t.b.x = True
        return t

    def op(self, eng, fn, reads=(), writes=(), dma=False):
        deps = {}
        for b in reads:
            if b.w:
                deps[b.w[0]] = max(deps.get(b.w[0], 0), b.w[1])
            if b.x:
                for k, c in b.r.items():
                    if k != eng:
                        deps[k] = max(deps.get(k, 0), c)
        for b in writes:
            if b.w:
                deps[b.w[0]] = max(deps.get(b.w[0], 0), b.w[1])
            for k, c in b.r.items():
                deps[k] = max(deps.get(k, 0), c)
        E = self.E[eng]
        if dma:
            if eng == "pool":
                key = "d%d" % (12 + self.ndma_pool % 4)
                self.ndma_pool += 1
            else:
                key = "d%d" % (self.ndma % 12)
                self.ndma += 1
            inc = 16
            if self.cnt[key] and self.seen.get((eng, key), 0) < self.cnt[key]:
                E.wait_ge(self.sem[key], self.cnt[key])
                self.seen[(eng, key)] = self.cnt[key]
        else:
            key, inc = eng, 1
        for pk, c in deps.items():
            if pk == key and (eng == "pe" or not SAFE_SAME_ENGINE):
                continue
            if self.seen.get((eng, pk), 0) >= c:
                continue
            E.wait_ge(self.sem[pk], c)
            self.seen[(eng, pk)] = c
        inst = fn(E)
        self.cnt[key] += inc
        inst.then_inc(self.sem[key], inc)
        c = self.cnt[key]
        for b in reads:
            b.r[key] = c
        for b in writes:
            b.w = (key, c)
            b.r = {}

    def dma(self, out, in_, reads=(), writes=(), q="sp", **kw):
        self.op(q, lambda e: e.dma_start(out=out, in_=in_, **kw), reads, writes, dma=True)

    def finish(self):
        sp = self.E["sp"]
        for k in self.sem:
            if self.cnt[k]:
                sp.wait_ge(self.sem[k], self.cnt[k])


def _b(x):
    return x.b if isinstance(x, Tile) else x


class _Stop(Exception):
    pass


def build_program(stop=None):
    def chk(name):
        if stop == name:
            raise _Stop(name)

    nc = bass.Bass("TRN2", target_bir_lowering=False)
    es = contextlib.ExitStack()

    def din(name, shape, dt=F32):
        return nc.dram_tensor(name, list(shape), dt, kind="ExternalInput").ap()

    def dout(name, shape):
        return nc.dram_tensor(name, list(shape), F32, kind="ExternalOutput").ap()

    xp = din("xp", [TP, D])
    xs = din("xs", [LS, D])
    cond = din("cond", [2, D])
    cckv = din("cckv", [DEPTH, PAST, 128])
    ckro = din("ckro", [DEPTH, PAST, 32])
    ropec = din("ropec", [LS, 32])
    ropes = din("ropes", [LS, 32])
    norm_w = din("norm_w", [DEPTH, D])
    ada_w = din("ada_w", [DEPTH, D, 3 * D])
    ada_b = din("ada_b", [DEPTH, 3 * D])
    w_in = din("w_in", [DEPTH, D, NIN])
    w_out = din("w_out", [DEPTH, D, D])
    qa_n = din("mla_qa_norm", [DEPTH, 192])
    kva_n = din("mla_kva_norm", [DEPTH, 128])
    w_uq = din("mla_w_uq", [DEPTH, 192, 384])
    w_ukv = din("mla_w_ukv", [DEPTH, 128, 512])
    q_n = din("mla_q_norm", [DEPTH, 96])
    k_n = din("mla_k_norm", [DEPTH, 96])
    gla_gw = din("gla_gw", [DEPTH, 2, 16, 128])
    gla_gb = din("gla_gb", [DEPTH, 2, 128])
    gla_nw = din("gla_norm", [DEPTH, 64])
    sgla = din("sgla", [DEPTH, 2, 128, 64])
    gcon = din("gcon", [128, GC])
    s5_are = din("s5_a_re", [DEPTH, 2, 16, 64])
    s5_aim = din("s5_a_im", [DEPTH, 2, 16, 64])
    s5_ldt = din("s5_log_dt", [DEPTH, 2, 16])
    s5_bre = din("s5_b_re", [DEPTH, 2, 16, 64, 16])
    s5_bim = din("s5_b_im", [DEPTH, 2, 16, 64, 16])
    s5_cre = din("s5_c_re", [DEPTH, 2, 16, 16, 64])
    s5_cim = din("s5_c_im", [DEPTH, 2, 16, 16, 64])
    s5_dd = din("s5_d", [DEPTH, 256])
    s5_gw = din("s5_glu_w", [DEPTH, 256, 256])
    s5_gb = din("s5_glu_b", [DEPTH, 256])
    ss5 = din("ss5", [DEPTH, 2, 1024, 2])
    scon = din("scon", [128, SC])
    qmask = din("qmask", [128, 4])
    hy_cw = din("hy_conv_w", [DEPTH, 3, 768])
    hy_cb = din("hy_conv_b", [DEPTH, 768])
    hy_w1 = din("hy_w1", [DEPTH, 33, 64])
    hy_b1 = din("hy_b1", [DEPTH, 64])
    hy_f1 = din("hy_freq1", [DEPTH, 64])
    hy_w2 = din("hy_w2", [DEPTH, 64, 64])
    hy_b2 = din("hy_b2", [DEPTH, 64])
    hy_f2 = din("hy_freq2", [DEPTH, 64])
    hy_w3 = din("hy_w3", [DEPTH, 64, 1024])
    hy_bi = din("hy_bias", [DEPTH, 2, 256])
    htab = {}
    for nm_, L_ in (("p", SEQ), ("s", LS)):
        htab[nm_] = dict(cos=din("hcos_" + nm_, [L_, L_], BF16), sin=din("hsin_" + nm_, [L_, L_], BF16),
                         feat=din("hfeat_" + nm_, [33, L_], BF16), win=din("hwin_" + nm_, [L_, 256]))
    hcon = din("hcon", [128, HC])
    haltr = din("haltr", [1, 512], BF16)

    yp = dout("yp", [TP, D])
    ys = dout("ys", [LS, D])
    o_ckv = dout("o_ckv", [NPS, DEPTH, SEQ, 128])
    o_kro = dout("o_kro", [NPS, DEPTH, SEQ, 32])
    o_s5 = dout("o_s5", [NPS, DEPTH, 2, 16, 64, 2])
    o_gla = dout("o_gla", [NPS, DEPTH, 2, 4, 32, 64])
    dbg = dout("dbg", [128, 2048]) if stop else None

    xmid_p = nc.dram_tensor("xmid_p", [TP, D], F32, kind="Internal").ap()
    xmid_s = nc.dram_tensor("xmid_s", [LS, D], F32, kind="Internal").ap()
    xmid_p_b, xmid_s_b = Buf(), Buf()
    s5w = nc.dram_tensor("s5w", [16, 128, 34 * 128], BF16, kind="Internal").ap()
    s5k = nc.dram_tensor("s5k", [2, 128, 16 * 128], BF16, kind="Internal").ap()
    s5w_b, s5k_b = Buf(), Buf()
    wc = {n_: (nc.dram_tensor("wc_" + n_, sh_, BF16, kind="Internal").ap(), Buf()) for n_, sh_ in (
        ("in", [128, 8 * NINU]), ("gla", [128, 8 * 800]), ("s5", [128, 8 * 512]), ("hy0", [128, 8 * 512]), ("hy1", [128, 8 * 512]),
        ("w30", [64, 512]), ("w31", [64, 512]), ("out", [128, 8 * D]))}

    with es:
        kb = KB(nc, es)
        sb, ps, op, dma = kb.sb, kb.ps, kb.op, kb.dma

        ident_f = sb([128, 128], F32)
        ident_b = sb([128, 128], BF16)
        ones_f = sb([1, 128], F32)
        eps_c = sb([128, 1], F32)
        zero_f = sb([128, 512], BF16)
        op("pool", lambda e: e.memset(ident_f[:], 1.0), writes=[ident_f.b])
        op("pool", lambda e: e.affine_select(out=ident_f[:], in_=ident_f[:], pattern=[[-1, 128]],
                                             compare_op=ALU.is_equal, fill=0.0, base=0,
                                             channel_multiplier=1), reads=[ident_f.b], writes=[ident_f.b])
        op("pool", lambda e: e.tensor_copy(out=ident_b[:], in_=ident_f[:]), reads=[ident_f.b], writes=[ident_b.b])
        op("pool", lambda e: e.memset(ones_f[:], 1.0), writes=[ones_f.b])
        op("pool", lambda e: e.memset(eps_c[:], EPS), writes=[eps_c.b])
        op("pool", lambda e: e.memset(zero_f[:], 0.0), writes=[zero_f.b])
        qm = sb([128, 4], F32)
        dma(qm[:, :], qmask[:, :], writes=[qm.b], q="pool")

        pf = [ps([128, 512], F32) for _ in range(6)]
        pb = [ps([128, 1024], BF16) for _ in range(2)]
        rr = {"f": 0, "b": 0}

        def psf():
            rr["f"] = (rr["f"] + 1) % 5
            return pf[rr["f"]]

        p_acc = pf[5]

        def psb():
            rr["b"] = (rr["b"] + 1) % len(pb)
            return pb[rr["b"]]

        qT = sb([128, H, LS], BF16)
        kT = sb([128, H, LS + PAST], BF16)
        vaug = sb([128, (LS + PAST) // 128, H, 66], BF16)
        hT = sb([128, 8, LS], BF16)
        brT = sb([128, 8, LS], BF16)
        w_uq_a = sb([128, 384], BF16)
        w_uq_c = sb([64, 384], BF16)
        w_ukv_b = sb([128, 512], BF16)
        bc_qa = sb([128, 192], F32)
        bc_kva = sb([128, 128], F32)
        bc_qn = sb([128, 96], F32)
        bc_kn = sb([128, 96], F32)
        gate_bc = [sb([128, D], F32) for _ in range(2)]
        effsc = sb([128, 2, 8], F32)
        shift = sb([128, 2, 8], F32)
        rowst = sb([1, D], F32)
        scond = sb([8, 2, 128], F32)
        scondT = sb([128, 2, 8], F32)
        screp = sb([128, 128], F32)
        colst = sb([24, 128], F32)
        colsT = sb([128, 3, 8], F32)
        nwst = sb([8, 128], F32)
        nwT = sb([128, 8], F32)
        modacc = sb([128, 32], F32)

        class _Alias:
            def __init__(self, ap, owner):
                self.ap, self.b = ap, owner.b

            def __getitem__(self, k):
                return self.ap[k]

        wstage = [_Alias(qT.t[:].bitcast(F32).rearrange("p a c -> p (a c)")[:, 0:3 * D], qT)] * 2
        adast = wstage
        w_out_b = _Alias(kT.t[:].rearrange("p h t -> p (h t)")[:, 0:8 * D].rearrange("p (k c) -> p k c", k=8), kT)
        wmix = _Alias(kT.t[:].rearrange("p h t -> p (h t)")[:, 0:6400].rearrange("p (k c) -> p k c", k=8), kT)
        gcon_t = sb([128, GC], F32)
        dma(gcon_t[:, :], gcon[:, :], writes=[gcon_t.b])
        maskF, maskB_ = gcon_t[:, 0:128], gcon_t[:, 128:256]
        headm, halfm, blockm, ones64 = gcon_t[:, 256:260], gcon_t[:, 260:262], gcon_t[:, 262:518], gcon_t[:, 518:582]
        gwp_f = sb([32, 2, 128], F32)
        gwp = sb([32, 2, 128], BF16)
        gbst = sb([2, 128], F32)
        ngb = sb([128, 2], F32)
        bc_gn = sb([128, 64], F32)
        kb.make_arena(10496)
        kb.phase("GLA")
        o_bwd = _Alias(qT.t[:].bitcast(F32).rearrange("p a (b c) -> p (a b) c", c=256), qT)
        g_qf, g_kf, g_sp, g_bcp, g_e1, g_e2, g_e3 = [sb([128, 128], F32) for _ in range(7)]
        g_glr = sb([32, 128], BF16)
        g_vt = sb([128, 256], BF16)
        g_qt, g_kt, g_khT, g_khA, g_khB, g_attm = [sb([128, 128], BF16) for _ in range(6)]
        g_km = [sb([128, 128], BF16) for _ in range(4)]
        g_QA = [sb([128, 128], BF16) for _ in range(4)]
        g_QB = [sb([128, 128], BF16) for _ in range(4)]
        g_nbl = sb([128, 2], F32)
        g_tot = sb([128, 2], F32)
        g_eb = sb([128, 2], F32)
        g_dS = sb([128, 256], F32)
        g_dSr = [sb([128, 64], F32) for _ in range(2)]
        g_S = [sb([128, 64], F32) for _ in range(3)]
        g_Sb = [sb([128, 64], BF16) for _ in range(2)]
        g_os = sb([128, 256], F32)
        g_sq = sb([128, 256], F32)
        g_ss = sb([128, 4], F32)
        g_rs = sb([128, 4], F32)
        g_sil = sb([128, 256], F32)
        g_og = sb([128, 256], BF16)

        kflat = kT.t[:].rearrange("p h t -> p (h t)")
        vflat = vaug.t[:].rearrange("p a h d -> p (a h d)")
        s_wmix = _Alias(kflat[:, 0:4096].rearrange("p (k c) -> p k c", k=8), kT)
        s_uT = _Alias(kflat[:, 4096:8192].rearrange("p (h t) -> p h t", h=2), kT)
        s_Kbd = _Alias(kflat[:, 8192:10240].rearrange("p (a c) -> p a c", a=16), kT)
        s_gT = _Alias(vflat[:, 0:4096].rearrange("p (h t) -> p h t", h=2), vaug)
        s_Kacc = _Alias(qT.t[:].bitcast(F32).rearrange("p a c -> p (a c)")[:, 0:2048].rearrange("p (a c) -> p a c", a=16), qT)
        kb.phase(None)
        scon_t = sb([128, SC], F32)
        dma(scon_t[:, :], scon[:, :], writes=[scon_t.b])
        s_prst = sb([16, 128], F32)
        s_ldst = sb([16, 2], F32)
        s_par = {n_: sb([128, 16], F32) for n_ in ("are", "aim", "dt", "lr", "th", "er", "ei", "t1", "t2", "ar1", "ai1", "nai1",
                                                    "kr", "ki", "nki", "dr", "den")}
        s_pwr = sb([128, 8, 16], F32)
        s_pwi = sb([128, 8, 16], F32)
        s_npwi = sb([128, 8, 16], F32)
        s_dcol = sb([128, 2], F32)
        s_gbcol = sb([128, 2], F32)
        s_st2 = sb([2, 128], F32)
        s_Dd = [sb([128, 128], BF16) for _ in range(2)]
        s_glu = sb([128, 2, 256], BF16)
        kb.phase("S5")
        s_B = [sb([128, 16], F32) for _ in range(2)]
        s_bb = [sb([128, 16], F32) for _ in range(2)]
        s_Xm = [[sb([128, 128], F32) for _ in range(2)] for _ in range(2)]
        s_Xb = [[sb([128, 128], BF16) for _ in range(2)] for _ in range(8)]
        s_W = sb([128, 34, 128], BF16)

        class _View:
            def __init__(self, ap, b):
                self.ap, self.b = ap, b

            def __getitem__(self, k):
                return self.ap[k]

        s_Wb = [[_View(s_W[:, 2 * k_ + r_, :], s_W.b) for r_ in range(2)] for k_ in range(8)]
        s_C = [sb([128, 64], F32) for _ in range(2)]
        s_Cp = sb([128, 128], F32)
        s_Cm = [[sb([128, 128], F32) for _ in range(2)] for _ in range(2)]
        s_Wc = [[_View(s_W[:, 16 + 2 * k_ + r_, :], s_W.b) for r_ in range(2)] for k_ in range(9)]
        s_XA = [sb([128, 256], F32) for _ in range(2)]
        s_XB = [sb([128, 256], F32) for _ in range(2)]
        s_Hs = [sb([128, 4 * 33 + 257], BF16) for _ in range(2)]
        s_h0 = sb([128, 2], F32)
        s_fin = sb([128, 2], F32)
        s_tmp = sb([128, 4], F32)
        s_ya = sb([128, 512], F32)
        s_yb = sb([128, 512], F32)
        kb.phase(None)

        def bcast_row(dst, src_row_ap, n):
            dma(rowst[0:1, 0:n], src_row_ap, writes=[rowst.b])
            for c0 in range(0, n, 512):
                w = min(512, n - c0)
                p = psf()
                op("pe", lambda e: e.matmul(p[:, 0:w], lhsT=ones_f[0:1, :], rhs=rowst[0:1, c0:c0 + w],
                                            start=True, stop=True),
                   reads=[ones_f.b, rowst.b], writes=[p.b])
                op("dve", lambda e: e.tensor_copy(out=dst[:, c0:c0 + w], in_=p[:, 0:w]),
                   reads=[p.b], writes=[dst.b])

        def rstd_from_ss(dst, ss, n):
            op("act", lambda e: e.activation(out=dst[:], in_=ss[:], func=AF.Sqrt, scale=1.0 / n,
                                             bias=eps_c[:, 0:1]), reads=[ss.b, eps_c.b], writes=[dst.b])
            op("dve", lambda e: e.reciprocal(out=dst[:], in_=dst[:]), reads=[dst.b], writes=[dst.b])

        hcon_t = sb([128, HC], F32)
        dma(hcon_t[:, :], hcon[:, :], writes=[hcon_t.b])
        haltr_t = sb([1, 512], BF16)
        dma(haltr_t[:, :], haltr[:, :], writes=[haltr_t.b])
        h_alt = sb([128, 128], BF16)
        op("dve", lambda e: e.tensor_copy(out=h_alt[:], in_=hcon_t[:, 0:128]), reads=[hcon_t.b], writes=[h_alt.b])
        ones_sq = sb([128, 128], F32)
        op("pool", lambda e: e.memset(ones_sq[:], 1.0), writes=[ones_sq.b])
        hw1_b = sb([33, 64], BF16)
        hw2_b = sb([64, 64], BF16)
        h_mrow = sb([4, 64], F32)
        h_mcol = sb([64, 4], F32)
        h_mlp = sb([64, 8], F32)
        h_rows = sb([14, 128], F32)
        h_colp = sb([128, 14], F32)
        h_w3 = sb([64, 4, 128], BF16)
        h_win = sb([128, 128], F32)
        h_feat = sb([33, 512], BF16)
        h_cr = [sb([128, 512], BF16) for _ in range(2)]
        h_sr = [sb([128, 512], BF16) for _ in range(2)]
        h_sig = _Alias(qT.t[:], qT)
        class _Sub(_Alias):
            def __init__(self, ap):
                self.ap, self.b = ap, Buf()

        h_tc = [_Sub(kflat[:, 2048 * i_:2048 * i_ + 2048].rearrange("p (a j) -> p a j", a=16)) for i_ in (0, 1)]
        h_ts = [_Sub(kflat[:, 2048 * i_:2048 * i_ + 2048].rearrange("p (a j) -> p a j", a=16)) for i_ in (2, 3)]
        h_utm = _Sub(kflat[:, 8192:10240].rearrange("p (a j) -> p a j", a=16))
        h_wmix = _Alias(vflat[:, 0:4096].rearrange("p (k c) -> p k c", k=8), vaug)
        kb.phase("HY")
        h_hid = [sb([64, LS], BF16) for _ in range(2)]
        h_ksd = sb([128, 16, 4, 128], BF16)
        h_Y = sb([128, 17, 2, 128], BF16)
        h_k = [sb([128, 128], F32) for _ in range(2)]
        h_t = [sb([128, 128], F32) for _ in range(4)]
        h_rn = sb([128, 2, 128], F32)
        kb.phase(None)
        h_fw = sb([128, 512], F32)
        h_abs = sb([128, 512], F32)
        h_s2 = sb([64, 512], F32)
        h_s4 = sb([64, 512], F32)

        kb.phase("PA")
        w_in_b = sb([128, 8, NINU], BF16)
        xt = sb([128, D], F32)
        xn = sb([128, D], BF16)
        junk = xn
        ss1 = sb([128, 1], F32)
        rs1 = sb([128, 1], F32)
        lat = sb([128, 352], F32)
        junk2 = sb([128, 192], BF16)
        gm = sb([128, 256], F32)
        ssq = sb([128, 2], F32)
        rsq = sb([128, 2], F32)
        cqn = sb([128, 192], BF16)
        ckvn = sb([128, 128], F32)
        ckvn_b = sb([128, 128], BF16)
        kro = sb([128, 32], F32)
        cqT = sb([128, 2, 128], BF16)
        ckvT = sb([128, 128], BF16)
        qpre = sb([128, H, QK], F32)
        kcat = sb([128, H, QK], F32)
        sqh = sb([128, H, QK], F32)
        ssh = sb([128, H], F32)
        rsh = sb([128, H], F32)
        qf = sb([128, H, QK], F32)
        qb = sb([128, H, QK], BF16)
        rtmp = sb([128, H, 32], F32)
        rtmp2 = sb([128, H, 32], F32)
        cosT = sb([128, 32], F32)
        sinT = sb([128, 32], F32)
        pT2 = [sb([128, 512], BF16) for _ in range(2)]
        omla4 = sb([128, 4, H, 64], F32)
        den = sb([128, H], F32)
        omla = sb([128, H, 64], F32)
        omg = sb([128, 256], BF16)
        silg = sb([128, 256], F32)
        ot = sb([128, D], F32)
        kb.phase(None)

        try:
          for l in range(DEPTH):
              st = wstage[0]
              dma(st[:, 0:384], w_uq[l, 0:128, :], writes=[st.b], q="pool")
              op("dve", lambda e: e.tensor_copy(out=w_uq_a[:], in_=st[:, 0:384]), reads=[st.b], writes=[w_uq_a.b])
              st = wstage[1]
              dma(st[0:64, 0:384], w_uq[l, 128:192, :], writes=[st.b], q="pool")
              op("dve", lambda e: e.tensor_copy(out=w_uq_c[:], in_=st[0:64, 0:384]), reads=[st.b], writes=[w_uq_c.b])
              st = wstage[0]
              dma(st[:, 0:512], w_ukv[l, :, :], writes=[st.b], q="pool")
              op("dve", lambda e: e.tensor_copy(out=w_ukv_b[:], in_=st[:, 0:512]), reads=[st.b], writes=[w_ukv_b.b])
              bcast_row(bc_qa, qa_n[l:l + 1, :], 192)
              bcast_row(bc_kva, kva_n[l:l + 1, :], 128)
              bcast_row(bc_qn, q_n[l:l + 1, :], 96)
              bcast_row(bc_kn, k_n[l:l + 1, :], 96)
              bcast_row(bc_gn, gla_nw[l:l + 1, :], 64)
              op("dve", lambda e: e.memset(gwp_f[:], 0.0), writes=[gwp_f.b])
              dma(gwp_f[0:16, 0, :], gla_gw[l, 0, :, :], reads=[gwp_f.b], writes=[gwp_f.b])
              dma(gwp_f[16:32, 1, :], gla_gw[l, 1, :, :], reads=[gwp_f.b], writes=[gwp_f.b])
              op("dve", lambda e: e.tensor_copy(out=gwp[:], in_=gwp_f[:]), reads=[gwp_f.b], writes=[gwp.b])
              dma(gbst[:, :], gla_gb[l, :, :], writes=[gbst.b])
              p = psf()
              op("pe", lambda e: e.transpose(p[:, 0:2], gbst[:, :], ident_f[0:2, 0:2]), reads=[gbst.b, ident_f.b], writes=[p.b])
              op("dve", lambda e: e.tensor_scalar(out=ngb[:], in0=p[:, 0:2], scalar1=-1.0, scalar2=None, op0=ALU.mult),
                 reads=[p.b], writes=[ngb.b])

              P_ = s_par

              def colparam(dst, rows_ap, nrow=16):
                  dma(s_prst[0:nrow, :], rows_ap, writes=[s_prst.b])
                  p_ = psf()
                  op("pe", lambda e: e.transpose(p_[:, 0:nrow], s_prst[0:nrow, :], ident_f[0:nrow, 0:nrow]),
                     reads=[s_prst.b, ident_f.b], writes=[p_.b])
                  op("dve", lambda e: e.tensor_copy(out=dst, in_=p_[:, 0:nrow]), reads=[p_.b], writes=[s_par_b])

              s_par_b = P_["are"].b
              for t_ in P_.values():
                  t_.b = s_par_b
              s_pwr.b = s_pwi.b = s_npwi.b = s_par_b
              colparam(P_["are"][:], s5_are[l].rearrange("d (st gl) p -> (d st) (gl p)", gl=2))
              colparam(P_["aim"][:], s5_aim[l].rearrange("d (st gl) p -> (d st) (gl p)", gl=2))
              dma(s_ldst[:, :], s5_ldt[l].rearrange("d (st gl) -> (d st) gl", gl=2), writes=[s_ldst.b])
              op("dve", lambda e: e.tensor_copy(out=s_prst[:, :].rearrange("r (gl p) -> r gl p", gl=2),
                                                in_=s_ldst[:, :].unsqueeze(2).to_broadcast([16, 2, 64])),
                 reads=[s_ldst.b], writes=[s_prst.b])
              p_ = psf()
              op("pe", lambda e: e.transpose(p_[:, 0:16], s_prst[:, :], ident_f[0:16, 0:16]), reads=[s_prst.b, ident_f.b], writes=[p_.b])
              op("act", lambda e: e.activation(out=P_["dt"][:], in_=p_[:, 0:16], func=AF.Exp), reads=[p_.b], writes=[s_par_b])

              def pp(eng, fn):
                  op(eng, fn, reads=[s_par_b, scon_t.b], writes=[s_par_b])

              are, aim, dt_, lr, th, er, ei, t1, t2 = (P_[n_] for n_ in ("are", "aim", "dt", "lr", "th", "er", "ei", "t1", "t2"))
              pp("dve", lambda e: e.tensor_scalar(out=are[:], in0=are[:], scalar1=-1e-4, scalar2=None, op0=ALU.min))
              pp("dve", lambda e: e.tensor_tensor(out=lr[:], in0=are[:], in1=dt_[:], op=ALU.mult))
              pp("dve", lambda e: e.tensor_tensor(out=th[:], in0=aim[:], in1=dt_[:], op=ALU.mult))
              pp("act", lambda e: e.activation(out=t1[:], in_=lr[:], func=AF.Exp, scale=1.0 / 16))
              pp("act", lambda e: e.activation(out=er[:], in_=th[:], func=AF.Sin, scale=1.0 / 16, bias=scon_t[:, 1024:1025]))
              pp("act", lambda e: e.activation(out=ei[:], in_=th[:], func=AF.Sin, scale=1.0 / 16))
              pp("dve", lambda e: e.tensor_tensor(out=er[:], in0=er[:], in1=t1[:], op=ALU.mult))
              pp("dve", lambda e: e.tensor_tensor(out=ei[:], in0=ei[:], in1=t1[:], op=ALU.mult))

              def csq():
                  pp("dve", lambda e: e.tensor_tensor(out=t1[:], in0=er[:], in1=er[:], op=ALU.mult))
                  pp("dve", lambda e: e.tensor_tensor(out=t2[:], in0=ei[:], in1=ei[:], op=ALU.mult))
                  pp("dve", lambda e: e.scalar_tensor_tensor(out=ei[:], in0=er[:], scalar=2.0, in1=ei[:], op0=ALU.mult, op1=ALU.mult))
                  pp("dve", lambda e: e.tensor_tensor(out=er[:], in0=t1[:], in1=t2[:], op=ALU.subtract))

              for _ in range(4):
                  csq()
              ar1, ai1, nai1, kr, ki, nki, dr, dn_ = (P_[n_] for n_ in ("ar1", "ai1", "nai1", "kr", "ki", "nki", "dr", "den"))
              pp("dve", lambda e: e.tensor_copy(out=ar1[:], in_=er[:]))
              pp("dve", lambda e: e.tensor_copy(out=ai1[:], in_=ei[:]))
              pp("dve", lambda e: e.tensor_scalar(out=nai1[:], in0=ei[:], scalar1=-1.0, scalar2=None, op0=ALU.mult))
              pp("dve", lambda e: e.tensor_scalar(out=dr[:], in0=ar1[:], scalar1=-1.0, scalar2=None, op0=ALU.add))
              pp("dve", lambda e: e.tensor_tensor(out=t1[:], in0=are[:], in1=are[:], op=ALU.mult))
              pp("dve", lambda e: e.tensor_tensor(out=t2[:], in0=aim[:], in1=aim[:], op=ALU.mult))
              pp("dve", lambda e: e.tensor_tensor(out=dn_[:], in0=t1[:], in1=t2[:], op=ALU.add))
              pp("dve", lambda e: e.reciprocal(out=dn_[:], in_=dn_[:]))
              pp("dve", lambda e: e.tensor_tensor(out=t1[:], in0=dr[:], in1=are[:], op=ALU.mult))
              pp("dve", lambda e: e.tensor_tensor(out=t2[:], in0=ai1[:], in1=aim[:], op=ALU.mult))
              pp("dve", lambda e: e.tensor_tensor(out=t1[:], in0=t1[:], in1=t2[:], op=ALU.add))
              pp("dve", lambda e: e.tensor_tensor(out=kr[:], in0=t1[:], in1=dn_[:], op=ALU.mult))
              pp("dve", lambda e: e.tensor_tensor(out=t1[:], in0=ai1[:], in1=are[:], op=ALU.mult))
              pp("dve", lambda e: e.tensor_tensor(out=t2[:], in0=dr[:], in1=aim[:], op=ALU.mult))
              pp("dve", lambda e: e.tensor_tensor(out=t1[:], in0=t1[:], in1=t2[:], op=ALU.subtract))
              pp("dve", lambda e: e.tensor_tensor(out=ki[:], in0=t1[:], in1=dn_[:], op=ALU.mult))
              pp("dve", lambda e: e.tensor_scalar(out=nki[:], in0=ki[:], scalar1=-1.0, scalar2=None, op0=ALU.mult))
              for _ in range(3):
                  csq()
              for k in range(8):
                  pp("dve", lambda e: e.tensor_copy(out=s_pwr[:, k, :], in_=er[:]))
                  pp("dve", lambda e: e.tensor_copy(out=s_pwi[:, k, :], in_=ei[:]))
                  pp("dve", lambda e: e.tensor_scalar(out=s_npwi[:, k, :], in0=ei[:], scalar1=-1.0, scalar2=None, op0=ALU.mult))
                  if k < 7:
                      csq()
              for dst_, src_ in ((s_dcol, s5_dd), (s_gbcol, s5_gb)):
                  dma(s_st2[:, :], src_[l].rearrange("(h p) -> h p", p=128), writes=[s_st2.b])
                  p_ = psf()
                  op("pe", lambda e: e.transpose(p_[:, 0:2], s_st2[:, :], ident_f[0:2, 0:2]), reads=[s_st2.b, ident_f.b], writes=[p_.b])
                  op("dve", lambda e: e.tensor_copy(out=dst_[:], in_=p_[:, 0:2]), reads=[p_.b], writes=[dst_.b])
              for hf in range(2):
                  op("dve", lambda e: e.tensor_scalar(out=s_Dd[hf][:], in0=ident_f[:], scalar1=s_dcol[:, hf:hf + 1], scalar2=None, op0=ALU.mult),
                     reads=[ident_f.b, s_dcol.b], writes=[s_Dd[hf].b])
                  st = wstage[0]
                  dma(st[:, 0:256], s5_gw[l, hf * 128:(hf + 1) * 128, :], writes=[st.b], q="pool")
                  op("dve", lambda e: e.tensor_copy(out=s_glu[:, hf, :], in_=st[:, 0:256]), reads=[st.b], writes=[s_glu.b])
              for r_, src_ in enumerate((hy_f1, hy_b1, hy_f2, hy_b2)):
                  dma(h_mrow[r_:r_ + 1, :], src_[l:l + 1, :], reads=[h_mrow.b], writes=[h_mrow.b])
              p_ = psf()
              op("pe", lambda e: e.transpose(p_[0:64, 0:4], h_mrow[:, :], ident_f[0:4, 0:4]), reads=[h_mrow.b, ident_f.b], writes=[p_.b])
              op("dve", lambda e: e.tensor_copy(out=h_mcol[:], in_=p_[0:64, 0:4]), reads=[p_.b], writes=[h_mcol.b])
              for li in range(2):
                  fcol, bcol = h_mcol[:, 2 * li:2 * li + 1], h_mcol[:, 2 * li + 1:2 * li + 2]
                  o4 = 4 * li
                  op("dve", lambda e: e.tensor_scalar(out=h_mlp[:, o4:o4 + 1], in0=fcol, scalar1=0.5, scalar2=None, op0=ALU.mult),
                     reads=[h_mcol.b, h_mlp.b], writes=[h_mlp.b])
                  op("dve", lambda e: e.scalar_tensor_tensor(out=h_mlp[:, o4 + 1:o4 + 2], in0=fcol, scalar=0.5, in1=bcol, op0=ALU.mult, op1=ALU.mult),
                     reads=[h_mcol.b, h_mlp.b], writes=[h_mlp.b])
                  op("dve", lambda e: e.tensor_scalar(out=h_mlp[:, o4 + 2:o4 + 3], in0=fcol, scalar1=0.25, scalar2=None, op0=ALU.mult),
                     reads=[h_mcol.b, h_mlp.b], writes=[h_mlp.b])
                  op("dve", lambda e: e.scalar_tensor_tensor(out=h_mlp[:, o4 + 3:o4 + 4], in0=fcol, scalar=0.25, in1=bcol, op0=ALU.mult, op1=ALU.mult),
                     reads=[h_mcol.b, h_mlp.b], writes=[h_mlp.b])
              st = wstage[0]
              dma(st[0:33, 0:64], hy_w1[l, :, :], writes=[st.b], q="pool")
              op("dve", lambda e: e.tensor_copy(out=hw1_b[:], in_=st[0:33, 0:64]), reads=[st.b], writes=[hw1_b.b])
              dma(st[0:64, 0:64], hy_w2[l, :, :], writes=[st.b], q="pool")
              op("dve", lambda e: e.tensor_copy(out=hw2_b[:], in_=st[0:64, 0:64]), reads=[st.b], writes=[hw2_b.b])
              chk('weights')
              if l == 0:
                  for c in range(2):
                      dma(scond[:, c, :], cond[c].rearrange("(k p) -> k p", p=128), writes=[scond.b])
                  op("act", lambda e: e.activation(out=scond[:], in_=scond[:], func=AF.Silu),
                     reads=[scond.b], writes=[scond.b])
                  for c in range(2):
                      p = psf()
                      op("pe", lambda e: e.transpose(p[:, 0:8], scond[:, c, :], ident_f[0:8, 0:8]),
                         reads=[scond.b, ident_f.b], writes=[p.b])
                      op("dve", lambda e: e.tensor_copy(out=scondT[:, c, :], in_=p[:, 0:8]),
                         reads=[p.b], writes=[scondT.b])
              dma(colst[:, :], ada_b[l].rearrange("(j p) -> j p", p=128), writes=[colst.b])
              p = psf()
              op("pe", lambda e: e.transpose(p[:, 0:24], colst[:, :], ident_f[0:24, 0:24]),
                 reads=[colst.b, ident_f.b], writes=[p.b])
              op("dve", lambda e: e.tensor_copy(out=colsT[:].rearrange("p a k -> p (a k)"), in_=p[:, 0:24]),
                 reads=[p.b], writes=[colsT.b])
              dma(nwst[:, :], norm_w[l].rearrange("(k p) -> k p", p=128), writes=[nwst.b])
              p = psf()
              op("pe", lambda e: e.transpose(p[:, 0:8], nwst[:, :], ident_f[0:8, 0:8]),
                 reads=[nwst.b, ident_f.b], writes=[p.b])
              op("dve", lambda e: e.tensor_copy(out=nwT[:], in_=p[:, 0:8]), reads=[p.b], writes=[nwT.b])

              pmod = modacc
              op("dve", lambda e: e.memset(modacc[:], 0.0), writes=[modacc.b])
              pk = psf()
              pg = [psf() for _ in range(4)]
              for k in range(8):
                  a = adast[k % 2]
                  dma(a[:, :], ada_w[l, k * 128:(k + 1) * 128, :], writes=[a.b])
                  for j in range(16):
                      op("pe", lambda e: e.matmul(pk[:, 2 * j:2 * j + 2], lhsT=a[:, j * 128:(j + 1) * 128],
                                                  rhs=scondT[:, :, k], start=True, stop=True),
                         reads=[a.b, scondT.b], writes=[pk.b])
                  op("dve", lambda e: e.tensor_tensor(out=modacc[:, 0:32], in0=modacc[:, 0:32], in1=pk[:, 0:32], op=ALU.add),
                     reads=[pk.b, modacc.b], writes=[modacc.b])
                  for c in range(2):
                      op("dve", lambda e: e.tensor_copy(out=screp[:], in_=scondT[:, c, k:k + 1].to_broadcast([128, 128])),
                         reads=[scondT.b], writes=[screp.b])
                      for hf in range(2):
                          pgt = pg[2 * c + hf]
                          op("pe", lambda e: e.matmul(pgt[:, :], lhsT=screp[:, :],
                                                      rhs=a[:, 2048 + hf * 512:2048 + (hf + 1) * 512],
                                                      start=(k == 0), stop=False),
                             reads=[a.b, screp.b], writes=[pgt.b])
              dma(rowst[0:1, 0:D], ada_b[l:l + 1, 2048:3072], writes=[rowst.b])
              for c in range(2):
                  for hf in range(2):
                      pgt = pg[2 * c + hf]
                      op("pe", lambda e: e.matmul(pgt[:, :], lhsT=ones_f[0:1, :], rhs=rowst[0:1, hf * 512:(hf + 1) * 512],
                                                  start=False, stop=True),
                         reads=[ones_f.b, rowst.b], writes=[pgt.b])
                      op("dve", lambda e: e.tensor_copy(out=gate_bc[c][:, hf * 512:(hf + 1) * 512], in_=pgt[:, :]),
                         reads=[pgt.b], writes=[gate_bc[c].b])
              pm = pmod[:, 0:32].rearrange("p (j c) -> p c j", c=2)
              for c in range(2):
                  op("dve", lambda e: e.tensor_tensor(out=shift[:, c, :], in0=pm[:, c, 0:8], in1=colsT[:, 0, :], op=ALU.add),
                     reads=[pmod.b, colsT.b], writes=[shift.b])
                  op("dve", lambda e: e.tensor_tensor(out=effsc[:, c, :], in0=pm[:, c, 8:16], in1=colsT[:, 1, :], op=ALU.add),
                     reads=[pmod.b, colsT.b], writes=[effsc.b])
                  op("dve", lambda e: e.scalar_tensor_tensor(out=effsc[:, c, :], in0=effsc[:, c, :], scalar=1.0,
                                                             in1=nwT[:, :], op0=ALU.add, op1=ALU.mult),
                     reads=[effsc.b, nwT.b], writes=[effsc.b])

              chk('ada')
              def gla_group(l, g):
                  c, T, L, sample = g["c"], g["T"], g["L"], g["sample"]
                  for t_ in g_QA + g_QB:
                      op("pool", lambda e: e.memset(t_[:], 0.0), writes=[t_.b])
                  tps = L // 128
                  nseq = T // L
                  if sample:
                      dma(wmix[:, :, :].rearrange("p k c -> p (k c)"), wc["gla"][0][:, :], reads=[wc["gla"][1]], writes=[wmix.b], q="pool")
                  else:
                      for k in range(8):
                          st = wstage[0]
                          dma(st[:, 0:800], w_in[l, k * 128:(k + 1) * 128, GLA0:GLA0 + 800], writes=[st.b], q="pool")
                          op("dve" if k % 2 == 0 else "pool", lambda e: e.tensor_copy(out=wmix[:, k, :], in_=st[:, 0:800]),
                             reads=[st.b], writes=[wmix.b])
                      dma(wc["gla"][0][:, :], wmix[:, :, :].rearrange("p k c -> p (k c)"), reads=[wmix.b], writes=[wc["gla"][1]])
                  for d in (1, 0):
                      mask_d = maskF if d == 0 else maskB_
                      for s_ in range(nseq):
                          S0, S1, S2 = g_S
                          if sample:
                              dma(S0[:, :], sgla[l, d, :, :], writes=[S0.b])
                          else:
                              op("dve", lambda e: e.memset(S0[:], 0.0), writes=[S0.b])
                          order = range(tps) if d == 0 else range(tps - 1, -1, -1)
                          for tt in order:
                              ti = s_ * tps + tt
                              r0 = ti * 128
                              pq_, pk_, pg_, pv_ = psf(), psf(), psf(), psf()
                              for k in range(8):
                                  op("pe", lambda e: e.matmul(pq_[:, 0:128], lhsT=wmix[:, k, 0:128], rhs=hT[:, k, r0:r0 + 128],
                                                              start=(k == 0), stop=(k == 7)), reads=[wmix.b, hT.b], writes=[pq_.b])
                              for k in range(8):
                                  op("pe", lambda e: e.matmul(pk_[:, 0:128], lhsT=wmix[:, k, 128:256], rhs=hT[:, k, r0:r0 + 128],
                                                              start=(k == 0), stop=(k == 7)), reads=[wmix.b, hT.b], writes=[pk_.b])
                              for k in range(8):
                                  op("pe", lambda e: e.matmul(pg_[0:32, 0:128], lhsT=wmix[:, k, 512:544], rhs=hT[:, k, r0:r0 + 128],
                                                              start=(k == 0), stop=(k == 7)), reads=[wmix.b, hT.b], writes=[pg_.b])
                              for k in range(8):
                                  op("pe", lambda e: e.matmul(pv_[:, 0:256], lhsT=hT[:, k, r0:r0 + 128], rhs=wmix[:, k, 256:512],
                                                              start=(k == 0), stop=(k == 7)), reads=[wmix.b, hT.b], writes=[pv_.b])
                              op("act", lambda e: e.activation(out=g_qf[:], in_=pq_[:, 0:128], func=AF.Copy), reads=[pq_.b], writes=[g_qf.b])
                              op("act", lambda e: e.activation(out=g_kf[:], in_=pk_[:, 0:128], func=AF.Copy), reads=[pk_.b], writes=[g_kf.b])
                              op("act", lambda e: e.activation(out=g_glr[:], in_=pg_[0:32, 0:128], func=AF.Copy), reads=[pg_.b], writes=[g_glr.b])
                              op("dve", lambda e: e.tensor_copy(out=g_vt[:], in_=pv_[:, 0:256]), reads=[pv_.b], writes=[g_vt.b])
                              pl_ = psf()
                              op("pe", lambda e: e.matmul(pl_[:, 0:128], lhsT=gwp[:, d, :], rhs=g_glr[:, :], start=True, stop=True),
                                 reads=[gwp.b, g_glr.b], writes=[pl_.b])
                              op("act", lambda e: e.activation(out=g_sp[:], in_=pl_[:, 0:128], func=AF.Exp, scale=-1.0, bias=ngb[:, d:d + 1]),
                                 reads=[pl_.b, ngb.b], writes=[g_sp.b])
                              op("act", lambda e: e.activation(out=g_sp[:], in_=g_sp[:], func=AF.Ln, bias=ones64[:, 0:1]),
                                 reads=[g_sp.b, gcon_t.b], writes=[g_sp.b])
                              for ch in range(2):
                                  c0 = 64 * ch
                                  op("dve", lambda e: e.tensor_tensor_scan(out=g_bcp[:, c0:c0 + 64], data0=ones64, data1=g_sp[:, c0:c0 + 64],
                                                                           initial=0.0, op0=ALU.mult, op1=ALU.add),
                                     reads=[g_sp.b, gcon_t.b, g_bcp.b], writes=[g_bcp.b])
                              if d == 1:
                                  for ch in range(2):
                                      c0 = 64 * ch
                                      op("dve", lambda e: e.tensor_copy(out=g_tot[:, ch:ch + 1], in_=g_bcp[:, c0 + 63:c0 + 64]),
                                         reads=[g_bcp.b, g_tot.b], writes=[g_tot.b])
                                  op("dve", lambda e: e.tensor_tensor(out=g_bcp[:], in0=g_sp[:], in1=g_bcp[:], op=ALU.subtract),
                                     reads=[g_sp.b, g_bcp.b], writes=[g_bcp.b])
                                  for ch in range(2):
                                      c0 = 64 * ch
                                      op("dve", lambda e: e.tensor_scalar(out=g_bcp[:, c0:c0 + 64], in0=g_bcp[:, c0:c0 + 64],
                                                                          scalar1=g_tot[:, ch:ch + 1], scalar2=None, op0=ALU.add),
                                         reads=[g_bcp.b, g_tot.b], writes=[g_bcp.b])
                              for ch in range(2):
                                  col = 64 * ch + (63 if d == 0 else 0)
                                  op("dve", lambda e: e.tensor_scalar(out=g_nbl[:, ch:ch + 1], in0=g_bcp[:, col:col + 1], scalar1=-1.0 / 16,
                                                                      scalar2=None, op0=ALU.mult), reads=[g_bcp.b, g_nbl.b], writes=[g_nbl.b])
                              op("act", lambda e: e.activation(out=g_eb[:], in_=g_nbl[:], func=AF.Exp), reads=[g_nbl.b], writes=[g_eb.b])
                              op("act", lambda e: e.activation(out=g_e1[:], in_=g_bcp[:], func=AF.Exp, scale=-1.0 / 16), reads=[g_bcp.b], writes=[g_e1.b])
                              op("act", lambda e: e.activation(out=g_e2[:], in_=g_bcp[:], func=AF.Exp, scale=1.0 / 16), reads=[g_bcp.b], writes=[g_e2.b])
                              for ch in range(2):
                                  c0 = 64 * ch
                                  op("act", lambda e: e.activation(out=g_e3[:, c0:c0 + 64], in_=g_bcp[:, c0:c0 + 64], func=AF.Exp, scale=1.0 / 16,
                                                                   bias=g_nbl[:, ch:ch + 1]), reads=[g_bcp.b, g_nbl.b, g_e3.b], writes=[g_e3.b])
                              op("dve", lambda e: e.scalar_tensor_tensor(out=g_qt[:], in0=g_qf[:], scalar=GLA_DK ** -0.5, in1=g_e1[:],
                                                                         op0=ALU.mult, op1=ALU.mult), reads=[g_qf.b, g_e1.b], writes=[g_qt.b])
                              op("dve", lambda e: e.tensor_tensor(out=g_kt[:], in0=g_kf[:], in1=g_e2[:], op=ALU.mult),
                                 reads=[g_kf.b, g_e2.b], writes=[g_kt.b])
                              op("dve", lambda e: e.tensor_tensor(out=g_khT[:], in0=g_kf[:], in1=g_e3[:], op=ALU.mult),
                                 reads=[g_kf.b, g_e3.b], writes=[g_khT.b])
                              pt_ = psb()
                              op("pe", lambda e: e.transpose(pt_[:, 0:128], g_khT[:, :], ident_b[:, :]), reads=[g_khT.b, ident_b.b], writes=[pt_.b])
                              op("dve", lambda e: e.tensor_scalar(out=g_khA[:], in0=pt_[:, 0:128], scalar1=halfm[:, 0:1], scalar2=None, op0=ALU.mult),
                                 reads=[pt_.b, gcon_t.b], writes=[g_khA.b])
                              op("dve", lambda e: e.tensor_scalar(out=g_khB[:], in0=pt_[:, 0:128], scalar1=halfm[:, 1:2], scalar2=None, op0=ALU.mult),
                                 reads=[pt_.b, gcon_t.b], writes=[g_khB.b])
                              for h in range(4):
                                  op("pool", lambda e: e.tensor_scalar(out=g_km[h][:], in0=g_kt[:], scalar1=headm[:, h:h + 1], scalar2=None, op0=ALU.mult),
                                     reads=[g_kt.b, gcon_t.b], writes=[g_km[h].b])
                                  op("pool", lambda e: e.tensor_scalar(out=g_QA[h][:, 0:64], in0=g_qt[:, 0:64], scalar1=headm[:, h:h + 1], scalar2=None,
                                                                       op0=ALU.mult), reads=[g_qt.b, gcon_t.b], writes=[g_QA[h].b])
                                  op("pool", lambda e: e.tensor_scalar(out=g_QB[h][:, 64:128], in0=g_qt[:, 64:128], scalar1=headm[:, h:h + 1], scalar2=None,
                                                                       op0=ALU.mult), reads=[g_qt.b, gcon_t.b], writes=[g_QB[h].b])
                              for ch, kh in ((0, g_khA), (1, g_khB)):
                                  pd_ = psf()
                                  op("pe", lambda e: e.matmul(pd_[:, 0:256], lhsT=kh[:, :], rhs=g_vt[:, :], start=True, stop=True),
                                     reads=[kh.b, g_vt.b], writes=[pd_.b])
                                  op("dve", lambda e: e.tensor_tensor(out=g_dS[:], in0=pd_[:, 0:256], in1=blockm, op=ALU.mult),
                                     reads=[pd_.b, gcon_t.b], writes=[g_dS.b])
                                  op("dve", lambda e: e.tensor_reduce(out=g_dSr[ch][:], in_=g_dS[:].rearrange("p (h v) -> p v h", h=4),
                                                                      axis=AX.X, op=ALU.add), reads=[g_dS.b], writes=[g_dSr[ch].b])
                              first, second = (0, 1) if d == 0 else (1, 0)
                              op("dve", lambda e: e.tensor_copy(out=g_Sb[0][:], in_=S0[:]), reads=[S0.b], writes=[g_Sb[0].b])
                              op("dve", lambda e: e.scalar_tensor_tensor(out=S1[:], in0=S0[:], scalar=g_eb[:, first:first + 1], in1=g_dSr[first][:],
                                                                         op0=ALU.mult, op1=ALU.add), reads=[S0.b, g_eb.b, g_dSr[first].b], writes=[S1.b])
                              op("dve", lambda e: e.tensor_copy(out=g_Sb[1][:], in_=S1[:]), reads=[S1.b], writes=[g_Sb[1].b])
                              op("dve", lambda e: e.scalar_tensor_tensor(out=S2[:], in0=S1[:], scalar=g_eb[:, second:second + 1], in1=g_dSr[second][:],
                                                                         op0=ALU.mult, op1=ALU.add), reads=[S1.b, g_eb.b, g_dSr[second].b], writes=[S2.b])
                              SbA, SbB = (g_Sb[0], g_Sb[1]) if d == 0 else (g_Sb[1], g_Sb[0])
                              po = p_acc
                              for h in range(4):
                                  pa_ = psf()
                                  op("pe", lambda e: e.matmul(pa_[:, 0:128], lhsT=g_km[h][:, :], rhs=g_qt[:, :], start=True, stop=True),
                                     reads=[g_km[h].b, g_qt.b], writes=[pa_.b])
                                  op("dve", lambda e: e.tensor_tensor(out=g_attm[:], in0=pa_[:, 0:128], in1=mask_d, op=ALU.mult),
                                     reads=[pa_.b, gcon_t.b], writes=[g_attm.b])
                                  op("pe", lambda e: e.matmul(po[:, 64 * h:64 * h + 64], lhsT=g_attm[:, :], rhs=g_vt[:, 64 * h:64 * h + 64],
                                                              start=True, stop=False), reads=[g_attm.b, g_vt.b], writes=[po.b])
                                  op("pe", lambda e: e.matmul(po[:, 64 * h:64 * h + 64], lhsT=g_QA[h][:, :], rhs=SbA[:, :], start=False, stop=False),
                                     reads=[g_QA[h].b, SbA.b], writes=[po.b])
                                  op("pe", lambda e: e.matmul(po[:, 64 * h:64 * h + 64], lhsT=g_QB[h][:, :], rhs=SbB[:, :], start=False, stop=True),
                                     reads=[g_QB[h].b, SbB.b], writes=[po.b])
                              if d == 1:
                                  op("act", lambda e: e.activation(out=o_bwd[:, ti, :], in_=po[:, 0:256], func=AF.Copy), reads=[po.b], writes=[o_bwd.b])
                              else:
                                  op("dve", lambda e: e.tensor_tensor(out=g_os[:], in0=po[:, 0:256], in1=o_bwd[:, ti, :], op=ALU.add),
                                     reads=[po.b, o_bwd.b], writes=[g_os.b])
                                  op("act", lambda e: e.activation(out=g_sq[:], in_=g_os[:], func=AF.Square), reads=[g_os.b], writes=[g_sq.b])
                                  op("dve", lambda e: e.tensor_reduce(out=g_ss[:], in_=g_sq[:].rearrange("p (h v) -> p h v", h=4), axis=AX.X, op=ALU.add),
                                     reads=[g_sq.b], writes=[g_ss.b])
                                  rstd_from_ss(g_rs, g_ss, 64)
                                  os3 = g_os[:].rearrange("p (h v) -> p h v", h=4)
                                  op("dve", lambda e: e.tensor_tensor(out=os3, in0=os3, in1=g_rs[:, :].unsqueeze(2).to_broadcast([128, 4, 64]), op=ALU.mult),
                                     reads=[g_os.b, g_rs.b], writes=[g_os.b])
                                  op("dve", lambda e: e.tensor_tensor(out=os3, in0=os3, in1=bc_gn[:, :].unsqueeze(1).to_broadcast([128, 4, 64]), op=ALU.mult),
                                     reads=[g_os.b, bc_gn.b], writes=[g_os.b])
                                  pgg = psf()
                                  for k in range(8):
                                      op("pe", lambda e: e.matmul(pgg[:, 0:256], lhsT=hT[:, k, r0:r0 + 128], rhs=wmix[:, k, 544:800],
                                                                  start=(k == 0), stop=(k == 7)), reads=[hT.b, wmix.b], writes=[pgg.b])
                                  op("act", lambda e: e.activation(out=g_sil[:], in_=pgg[:, 0:256], func=AF.Silu), reads=[pgg.b], writes=[g_sil.b])
                                  op("dve", lambda e: e.tensor_tensor(out=g_og[:], in0=g_os[:], in1=g_sil[:], op=ALU.mult),
                                     reads=[g_os.b, g_sil.b], writes=[g_og.b])
                                  pbr_ = psb()
                                  for k2 in range(2):
                                      op("pe", lambda e: e.transpose(pbr_[:, k2 * 128:(k2 + 1) * 128], g_og[:, k2 * 128:(k2 + 1) * 128], ident_b[:, :]),
                                         reads=[g_og.b, ident_b.b], writes=[pbr_.b])
                                  op("act", lambda e: e.activation(out=brT[:, 6:8, r0:r0 + 128], in_=pbr_[:, 0:256].rearrange("p (k t) -> p k t", k=2),
                                                                   func=AF.Copy), reads=[pbr_.b], writes=[brT.b])
                              S0, S1, S2 = S2, S0, S1
                          if not sample:
                              dma(o_gla[s_, l, d].rearrange("h k v -> (h k) v"), S0[:, :], reads=[S0.b])

              def s5_group(l, g):
                  c, T, L, sample = g["c"], g["T"], g["L"], g["sample"]
                  nseq, Mseq, M, nb = T // L, L // 8, T // 8, T // 512
                  nsteps = Mseq.bit_length() - 1
                  reuse = sample
                  P_ = s_par
                  pb_ = s_par["are"].b
                  ybank = pf[1:1 + nb]
                  ptmp = pf[0]
                  if sample:
                      dma(s_wmix[:, :, :].rearrange("p k c -> p (k c)"), wc["s5"][0][:, :], reads=[wc["s5"][1]], writes=[s_wmix.b], q="pool")
                  else:
                      for k in range(8):
                          st_ = wstage[0]
                          dma(st_[:, 0:512], w_in[l, k * 128:(k + 1) * 128, S50:S50 + 512], writes=[st_.b], q="pool")
                          op("dve" if k % 2 == 0 else "pool", lambda e: e.tensor_copy(out=s_wmix[:, k, :], in_=st_[:, 0:512]),
                             reads=[st_.b], writes=[s_wmix.b])
                      dma(wc["s5"][0][:, :], s_wmix[:, :, :].rearrange("p k c -> p (k c)"), reads=[s_wmix.b], writes=[wc["s5"][1]])
                  for hf in range(2):
                      for bk in range(nb):
                          for k in range(8):
                              op("pe", lambda e: e.matmul(ptmp[:, :], lhsT=s_wmix[:, k, hf * 128:(hf + 1) * 128], rhs=hT[:, k, bk * 512:(bk + 1) * 512],
                                                          start=(k == 0), stop=(k == 7)), reads=[s_wmix.b, hT.b], writes=[ptmp.b])
                          op("act", lambda e: e.activation(out=s_uT[:, hf, bk * 512:(bk + 1) * 512], in_=ptmp[:, :], func=AF.Copy),
                             reads=[ptmp.b], writes=[s_uT.b])
                  for hf in range(2):
                      for bk in range(nb):
                          op("pe", lambda e: e.matmul(ybank[bk][:, :], lhsT=s_Dd[hf][:, :], rhs=s_uT[:, hf, bk * 512:(bk + 1) * 512],
                                                      start=True, stop=False), reads=[s_Dd[hf].b, s_uT.b], writes=[ybank[bk].b])
                      if not reuse:
                          op("dve", lambda e: e.memset(s_Kacc[:], 0.0), writes=[s_Kacc.b])
                      for d in range(2):
                          for ri, src_ in ((0, s5_cre), (1, s5_cim)):
                              if not reuse:
                                  dma(s_C[ri][:, :], src_[l, d, 8 * hf:8 * hf + 8].rearrange("g i p -> (g i) p"), writes=[s_C[ri].b], q="pool")
                          for k4 in range(4):
                              st = 4 * hf + k4
                              td = d * 8 + st
                              ar, ai, nai = P_["ar1"][:, td:td + 1], P_["ai1"][:, td:td + 1], P_["nai1"][:, td:td + 1]
                              kr_, ki_, nki_ = P_["kr"][:, td:td + 1], P_["ki"][:, td:td + 1], P_["nki"][:, td:td + 1]
                              if reuse:
                                  dma(s_W[:, :, :].rearrange("p a c -> p (a c)"), s5w[td], reads=[s5w_b], writes=[s_W.b], q="pool")
                              else:
                                  for ri, src_ in ((0, s5_bre), (1, s5_bim)):
                                      dma(s_B[ri][:, :], src_[l, d, 2 * st:2 * st + 2].rearrange("g p i -> (g p) i"), writes=[s_B[ri].b], q="pool")
                                  op("dve", lambda e: e.tensor_scalar(out=s_bb[0][:], in0=s_B[0][:], scalar1=kr_, scalar2=None, op0=ALU.mult),
                                     reads=[s_B[0].b, pb_], writes=[s_bb[0].b])
                                  op("dve", lambda e: e.tensor_scalar(out=s_bb[1][:], in0=s_B[0][:], scalar1=ki_, scalar2=None, op0=ALU.mult),
                                     reads=[s_B[0].b, pb_], writes=[s_bb[1].b])
                                  op("dve", lambda e: e.scalar_tensor_tensor(out=s_bb[0][:], in0=s_B[1][:], scalar=nki_, in1=s_bb[0][:], op0=ALU.mult, op1=ALU.add),
                                     reads=[s_B[1].b, pb_, s_bb[0].b], writes=[s_bb[0].b])
                                  op("dve", lambda e: e.scalar_tensor_tensor(out=s_bb[1][:], in0=s_B[1][:], scalar=kr_, in1=s_bb[1][:], op0=ALU.mult, op1=ALU.add),
                                     reads=[s_B[1].b, pb_, s_bb[1].b], writes=[s_bb[1].b])
                                  mB3 = scon_t[:, 128 * k4:128 * k4 + 128].rearrange("p (g i) -> p g i", g=8)
                                  for ri in range(2):
                                      op("dve", lambda e: e.tensor_tensor(out=s_Xm[0][ri][:].rearrange("p (g i) -> p g i", g=8), in0=mB3,
                                                                          in1=s_bb[ri][:, :].unsqueeze(1).to_broadcast([128, 8, 16]), op=ALU.mult),
                                         reads=[scon_t.b, s_bb[ri].b], writes=[s_Xm[0][ri].b])
                                  mC3 = scon_t[:, 512 + 128 * k4:512 + 128 * k4 + 128].rearrange("p (a q) -> p a q", a=2)
                                  for ri in range(2):
                                      op("dve", lambda e: e.tensor_tensor(out=s_Cp[:].rearrange("p (a q) -> p a q", a=2), in0=mC3,
                                                                          in1=s_C[ri][:, :].unsqueeze(1).to_broadcast([128, 2, 64]), op=ALU.mult),
                                         reads=[scon_t.b, s_C[ri].b], writes=[s_Cp.b])
                                      op("pe", lambda e: e.transpose(ptmp[:, 0:128], s_Cp[:, :], ident_f[:, :]), reads=[s_Cp.b, ident_f.b], writes=[ptmp.b])
                                      op("dve", lambda e: e.tensor_copy(out=s_Cm[0][ri][:], in_=ptmp[:, 0:128]), reads=[ptmp.b], writes=[s_Cm[0][ri].b])

                                  def half1(cur, nxt):
                                      op("dve", lambda e: e.tensor_scalar(out=nxt[0][:], in0=cur[0][:], scalar1=ar, scalar2=None, op0=ALU.mult),
                                         reads=[cur[0].b, pb_], writes=[nxt[0].b])
                                      op("dve", lambda e: e.tensor_scalar(out=nxt[1][:], in0=cur[0][:], scalar1=ai, scalar2=None, op0=ALU.mult),
                                         reads=[cur[0].b, pb_], writes=[nxt[1].b])

                                  def half2(cur, nxt):
                                      op("dve", lambda e: e.scalar_tensor_tensor(out=nxt[0][:], in0=cur[1][:], scalar=nai, in1=nxt[0][:], op0=ALU.mult, op1=ALU.add),
                                         reads=[cur[1].b, pb_, nxt[0].b], writes=[nxt[0].b])
                                      op("dve", lambda e: e.scalar_tensor_tensor(out=nxt[1][:], in0=cur[1][:], scalar=ar, in1=nxt[1][:], op0=ALU.mult, op1=ALU.add),
                                         reads=[cur[1].b, pb_, nxt[1].b], writes=[nxt[1].b])

                                  for k in range(9):
                                      Xc, Xn = s_Xm[k % 2], s_Xm[(k + 1) % 2]
                                      Cc, Cn = s_Cm[k % 2], s_Cm[(k + 1) % 2]
                                      if k < 8:
                                          for ri in range(2):
                                              op("act", lambda e: e.activation(out=s_Xb[k][ri][:], in_=Xc[ri][:], func=AF.Copy),
                                                 reads=[Xc[ri].b], writes=[s_Xb[k][ri].b])
                                      op("act", lambda e: e.activation(out=s_Wc[k][0][:], in_=Cc[0][:], func=AF.Copy), reads=[Cc[0].b], writes=[s_Wc[k][0].b])
                                      op("act", lambda e: e.activation(out=s_Wc[k][1][:], in_=Cc[1][:], func=AF.Copy, scale=-1.0),
                                         reads=[Cc[1].b], writes=[s_Wc[k][1].b])
                                      if k < 7:
                                          half1(Xc, Xn)
                                      if k < 8:
                                          half1(Cc, Cn)
                                      if k < 7:
                                          half2(Xc, Xn)
                                      if k < 8:
                                          half2(Cc, Cn)
                                      if k < 8:
                                          pt_ = psb()
                                          for ri in range(2):
                                              op("pe", lambda e: e.transpose(pt_[:, ri * 128:(ri + 1) * 128], s_Xb[k][ri][:, :], ident_b[:, :]),
                                                 reads=[s_Xb[k][ri].b, ident_b.b], writes=[pt_.b])
                                          for ri in range(2):
                                              op("act", lambda e: e.activation(out=s_Wb[k][ri][:], in_=pt_[:, ri * 128:(ri + 1) * 128], func=AF.Copy),
                                                 reads=[pt_.b], writes=[s_Wb[k][ri].b])
                                  for q4 in range(2):
                                      for dd in range(4):
                                          dl = 4 * q4 + dd
                                          op("pe", lambda e: e.matmul(ptmp[:, dd * 128:(dd + 1) * 128], lhsT=s_Xb[dl][0][:, :], rhs=s_Wc[0][0][:, :],
                                                                      start=True, stop=False), reads=[s_Xb[dl][0].b, s_Wc[0][0].b], writes=[ptmp.b])
                                          op("pe", lambda e: e.matmul(ptmp[:, dd * 128:(dd + 1) * 128], lhsT=s_Xb[dl][1][:, :], rhs=s_Wc[0][1][:, :],
                                                                      start=False, stop=True), reads=[s_Xb[dl][1].b, s_Wc[0][1].b], writes=[ptmp.b])
                                      ka = s_Kacc[:, d * 8 + 4 * q4:d * 8 + 4 * q4 + 4, :]
                                      op("dve", lambda e: e.tensor_tensor(out=ka, in0=ka, in1=ptmp[:, :].rearrange("p (a c) -> p a c", a=4), op=ALU.add),
                                         reads=[ptmp.b, s_Kacc.b], writes=[s_Kacc.b])
                                  dma(s5w[td], s_W[:, :, :].rearrange("p a c -> p (a c)"), reads=[s_W.b], writes=[s5w_b])
                              ub = s_uT[:, hf, 0:T].rearrange("p (m j) -> p m j", j=8)
                              for ri in range(2):
                                  for s_ in range(8):
                                      kk = 7 - s_ if d == 0 else s_
                                      op("pe", lambda e: e.matmul(p_acc[:, ri * 256:ri * 256 + M], lhsT=s_Wb[kk][ri][:, :], rhs=ub[:, :, s_],
                                                                  start=(s_ == 0), stop=(s_ == 7)), reads=[s_Wb[kk][ri].b, s_uT.b], writes=[p_acc.b])
                              for ri in range(2):
                                  op("dve", lambda e: e.tensor_copy(out=s_XA[ri][:, 0:M], in_=p_acc[:, ri * 256:ri * 256 + M]),
                                     reads=[p_acc.b], writes=[s_XA[ri].b])
                              if sample:
                                  dma(s_h0[:, :], ss5[l, d, 128 * st:128 * st + 128, :], writes=[s_h0.b], q="pool")
                                  a8r, a8i, na8i = s_pwr[:, 0, td:td + 1], s_pwi[:, 0, td:td + 1], s_npwi[:, 0, td:td + 1]
                                  cc = 0 if d == 0 else M - 1
                                  for ri, (sa, sb_) in enumerate(((a8r, na8i), (a8i, a8r))):
                                      xc = s_XA[ri][:, cc:cc + 1]
                                      op("dve", lambda e: e.scalar_tensor_tensor(out=xc, in0=s_h0[:, 0:1], scalar=sa, in1=xc, op0=ALU.mult, op1=ALU.add),
                                         reads=[s_h0.b, pb_, s_XA[ri].b], writes=[s_XA[ri].b])
                                      op("dve", lambda e: e.scalar_tensor_tensor(out=xc, in0=s_h0[:, 1:2], scalar=sb_, in1=xc, op0=ALU.mult, op1=ALU.add),
                                         reads=[s_h0.b, pb_, s_XA[ri].b], writes=[s_XA[ri].b])
                              src, dst = s_XA, s_XB
                              v3 = lambda t_: t_[:, 0:M].rearrange("p (s m) -> p s m", s=nseq)
                              for k in range(nsteps):
                                  sh = 1 << k
                                  pr, pi, npi = s_pwr[:, k, td:td + 1], s_pwi[:, k, td:td + 1], s_npwi[:, k, td:td + 1]
                                  if d == 0:
                                      lo, ls, kp = slice(sh, Mseq), slice(0, Mseq - sh), slice(0, sh)
                                  else:
                                      lo, ls, kp = slice(0, Mseq - sh), slice(sh, Mseq), slice(Mseq - sh, Mseq)
                                  for ri in range(2):
                                      op("act", lambda e: e.activation(out=v3(dst[ri])[:, :, kp], in_=v3(src[ri])[:, :, kp], func=AF.Copy),
                                         reads=[src[ri].b, dst[ri].b], writes=[dst[ri].b])
                                  for ri, m1 in enumerate((pr, pr)):
                                      op("dve", lambda e: e.scalar_tensor_tensor(out=v3(dst[ri])[:, :, lo], in0=v3(src[ri])[:, :, ls], scalar=m1,
                                                                                 in1=v3(src[ri])[:, :, lo], op0=ALU.mult, op1=ALU.add),
                                         reads=[src[ri].b, pb_, dst[ri].b], writes=[dst[ri].b])
                                  for ri, m2 in enumerate((npi, pi)):
                                      other = src[1 - ri]
                                      op("dve", lambda e: e.scalar_tensor_tensor(out=v3(dst[ri])[:, :, lo], in0=v3(other)[:, :, ls], scalar=m2,
                                                                                 in1=v3(dst[ri])[:, :, lo], op0=ALU.mult, op1=ALU.add),
                                         reads=[other.b, pb_, dst[ri].b], writes=[dst[ri].b])
                                  src, dst = dst, src
                              Hh = src
                              if not sample:
                                  for s_ in range(nseq):
                                      cc = s_ * Mseq + (Mseq - 1 if d == 0 else 0)
                                      for ri in range(2):
                                          op("dve", lambda e: e.tensor_copy(out=s_fin[:, ri:ri + 1], in_=Hh[ri][:, cc:cc + 1]),
                                             reads=[Hh[ri].b, s_fin.b], writes=[s_fin.b])
                                      dma(o_s5[s_, l, d].rearrange("g p r -> (g p) r")[128 * st:128 * st + 128, :], s_fin[:, :], reads=[s_fin.b])
                              W1 = Mseq + 1
                              for ri in range(2):
                                  hv = s_Hs[ri][:, 0:nseq * W1].rearrange("p (s m) -> p s m", s=nseq)
                                  body = hv[:, :, 1:W1] if d == 0 else hv[:, :, 0:Mseq]
                                  edge = hv[:, :, 0:1] if d == 0 else hv[:, :, Mseq:W1]
                                  op("act", lambda e: e.activation(out=body, in_=v3(Hh[ri]), func=AF.Copy), reads=[Hh[ri].b, s_Hs[ri].b], writes=[s_Hs[ri].b])
                                  if sample:
                                      op("dve", lambda e: e.tensor_copy(out=edge, in_=s_h0[:, ri:ri + 1].unsqueeze(1)), reads=[s_h0.b, s_Hs[ri].b], writes=[s_Hs[ri].b])
                                  else:
                                      op("dve", lambda e: e.memset(edge, 0.0), reads=[s_Hs[ri].b], writes=[s_Hs[ri].b])
                              off = 0 if d == 0 else 1
                              for bk in range(nb):
                                  for j in range(8):
                                      kk = j + 1 if d == 0 else 8 - j
                                      for ri in range(2):
                                          hv = s_Hs[ri][:, 0:nseq * W1].rearrange("p (s m) -> p s m", s=nseq)
                                          if nseq == 1:
                                              rhs_ = hv[:, 0, off + 64 * bk:off + 64 * bk + 64]
                                              out_ = ybank[bk][:, :].rearrange("p (m j) -> p m j", j=8)[:, :, j]
                                          else:
                                              rhs_ = hv[:, 2 * bk:2 * bk + 2, off:off + Mseq]
                                              out_ = ybank[bk][:, :].rearrange("p (s m j) -> p s m j", s=2, j=8)[:, :, :, j]
                                          op("pe", lambda e: e.matmul(out_, lhsT=s_Wc[kk][ri][:, :], rhs=rhs_, start=False, stop=False),
                                             reads=[s_Wc[kk][ri].b, s_Hs[ri].b], writes=[ybank[bk].b])
                      if reuse:
                          dma(s_Kbd[:, :, :].rearrange("p a c -> p (a c)"), s5k[hf], reads=[s5k_b], writes=[s_Kbd.b], q="pool")
                      else:
                          for a_ in range(16):
                              op("act", lambda e: e.activation(out=s_Kbd[:, a_, :], in_=s_Kacc[:, a_, :], func=AF.Copy), reads=[s_Kacc.b], writes=[s_Kbd.b])
                          dma(s5k[hf], s_Kbd[:, :, :].rearrange("p a c -> p (a c)"), reads=[s_Kbd.b], writes=[s5k_b])
                      for bk in range(nb):
                          u3 = s_uT[:, hf, bk * 512:(bk + 1) * 512].rearrange("p (m j) -> p m j", j=8)
                          y3 = ybank[bk][:, :].rearrange("p (m j) -> p m j", j=8)
                          for dl in range(8):
                              op("pe", lambda e: e.matmul(y3[:, :, dl:8], lhsT=s_Kbd[:, dl, :], rhs=u3[:, :, 0:8 - dl], start=False, stop=False),
                                 reads=[s_Kbd.b, s_uT.b], writes=[ybank[bk].b])
                              op("pe", lambda e: e.matmul(y3[:, :, 0:8 - dl], lhsT=s_Kbd[:, 8 + dl, :], rhs=u3[:, :, dl:8], start=False, stop=(dl == 7)),
                                 reads=[s_Kbd.b, s_uT.b], writes=[ybank[bk].b])
                      for bk in range(nb):
                          yb_ = ybank[bk]
                          op("act", lambda e: e.activation(out=s_ya[:], in_=yb_[:, :], func=AF.Square), reads=[yb_.b], writes=[s_ya.b])
                          op("dve", lambda e: e.tensor_scalar(out=s_ya[:], in0=s_ya[:], scalar1=0.044715, scalar2=1.0, op0=ALU.mult, op1=ALU.add),
                             reads=[s_ya.b], writes=[s_ya.b])
                          op("dve", lambda e: e.tensor_tensor(out=s_ya[:], in0=s_ya[:], in1=yb_[:, :], op=ALU.mult), reads=[s_ya.b, yb_.b], writes=[s_ya.b])
                          op("act", lambda e: e.activation(out=s_yb[:], in_=s_ya[:], func=AF.Sigmoid, scale=1.5957691216057308),
                             reads=[s_ya.b], writes=[s_yb.b])
                          op("dve", lambda e: e.tensor_tensor(out=s_gT[:, hf, bk * 512:(bk + 1) * 512], in0=s_yb[:], in1=yb_[:, :], op=ALU.mult),
                             reads=[s_yb.b, yb_.b], writes=[s_gT.b])
                  for oh in range(2):
                      for bk in range(nb):
                          tok = slice(bk * 512, (bk + 1) * 512)
                          for ih in range(2):
                              op("pe", lambda e: e.matmul(ptmp[:, :], lhsT=s_glu[:, ih, oh * 128:(oh + 1) * 128], rhs=s_gT[:, ih, tok],
                                                          start=(ih == 0), stop=(ih == 1)), reads=[s_glu.b, s_gT.b], writes=[ptmp.b])
                          op("act", lambda e: e.activation(out=s_ya[:], in_=ptmp[:, :], func=AF.Sigmoid, bias=s_gbcol[:, oh:oh + 1]),
                             reads=[ptmp.b, s_gbcol.b], writes=[s_ya.b])
                          for k in range(8):
                              op("pe", lambda e: e.matmul(p_acc[:, :], lhsT=s_wmix[:, k, 256 + oh * 128:256 + (oh + 1) * 128], rhs=hT[:, k, tok],
                                                          start=(k == 0), stop=(k == 7)), reads=[s_wmix.b, hT.b], writes=[p_acc.b])
                          op("act", lambda e: e.activation(out=s_yb[:], in_=p_acc[:, :], func=AF.Silu), reads=[p_acc.b], writes=[s_yb.b])
                          op("dve", lambda e: e.tensor_tensor(out=s_ya[:], in0=s_ya[:], in1=s_gT[:, oh, tok], op=ALU.mult),
                             reads=[s_ya.b, s_gT.b], writes=[s_ya.b])
                          op("dve", lambda e: e.tensor_tensor(out=brT[:, 4 + oh, tok], in0=s_ya[:], in1=s_yb[:], op=ALU.mult),
                             reads=[s_ya.b, s_yb.b], writes=[brT.b])

              def hy_group(l, g):
                  c, T, L, sample = g["c"], g["T"], g["L"], g["sample"]
                  tb = htab["s" if sample else "p"]
                  nseq, ntt = T // L, L // 128
                  nfb = ntt + 1
                  wbase = 136 if sample else 128
                  NB = min(512, L)
                  nbank = L // NB
                  cos_cb = tb["cos"].rearrange("(a p) j -> p a j", p=128)
                  sin_cb = tb["sin"].rearrange("(a p) j -> p a j", p=128)

                  def sin_layer(pp_, li, dst, w):
                      o4 = 4 * li
                      op("act", lambda e: e.activation(out=h_s2[:, 0:w], in_=pp_[0:64, 0:w], func=AF.Sin, scale=h_mlp[:, o4:o4 + 1],
                                                       bias=h_mlp[:, o4 + 1:o4 + 2]), reads=[pp_.b, h_mlp.b], writes=[h_s2.b])
                      op("act", lambda e: e.activation(out=h_s4[:, 0:w], in_=pp_[0:64, 0:w], func=AF.Sin, scale=h_mlp[:, o4 + 2:o4 + 3],
                                                       bias=h_mlp[:, o4 + 3:o4 + 4]), reads=[pp_.b, h_mlp.b], writes=[h_s4.b])
                      op("dve", lambda e: e.tensor_tensor(out=h_s4[:, 0:w], in0=h_s4[:, 0:w], in1=h_s4[:, 0:w], op=ALU.mult),
                         reads=[h_s4.b], writes=[h_s4.b])
                      op("dve", lambda e: e.tensor_scalar(out=h_s4[:, 0:w], in0=h_s4[:, 0:w], scalar1=-2.0, scalar2=1.0, op0=ALU.mult, op1=ALU.add),
                         reads=[h_s4.b], writes=[h_s4.b])
                      op("dve", lambda e: e.scalar_tensor_tensor(out=dst, in0=h_s2[:, 0:w], scalar=2.0, in1=h_s4[:, 0:w], op0=ALU.mult, op1=ALU.mult),
                         reads=[h_s2.b, h_s4.b], writes=[h_hid[li].b])

                  for c0 in range(0, L, 512):
                      w = min(512, L - c0)
                      dma(h_feat[:, 0:w], tb["feat"][:, c0:c0 + w], writes=[h_feat.b], q="pool")
                      pp_ = psf()
                      op("pe", lambda e: e.matmul(pp_[0:64, 0:w], lhsT=hw1_b[:, :], rhs=h_feat[:, 0:w], start=True, stop=True),
                         reads=[hw1_b.b, h_feat.b], writes=[pp_.b])
                      sin_layer(pp_, 0, h_hid[0][:, c0:c0 + w], w)
                  for c0 in range(0, L, 512):
                      w = min(512, L - c0)
                      pp_ = psf()
                      op("pe", lambda e: e.matmul(pp_[0:64, 0:w], lhsT=hw2_b[:, :], rhs=h_hid[0][:, c0:c0 + w], start=True, stop=True),
                         reads=[hw2_b.b, h_hid[0].b], writes=[pp_.b])
                      sin_layer(pp_, 1, h_hid[1][:, c0:c0 + w], w)

                  if not sample:
                      for b in range(ntt):
                          dma(h_tc[b][:, 0:ntt, :], cos_cb[:, :, b * 128:(b + 1) * 128], writes=[h_tc[b].b], q="pool")
                          dma(h_ts[b][:, 0:ntt, :], sin_cb[:, :, b * 128:(b + 1) * 128], writes=[h_ts[b].b], q="pool")
                          dma(h_cr[b][:, 0:NB], tb["cos"][b * 128:(b + 1) * 128, 0:NB], writes=[h_cr[b].b], q="pool")
                          dma(h_sr[b][:, 0:NB], tb["sin"][b * 128:(b + 1) * 128, 0:NB], writes=[h_sr[b].b], q="pool")
                  for hc in range(2):
                      r_ = 0
                      for k in range(3):
                          for wh in range(3):
                              dma(h_rows[r_:r_ + 1, :], hy_cw[l, k:k + 1, wh * 256 + hc * 128:wh * 256 + hc * 128 + 128], reads=[h_rows.b], writes=[h_rows.b])
                              r_ += 1
                      for wh in range(3):
                          dma(h_rows[r_:r_ + 1, :], hy_cb[l:l + 1, wh * 256 + hc * 128:wh * 256 + hc * 128 + 128], reads=[h_rows.b], writes=[h_rows.b])
                          r_ += 1
                      for o in range(2):
                          dma(h_rows[r_:r_ + 1, :], hy_bi[l, o:o + 1, hc * 128:hc * 128 + 128], reads=[h_rows.b], writes=[h_rows.b])
                          r_ += 1
                      pp_ = psf()
                      op("pe", lambda e: e.transpose(pp_[:, 0:14], h_rows[:, :], ident_f[0:14, 0:14]), reads=[h_rows.b, ident_f.b], writes=[pp_.b])
                      op("dve", lambda e: e.tensor_copy(out=h_colp[:], in_=pp_[:, 0:14]), reads=[pp_.b], writes=[h_colp.b])
                      if sample:
                          dma(h_wmix[:, :, :].rearrange("p k c -> p (k c)"), wc["hy%d" % hc][0][:, :], reads=[wc["hy%d" % hc][1]], writes=[h_wmix.b], q="pool")
                      else:
                          for k in range(8):
                              st_ = wstage[0]
                              for j in range(4):
                                  dma(st_[:, j * 128:(j + 1) * 128], w_in[l, k * 128:(k + 1) * 128, HY0 + j * 256 + hc * 128:HY0 + j * 256 + hc * 128 + 128],
                                      reads=[st_.b], writes=[st_.b])
                              op("dve" if k % 2 == 0 else "pool", lambda e: e.tensor_copy(out=h_wmix[:, k, :], in_=st_[:, 0:512]),
                                 reads=[st_.b], writes=[h_wmix.b])
                          dma(wc["hy%d" % hc][0][:, :], h_wmix[:, :, :].rearrange("p k c -> p (k c)"), reads=[h_wmix.b], writes=[wc["hy%d" % hc][1]])
                      if sample:
                          dma(h_w3[:].rearrange("p j c -> p (j c)"), wc["w3%d" % hc][0][:, :], reads=[wc["w3%d" % hc][1]], writes=[h_w3.b], q="pool")
                      else:
                          st_ = wstage[0]
                          for j in range(4):
                              dma(st_[0:64, j * 128:(j + 1) * 128], hy_w3[l, :, j * 256 + hc * 128:j * 256 + hc * 128 + 128], reads=[st_.b], writes=[st_.b], q="pool")
                          op("dve", lambda e: e.tensor_copy(out=h_w3[:].rearrange("p j c -> p (j c)"), in_=st_[0:64, 0:512]), reads=[st_.b], writes=[h_w3.b])
                          dma(wc["w3%d" % hc][0][:, :], h_w3[:].rearrange("p j c -> p (j c)"), reads=[h_w3.b], writes=[wc["w3%d" % hc][1]])
                      zT = h_sig[:, 3, 0:T]
                      z3 = zT.rearrange("p (s t) -> p s t", s=nseq)
                      for wh in range(3):
                          for c0 in range(0, T, 512):
                              pp_ = psf()
                              for k in range(8):
                                  op("pe", lambda e: e.matmul(pp_[:, :], lhsT=h_wmix[:, k, wh * 128:(wh + 1) * 128], rhs=hT[:, k, c0:c0 + 512],
                                                              start=(k == 0), stop=(k == 7)), reads=[h_wmix.b, hT.b], writes=[pp_.b])
                              op("act", lambda e: e.activation(out=h_sig[:, 3, c0:c0 + 512], in_=pp_[:, :], func=AF.Copy), reads=[pp_.b], writes=[h_sig.b])
                          dst = h_sig[:, wh, 0:T]
                          d3 = dst.rearrange("p (s t) -> p s t", s=nseq)
                          op("dve", lambda e: e.tensor_scalar(out=dst, in0=zT, scalar1=h_colp[:, 3 + wh:4 + wh], scalar2=h_colp[:, 9 + wh:10 + wh],
                                                              op0=ALU.mult, op1=ALU.add), reads=[h_sig.b, h_colp.b], writes=[h_sig.b])
                          op("dve", lambda e: e.scalar_tensor_tensor(out=d3[:, :, 1:L], in0=z3[:, :, 0:L - 1], scalar=h_colp[:, wh:wh + 1], in1=d3[:, :, 1:L],
                                                                     op0=ALU.mult, op1=ALU.add), reads=[h_sig.b, h_colp.b], writes=[h_sig.b])
                          op("dve", lambda e: e.scalar_tensor_tensor(out=d3[:, :, 0:L - 1], in0=z3[:, :, 1:L], scalar=h_colp[:, 6 + wh:7 + wh], in1=d3[:, :, 0:L - 1],
                                                                     op0=ALU.mult, op1=ALU.add), reads=[h_sig.b, h_colp.b], writes=[h_sig.b])
                      for lt in range(ntt):
                          pp_ = psf()
                          op("pe", lambda e: e.matmul(pp_[:, :], lhsT=h_hid[1][:, lt * 128:(lt + 1) * 128], rhs=h_w3[:].rearrange("p j c -> p (j c)"),
                                                      start=True, stop=True), reads=[h_hid[1].b, h_w3.b], writes=[pp_.b])
                          dma(h_win[:, :], tb["win"][lt * 128:(lt + 1) * 128, hc * 128:(hc + 1) * 128], writes=[h_win.b], q="pool")
                          f4 = h_fw[:].rearrange("p (j c) -> p j c", j=4)
                          op("dve", lambda e: e.tensor_tensor(out=f4, in0=pp_[:, :].rearrange("p (j c) -> p j c", j=4),
                                                              in1=h_win[:, :].unsqueeze(1).to_broadcast([128, 4, 128]), op=ALU.mult),
                             reads=[pp_.b, h_win.b], writes=[h_fw.b])
                          if lt == 0:
                              op("dve", lambda e: e.tensor_scalar(out=h_fw[:, 256:512], in0=h_fw[:, 256:512], scalar1=hcon_t[:, 160:161], scalar2=None,
                                                                  op0=ALU.mult), reads=[h_fw.b, hcon_t.b], writes=[h_fw.b])
                          op("act", lambda e: e.activation(out=h_abs[:], in_=h_fw[:], func=AF.Abs), reads=[h_fw.b], writes=[h_abs.b])
                          op("pe", lambda e: e.matmul(p_acc[:, :], lhsT=ones_sq[:, :], rhs=h_abs[:, :], start=(lt == 0), stop=(lt == ntt - 1)),
                             reads=[ones_sq.b, h_abs.b], writes=[p_acc.b])
                          op("dve", lambda e: e.tensor_tensor(out=h_ksd[:, lt, 0:2, :], in0=f4[:, 0:2, :], in1=f4[:, 2:4, :], op=ALU.add),
                             reads=[h_fw.b], writes=[h_ksd.b])
                          op("dve", lambda e: e.tensor_tensor(out=h_ksd[:, lt, 2:4, :], in0=f4[:, 2:4, :], in1=f4[:, 0:2, :], op=ALU.subtract),
                             reads=[h_fw.b, h_ksd.b], writes=[h_ksd.b])
                      n4 = p_acc[:, :].rearrange("p (j c) -> p j c", j=4)
                      op("dve", lambda e: e.tensor_copy(out=h_rn[:], in_=n4[:, 0:2, :]), reads=[p_acc.b], writes=[h_rn.b])
                      op("dve", lambda e: e.tensor_tensor(out=h_rn[:], in0=h_rn[:], in1=n4[:, 2:4, :], op=ALU.add), reads=[p_acc.b, h_rn.b], writes=[h_rn.b])
                      op("dve", lambda e: e.reciprocal(out=h_rn[:], in_=h_rn[:]), reads=[h_rn.b], writes=[h_rn.b])

                      def long_conv(s_, o, src_idx, combine):
                          t0 = s_ * L
                          for a in range(ntt):
                              pt_ = psb()
                              op("pe", lambda e: e.transpose(pt_[:, 0:128], h_sig[:, src_idx, t0 + a * 128:t0 + (a + 1) * 128], ident_b[:, :]),
                                 reads=[h_sig.b, ident_b.b], writes=[pt_.b])
                              op("act", lambda e: e.activation(out=h_utm[:, a, :], in_=pt_[:, 0:128], func=AF.Copy), reads=[pt_.b], writes=[h_utm.b])
                          for b in range(nfb):
                              nyq = (b == ntt)
                              tcb, tsb = h_tc[b % 2], h_ts[b % 2]
                              if not nyq and sample:
                                  dma(tcb[:, 0:ntt, :], cos_cb[:, :, b * 128:(b + 1) * 128], writes=[tcb.b], q="pool")
                                  dma(tsb[:, 0:ntt, :], sin_cb[:, :, b * 128:(b + 1) * 128], writes=[tsb.b], q="pool")
                              pu, pk = psf(), psf()
                              for dst_, col, tab, rhs_of in ((pu, 0, "c", lambda a: h_utm[:, a, :]), (pu, 128, "s", lambda a: h_utm[:, a, :]),
                                                             (pk, 0, "c", lambda a: h_ksd[:, a, o, :]), (pk, 128, "s", lambda a: h_ksd[:, a, 2 + o, :])):
                                  if nyq and tab == "s":
                                      continue
                                  for a in range(ntt):
                                      lt_ = h_alt[:, :] if nyq else (tcb if tab == "c" else tsb)[:, a, :]
                                      rd_ = [h_alt.b] if nyq else [(tcb if tab == "c" else tsb).b]
                                      op("pe", lambda e: e.matmul(dst_[:, col:col + 128], lhsT=lt_, rhs=rhs_of(a), start=(a == 0), stop=(a == ntt - 1)),
                                         reads=rd_ + [h_utm.b, h_ksd.b], writes=[dst_.b])
                              wc = hcon_t[:, wbase + b:wbase + b + 1]
                              op("dve", lambda e: e.scalar_tensor_tensor(out=h_k[0][:], in0=pk[:, 0:128], scalar=wc, in1=h_rn[:, o, :], op0=ALU.mult, op1=ALU.mult),
                                 reads=[pk.b, hcon_t.b, h_rn.b], writes=[h_k[0].b])
                              if nyq:
                                  op("dve", lambda e: e.tensor_tensor(out=h_Y[:, b, 0, :], in0=pu[:, 0:128], in1=h_k[0][:], op=ALU.mult),
                                     reads=[pu.b, h_k[0].b], writes=[h_Y.b])
                                  continue
                              op("dve", lambda e: e.scalar_tensor_tensor(out=h_k[1][:], in0=pk[:, 128:256], scalar=wc, in1=h_rn[:, o, :], op0=ALU.mult, op1=ALU.mult),
                                 reads=[pk.b, hcon_t.b, h_rn.b], writes=[h_k[1].b])
                              op("dve", lambda e: e.tensor_tensor(out=h_t[0][:], in0=pu[:, 0:128], in1=h_k[0][:], op=ALU.mult), reads=[pu.b, h_k[0].b], writes=[h_t[0].b])
                              op("dve", lambda e: e.tensor_tensor(out=h_t[1][:], in0=pu[:, 128:256], in1=h_k[1][:], op=ALU.mult), reads=[pu.b, h_k[1].b], writes=[h_t[1].b])
                              op("dve", lambda e: e.tensor_tensor(out=h_t[2][:], in0=pu[:, 128:256], in1=h_k[0][:], op=ALU.mult), reads=[pu.b, h_k[0].b], writes=[h_t[2].b])
                              op("dve", lambda e: e.tensor_tensor(out=h_t[3][:], in0=pu[:, 0:128], in1=h_k[1][:], op=ALU.mult), reads=[pu.b, h_k[1].b], writes=[h_t[3].b])
                              op("dve", lambda e: e.tensor_tensor(out=h_Y[:, b, 0, :], in0=h_t[0][:], in1=h_t[1][:], op=ALU.add),
                                 reads=[h_t[0].b, h_t[1].b], writes=[h_Y.b])
                              op("dve", lambda e: e.tensor_tensor(out=h_Y[:, b, 1, :], in0=h_t[2][:], in1=h_t[3][:], op=ALU.subtract),
                                 reads=[h_t[2].b, h_t[3].b, h_Y.b], writes=[h_Y.b])
                          for bank in range(nbank):
                              c0 = bank * NB
                              for b in range(ntt):
                                  crb, srb = h_cr[b % 2], h_sr[b % 2]
                                  if sample:
                                      dma(crb[:, 0:NB], tb["cos"][b * 128:(b + 1) * 128, c0:c0 + NB], writes=[crb.b], q="pool")
                                      dma(srb[:, 0:NB], tb["sin"][b * 128:(b + 1) * 128, c0:c0 + NB], writes=[srb.b], q="pool")
                                  op("pe", lambda e: e.matmul(p_acc[:, 0:NB], lhsT=h_Y[:, b, 0, :], rhs=crb[:, 0:NB], start=(b == 0), stop=False),
                                     reads=[h_Y.b, crb.b], writes=[p_acc.b])
                                  op("pe", lambda e: e.matmul(p_acc[:, 0:NB], lhsT=h_Y[:, b, 1, :], rhs=srb[:, 0:NB], start=False, stop=False),
                                     reads=[h_Y.b, srb.b], writes=[p_acc.b])
                              op("pe", lambda e: e.matmul(p_acc[:, 0:NB], lhsT=h_Y[0:1, ntt, 0, :], rhs=haltr_t[0:1, 0:NB], start=False, stop=True),
                                 reads=[h_Y.b, haltr_t.b], writes=[p_acc.b])
                              combine(t0 + c0)

                      def comb1(tk):
                          op("dve", lambda e: e.scalar_tensor_tensor(out=h_fw[:, 0:NB], in0=h_sig[:, 0, tk:tk + NB], scalar=h_colp[:, 12:13], in1=p_acc[:, 0:NB],
                                                                     op0=ALU.mult, op1=ALU.add), reads=[h_sig.b, h_colp.b, p_acc.b], writes=[h_fw.b])
                          op("dve", lambda e: e.tensor_tensor(out=h_sig[:, 3, tk:tk + NB], in0=h_fw[:, 0:NB], in1=h_sig[:, 1, tk:tk + NB], op=ALU.mult),
                             reads=[h_fw.b, h_sig.b], writes=[h_sig.b])

                      def comb2(tk):
                          op("dve", lambda e: e.scalar_tensor_tensor(out=h_fw[:, 0:NB], in0=h_sig[:, 3, tk:tk + NB], scalar=h_colp[:, 13:14], in1=p_acc[:, 0:NB],
                                                                     op0=ALU.mult, op1=ALU.add), reads=[h_sig.b, h_colp.b, p_acc.b], writes=[h_fw.b])
                          op("dve", lambda e: e.tensor_tensor(out=h_fw[:, 0:NB], in0=h_fw[:, 0:NB], in1=h_sig[:, 2, tk:tk + NB], op=ALU.mult),
                             reads=[h_fw.b, h_sig.b], writes=[h_fw.b])
                          pg_ = psf()
                          for k in range(8):
                              op("pe", lambda e: e.matmul(pg_[:, 0:NB], lhsT=h_wmix[:, k, 384:512], rhs=hT[:, k, tk:tk + NB], start=(k == 0), stop=(k == 7)),
                                 reads=[h_wmix.b, hT.b], writes=[pg_.b])
                          op("act", lambda e: e.activation(out=h_abs[:, 0:NB], in_=pg_[:, 0:NB], func=AF.Silu), reads=[pg_.b], writes=[h_abs.b])
                          op("dve", lambda e: e.tensor_tensor(out=brT[:, 2 + hc, tk:tk + NB], in0=h_fw[:, 0:NB], in1=h_abs[:, 0:NB], op=ALU.mult),
                             reads=[h_fw.b, h_abs.b], writes=[brT.b])

                      for s_ in range(nseq):
                          long_conv(s_, 0, 0, comb1)
                      for s_ in range(nseq):
                          long_conv(s_, 1, 3, comb2)

              groups = [
                  dict(c=0, src=(xp if l == 0 else xmid_p), srcb=xmid_p_b, T=TP, L=SEQ, sample=False),
                  dict(c=1, src=(xs if l == 0 else xmid_s), srcb=xmid_s_b, T=LS, L=LS, sample=True),
              ]
              for g in groups:
                  c, T, L, sample = g["c"], g["T"], g["L"], g["sample"]
                  ntile = T // 128
                  nseq = T // L
                  kb.barrier()
                  if sample:
                      dma(w_in_b[:, :, :].rearrange("p k c -> p (k c)"), wc["in"][0][:, :], reads=[wc["in"][1]], writes=[w_in_b.b], q="pool")
                  else:
                      for k in range(8):
                          st = wstage[k % 2]
                          dma(st[:, 0:NINU], w_in[l, k * 128:(k + 1) * 128, 0:NINU], writes=[st.b], q="pool")
                          eng = "dve" if k % 2 == 0 else "pool"
                          op(eng, lambda e: e.tensor_copy(out=w_in_b[:, k, :], in_=st[:, 0:NINU]), reads=[st.b], writes=[w_in_b.b])
                      dma(wc["in"][0][:, :], w_in_b[:, :, :].rearrange("p k c -> p (k c)"), reads=[w_in_b.b], writes=[wc["in"][1]])

                  def k_path(src_ckv_f32, src_kro_f32, rope_tile, kcol, vt, rd):
                      op("dve", lambda e: e.tensor_copy(out=ckvn_b[:], in_=src_ckv_f32), reads=rd, writes=[ckvn_b.b])
                      p = psb()
                      op("pe", lambda e: e.transpose(p[:, 0:128], ckvn_b[:, :], ident_b[:, :]),
                         reads=[ckvn_b.b, ident_b.b], writes=[p.b])
                      op("act", lambda e: e.activation(out=ckvT[:], in_=p[:, 0:128], func=AF.Copy),
                         reads=[p.b], writes=[ckvT.b])
                      pkv = psf()
                      op("pe", lambda e: e.matmul(pkv[:, :], lhsT=ckvT[:, :], rhs=w_ukv_b[:, :], start=True, stop=True),
                         reads=[ckvT.b, w_ukv_b.b], writes=[pkv.b])
                      chk('k1')
                      kv3 = pkv[:, :].rearrange("p (h d) -> p h d", h=H)
                      op("dve", lambda e: e.tensor_copy(out=kcat[:, :, 0:64], in_=kv3[:, :, 0:64]),
                         reads=[pkv.b], writes=[kcat.b])
                      op("pool", lambda e: e.tensor_copy(out=kcat[:, :, 64:96],
                                                         in_=src_kro_f32.unsqueeze(1).to_broadcast([128, H, 32])),
                         reads=rd + [kcat.b], writes=[kcat.b])
                      chk('k2')
                      op("act", lambda e: e.activation(out=vaug[:, vt, :, 0:64], in_=kv3[:, :, 64:128], func=AF.Copy),
                         reads=[pkv.b], writes=[vaug.b])
                      op("pool", lambda e: e.memset(vaug[:, vt, :, 64:65], 1.0), reads=[vaug.b], writes=[vaug.b])
                      chk('k3')
                      head_norm(kcat, bc_kn, rope_tile)
                      chk('k4')
                      pq = psb()
                      for h in range(H):
                          op("pe", lambda e: e.transpose(pq[0:QK, h * 128:(h + 1) * 128], qb[:, h, :], ident_b[:, :]),
                             reads=[qb.b, ident_b.b], writes=[pq.b])
                      op("act", lambda e: e.activation(out=kT[0:QK, :, kcol:kcol + 128],
                                                       in_=pq[0:QK, 0:512].rearrange("p (h t) -> p h t", h=H),
                                                       func=AF.Copy), reads=[pq.b], writes=[kT.b])

                  def head_norm(src, wbc, rope_tile):
                      op("act", lambda e: e.activation(out=sqh[:], in_=src[:], func=AF.Square),
                         reads=[src.b], writes=[sqh.b])
                      op("dve", lambda e: e.tensor_reduce(out=ssh[:], in_=sqh[:], axis=AX.X, op=ALU.add),
                         reads=[sqh.b], writes=[ssh.b])
                      rstd_from_ss(rsh, ssh, QK)
                      op("dve", lambda e: e.tensor_tensor(out=qf[:], in0=src[:],
                                                          in1=rsh[:, :].unsqueeze(2).to_broadcast([128, H, QK]),
                                                          op=ALU.mult), reads=[src.b, rsh.b], writes=[qf.b])
                      op("dve", lambda e: e.tensor_tensor(out=qf[:], in0=qf[:],
                                                          in1=wbc[:, :].unsqueeze(1).to_broadcast([128, H, QK]),
                                                          op=ALU.mult), reads=[qf.b, wbc.b], writes=[qf.b])
                      if rope_tile is not None:
                          r5 = qf[:, :, 64:96].rearrange("p h (a f j) -> p h a f j", a=2, f=2)
                          t5 = rtmp[:].rearrange("p h (a f j) -> p h a f j", a=2, f=2)
                          op("dve", lambda e: e.tensor_copy(out=t5[:, :, :, 0, :], in_=r5[:, :, :, 1, :]),
                             reads=[qf.b], writes=[rtmp.b])
                          op("dve", lambda e: e.tensor_copy(out=t5[:, :, :, 1, :], in_=r5[:, :, :, 0, :]),
                             reads=[qf.b, rtmp.b], writes=[rtmp.b])
                          op("dve", lambda e: e.tensor_tensor(out=rtmp[:], in0=rtmp[:],
                                                              in1=sinT[:, :].unsqueeze(1).to_broadcast([128, H, 32]),
                                                              op=ALU.mult), reads=[rtmp.b, sinT.b], writes=[rtmp.b])
                          op("dve", lambda e: e.tensor_tensor(out=rtmp2[:], in0=qf[:, :, 64:96],
                                                              in1=cosT[:, :].unsqueeze(1).to_broadcast([128, H, 32]),
                                                              op=ALU.mult), reads=[qf.b, cosT.b], writes=[rtmp2.b])
                          op("dve", lambda e: e.tensor_tensor(out=qf[:, :, 64:96], in0=rtmp2[:], in1=rtmp[:], op=ALU.add),
                             reads=[rtmp.b, rtmp2.b, qf.b], writes=[qf.b])
                      op("dve", lambda e: e.tensor_copy(out=qb[:], in_=qf[:]), reads=[qf.b], writes=[qb.b])

                  for s in range(nseq):
                      tps = L // 128
                      nctx = 0
                      if sample:
                          nctx = PAST // 128
                          for t in range(nctx):
                              dma(ckvn[:, :], cckv[l, t * 128:(t + 1) * 128, :], writes=[ckvn.b], q="pool")
                              dma(kro[:, :], ckro[l, t * 128:(t + 1) * 128, :], writes=[kro.b], q="pool")
                              k_path(ckvn[:, :], kro[:, :], None, t * 128, t, [ckvn.b, kro.b])
                      for tt in range(tps):
                          ti = s * tps + tt
                          r0 = ti * 128
                          dma(xt[:, :], g["src"][r0:r0 + 128, :], reads=[g["srcb"]], writes=[xt.b], q="pool")
                          op("act", lambda e: e.activation(out=junk[:], in_=xt[:], func=AF.Square, accum_out=ss1[:, 0:1]),
                             reads=[xt.b], writes=[xn.b, ss1.b])
                          rstd_from_ss(rs1, ss1, D)
                          op("dve", lambda e: e.tensor_scalar(out=xn[:], in0=xt[:], scalar1=rs1[:, 0:1], scalar2=None,
                                                              op0=ALU.mult), reads=[xt.b, rs1.b], writes=[xn.b])
                          pt = psb()
                          for k in range(8):
                              op("pe", lambda e: e.transpose(pt[:, k * 128:(k + 1) * 128], xn[:, k * 128:(k + 1) * 128],
                                                             ident_b[:, :]), reads=[xn.b, ident_b.b], writes=[pt.b])
                          for k in range(8):
                              op("act", lambda e: e.activation(out=hT[:, k, r0:r0 + 128], in_=pt[:, k * 128:(k + 1) * 128],
                                                               func=AF.Identity, scale=effsc[:, c, k:k + 1],
                                                               bias=shift[:, c, k:k + 1]),
                                 reads=[pt.b, effsc.b, shift.b], writes=[hT.b])
                          chk('normT')
                          pl = psf()
                          for k in range(8):
                              op("pe", lambda e: e.matmul(pl[:, 0:352], lhsT=hT[:, k, r0:r0 + 128], rhs=w_in_b[:, k, 0:352],
                                                          start=(k == 0), stop=(k == 7)),
                                 reads=[hT.b, w_in_b.b], writes=[pl.b])
                          op("act", lambda e: e.activation(out=lat[:], in_=pl[:, 0:352], func=AF.Copy),
                             reads=[pl.b], writes=[lat.b])
                          op("act", lambda e: e.activation(out=junk2[:, 0:192], in_=lat[:, 0:192], func=AF.Square,
                                                           accum_out=ssq[:, 0:1]), reads=[lat.b], writes=[junk2.b, ssq.b])
                          op("act", lambda e: e.activation(out=junk2[:, 0:128], in_=lat[:, 192:320], func=AF.Square,
                                                           accum_out=ssq[:, 1:2]), reads=[lat.b, ssq.b], writes=[junk2.b, ssq.b])
                          op("act", lambda e: e.activation(out=rsq[:, 0:1], in_=ssq[:, 0:1], func=AF.Sqrt, scale=1.0 / 192,
                                                           bias=eps_c[:, 0:1]), reads=[ssq.b, eps_c.b], writes=[rsq.b])
                          op("act", lambda e: e.activation(out=rsq[:, 1:2], in_=ssq[:, 1:2], func=AF.Sqrt, scale=1.0 / 128,
                                                           bias=eps_c[:, 0:1]), reads=[ssq.b, eps_c.b, rsq.b], writes=[rsq.b])
                          op("dve", lambda e: e.reciprocal(out=rsq[:], in_=rsq[:]), reads=[rsq.b], writes=[rsq.b])
                          op("dve", lambda e: e.scalar_tensor_tensor(out=cqn[:], in0=lat[:, 0:192], scalar=rsq[:, 0:1],
                                                                     in1=bc_qa[:, :], op0=ALU.mult, op1=ALU.mult),
                             reads=[lat.b, rsq.b, bc_qa.b], writes=[cqn.b])
                          op("dve", lambda e: e.scalar_tensor_tensor(out=ckvn[:], in0=lat[:, 192:320], scalar=rsq[:, 1:2],
                                                                     in1=bc_kva[:, :], op0=ALU.mult, op1=ALU.mult),
                             reads=[lat.b, rsq.b, bc_kva.b], writes=[ckvn.b])
                          if not sample:
                              dma(o_ckv[s, l, tt * 128:(tt + 1) * 128, :], ckvn[:, :], reads=[ckvn.b])
                              dma(o_kro[s, l, tt * 128:(tt + 1) * 128, :], lat[:, 320:352], reads=[lat.b])
                          rope_tile = None
                          if sample:
                              dma(cosT[:, :], ropec[r0:r0 + 128, :], writes=[cosT.b], q="pool")
                              dma(sinT[:, :], ropes[r0:r0 + 128, :], writes=[sinT.b], q="pool")
                              rope_tile = True
                          chk('latent')
                          pc = psb()
                          op("pe", lambda e: e.transpose(pc[:, 0:128], cqn[:, 0:128], ident_b[:, :]),
                             reads=[cqn.b, ident_b.b], writes=[pc.b])
                          op("pe", lambda e: e.transpose(pc[0:64, 128:256], cqn[:, 128:192], ident_b[:, :]),
                             reads=[cqn.b, ident_b.b], writes=[pc.b])
                          op("act", lambda e: e.activation(out=cqT[:, 0, :], in_=pc[:, 0:128], func=AF.Copy),
                             reads=[pc.b], writes=[cqT.b])
                          op("act", lambda e: e.activation(out=cqT[0:64, 1, :], in_=pc[0:64, 128:256], func=AF.Copy),
                             reads=[pc.b, cqT.b], writes=[cqT.b])
                          pqp = psf()
                          op("pe", lambda e: e.matmul(pqp[:, 0:384], lhsT=cqT[:, 0, :], rhs=w_uq_a[:, :], start=True, stop=False),
                             reads=[cqT.b, w_uq_a.b], writes=[pqp.b])
                          op("pe", lambda e: e.matmul(pqp[:, 0:384], lhsT=cqT[0:64, 1, :], rhs=w_uq_c[:, :], start=False, stop=True),
                             reads=[cqT.b, w_uq_c.b], writes=[pqp.b])
                          op("act", lambda e: e.activation(out=qpre[:].rearrange("p h d -> p (h d)"), in_=pqp[:, 0:384],
                                                           func=AF.Copy), reads=[pqp.b], writes=[qpre.b])
                          head_norm(qpre, bc_qn, rope_tile)
                          pq = psb()
                          for h in range(H):
                              op("pe", lambda e: e.transpose(pq[0:QK, h * 128:(h + 1) * 128], qb[:, h, :], ident_b[:, :]),
                                 reads=[qb.b, ident_b.b], writes=[pq.b])
                          op("act", lambda e: e.activation(out=qT[0:QK, :, tt * 128:(tt + 1) * 128],
                                                           in_=pq[0:QK, 0:512].rearrange("p (h t) -> p h t", h=H),
                                                           func=AF.Copy), reads=[pq.b], writes=[qT.b])
                          chk('queries')
                          k_path(ckvn[:, :], lat[:, 320:352], rope_tile, (nctx + tt) * 128, nctx + tt, [ckvn.b, lat.b])

                      chk('tile0') if False else None
                      chk('phaseA')
                      nk = nctx + tps
                      QB = min(512, L)
                      nqs = QB // 128
                      po_b = [pf[5], pf[4], pf[3], pf[2]][:nqs]
                      qsplit = sample and l == DEPTH - 1
                      if qsplit:
                          qsel = qT
                          hsel = Tile(brT[:, 2:4, :].rearrange("p r (b t) -> p (r b) t", t=512))
                          for dst_, src_, np_ in ((qsel, qT, QK), (hsel, hT, 128)):
                              op("dve", lambda e: e.tensor_scalar(out=dst_[0:np_, :, 0:512], in0=src_[0:np_, :, 0:512], scalar1=qm[0:np_, 0:1],
                                                                  scalar2=None, op0=ALU.mult), reads=[src_.b, qm.b], writes=[dst_.b])
                              for j in range(1, 4):
                                  op("dve", lambda e: e.scalar_tensor_tensor(out=dst_[0:np_, :, 0:512], in0=src_[0:np_, :, j * 512:(j + 1) * 512],
                                                                             scalar=qm[0:np_, j:j + 1], in1=dst_[0:np_, :, 0:512],
                                                                             op0=ALU.mult, op1=ALU.add),
                                     reads=[src_.b, qm.b, dst_.b], writes=[dst_.b])
                      for qb_ in range(1 if qsplit else L // QB):
                          q0 = qb_ * QB
                          qsrc, hsrc = (qsel, hsel) if qsplit else (qT, hT)
                          for h in range(H):
                              for kt in range(nk):
                                  pss = pf[kt % 2]
                                  pTb = pT2[kt % 2]
                                  op("pe", lambda e: e.matmul(pss[:, 0:QB], lhsT=kT[0:QK, h, kt * 128:(kt + 1) * 128],
                                                              rhs=qsrc[0:QK, h, q0:q0 + QB], start=True, stop=True),
                                     reads=[kT.b, qsrc.b], writes=[pss.b])
                                  op("act", lambda e: e.activation(out=pTb[:, 0:QB], in_=pss[:, 0:QB], func=AF.Exp, scale=QK ** -0.5),
                                     reads=[pss.b], writes=[pTb.b])
                                  for qs in range(nqs):
                                      op("pe", lambda e: e.matmul(po_b[qs][:, 0:65], lhsT=pTb[:, qs * 128:(qs + 1) * 128], rhs=vaug[:, kt, h, 0:65],
                                                                  start=(kt == 0), stop=(kt == nk - 1)),
                                         reads=[pTb.b, vaug.b], writes=[po_b[qs].b])
                              for qs in range(nqs):
                                  op("dve", lambda e: e.reciprocal(out=den[:, 0:1], in_=po_b[qs][:, 64:65]), reads=[po_b[qs].b], writes=[den.b])
                                  op("dve", lambda e: e.tensor_scalar(out=omla4[:, qs, h, :], in0=po_b[qs][:, 0:64], scalar1=den[:, 0:1], scalar2=None,
                                                                      op0=ALU.mult), reads=[po_b[qs].b, den.b], writes=[omla4.b])
                          for qs in range(nqs):
                              r0 = s * L + q0 + qs * 128
                              pgm = pf[0]
                              for k in range(8):
                                  op("pe", lambda e: e.matmul(pgm[:, 0:256], lhsT=hsrc[:, k, r0:r0 + 128], rhs=w_in_b[:, k, 352:608],
                                                              start=(k == 0), stop=(k == 7)),
                                     reads=[hsrc.b, w_in_b.b], writes=[pgm.b])
                              op("act", lambda e: e.activation(out=silg[:], in_=pgm[:, 0:256], func=AF.Silu),
                                 reads=[pgm.b], writes=[silg.b])
                              op("dve", lambda e: e.tensor_tensor(out=omg[:], in0=omla4[:, qs, :, :].rearrange("p h d -> p (h d)"),
                                                                  in1=silg[:], op=ALU.mult), reads=[omla4.b, silg.b], writes=[omg.b])
                              pbr = psb()
                              for k2 in range(2):
                                  op("pe", lambda e: e.transpose(pbr[:, k2 * 128:(k2 + 1) * 128], omg[:, k2 * 128:(k2 + 1) * 128],
                                                                 ident_b[:, :]), reads=[omg.b, ident_b.b], writes=[pbr.b])
                              if qsplit:
                                  for j in range(4):
                                      op("dve", lambda e: e.tensor_scalar(out=brT[:, 0:2, j * 512 + r0:j * 512 + r0 + 128],
                                                                          in0=pbr[:, 0:256].rearrange("p (k t) -> p k t", k=2),
                                                                          scalar1=qm[:, j:j + 1], scalar2=None, op0=ALU.mult),
                                         reads=[pbr.b, qm.b], writes=[brT.b])
                              else:
                                  op("act", lambda e: e.activation(out=brT[:, 0:2, r0:r0 + 128],
                                                                   in_=pbr[:, 0:256].rearrange("p (k t) -> p k t", k=2),
                                                                   func=AF.Copy), reads=[pbr.b], writes=[brT.b])

                  chk('attn%d' % c)
                  kb.barrier()
                  gla_group(l, g)
                  chk('gla%d' % c)
                  kb.barrier()
                  s5_group(l, g)
                  chk('s5%d' % c)
                  kb.barrier()
                  hy_group(l, g)
                  chk('hy%d' % c)
                  kb.barrier()
                  if sample:
                      dma(w_out_b[:, :, :].rearrange("p k c -> p (k c)"), wc["out"][0][:, :], reads=[wc["out"][1]], writes=[w_out_b.b], q="pool")
                  else:
                      for k in range(8):
                          st = wstage[k % 2]
                          dma(st[:, 0:D], w_out[l, k * 128:(k + 1) * 128, :], writes=[st.b], q="pool")
                          eng = "dve" if k % 2 == 0 else "pool"
                          op(eng, lambda e: e.tensor_copy(out=w_out_b[:, k, :], in_=st[:, 0:D]), reads=[st.b], writes=[w_out_b.b])
                      dma(wc["out"][0][:, :], w_out_b[:, :, :].rearrange("p k c -> p (k c)"), reads=[w_out_b.b], writes=[wc["out"][1]])
                  for ti in range(ntile):
                      r0 = ti * 128
                      last = (l == DEPTH - 1)
                      dma(xt[:, :], g["src"][r0:r0 + 128, :], reads=[g["srcb"]], writes=[xt.b], q="pool")
                      for hf in range(2):
                          p = psf()
                          for k in range(8):
                              op("pe", lambda e: e.matmul(p[:, :], lhsT=brT[:, k, r0:r0 + 128],
                                                          rhs=w_out_b[:, k, hf * 512:(hf + 1) * 512],
                                                          start=(k == 0), stop=(k == 7)),
                                 reads=[brT.b, w_out_b.b], writes=[p.b])
                          op("dve", lambda e: e.tensor_tensor(out=ot[:, hf * 512:(hf + 1) * 512], in0=p[:, :],
                                                              in1=gate_bc[c][:, hf * 512:(hf + 1) * 512], op=ALU.mult),
                             reads=[p.b, gate_bc[c].b], writes=[ot.b])
                      op("dve", lambda e: e.tensor_tensor(out=ot[:], in0=ot[:], in1=xt[:], op=ALU.add),
                         reads=[ot.b, xt.b], writes=[ot.b])
                      if not last:
                          dst = (xmid_s if sample else xmid_p)
                          dma(dst[r0:r0 + 128, :], ot[:, :], reads=[ot.b], writes=[xmid_s_b if sample else xmid_p_b])
                      elif not sample:
                          dma(yp[r0:r0 + 128, :], ot[:, :], reads=[ot.b])
                      else:
                          dma(ys[r0:r0 + 128, :], ot[:, :], reads=[ot.b])

              chk('layer0')
        except _Stop:
            if stop == 'layer0':
                dma(yp[:, :], xmid_p[:, :], reads=[xmid_p_b])
                dma(ys[:, :], xmid_s[:, :], reads=[xmid_s_b])
            kb.barrier()
            kb.phase("DBG")
            dbt = sb([128, 2048], F32)
            kb.phase(None)
            op("dve", lambda e: e.memset(dbt[:], 0.0), writes=[dbt.b])
            op("dve", lambda e: e.tensor_copy(out=dbt[:, 0:16], in_=effsc[:].rearrange("p c k -> p (c k)")), reads=[effsc.b, dbt.b], writes=[dbt.b])
            op("dve", lambda e: e.tensor_copy(out=dbt[:, 16:32], in_=shift[:].rearrange("p c k -> p (c k)")), reads=[shift.b, dbt.b], writes=[dbt.b])
            op("dve", lambda e: e.tensor_copy(out=dbt[:, 32:33], in_=rs1[:, 0:1]), reads=[rs1.b, dbt.b], writes=[dbt.b])
            op("dve", lambda e: e.tensor_copy(out=dbt[:, 64:416], in_=lat[:, :]), reads=[lat.b, dbt.b], writes=[dbt.b])
            op("dve", lambda e: e.tensor_copy(out=dbt[:, 512:1536].rearrange("p (k t) -> p k t", k=8), in_=hT[:, :, 0:128]), reads=[hT.b, dbt.b], writes=[dbt.b])
            op("dve", lambda e: e.tensor_copy(out=dbt[:, 1536:2048], in_=gate_bc[0][:, 0:512]), reads=[gate_bc[0].b, dbt.b], writes=[dbt.b])
            if stop == 'weights':
                for i_, t_ in enumerate((s_par["ar1"][:], s_par["ai1"][:], s_par["kr"][:], s_par["ki"][:], s_pwr[:, 0, :], s_pwi[:, 0, :],
                                         s_pwr[:, 7, :], s_pwi[:, 7, :])):
                    op("dve", lambda e: e.tensor_copy(out=dbt[:, 512 + 16 * i_:528 + 16 * i_], in_=t_), reads=[s_par["are"].b, dbt.b], writes=[dbt.b])
            dma(dbg[:, :], dbt[:, :], reads=[dbt.b])
        kb.finish()
    return nc


_CACHE = {}


def _rope_tables():
    half = 16
    inv = (10000.0 ** (-np.arange(0, half, 2, dtype=np.float32) / half)).astype(np.float32)
    t = np.arange(LS)
    row = (t // 64).astype(np.float32)
    col = (t % 64).astype(np.float32)
    cos = np.zeros((LS, 32), np.float32)
    sin = np.zeros((LS, 32), np.float32)
    for a, pos in enumerate((row, col)):
        ang = pos[:, None] * inv[None, :]
        ang = np.concatenate([ang, ang], axis=-1)
        cos[:, 16 * a:16 * a + 16] = np.cos(ang)
        s = np.sin(ang)
        s[:, 0:8] *= -1.0
        sin[:, 16 * a:16 * a + 16] = s
    return cos, sin


def _gla_consts():
    p = np.arange(128)[:, None]
    q = np.arange(128)[None, :]
    same = (p // 64) == (q // 64)
    c = np.zeros((128, GC), np.float32)
    c[:, 0:128] = same & (p <= q)
    c[:, 128:256] = same & (p >= q)
    c[:, 256:260] = (p // 32) == np.arange(4)[None, :]
    c[:, 260] = (p[:, 0] < 64)
    c[:, 261] = (p[:, 0] >= 64)
    c[:, 262:518] = (p // 32) == (np.arange(256)[None, :] // 64)
    c[:, 518:646] = 1.0
    return c


def _s5_consts():
    c = np.zeros((128, SC), np.float32)
    r = np.arange(128)[:, None]
    q = np.arange(128)[None, :]
    for k in range(4):
        c[:, 128 * k:128 * k + 128] = (q // 16) == (2 * k + r // 64)
        c[:, 512 + 128 * k:512 + 128 * k + 128] = (r // 16) == (2 * k + q // 64)
    c[:, 1024] = math.pi / 2
    return c


def _hy_tables(L):
    i = np.arange(L, dtype=np.float64)
    ang = (2.0 * np.pi / (2 * L)) * np.outer(i, i)
    cos = np.cos(ang).astype(ml_dtypes.bfloat16)
    sin = np.sin(ang).astype(ml_dtypes.bfloat16)
    pos = np.arange(L, dtype=np.float32)
    t = pos / L
    w = (2.0 * math.pi * pos / L).astype(np.float32)
    bands = np.linspace(1e-4, 15, 16, dtype=np.float32)
    feat = np.concatenate([t[:, None], np.cos(w[:, None] * bands), np.sin(w[:, None] * bands)], axis=-1)
    deltas = np.linspace(math.log(100.0) / 0.3, math.log(100.0) / 1.5, 256, dtype=np.float32)
    win = (np.exp(-t[:, None] * deltas[None, :]) + 0.05).astype(np.float32)
    return cos, sin, np.ascontiguousarray(feat.T).astype(ml_dtypes.bfloat16), win


def _hy_consts():
    c = np.zeros((128, HC), np.float32)
    p = np.arange(128)
    c[:, 0] = 1.0 - 2.0 * (p % 2)
    for base, L in ((128, SEQ), (136, LS)):
        n = 2 * L
        nt = L // 128
        c[:, base:base + nt] = 2.0 / n
        c[0, base] = 1.0 / n
        c[:, base + nt] = 1.0 / n
    c[:, 160] = 1.0
    c[0, 160] = 0.0
    altr = (1.0 - 2.0 * (np.arange(512) % 2)).astype(ml_dtypes.bfloat16)[None, :]
    return c, altr


W_NAMES = ["norm_w", "ada_w", "ada_b", "w_in", "w_out", "mla_qa_norm", "mla_kva_norm", "mla_w_uq",
           "mla_w_ukv", "mla_q_norm", "mla_k_norm", "gla_gw", "gla_gb", "gla_norm",
           "s5_a_re", "s5_a_im", "s5_log_dt", "s5_b_re", "s5_b_im", "s5_c_re", "s5_c_im", "s5_d", "s5_glu_w", "s5_glu_b",
           "hy_conv_w", "hy_conv_b", "hy_w1", "hy_b1", "hy_freq1", "hy_w2", "hy_b2", "hy_freq2", "hy_w3", "hy_bias"]


def _f32(a):
    return np.ascontiguousarray(np.asarray(a, dtype=np.float32))


def core_inputs(inp, i):
    b = i // 4
    cos, sin = _rope_tables()
    m = {
        "xp": _f32(inp["x_prompt"])[4 * i:4 * i + 4].reshape(TP, D),
        "xs": _f32(inp["x_sample"])[b],
        "cond": np.stack([_f32(inp["c_ctx"]), _f32(inp["c"])[b]]),
        "cckv": _f32(inp["cache_mla_ckv"])[b],
        "ckro": _f32(inp["cache_mla_krope"])[b],
        "ropec": cos, "ropes": sin,
        "sgla": _f32(inp["state_gla"])[b].reshape(DEPTH, 2, 128, 64),
        "gcon": _gla_consts(),
        "ss5": _f32(inp["state_s5"])[b].reshape(DEPTH, 2, 1024, 2),
        "scon": _s5_consts(),
        "qmask": np.ascontiguousarray(np.tile(np.eye(4, dtype=np.float32)[i % 4], (128, 1))),
    }
    if "hy" not in _CACHE:
        _CACHE["hy"] = {"p": _hy_tables(SEQ), "s": _hy_tables(LS), "c": _hy_consts()}
    for nm_ in ("p", "s"):
        cos_, sin_, feat_, win_ = _CACHE["hy"][nm_]
        m["hcos_" + nm_], m["hsin_" + nm_], m["hfeat_" + nm_], m["hwin_" + nm_] = cos_, sin_, feat_, win_
    m["hcon"], m["haltr"] = _CACHE["hy"]["c"]
    for n in W_NAMES:
        m[n] = _f32(inp[n])
    return m


def kernel(**inp):
    if "nc" not in _CACHE:
        _CACHE["nc"] = build_program()
    nc = _CACHE["nc"]
    in_maps = [core_inputs(inp, i) for i in range(8)]
    res = run_bass_kernel_spmd(nc, in_maps, core_ids=list(range(8))).results
    y_prompt = np.concatenate([r["yp"].reshape(NPS, SEQ, D) for r in res], axis=0)
    y_sample = np.stack([np.concatenate([res[4 * b + q]["ys"][512 * q:512 * (q + 1)] for q in range(4)], axis=0)
                         for b in range(2)], axis=0)
    ckv = np.concatenate([r["o_ckv"] for r in res], axis=0)
    kro = np.concatenate([r["o_kro"] for r in res], axis=0)
    s5 = np.concatenate([r["o_s5"] for r in res], axis=0)
    gla = np.concatenate([r["o_gla"] for r in res], axis=0)
    return (y_prompt.astype(np.float32), y_sample.astype(np.float32), ckv.astype(np.float32),
            kro.astype(np.float32), s5.astype(np.float32), gla.astype(np.float32))
```

```python
import contextlib
import math

import numpy as np
import ml_dtypes

import concourse.bass as bass
import concourse.mybir as mybir
from concourse.bass_utils import run_bass_kernel_spmd

F32 = mybir.dt.float32
BF16 = mybir.dt.bfloat16
AF = mybir.ActivationFunctionType
ALU = mybir.AluOpType
AX = mybir.AxisListType

D = 1024
DEPTH = 2
NIN = 2944
NINU = 608
EPS = 1e-6
SEQ = 256
NPS = 4
TP = NPS * SEQ
LS = 2048
PAST = 512
H = 4
QK = 96
GLA_DK = 32
GC = 646
SC = 1025
HC = 192
HY0 = 608
S50 = 1632
GLA0 = 2144
SAFE_SAME_ENGINE = True


class Buf:
    __slots__ = ("w", "r", "x")

    def __init__(self, exclusive=False):
        self.w = None
        self.r = {}
        self.x = exclusive


class Tile:
    def __init__(self, t, n=1):
        self.t = t
        self.b = Buf()
        self.bs = [Buf() for _ in range(n)]

    def __getitem__(self, k):
        return self.t[k]


class KB:
    def __init__(self, nc, es):
        self.nc, self.es = nc, es
        self.E = {"pe": nc.tensor, "act": nc.scalar, "dve": nc.vector, "pool": nc.gpsimd, "sp": nc.sync}
        self.sem = {k: es.enter_context(nc.semaphore("s_" + k)) for k in ("pe", "act", "dve", "pool")}
        self.NDS = 16
        for i in range(self.NDS):
            self.sem["d%d" % i] = es.enter_context(nc.semaphore("s_d%d" % i))
        self.cnt = dict.fromkeys(self.sem, 0)
        self.ndma = 0
        self.ndma_pool = 0
        self.seen = {}
        self.n = 0

    def sb(self, shape, dt, n=1):
        self.n += 1
        if getattr(self, "arena", None) is not None and self.in_arena:
            return self._carve(list(shape), dt)
        return Tile(self.es.enter_context(self.nc.sbuf_tensor("sb%d" % self.n, list(shape), dt)), n)

    def make_arena(self, ncol_f32):
        self.arena = self.es.enter_context(self.nc.sbuf_tensor("arena", [128, ncol_f32], F32))
        self.arena_cols = ncol_f32
        self.in_arena = False
        self.aoff = 0

    def phase(self, name):
        self.in_arena = name is not None
        self.aoff = 0

    def _carve(self, shape, dt):
        nel = 1
        for d_ in shape[1:]:
            nel *= d_
        ncol = nel if dt == F32 else (nel + 1) // 2
        ncol = (ncol + 7) // 8 * 8
        assert self.aoff + ncol <= self.arena_cols, ("arena overflow", self.aoff, ncol, shape)
        ap = self.arena[0:shape[0], self.aoff:self.aoff + ncol]
        self.aoff += ncol
        if dt != F32:
            ap = ap.bitcast(dt)
        ap = ap[:, 0:nel]
        if len(shape) == 3:
            ap = ap.rearrange("p (a b) -> p a b", a=shape[1])
        elif len(shape) == 4:
            ap = ap.rearrange("p (a b c) -> p a b c", a=shape[1], b=shape[2])
        return Tile(ap)

    def barrier(self):
        for eng, E in self.E.items():
            for k, c in self.cnt.items():
                if c and k != eng and self.seen.get((eng, k), 0) < c:
                    E.wait_ge(self.sem[k], c)
                    self.seen[(eng, k)] = c

    def ps(self, shape, dt):
        self.n += 1
        t = Tile(self.es.enter_context(self.nc.psum_tensor("ps%d" % self.n, list(shape), dt)))
        t.b.x = True
        return t

    def op(self, eng, fn, reads=(), writes=(), dma=False):
        deps = {}
        for b in reads:
            if b.w:
                deps[b.w[0]] = max(deps.get(b.w[0], 0), b.w[1])
            if b.x:
                for k, c in b.r.items():
                    if k != eng:
                        deps[k] = max(deps.get(k, 0), c)
        for b in writes:
            if b.w:
                deps[b.w[0]] = max(deps.get(b.w[0], 0), b.w[1])
            for k, c in b.r.items():
                deps[k] = max(deps.get(k, 0), c)
        E = self.E[eng]
        if dma:
            if eng == "pool":
                key = "d%d" % (12 + self.ndma_pool % 4)
                self.ndma_pool += 1
            else:
                key = "d%d" % (self.ndma % 12)
                self.ndma += 1
            inc = 16
            if self.cnt[key] and self.seen.get((eng, key), 0) < self.cnt[key]:
                E.wait_ge(self.sem[key], self.cnt[key])
                self.seen[(eng, key)] = self.cnt[key]
        else:
            key, inc = eng, 1
        for pk, c in deps.items():
            if pk == key and (eng == "pe" or not SAFE_SAME_ENGINE):
                continue
            if self.seen.get((eng, pk), 0) >= c:
                continue
            E.wait_ge(self.sem[pk], c)
            self.seen[(eng, pk)] = c
        inst = fn(E)
        self.cnt[key] += inc
        inst.then_inc(self.sem[key], inc)
        c = self.cnt[key]
        for b in reads:
            b.r[key] = c
        for b in writes:
            b.w = (key, c)
            b.r = {}

    def dma(self, out, in_, reads=(), writes=(), q="sp", **kw):
        self.op(q, lambda e: e.dma_start(out=out, in_=in_, **kw), reads, writes, dma=True)

    def finish(self):
        sp = self.E["sp"]
        for k in self.sem:
            if self.cnt[k]:
                sp.wait_ge(self.sem[k], self.cnt[k])


def _b(x):
    return x.b if isinstance(x, Tile) else x


class _Stop(Exception):
    pass


def build_program(stop=None):
    def chk(name):
        if stop == name:
            raise _Stop(name)

    nc = bass.Bass("TRN2", target_bir_lowering=False)
    es = contextlib.ExitStack()

    def din(name, shape, dt=F32):
        return nc.dram_tensor(name, list(shape), dt, kind="ExternalInput").ap()

    def dout(name, shape):
        return nc.dram_tensor(name, list(shape), F32, kind="ExternalOutput").ap()

    xp = din("xp", [TP, D])
    xs = din("xs", [LS, D])
    cond = din("cond", [2, D])
    cckv = din("cckv", [DEPTH, PAST, 128])
    ckro = din("ckro", [DEPTH, PAST, 32])
    ropec = din("ropec", [LS, 32])
    ropes = din("ropes", [LS, 32])
    norm_w = din("norm_w", [DEPTH, D])
    ada_w = din("ada_w", [DEPTH, D, 3 * D])
    ada_b = din("ada_b", [DEPTH, 3 * D])
    w_in = din("w_in", [DEPTH, D, NIN])
    w_out = din("w_out", [DEPTH, D, D])
    qa_n = din("mla_qa_norm", [DEPTH, 192])
    kva_n = din("mla_kva_norm", [DEPTH, 128])
    w_uq = din("mla_w_uq", [DEPTH, 192, 384])
    w_ukv = din("mla_w_ukv", [DEPTH, 128, 512])
    q_n = din("mla_q_norm", [DEPTH, 96])
    k_n = din("mla_k_norm", [DEPTH, 96])
    gla_gw = din("gla_gw", [DEPTH, 2, 16, 128])
    gla_gb = din("gla_gb", [DEPTH, 2, 128])
    gla_nw = din("gla_norm", [DEPTH, 64])
    sgla = din("sgla", [DEPTH, 2, 128, 64])
    gcon = din("gcon", [128, GC])
    s5_are = din("s5_a_re", [DEPTH, 2, 16, 64])
    s5_aim = din("s5_a_im", [DEPTH, 2, 16, 64])
    s5_ldt = din("s5_log_dt", [DEPTH, 2, 16])
    s5_bre = din("s5_b_re", [DEPTH, 2, 16, 64, 16])
    s5_bim = din("s5_b_im", [DEPTH, 2, 16, 64, 16])
    s5_cre = din("s5_c_re", [DEPTH, 2, 16, 16, 64])
    s5_cim = din("s5_c_im", [DEPTH, 2, 16, 16, 64])
    s5_dd = din("s5_d", [DEPTH, 256])
    s5_gw = din("s5_glu_w", [DEPTH, 256, 256])
    s5_gb = din("s5_glu_b", [DEPTH, 256])
    ss5 = din("ss5", [DEPTH, 2, 1024, 2])
    scon = din("scon", [128, SC])
    qmask = din("qmask", [128, 4])
    ropecq = din("ropecq", [512, 32])
    ropesq = din("ropesq", [512, 32])
    hy_cw = din("hy_conv_w", [DEPTH, 3, 768])
    hy_cb = din("hy_conv_b", [DEPTH, 768])
    hy_w1 = din("hy_w1", [DEPTH, 33, 64])
    hy_b1 = din("hy_b1", [DEPTH, 64])
    hy_f1 = din("hy_freq1", [DEPTH, 64])
    hy_w2 = din("hy_w2", [DEPTH, 64, 64])
    hy_b2 = din("hy_b2", [DEPTH, 64])
    hy_f2 = din("hy_freq2", [DEPTH, 64])
    hy_w3 = din("hy_w3", [DEPTH, 64, 1024])
    hy_bi = din("hy_bias", [DEPTH, 2, 256])
    htab = {}
    for nm_, L_ in (("p", SEQ), ("s", LS)):
        htab[nm_] = dict(cos=din("hcos_" + nm_, [L_, L_], BF16), sin=din("hsin_" + nm_, [L_, L_], BF16),
                         feat=din("hfeat_" + nm_, [33, L_], BF16), win=din("hwin_" + nm_, [L_, 256]))
    hcon = din("hcon", [128, HC])
    haltr = din("haltr", [1, 512], BF16)

    yp = dout("yp", [TP, D])
    ys = dout("ys", [LS, D])
    o_ckv = dout("o_ckv", [NPS, DEPTH, SEQ, 128])
    o_kro = dout("o_kro", [NPS, DEPTH, SEQ, 32])
    o_s5 = dout("o_s5", [NPS, DEPTH, 2, 16, 64, 2])
    o_gla = dout("o_gla", [NPS, DEPTH, 2, 4, 32, 64])
    dbg = dout("dbg", [128, 2048]) if stop else None

    xmid_p = nc.dram_tensor("xmid_p", [TP, D], F32, kind="Internal").ap()
    xmid_s = nc.dram_tensor("xmid_s", [LS, D], F32, kind="Internal").ap()
    xmid_p_b, xmid_s_b = Buf(), Buf()
    s5w = nc.dram_tensor("s5w", [16, 128, 34 * 128], BF16, kind="Internal").ap()
    s5k = nc.dram_tensor("s5k", [2, 128, 16 * 128], BF16, kind="Internal").ap()
    s5w_b, s5k_b = Buf(), Buf()
    wc = {n_: (nc.dram_tensor("wc_" + n_, sh_, BF16, kind="Internal").ap(), Buf()) for n_, sh_ in (
        ("in", [128, 8 * NINU]), ("gla", [128, 8 * 800]), ("s5", [128, 8 * 512]), ("hy0", [128, 8 * 512]), ("hy1", [128, 8 * 512]),
        ("w30", [64, 512]), ("w31", [64, 512]), ("out", [128, 8 * D]))}

    with es:
        kb = KB(nc, es)
        sb, ps, op, dma = kb.sb, kb.ps, kb.op, kb.dma

        ident_f = sb([128, 128], F32)
        ident_b = sb([128, 128], BF16)
        ones_f = sb([1, 128], F32)
        eps_c = sb([128, 1], F32)
        zero_f = sb([128, 512], BF16)
        op("pool", lambda e: e.memset(ident_f[:], 1.0), writes=[ident_f.b])
        op("pool", lambda e: e.affine_select(out=ident_f[:], in_=ident_f[:], pattern=[[-1, 128]],
                                             compare_op=ALU.is_equal, fill=0.0, base=0,
                                             channel_multiplier=1), reads=[ident_f.b], writes=[ident_f.b])
        op("pool", lambda e: e.tensor_copy(out=ident_b[:], in_=ident_f[:]), reads=[ident_f.b], writes=[ident_b.b])
        op("pool", lambda e: e.memset(ones_f[:], 1.0), writes=[ones_f.b])
        op("pool", lambda e: e.memset(eps_c[:], EPS), writes=[eps_c.b])
        op("pool", lambda e: e.memset(zero_f[:], 0.0), writes=[zero_f.b])
        qm = sb([128, 4], F32)
        dma(qm[:, :], qmask[:, :], writes=[qm.b], q="pool")

        pf = [ps([128, 512], F32) for _ in range(6)]
        pb = [ps([128, 1024], BF16) for _ in range(2)]
        rr = {"f": 0, "b": 0}

        def psf():
            rr["f"] = (rr["f"] + 1) % 5
            return pf[rr["f"]]

        p_acc = pf[5]

        def psb():
            rr["b"] = (rr["b"] + 1) % len(pb)
            return pb[rr["b"]]

        qT = sb([128, H, LS], BF16)
        kT = sb([128, H, LS + PAST], BF16)
        vaug = sb([128, (LS + PAST) // 128, H, 66], BF16)
        hT = sb([128, 8, LS], BF16)
        brT = sb([128, 8, LS], BF16)
        w_uq_a = sb([128, 384], BF16)
        w_uq_c = sb([64, 384], BF16)
        w_ukv_b = sb([128, 512], BF16)
        bc_qa = sb([128, 192], F32)
        bc_kva = sb([128, 128], F32)
        bc_qn = sb([128, 96], F32)
        bc_kn = sb([128, 96], F32)
        gate_bc = [sb([128, D], F32) for _ in range(2)]
        effsc = sb([128, 2, 8], F32)
        shift = sb([128, 2, 8], F32)
        rowst = sb([1, D], F32)
        scond = sb([8, 2, 128], F32)
        scondT = sb([128, 2, 8], F32)
        screp = sb([128, 128], F32)
        colst = sb([24, 128], F32)
        colsT = sb([128, 3, 8], F32)
        nwst = sb([8, 128], F32)
        nwT = sb([128, 8], F32)
        modacc = sb([128, 32], F32)

        class _Alias:
            def __init__(self, ap, owner):
                self.ap, self.b = ap, owner.b

            def __getitem__(self, k):
                return self.ap[k]

        wstage = [_Alias(qT.t[:].bitcast(F32).rearrange("p a c -> p (a c)")[:, 0:3 * D], qT)] * 2
        adast = wstage
        w_out_b = _Alias(kT.t[:].rearrange("p h t -> p (h t)")[:, 0:8 * D].rearrange("p (k c) -> p k c", k=8), kT)
        wmix = _Alias(kT.t[:].rearrange("p h t -> p (h t)")[:, 0:6400].rearrange("p (k c) -> p k c", k=8), kT)
        gcon_t = sb([128, GC], F32)
        dma(gcon_t[:, :], gcon[:, :], writes=[gcon_t.b])
        maskF, maskB_ = gcon_t[:, 0:128], gcon_t[:, 128:256]
        headm, halfm, blockm, ones64 = gcon_t[:, 256:260], gcon_t[:, 260:262], gcon_t[:, 262:518], gcon_t[:, 518:582]
        gwp_f = sb([32, 2, 128], F32)
        gwp = sb([32, 2, 128], BF16)
        gbst = sb([2, 128], F32)
        ngb = sb([128, 2], F32)
        bc_gn = sb([128, 64], F32)
        kb.make_arena(10496)
        kb.phase("GLA")
        o_bwd = _Alias(qT.t[:].bitcast(F32).rearrange("p a (b c) -> p (a b) c", c=256), qT)
        g_qf, g_kf, g_sp, g_bcp, g_e1, g_e2, g_e3 = [sb([128, 128], F32) for _ in range(7)]
        g_glr = sb([32, 128], BF16)
        g_vt = sb([128, 256], BF16)
        g_qt, g_kt, g_khT, g_khA, g_khB, g_attm = [sb([128, 128], BF16) for _ in range(6)]
        g_km = [sb([128, 128], BF16) for _ in range(4)]
        g_QA = [sb([128, 128], BF16) for _ in range(4)]
        g_QB = [sb([128, 128], BF16) for _ in range(4)]
        g_nbl = sb([128, 2], F32)
        g_tot = sb([128, 2], F32)
        g_eb = sb([128, 2], F32)
        g_dS = sb([128, 256], F32)
        g_dSr = [sb([128, 64], F32) for _ in range(2)]
        g_S = [sb([128, 64], F32) for _ in range(3)]
        g_Sb = [sb([128, 64], BF16) for _ in range(2)]
        g_os = sb([128, 256], F32)
        g_sq = sb([128, 256], F32)
        g_ss = sb([128, 4], F32)
        g_rs = sb([128, 4], F32)
        g_sil = sb([128, 256], F32)
        g_og = sb([128, 256], BF16)

        kflat = kT.t[:].rearrange("p h t -> p (h t)")
        vflat = vaug.t[:].rearrange("p a h d -> p (a h d)")
        s_wmix = _Alias(kflat[:, 0:4096].rearrange("p (k c) -> p k c", k=8), kT)
        s_uT = _Alias(kflat[:, 4096:8192].rearrange("p (h t) -> p h t", h=2), kT)
        s_Kbd = _Alias(kflat[:, 8192:10240].rearrange("p (a c) -> p a c", a=16), kT)
        s_gT = _Alias(vflat[:, 0:4096].rearrange("p (h t) -> p h t", h=2), vaug)
        s_Kacc = _Alias(qT.t[:].bitcast(F32).rearrange("p a c -> p (a c)")[:, 0:2048].rearrange("p (a c) -> p a c", a=16), qT)
        kb.phase(None)
        scon_t = sb([128, SC], F32)
        dma(scon_t[:, :], scon[:, :], writes=[scon_t.b])
        s_prst = sb([16, 128], F32)
        s_ldst = sb([16, 2], F32)
        s_par = {n_: sb([128, 16], F32) for n_ in ("are", "aim", "dt", "lr", "th", "er", "ei", "t1", "t2", "ar1", "ai1", "nai1",
                                                    "kr", "ki", "nki", "dr", "den")}
        s_pwr = sb([128, 8, 16], F32)
        s_pwi = sb([128, 8, 16], F32)
        s_npwi = sb([128, 8, 16], F32)
        s_dcol = sb([128, 2], F32)
        s_gbcol = sb([128, 2], F32)
        s_st2 = sb([2, 128], F32)
        s_Dd = [sb([128, 128], BF16) for _ in range(2)]
        s_glu = sb([128, 2, 256], BF16)
        kb.phase("S5")
        s_B = [sb([128, 16], F32) for _ in range(2)]
        s_bb = [sb([128, 16], F32) for _ in range(2)]
        s_Xm = [[sb([128, 128], F32) for _ in range(2)] for _ in range(2)]
        s_Xb = [[sb([128, 128], BF16) for _ in range(2)] for _ in range(8)]
        s_W = sb([128, 34, 128], BF16)

        class _View:
            def __init__(self, ap, b):
                self.ap, self.b = ap, b

            def __getitem__(self, k):
                return self.ap[k]

        s_Wb = [[_View(s_W[:, 2 * k_ + r_, :], s_W.b) for r_ in range(2)] for k_ in range(8)]
        s_C = [sb([128, 64], F32) for _ in range(2)]
        s_Cp = sb([128, 128], F32)
        s_Cm = [[sb([128, 128], F32) for _ in range(2)] for _ in range(2)]
        s_Wc = [[_View(s_W[:, 16 + 2 * k_ + r_, :], s_W.b) for r_ in range(2)] for k_ in range(9)]
        s_XA = [sb([128, 256], F32) for _ in range(2)]
        s_XB = [sb([128, 256], F32) for _ in range(2)]
        s_Hs = [sb([128, 4 * 33 + 257], BF16) for _ in range(2)]
        s_h0 = sb([128, 2], F32)
        s_fin = sb([128, 2], F32)
        s_tmp = sb([128, 4], F32)
        s_ya = sb([128, 512], F32)
        s_yb = sb([128, 512], F32)
        kb.phase(None)

        def bcast_row(dst, src_row_ap, n):
            dma(rowst[0:1, 0:n], src_row_ap, writes=[rowst.b])
            for c0 in range(0, n, 512):
                w = min(512, n - c0)
                p = psf()
                op("pe", lambda e: e.matmul(p[:, 0:w], lhsT=ones_f[0:1, :], rhs=rowst[0:1, c0:c0 + w],
                                            start=True, stop=True),
                   reads=[ones_f.b, rowst.b], writes=[p.b])
                op("dve", lambda e: e.tensor_copy(out=dst[:, c0:c0 + w], in_=p[:, 0:w]),
                   reads=[p.b], writes=[dst.b])

        def rstd_from_ss(dst, ss, n):
            op("act", lambda e: e.activation(out=dst[:], in_=ss[:], func=AF.Sqrt, scale=1.0 / n,
                                             bias=eps_c[:, 0:1]), reads=[ss.b, eps_c.b], writes=[dst.b])
            op("dve", lambda e: e.reciprocal(out=dst[:], in_=dst[:]), reads=[dst.b], writes=[dst.b])

        hcon_t = sb([128, HC], F32)
        dma(hcon_t[:, :], hcon[:, :], writes=[hcon_t.b])
        haltr_t = sb([1, 512], BF16)
        dma(haltr_t[:, :], haltr[:, :], writes=[haltr_t.b])
        h_alt = sb([128, 128], BF16)
        op("dve", lambda e: e.tensor_copy(out=h_alt[:], in_=hcon_t[:, 0:128]), reads=[hcon_t.b], writes=[h_alt.b])
        ones_sq = sb([128, 128], F32)
        op("pool", lambda e: e.memset(ones_sq[:], 1.0), writes=[ones_sq.b])
        hw1_b = sb([33, 64], BF16)
        hw2_b = sb([64, 64], BF16)
        h_mrow = sb([4, 64], F32)
        h_mcol = sb([64, 4], F32)
        h_mlp = sb([64, 8], F32)
        h_rows = sb([14, 128], F32)
        h_colp = sb([128, 14], F32)
        h_w3 = sb([64, 4, 128], BF16)
        h_win = sb([128, 128], F32)
        h_feat = sb([33, 512], BF16)
        h_cr = [sb([128, 512], BF16) for _ in range(2)]
        h_sr = [sb([128, 512], BF16) for _ in range(2)]
        h_sig = _Alias(qT.t[:], qT)
        class _Sub(_Alias):
            def __init__(self, ap):
                self.ap, self.b = ap, Buf()

        h_tc = [_Sub(kflat[:, 2048 * i_:2048 * i_ + 2048].rearrange("p (a j) -> p a j", a=16)) for i_ in (0, 1)]
        h_ts = [_Sub(kflat[:, 2048 * i_:2048 * i_ + 2048].rearrange("p (a j) -> p a j", a=16)) for i_ in (2, 3)]
        h_utm = _Sub(kflat[:, 8192:10240].rearrange("p (a j) -> p a j", a=16))
        h_wmix = _Alias(vflat[:, 0:4096].rearrange("p (k c) -> p k c", k=8), vaug)
        kb.phase("HY")
        h_hid = [sb([64, LS], BF16) for _ in range(2)]
        h_ksd = sb([128, 16, 4, 128], BF16)
        h_Y = sb([128, 17, 2, 128], BF16)
        h_k = [sb([128, 128], F32) for _ in range(2)]
        h_t = [sb([128, 128], F32) for _ in range(4)]
        h_rn = sb([128, 2, 128], F32)
        kb.phase(None)
        h_fw = sb([128, 512], F32)
        h_abs = sb([128, 512], F32)
        h_s2 = sb([64, 512], F32)
        h_s4 = sb([64, 512], F32)

        kb.phase("PA")
        w_in_b = sb([128, 8, NINU], BF16)
        xt = sb([128, D], F32)
        xn = sb([128, D], BF16)
        junk = xn
        ss1 = sb([128, 1], F32)
        rs1 = sb([128, 1], F32)
        lat = sb([128, 352], F32)
        junk2 = sb([128, 192], BF16)
        gm = sb([128, 256], F32)
        ssq = sb([128, 2], F32)
        rsq = sb([128, 2], F32)
        cqn = sb([128, 192], BF16)
        ckvn = sb([128, 128], F32)
        ckvn_b = sb([128, 128], BF16)
        kro = sb([128, 32], F32)
        cqT = sb([128, 2, 128], BF16)
        ckvT = sb([128, 128], BF16)
        qpre = sb([128, H, QK], F32)
        kcat = sb([128, H, QK], F32)
        sqh = sb([128, H, QK], F32)
        ssh = sb([128, H], F32)
        rsh = sb([128, H], F32)
        qf = sb([128, H, QK], F32)
        qb = sb([128, H, QK], BF16)
        rtmp = sb([128, H, 32], F32)
        rtmp2 = sb([128, H, 32], F32)
        cosT = sb([128, 32], F32)
        sinT = sb([128, 32], F32)
        pT2 = [sb([128, 512], BF16) for _ in range(2)]
        omla4 = sb([128, 4, H, 64], F32)
        den = sb([128, H], F32)
        omla = sb([128, H, 64], F32)
        omg = sb([128, 256], BF16)
        silg = sb([128, 256], F32)
        ot = sb([128, D], F32)
        kb.phase(None)

        try:
          for l in range(DEPTH):
              st = wstage[0]
              dma(st[:, 0:384], w_uq[l, 0:128, :], writes=[st.b], q="pool")
              op("dve", lambda e: e.tensor_copy(out=w_uq_a[:], in_=st[:, 0:384]), reads=[st.b], writes=[w_uq_a.b])
              st = wstage[1]
              dma(st[0:64, 0:384], w_uq[l, 128:192, :], writes=[st.b], q="pool")
              op("dve", lambda e: e.tensor_copy(out=w_uq_c[:], in_=st[0:64, 0:384]), reads=[st.b], writes=[w_uq_c.b])
              st = wstage[0]
              dma(st[:, 0:512], w_ukv[l, :, :], writes=[st.b], q="pool")
              op("dve", lambda e: e.tensor_copy(out=w_ukv_b[:], in_=st[:, 0:512]), reads=[st.b], writes=[w_ukv_b.b])
              bcast_row(bc_qa, qa_n[l:l + 1, :], 192)
              bcast_row(bc_kva, kva_n[l:l + 1, :], 128)
              bcast_row(bc_qn, q_n[l:l + 1, :], 96)
              bcast_row(bc_kn, k_n[l:l + 1, :], 96)
              bcast_row(bc_gn, gla_nw[l:l + 1, :], 64)
              op("dve", lambda e: e.memset(gwp_f[:], 0.0), writes=[gwp_f.b])
              dma(gwp_f[0:16, 0, :], gla_gw[l, 0, :, :], reads=[gwp_f.b], writes=[gwp_f.b])
              dma(gwp_f[16:32, 1, :], gla_gw[l, 1, :, :], reads=[gwp_f.b], writes=[gwp_f.b])
              op("dve", lambda e: e.tensor_copy(out=gwp[:], in_=gwp_f[:]), reads=[gwp_f.b], writes=[gwp.b])
              dma(gbst[:, :], gla_gb[l, :, :], writes=[gbst.b])
              p = psf()
              op("pe", lambda e: e.transpose(p[:, 0:2], gbst[:, :], ident_f[0:2, 0:2]), reads=[gbst.b, ident_f.b], writes=[p.b])
              op("dve", lambda e: e.tensor_scalar(out=ngb[:], in0=p[:, 0:2], scalar1=-1.0, scalar2=None, op0=ALU.mult),
                 reads=[p.b], writes=[ngb.b])

              P_ = s_par

              def colparam(dst, rows_ap, nrow=16):
                  dma(s_prst[0:nrow, :], rows_ap, writes=[s_prst.b])
                  p_ = psf()
                  op("pe", lambda e: e.transpose(p_[:, 0:nrow], s_prst[0:nrow, :], ident_f[0:nrow, 0:nrow]),
                     reads=[s_prst.b, ident_f.b], writes=[p_.b])
                  op("dve", lambda e: e.tensor_copy(out=dst, in_=p_[:, 0:nrow]), reads=[p_.b], writes=[s_par_b])

              s_par_b = P_["are"].b
              for t_ in P_.values():
                  t_.b = s_par_b
              s_pwr.b = s_pwi.b = s_npwi.b = s_par_b
              colparam(P_["are"][:], s5_are[l].rearrange("d (st gl) p -> (d st) (gl p)", gl=2))
              colparam(P_["aim"][:], s5_aim[l].rearrange("d (st gl) p -> (d st) (gl p)", gl=2))
              dma(s_ldst[:, :], s5_ldt[l].rearrange("d (st gl) -> (d st) gl", gl=2), writes=[s_ldst.b])
              op("dve", lambda e: e.tensor_copy(out=s_prst[:, :].rearrange("r (gl p) -> r gl p", gl=2),
                                                in_=s_ldst[:, :].unsqueeze(2).to_broadcast([16, 2, 64])),
                 reads=[s_ldst.b], writes=[s_prst.b])
              p_ = psf()
              op("pe", lambda e: e.transpose(p_[:, 0:16], s_prst[:, :], ident_f[0:16, 0:16]), reads=[s_prst.b, ident_f.b], writes=[p_.b])
              op("act", lambda e: e.activation(out=P_["dt"][:], in_=p_[:, 0:16], func=AF.Exp), reads=[p_.b], writes=[s_par_b])

              def pp(eng, fn):
                  op(eng, fn, reads=[s_par_b, scon_t.b], writes=[s_par_b])

              are, aim, dt_, lr, th, er, ei, t1, t2 = (P_[n_] for n_ in ("are", "aim", "dt", "lr", "th", "er", "ei", "t1", "t2"))
              pp("dve", lambda e: e.tensor_scalar(out=are[:], in0=are[:], scalar1=-1e-4, scalar2=None, op0=ALU.min))
              pp("dve", lambda e: e.tensor_tensor(out=lr[:], in0=are[:], in1=dt_[:], op=ALU.mult))
              pp("dve", lambda e: e.tensor_tensor(out=th[:], in0=aim[:], in1=dt_[:], op=ALU.mult))
              pp("act", lambda e: e.activation(out=t1[:], in_=lr[:], func=AF.Exp, scale=1.0 / 16))
              pp("act", lambda e: e.activation(out=er[:], in_=th[:], func=AF.Sin, scale=1.0 / 16, bias=scon_t[:, 1024:1025]))
              pp("act", lambda e: e.activation(out=ei[:], in_=th[:], func=AF.Sin, scale=1.0 / 16))
              pp("dve", lambda e: e.tensor_tensor(out=er[:], in0=er[:], in1=t1[:], op=ALU.mult))
              pp("dve", lambda e: e.tensor_tensor(out=ei[:], in0=ei[:], in1=t1[:], op=ALU.mult))

              def csq():
                  pp("dve", lambda e: e.tensor_tensor(out=t1[:], in0=er[:], in1=er[:], op=ALU.mult))
                  pp("dve", lambda e: e.tensor_tensor(out=t2[:], in0=ei[:], in1=ei[:], op=ALU.mult))
                  pp("dve", lambda e: e.scalar_tensor_tensor(out=ei[:], in0=er[:], scalar=2.0, in1=ei[:], op0=ALU.mult, op1=ALU.mult))
                  pp("dve", lambda e: e.tensor_tensor(out=er[:], in0=t1[:], in1=t2[:], op=ALU.subtract))

              for _ in range(4):
                  csq()
              ar1, ai1, nai1, kr, ki, nki, dr, dn_ = (P_[n_] for n_ in ("ar1", "ai1", "nai1", "kr", "ki", "nki", "dr", "den"))
              pp("dve", lambda e: e.tensor_copy(out=ar1[:], in_=er[:]))
              pp("dve", lambda e: e.tensor_copy(out=ai1[:], in_=ei[:]))
              pp("dve", lambda e: e.tensor_scalar(out=nai1[:], in0=ei[:], scalar1=-1.0, scalar2=None, op0=ALU.mult))
              pp("dve", lambda e: e.tensor_scalar(out=dr[:], in0=ar1[:], scalar1=-1.0, scalar2=None, op0=ALU.add))
              pp("dve", lambda e: e.tensor_tensor(out=t1[:], in0=are[:], in1=are[:], op=ALU.mult))
              pp("dve", lambda e: e.tensor_tensor(out=t2[:], in0=aim[:], in1=aim[:], op=ALU.mult))
              pp("dve", lambda e: e.tensor_tensor(out=dn_[:], in0=t1[:], in1=t2[:], op=ALU.add))
              pp("dve", lambda e: e.reciprocal(out=dn_[:], in_=dn_[:]))
              pp("dve", lambda e: e.tensor_tensor(out=t1[:], in0=dr[:], in1=are[:], op=ALU.mult))
              pp("dve", lambda e: e.tensor_tensor(out=t2[:], in0=ai1[:], in1=aim[:], op=ALU.mult))
              pp("dve", lambda e: e.tensor_tensor(out=t1[:], in0=t1[:], in1=t2[:], op=ALU.add))
              pp("dve", lambda e: e.tensor_tensor(out=kr[:], in0=t1[:], in1=dn_[:], op=ALU.mult))
              pp("dve", lambda e: e.tensor_tensor(out=t1[:], in0=ai1[:], in1=are[:], op=ALU.mult))
              pp("dve", lambda e: e.tensor_tensor(out=t2[:], in0=dr[:], in1=aim[:], op=ALU.mult))
              pp("dve", lambda e: e.tensor_tensor(out=t1[:], in0=t1[:], in1=t2[:], op=ALU.subtract))
              pp("dve", lambda e: e.tensor_tensor(out=ki[:], in0=t1[:], in1=dn_[:], op=ALU.mult))
              pp("dve", lambda e: e.tensor_scalar(out=nki[:], in0=ki[:], scalar1=-1.0, scalar2=None, op0=ALU.mult))
              for _ in range(3):
                  csq()
              for k in range(8):
                  pp("dve", lambda e: e.tensor_copy(out=s_pwr[:, k, :], in_=er[:]))
                  pp("dve", lambda e: e.tensor_copy(out=s_pwi[:, k, :], in_=ei[:]))
                  pp("dve", lambda e: e.tensor_scalar(out=s_npwi[:, k, :], in0=ei[:], scalar1=-1.0, scalar2=None, op0=ALU.mult))
                  if k < 7:
                      csq()
              for dst_, src_ in ((s_dcol, s5_dd), (s_gbcol, s5_gb)):
                  dma(s_st2[:, :], src_[l].rearrange("(h p) -> h p", p=128), writes=[s_st2.b])
                  p_ = psf()
                  op("pe", lambda e: e.transpose(p_[:, 0:2], s_st2[:, :], ident_f[0:2, 0:2]), reads=[s_st2.b, ident_f.b], writes=[p_.b])
                  op("dve", lambda e: e.tensor_copy(out=dst_[:], in_=p_[:, 0:2]), reads=[p_.b], writes=[dst_.b])
              for hf in range(2):
                  op("dve", lambda e: e.tensor_scalar(out=s_Dd[hf][:], in0=ident_f[:], scalar1=s_dcol[:, hf:hf + 1], scalar2=None, op0=ALU.mult),
                     reads=[ident_f.b, s_dcol.b], writes=[s_Dd[hf].b])
                  st = wstage[0]
                  dma(st[:, 0:256], s5_gw[l, hf * 128:(hf + 1) * 128, :], writes=[st.b], q="pool")
                  op("dve", lambda e: e.tensor_copy(out=s_glu[:, hf, :], in_=st[:, 0:256]), reads=[st.b], writes=[s_glu.b])
              for r_, src_ in enumerate((hy_f1, hy_b1, hy_f2, hy_b2)):
                  dma(h_mrow[r_:r_ + 1, :], src_[l:l + 1, :], reads=[h_mrow.b], writes=[h_mrow.b])
              p_ = psf()
              op("pe", lambda e: e.transpose(p_[0:64, 0:4], h_mrow[:, :], ident_f[0:4, 0:4]), reads=[h_mrow.b, ident_f.b], writes=[p_.b])
              op("dve", lambda e: e.tensor_copy(out=h_mcol[:], in_=p_[0:64, 0:4]), reads=[p_.b], writes=[h_mcol.b])
              for li in range(2):
                  fcol, bcol = h_mcol[:, 2 * li:2 * li + 1], h_mcol[:, 2 * li + 1:2 * li + 2]
                  o4 = 4 * li
                  op("dve", lambda e: e.tensor_scalar(out=h_mlp[:, o4:o4 + 1], in0=fcol, scalar1=0.5, scalar2=None, op0=ALU.mult),
                     reads=[h_mcol.b, h_mlp.b], writes=[h_mlp.b])
                  op("dve", lambda e: e.scalar_tensor_tensor(out=h_mlp[:, o4 + 1:o4 + 2], in0=fcol, scalar=0.5, in1=bcol, op0=ALU.mult, op1=ALU.mult),
                     reads=[h_mcol.b, h_mlp.b], writes=[h_mlp.b])
                  op("dve", lambda e: e.tensor_scalar(out=h_mlp[:, o4 + 2:o4 + 3], in0=fcol, scalar1=0.25, scalar2=None, op0=ALU.mult),
                     reads=[h_mcol.b, h_mlp.b], writes=[h_mlp.b])
                  op("dve", lambda e: e.scalar_tensor_tensor(out=h_mlp[:, o4 + 3:o4 + 4], in0=fcol, scalar=0.25, in1=bcol, op0=ALU.mult, op1=ALU.mult),
                     reads=[h_mcol.b, h_mlp.b], writes=[h_mlp.b])
              st = wstage[0]
              dma(st[0:33, 0:64], hy_w1[l, :, :], writes=[st.b], q="pool")
              op("dve", lambda e: e.tensor_copy(out=hw1_b[:], in_=st[0:33, 0:64]), reads=[st.b], writes=[hw1_b.b])
              dma(st[0:64, 0:64], hy_w2[l, :, :], writes=[st.b], q="pool")
              op("dve", lambda e: e.tensor_copy(out=hw2_b[:], in_=st[0:64, 0:64]), reads=[st.b], writes=[hw2_b.b])
              chk('weights')
              if l == 0:
                  for c in range(2):
                      dma(scond[:, c, :], cond[c].rearrange("(k p) -> k p", p=128), writes=[scond.b])
                  op("act", lambda e: e.activation(out=scond[:], in_=scond[:], func=AF.Silu),
                     reads=[scond.b], writes=[scond.b])
                  for c in range(2):
                      p = psf()
                      op("pe", lambda e: e.transpose(p[:, 0:8], scond[:, c, :], ident_f[0:8, 0:8]),
                         reads=[scond.b, ident_f.b], writes=[p.b])
                      op("dve", lambda e: e.tensor_copy(out=scondT[:, c, :], in_=p[:, 0:8]),
                         reads=[p.b], writes=[scondT.b])
              dma(colst[:, :], ada_b[l].rearrange("(j p) -> j p", p=128), writes=[colst.b])
              p = psf()
              op("pe", lambda e: e.transpose(p[:, 0:24], colst[:, :], ident_f[0:24, 0:24]),
                 reads=[colst.b, ident_f.b], writes=[p.b])
              op("dve", lambda e: e.tensor_copy(out=colsT[:].rearrange("p a k -> p (a k)"), in_=p[:, 0:24]),
                 reads=[p.b], writes=[colsT.b])
              dma(nwst[:, :], norm_w[l].rearrange("(k p) -> k p", p=128), writes=[nwst.b])
              p = psf()
              op("pe", lambda e: e.transpose(p[:, 0:8], nwst[:, :], ident_f[0:8, 0:8]),
                 reads=[nwst.b, ident_f.b], writes=[p.b])
              op("dve", lambda e: e.tensor_copy(out=nwT[:], in_=p[:, 0:8]), reads=[p.b], writes=[nwT.b])

              pmod = modacc
              op("dve", lambda e: e.memset(modacc[:], 0.0), writes=[modacc.b])
              pk = psf()
              pg = [psf() for _ in range(4)]
              for k in range(8):
                  a = adast[k % 2]
                  dma(a[:, :], ada_w[l, k * 128:(k + 1) * 128, :], writes=[a.b])
                  for j in range(16):
                      op("pe", lambda e: e.matmul(pk[:, 2 * j:2 * j + 2], lhsT=a[:, j * 128:(j + 1) * 128],
                                                  rhs=scondT[:, :, k], start=True, stop=True),
                         reads=[a.b, scondT.b], writes=[pk.b])
                  op("dve", lambda e: e.tensor_tensor(out=modacc[:, 0:32], in0=modacc[:, 0:32], in1=pk[:, 0:32], op=ALU.add),
                     reads=[pk.b, modacc.b], writes=[modacc.b])
                  for c in range(2):
                      op("dve", lambda e: e.tensor_copy(out=screp[:], in_=scondT[:, c, k:k + 1].to_broadcast([128, 128])),
                         reads=[scondT.b], writes=[screp.b])
                      for hf in range(2):
                          pgt = pg[2 * c + hf]
                          op("pe", lambda e: e.matmul(pgt[:, :], lhsT=screp[:, :],
                                                      rhs=a[:, 2048 + hf * 512:2048 + (hf + 1) * 512],
                                                      start=(k == 0), stop=False),
                             reads=[a.b, screp.b], writes=[pgt.b])
              dma(rowst[0:1, 0:D], ada_b[l:l + 1, 2048:3072], writes=[rowst.b])
              for c in range(2):
                  for hf in range(2):
                      pgt = pg[2 * c + hf]
                      op("pe", lambda e: e.matmul(pgt[:, :], lhsT=ones_f[0:1, :], rhs=rowst[0:1, hf * 512:(hf + 1) * 512],
                                                  start=False, stop=True),
                         reads=[ones_f.b, rowst.b], writes=[pgt.b])
                      op("dve", lambda e: e.tensor_copy(out=gate_bc[c][:, hf * 512:(hf + 1) * 512], in_=pgt[:, :]),
                         reads=[pgt.b], writes=[gate_bc[c].b])
              pm = pmod[:, 0:32].rearrange("p (j c) -> p c j", c=2)
              for c in range(2):
                  op("dve", lambda e: e.tensor_tensor(out=shift[:, c, :], in0=pm[:, c, 0:8], in1=colsT[:, 0, :], op=ALU.add),
                     reads=[pmod.b, colsT.b], writes=[shift.b])
                  op("dve", lambda e: e.tensor_tensor(out=effsc[:, c, :], in0=pm[:, c, 8:16], in1=colsT[:, 1, :], op=ALU.add),
                     reads=[pmod.b, colsT.b], writes=[effsc.b])
                  op("dve", lambda e: e.scalar_tensor_tensor(out=effsc[:, c, :], in0=effsc[:, c, :], scalar=1.0,
                                                             in1=nwT[:, :], op0=ALU.add, op1=ALU.mult),
                     reads=[effsc.b, nwT.b], writes=[effsc.b])

              chk('ada')
              def gla_group(l, g):
                  c, T, L, sample = g["c"], g["T"], g["L"], g["sample"]
                  for t_ in g_QA + g_QB:
                      op("pool", lambda e: e.memset(t_[:], 0.0), writes=[t_.b])
                  tps = L // 128
                  nseq = T // L
                  if sample:
                      dma(wmix[:, :, :].rearrange("p k c -> p (k c)"), wc["gla"][0][:, :], reads=[wc["gla"][1]], writes=[wmix.b], q="pool")
                  else:
                      for k in range(8):
                          st = wstage[0]
                          dma(st[:, 0:800], w_in[l, k * 128:(k + 1) * 128, GLA0:GLA0 + 800], writes=[st.b], q="pool")
                          op("dve" if k % 2 == 0 else "pool", lambda e: e.tensor_copy(out=wmix[:, k, :], in_=st[:, 0:800]),
                             reads=[st.b], writes=[wmix.b])
                      dma(wc["gla"][0][:, :], wmix[:, :, :].rearrange("p k c -> p (k c)"), reads=[wmix.b], writes=[wc["gla"][1]])
                  for d in (1, 0):
                      mask_d = maskF if d == 0 else maskB_
                      for s_ in range(nseq):
                          S0, S1, S2 = g_S
                          if sample:
                              dma(S0[:, :], sgla[l, d, :, :], writes=[S0.b])
                          else:
                              op("dve", lambda e: e.memset(S0[:], 0.0), writes=[S0.b])
                          order = range(tps) if d == 0 else range(tps - 1, -1, -1)
                          for tt in order:
                              ti = s_ * tps + tt
                              r0 = ti * 128
                              pq_, pk_, pg_, pv_ = psf(), psf(), psf(), psf()
                              for k in range(8):
                                  op("pe", lambda e: e.matmul(pq_[:, 0:128], lhsT=wmix[:, k, 0:128], rhs=hT[:, k, r0:r0 + 128],
                                                              start=(k == 0), stop=(k == 7)), reads=[wmix.b, hT.b], writes=[pq_.b])
                              for k in range(8):
                                  op("pe", lambda e: e.matmul(pk_[:, 0:128], lhsT=wmix[:, k, 128:256], rhs=hT[:, k, r0:r0 + 128],
                                                              start=(k == 0), stop=(k == 7)), reads=[wmix.b, hT.b], writes=[pk_.b])
                              for k in range(8):
                                  op("pe", lambda e: e.matmul(pg_[0:32, 0:128], lhsT=wmix[:, k, 512:544], rhs=hT[:, k, r0:r0 + 128],
                                                              start=(k == 0), stop=(k == 7)), reads=[wmix.b, hT.b], writes=[pg_.b])
                              for k in range(8):
                                  op("pe", lambda e: e.matmul(pv_[:, 0:256], lhsT=hT[:, k, r0:r0 + 128], rhs=wmix[:, k, 256:512],
                                                              start=(k == 0), stop=(k == 7)), reads=[wmix.b, hT.b], writes=[pv_.b])
                              op("act", lambda e: e.activation(out=g_qf[:], in_=pq_[:, 0:128], func=AF.Copy), reads=[pq_.b], writes=[g_qf.b])
                              op("act", lambda e: e.activation(out=g_kf[:], in_=pk_[:, 0:128], func=AF.Copy), reads=[pk_.b], writes=[g_kf.b])
                              op("act", lambda e: e.activation(out=g_glr[:], in_=pg_[0:32, 0:128], func=AF.Copy), reads=[pg_.b], writes=[g_glr.b])
                              op("dve", lambda e: e.tensor_copy(out=g_vt[:], in_=pv_[:, 0:256]), reads=[pv_.b], writes=[g_vt.b])
                              pl_ = psf()
                              op("pe", lambda e: e.matmul(pl_[:, 0:128], lhsT=gwp[:, d, :], rhs=g_glr[:, :], start=True, stop=True),
                                 reads=[gwp.b, g_glr.b], writes=[pl_.b])
                              op("act", lambda e: e.activation(out=g_sp[:], in_=pl_[:, 0:128], func=AF.Exp, scale=-1.0, bias=ngb[:, d:d + 1]),
                                 reads=[pl_.b, ngb.b], writes=[g_sp.b])
                              op("act", lambda e: e.activation(out=g_sp[:], in_=g_sp[:], func=AF.Ln, bias=ones64[:, 0:1]),
                                 reads=[g_sp.b, gcon_t.b], writes=[g_sp.b])
                              for ch in range(2):
                                  c0 = 64 * ch
                                  op("dve", lambda e: e.tensor_tensor_scan(out=g_bcp[:, c0:c0 + 64], data0=ones64, data1=g_sp[:, c0:c0 + 64],
                                                                           initial=0.0, op0=ALU.mult, op1=ALU.add),
                                     reads=[g_sp.b, gcon_t.b, g_bcp.b], writes=[g_bcp.b])
                              if d == 1:
                                  for ch in range(2):
                                      c0 = 64 * ch
                                      op("dve", lambda e: e.tensor_copy(out=g_tot[:, ch:ch + 1], in_=g_bcp[:, c0 + 63:c0 + 64]),
                                         reads=[g_bcp.b, g_tot.b], writes=[g_tot.b])
                                  op("dve", lambda e: e.tensor_tensor(out=g_bcp[:], in0=g_sp[:], in1=g_bcp[:], op=ALU.subtract),
                                     reads=[g_sp.b, g_bcp.b], writes=[g_bcp.b])
                                  for ch in range(2):
                                      c0 = 64 * ch
                                      op("dve", lambda e: e.tensor_scalar(out=g_bcp[:, c0:c0 + 64], in0=g_bcp[:, c0:c0 + 64],
                                                                          scalar1=g_tot[:, ch:ch + 1], scalar2=None, op0=ALU.add),
                                         reads=[g_bcp.b, g_tot.b], writes=[g_bcp.b])
                              for ch in range(2):
                                  col = 64 * ch + (63 if d == 0 else 0)
                                  op("dve", lambda e: e.tensor_scalar(out=g_nbl[:, ch:ch + 1], in0=g_bcp[:, col:col + 1], scalar1=-1.0 / 16,
                                                                      scalar2=None, op0=ALU.mult), reads=[g_bcp.b, g_nbl.b], writes=[g_nbl.b])
                              op("act", lambda e: e.activation(out=g_eb[:], in_=g_nbl[:], func=AF.Exp), reads=[g_nbl.b], writes=[g_eb.b])
                              op("act", lambda e: e.activation(out=g_e1[:], in_=g_bcp[:], func=AF.Exp, scale=-1.0 / 16), reads=[g_bcp.b], writes=[g_e1.b])
                              op("act", lambda e: e.activation(out=g_e2[:], in_=g_bcp[:], func=AF.Exp, scale=1.0 / 16), reads=[g_bcp.b], writes=[g_e2.b])
                              for ch in range(2):
                                  c0 = 64 * ch
                                  op("act", lambda e: e.activation(out=g_e3[:, c0:c0 + 64], in_=g_bcp[:, c0:c0 + 64], func=AF.Exp, scale=1.0 / 16,
                                                                   bias=g_nbl[:, ch:ch + 1]), reads=[g_bcp.b, g_nbl.b, g_e3.b], writes=[g_e3.b])
                              op("dve", lambda e: e.scalar_tensor_tensor(out=g_qt[:], in0=g_qf[:], scalar=GLA_DK ** -0.5, in1=g_e1[:],
                                                                         op0=ALU.mult, op1=ALU.mult), reads=[g_qf.b, g_e1.b], writes=[g_qt.b])
                              op("dve", lambda e: e.tensor_tensor(out=g_kt[:], in0=g_kf[:], in1=g_e2[:], op=ALU.mult),
                                 reads=[g_kf.b, g_e2.b], writes=[g_kt.b])
                              op("dve", lambda e: e.tensor_tensor(out=g_khT[:], in0=g_kf[:], in1=g_e3[:], op=ALU.mult),
                                 reads=[g_kf.b, g_e3.b], writes=[g_khT.b])
                              pt_ = psb()
                              op("pe", lambda e: e.transpose(pt_[:, 0:128], g_khT[:, :], ident_b[:, :]), reads=[g_khT.b, ident_b.b], writes=[pt_.b])
                              op("dve", lambda e: e.tensor_scalar(out=g_khA[:], in0=pt_[:, 0:128], scalar1=halfm[:, 0:1], scalar2=None, op0=ALU.mult),
                                 reads=[pt_.b, gcon_t.b], writes=[g_khA.b])
                              op("dve", lambda e: e.tensor_scalar(out=g_khB[:], in0=pt_[:, 0:128], scalar1=halfm[:, 1:2], scalar2=None, op0=ALU.mult),
                                 reads=[pt_.b, gcon_t.b], writes=[g_khB.b])
                              for h in range(4):
                                  op("pool", lambda e: e.tensor_scalar(out=g_km[h][:], in0=g_kt[:], scalar1=headm[:, h:h + 1], scalar2=None, op0=ALU.mult),
                                     reads=[g_kt.b, gcon_t.b], writes=[g_km[h].b])
                                  op("pool", lambda e: e.tensor_scalar(out=g_QA[h][:, 0:64], in0=g_qt[:, 0:64], scalar1=headm[:, h:h + 1], scalar2=None,
                                                                       op0=ALU.mult), reads=[g_qt.b, gcon_t.b], writes=[g_QA[h].b])
                                  op("pool", lambda e: e.tensor_scalar(out=g_QB[h][:, 64:128], in0=g_qt[:, 64:128], scalar1=headm[:, h:h + 1], scalar2=None,
                                                                       op0=ALU.mult), reads=[g_qt.b, gcon_t.b], writes=[g_QB[h].b])
                              for ch, kh in ((0, g_khA), (1, g_khB)):
                                  pd_ = psf()
                                  op("pe", lambda e: e.matmul(pd_[:, 0:256], lhsT=kh[:, :], rhs=g_vt[:, :], start=True, stop=True),
                                     reads=[kh.b, g_vt.b], writes=[pd_.b])
                                  op("dve", lambda e: e.tensor_tensor(out=g_dS[:], in0=pd_[:, 0:256], in1=blockm, op=ALU.mult),
                                     reads=[pd_.b, gcon_t.b], writes=[g_dS.b])
                                  op("dve", lambda e: e.tensor_reduce(out=g_dSr[ch][:], in_=g_dS[:].rearrange("p (h v) -> p v h", h=4),
                                                                      axis=AX.X, op=ALU.add), reads=[g_dS.b], writes=[g_dSr[ch].b])
                              first, second = (0, 1) if d == 0 else (1, 0)
                              op("dve", lambda e: e.tensor_copy(out=g_Sb[0][:], in_=S0[:]), reads=[S0.b], writes=[g_Sb[0].b])
                              op("dve", lambda e: e.scalar_tensor_tensor(out=S1[:], in0=S0[:], scalar=g_eb[:, first:first + 1], in1=g_dSr[first][:],
                                                                         op0=ALU.mult, op1=ALU.add), reads=[S0.b, g_eb.b, g_dSr[first].b], writes=[S1.b])
                              op("dve", lambda e: e.tensor_copy(out=g_Sb[1][:], in_=S1[:]), reads=[S1.b], writes=[g_Sb[1].b])
                              op("dve", lambda e: e.scalar_tensor_tensor(out=S2[:], in0=S1[:], scalar=g_eb[:, second:second + 1], in1=g_dSr[second][:],
                                                                         op0=ALU.mult, op1=ALU.add), reads=[S1.b, g_eb.b, g_dSr[second].b], writes=[S2.b])
                              SbA, SbB = (g_Sb[0], g_Sb[1]) if d == 0 else (g_Sb[1], g_Sb[0])
                              po = p_acc
                              for h in range(4):
                                  pa_ = psf()
                                  op("pe", lambda e: e.matmul(pa_[:, 0:128], lhsT=g_km[h][:, :], rhs=g_qt[:, :], start=True, stop=True),
                                     reads=[g_km[h].b, g_qt.b], writes=[pa_.b])
                                  op("dve", lambda e: e.tensor_tensor(out=g_attm[:], in0=pa_[:, 0:128], in1=mask_d, op=ALU.mult),
                                     reads=[pa_.b, gcon_t.b], writes=[g_attm.b])
                                  op("pe", lambda e: e.matmul(po[:, 64 * h:64 * h + 64], lhsT=g_attm[:, :], rhs=g_vt[:, 64 * h:64 * h + 64],
                                                              start=True, stop=False), reads=[g_attm.b, g_vt.b], writes=[po.b])
                                  op("pe", lambda e: e.matmul(po[:, 64 * h:64 * h + 64], lhsT=g_QA[h][:, :], rhs=SbA[:, :], start=False, stop=False),
                                     reads=[g_QA[h].b, SbA.b], writes=[po.b])
                                  op("pe", lambda e: e.matmul(po[:, 64 * h:64 * h + 64], lhsT=g_QB[h][:, :], rhs=SbB[:, :], start=False, stop=True),
                                     reads=[g_QB[h].b, SbB.b], writes=[po.b])
                              if d == 1:
                                  op("act", lambda e: e.activation(out=o_bwd[:, ti, :], in_=po[:, 0:256], func=AF.Copy), reads=[po.b], writes=[o_bwd.b])
                              else:
                                  op("dve", lambda e: e.tensor_tensor(out=g_os[:], in0=po[:, 0:256], in1=o_bwd[:, ti, :], op=ALU.add),
                                     reads=[po.b, o_bwd.b], writes=[g_os.b])
                                  op("act", lambda e: e.activation(out=g_sq[:], in_=g_os[:], func=AF.Square), reads=[g_os.b], writes=[g_sq.b])
                                  op("dve", lambda e: e.tensor_reduce(out=g_ss[:], in_=g_sq[:].rearrange("p (h v) -> p h v", h=4), axis=AX.X, op=ALU.add),
                                     reads=[g_sq.b], writes=[g_ss.b])
                                  rstd_from_ss(g_rs, g_ss, 64)
                                  os3 = g_os[:].rearrange("p (h v) -> p h v", h=4)
                                  op("dve", lambda e: e.tensor_tensor(out=os3, in0=os3, in1=g_rs[:, :].unsqueeze(2).to_broadcast([128, 4, 64]), op=ALU.mult),
                                     reads=[g_os.b, g_rs.b], writes=[g_os.b])
                                  op("dve", lambda e: e.tensor_tensor(out=os3, in0=os3, in1=bc_gn[:, :].unsqueeze(1).to_broadcast([128, 4, 64]), op=ALU.mult),
                                     reads=[g_os.b, bc_gn.b], writes=[g_os.b])
                                  pgg = psf()
                                  for k in range(8):
                                      op("pe", lambda e: e.matmul(pgg[:, 0:256], lhsT=hT[:, k, r0:r0 + 128], rhs=wmix[:, k, 544:800],
                                                                  start=(k == 0), stop=(k == 7)), reads=[hT.b, wmix.b], writes=[pgg.b])
                                  op("act", lambda e: e.activation(out=g_sil[:], in_=pgg[:, 0:256], func=AF.Silu), reads=[pgg.b], writes=[g_sil.b])
                                  op("dve", lambda e: e.tensor_tensor(out=g_og[:], in0=g_os[:], in1=g_sil[:], op=ALU.mult),
                                     reads=[g_os.b, g_sil.b], writes=[g_og.b])
                                  pbr_ = psb()
                                  for k2 in range(2):
                                      op("pe", lambda e: e.transpose(pbr_[:, k2 * 128:(k2 + 1) * 128], g_og[:, k2 * 128:(k2 + 1) * 128], ident_b[:, :]),
                                         reads=[g_og.b, ident_b.b], writes=[pbr_.b])
                                  op("act", lambda e: e.activation(out=brT[:, 6:8, r0:r0 + 128], in_=pbr_[:, 0:256].rearrange("p (k t) -> p k t", k=2),
                                                                   func=AF.Copy), reads=[pbr_.b], writes=[brT.b])
                              S0, S1, S2 = S2, S0, S1
                          if not sample:
                              dma(o_gla[s_, l, d].rearrange("h k v -> (h k) v"), S0[:, :], reads=[S0.b])

              def s5_group(l, g):
                  c, T, L, sample = g["c"], g["T"], g["L"], g["sample"]
                  nseq, Mseq, M, nb = T // L, L // 8, T // 8, T // 512
                  nsteps = Mseq.bit_length() - 1
                  reuse = sample
                  P_ = s_par
                  pb_ = s_par["are"].b
                  ybank = pf[1:1 + nb]
                  ptmp = pf[0]
                  if sample:
                      dma(s_wmix[:, :, :].rearrange("p k c -> p (k c)"), wc["s5"][0][:, :], reads=[wc["s5"][1]], writes=[s_wmix.b], q="pool")
                  else:
                      for k in range(8):
                          st_ = wstage[0]
                          dma(st_[:, 0:512], w_in[l, k * 128:(k + 1) * 128, S50:S50 + 512], writes=[st_.b], q="pool")
                          op("dve" if k % 2 == 0 else "pool", lambda e: e.tensor_copy(out=s_wmix[:, k, :], in_=st_[:, 0:512]),
                             reads=[st_.b], writes=[s_wmix.b])
                      dma(wc["s5"][0][:, :], s_wmix[:, :, :].rearrange("p k c -> p (k c)"), reads=[s_wmix.b], writes=[wc["s5"][1]])
                  for hf in range(2):
                      for bk in range(nb):
                          for k in range(8):
                              op("pe", lambda e: e.matmul(ptmp[:, :], lhsT=s_wmix[:, k, hf * 128:(hf + 1) * 128], rhs=hT[:, k, bk * 512:(bk + 1) * 512],
                                                          start=(k == 0), stop=(k == 7)), reads=[s_wmix.b, hT.b], writes=[ptmp.b])
                          op("act", lambda e: e.activation(out=s_uT[:, hf, bk * 512:(bk + 1) * 512], in_=ptmp[:, :], func=AF.Copy),
                             reads=[ptmp.b], writes=[s_uT.b])
                  for hf in range(2):
                      for bk in range(nb):
                          op("pe", lambda e: e.matmul(ybank[bk][:, :], lhsT=s_Dd[hf][:, :], rhs=s_uT[:, hf, bk * 512:(bk + 1) * 512],
                                                      start=True, stop=False), reads=[s_Dd[hf].b, s_uT.b], writes=[ybank[bk].b])
                      if not reuse:
                          op("dve", lambda e: e.memset(s_Kacc[:], 0.0), writes=[s_Kacc.b])
                      for d in range(2):
                          for ri, src_ in ((0, s5_cre), (1, s5_cim)):
                              if not reuse:
                                  dma(s_C[ri][:, :], src_[l, d, 8 * hf:8 * hf + 8].rearrange("g i p -> (g i) p"), writes=[s_C[ri].b], q="pool")
                          for k4 in range(4):
                              st = 4 * hf + k4
                              td = d * 8 + st
                              ar, ai, nai = P_["ar1"][:, td:td + 1], P_["ai1"][:, td:td + 1], P_["nai1"][:, td:td + 1]
                              kr_, ki_, nki_ = P_["kr"][:, td:td + 1], P_["ki"][:, td:td + 1], P_["nki"][:, td:td + 1]
                              if reuse:
                                  dma(s_W[:, :, :].rearrange("p a c -> p (a c)"), s5w[td], reads=[s5w_b], writes=[s_W.b], q="pool")
                              else:
                                  for ri, src_ in ((0, s5_bre), (1, s5_bim)):
                                      dma(s_B[ri][:, :], src_[l, d, 2 * st:2 * st + 2].rearrange("g p i -> (g p) i"), writes=[s_B[ri].b], q="pool")
                                  op("dve", lambda e: e.tensor_scalar(out=s_bb[0][:], in0=s_B[0][:], scalar1=kr_, scalar2=None, op0=ALU.mult),
                                     reads=[s_B[0].b, pb_], writes=[s_bb[0].b])
                                  op("dve", lambda e: e.tensor_scalar(out=s_bb[1][:], in0=s_B[0][:], scalar1=ki_, scalar2=None, op0=ALU.mult),
                                     reads=[s_B[0].b, pb_], writes=[s_bb[1].b])
                                  op("dve", lambda e: e.scalar_tensor_tensor(out=s_bb[0][:], in0=s_B[1][:], scalar=nki_, in1=s_bb[0][:], op0=ALU.mult, op1=ALU.add),
                                     reads=[s_B[1].b, pb_, s_bb[0].b], writes=[s_bb[0].b])
                                  op("dve", lambda e: e.scalar_tensor_tensor(out=s_bb[1][:], in0=s_B[1][:], scalar=kr_, in1=s_bb[1][:], op0=ALU.mult, op1=ALU.add),
                                     reads=[s_B[1].b, pb_, s_bb[1].b], writes=[s_bb[1].b])
                                  mB3 = scon_t[:, 128 * k4:128 * k4 + 128].rearrange("p (g i) -> p g i", g=8)
                                  for ri in range(2):
                                      op("dve", lambda e: e.tensor_tensor(out=s_Xm[0][ri][:].rearrange("p (g i) -> p g i", g=8), in0=mB3,
                                                                          in1=s_bb[ri][:, :].unsqueeze(1).to_broadcast([128, 8, 16]), op=ALU.mult),
                                         reads=[scon_t.b, s_bb[ri].b], writes=[s_Xm[0][ri].b])
                                  mC3 = scon_t[:, 512 + 128 * k4:512 + 128 * k4 + 128].rearrange("p (a q) -> p a q", a=2)
                                  for ri in range(2):
                                      op("dve", lambda e: e.tensor_tensor(out=s_Cp[:].rearrange("p (a q) -> p a q", a=2), in0=mC3,
                                                                          in1=s_C[ri][:, :].unsqueeze(1).to_broadcast([128, 2, 64]), op=ALU.mult),
                                         reads=[scon_t.b, s_C[ri].b], writes=[s_Cp.b])
                                      op("pe", lambda e: e.transpose(ptmp[:, 0:128], s_Cp[:, :], ident_f[:, :]), reads=[s_Cp.b, ident_f.b], writes=[ptmp.b])
                                      op("dve", lambda e: e.tensor_copy(out=s_Cm[0][ri][:], in_=ptmp[:, 0:128]), reads=[ptmp.b], writes=[s_Cm[0][ri].b])

                                  def half1(cur, nxt):
                                      op("dve", lambda e: e.tensor_scalar(out=nxt[0][:], in0=cur[0][:], scalar1=ar, scalar2=None, op0=ALU.mult),
                                         reads=[cur[0].b, pb_], writes=[nxt[0].b])
                                      op("dve", lambda e: e.tensor_scalar(out=nxt[1][:], in0=cur[0][:], scalar1=ai, scalar2=None, op0=ALU.mult),
                                         reads=[cur[0].b, pb_], writes=[nxt[1].b])

                                  def half2(cur, nxt):
                                      op("dve", lambda e: e.scalar_tensor_tensor(out=nxt[0][:], in0=cur[1][:], scalar=nai, in1=nxt[0][:], op0=ALU.mult, op1=ALU.add),
                                         reads=[cur[1].b, pb_, nxt[0].b], writes=[nxt[0].b])
                                      op("dve", lambda e: e.scalar_tensor_tensor(out=nxt[1][:], in0=cur[1][:], scalar=ar, in1=nxt[1][:], op0=ALU.mult, op1=ALU.add),
                                         reads=[cur[1].b, pb_, nxt[1].b], writes=[nxt[1].b])

                                  for k in range(9):
                                      Xc, Xn = s_Xm[k % 2], s_Xm[(k + 1) % 2]
                                      Cc, Cn = s_Cm[k % 2], s_Cm[(k + 1) % 2]
                                      if k < 8:
                                          for ri in range(2):
                                              op("act", lambda e: e.activation(out=s_Xb[k][ri][:], in_=Xc[ri][:], func=AF.Copy),
                                                 reads=[Xc[ri].b], writes=[s_Xb[k][ri].b])
                                      op("act", lambda e: e.activation(out=s_Wc[k][0][:], in_=Cc[0][:], func=AF.Copy), reads=[Cc[0].b], writes=[s_Wc[k][0].b])
                                      op("act", lambda e: e.activation(out=s_Wc[k][1][:], in_=Cc[1][:], func=AF.Copy, scale=-1.0),
                                         reads=[Cc[1].b], writes=[s_Wc[k][1].b])
                                      if k < 7:
                                          half1(Xc, Xn)
                                      if k < 8:
                                          half1(Cc, Cn)
                                      if k < 7:
                                          half2(Xc, Xn)
                                      if k < 8:
                                          half2(Cc, Cn)
                                      if k < 8:
                                          pt_ = psb()
                                          for ri in range(2):
                                              op("pe", lambda e: e.transpose(pt_[:, ri * 128:(ri + 1) * 128], s_Xb[k][ri][:, :], ident_b[:, :]),
                                                 reads=[s_Xb[k][ri].b, ident_b.b], writes=[pt_.b])
                                          for ri in range(2):
                                              op("act", lambda e: e.activation(out=s_Wb[k][ri][:], in_=pt_[:, ri * 128:(ri + 1) * 128], func=AF.Copy),
                                                 reads=[pt_.b], writes=[s_Wb[k][ri].b])
                                  for q4 in range(2):
                                      for dd in range(4):
                                          dl = 4 * q4 + dd
                                          op("pe", lambda e: e.matmul(ptmp[:, dd * 128:(dd + 1) * 128], lhsT=s_Xb[dl][0][:, :], rhs=s_Wc[0][0][:, :],
                                                                      start=True, stop=False), reads=[s_Xb[dl][0].b, s_Wc[0][0].b], writes=[ptmp.b])
                                          op("pe", lambda e: e.matmul(ptmp[:, dd * 128:(dd + 1) * 128], lhsT=s_Xb[dl][1][:, :], rhs=s_Wc[0][1][:, :],
                                                                      start=False, stop=True), reads=[s_Xb[dl][1].b, s_Wc[0][1].b], writes=[ptmp.b])
                                      ka = s_Kacc[:, d * 8 + 4 * q4:d * 8 + 4 * q4 + 4, :]
                                      op("dve", lambda e: e.tensor_tensor(out=ka, in0=ka, in1=ptmp[:, :].rearrange("p (a c) -> p a c", a=4), op=ALU.add),
                                         reads=[ptmp.b, s_Kacc.b], writes=[s_Kacc.b])
                                  dma(s5w[td], s_W[:, :, :].rearrange("p a c -> p (a c)"), reads=[s_W.b], writes=[s5w_b])
                              ub = s_uT[:, hf, 0:T].rearrange("p (m j) -> p m j", j=8)
                              for ri in range(2):
                                  for s_ in range(8):
                                      kk = 7 - s_ if d == 0 else s_
                                      op("pe", lambda e: e.matmul(p_acc[:, ri * 256:ri * 256 + M], lhsT=s_Wb[kk][ri][:, :], rhs=ub[:, :, s_],
                                                                  start=(s_ == 0), stop=(s_ == 7)), reads=[s_Wb[kk][ri].b, s_uT.b], writes=[p_acc.b])
                              for ri in range(2):
                                  op("dve", lambda e: e.tensor_copy(out=s_XA[ri][:, 0:M], in_=p_acc[:, ri * 256:ri * 256 + M]),
                                     reads=[p_acc.b], writes=[s_XA[ri].b])
                              if sample:
                                  dma(s_h0[:, :], ss5[l, d, 128 * st:128 * st + 128, :], writes=[s_h0.b], q="pool")
                                  a8r, a8i, na8i = s_pwr[:, 0, td:td + 1], s_pwi[:, 0, td:td + 1], s_npwi[:, 0, td:td + 1]
                                  cc = 0 if d == 0 else M - 1
                                  for ri, (sa, sb_) in enumerate(((a8r, na8i), (a8i, a8r))):
                                      xc = s_XA[ri][:, cc:cc + 1]
                                      op("dve", lambda e: e.scalar_tensor_tensor(out=xc, in0=s_h0[:, 0:1], scalar=sa, in1=xc, op0=ALU.mult, op1=ALU.add),
                                         reads=[s_h0.b, pb_, s_XA[ri].b], writes=[s_XA[ri].b])
                                      op("dve", lambda e: e.scalar_tensor_tensor(out=xc, in0=s_h0[:, 1:2], scalar=sb_, in1=xc, op0=ALU.mult, op1=ALU.add),
                                         reads=[s_h0.b, pb_, s_XA[ri].b], writes=[s_XA[ri].b])
                              src, dst = s_XA, s_XB
                              v3 = lambda t_: t_[:, 0:M].rearrange("p (s m) -> p s m", s=nseq)
                              for k in range(nsteps):
                                  sh = 1 << k
                                  pr, pi, npi = s_pwr[:, k, td:td + 1], s_pwi[:, k, td:td + 1], s_npwi[:, k, td:td + 1]
                                  if d == 0:
                                      lo, ls, kp = slice(sh, Mseq), slice(0, Mseq - sh), slice(0, sh)
                                  else:
                                      lo, ls, kp = slice(0, Mseq - sh), slice(sh, Mseq), slice(Mseq - sh, Mseq)
                                  for ri in range(2):
                                      op("act", lambda e: e.activation(out=v3(dst[ri])[:, :, kp], in_=v3(src[ri])[:, :, kp], func=AF.Copy),
                                         reads=[src[ri].b, dst[ri].b], writes=[dst[ri].b])
                                  for ri, m1 in enumerate((pr, pr)):
                                      op("dve", lambda e: e.scalar_tensor_tensor(out=v3(dst[ri])[:, :, lo], in0=v3(src[ri])[:, :, ls], scalar=m1,
                                                                                 in1=v3(src[ri])[:, :, lo], op0=ALU.mult, op1=ALU.add),
                                         reads=[src[ri].b, pb_, dst[ri].b], writes=[dst[ri].b])
                                  for ri, m2 in enumerate((npi, pi)):
                                      other = src[1 - ri]
                                      op("dve", lambda e: e.scalar_tensor_tensor(out=v3(dst[ri])[:, :, lo], in0=v3(other)[:, :, ls], scalar=m2,
                                                                                 in1=v3(dst[ri])[:, :, lo], op0=ALU.mult, op1=ALU.add),
                                         reads=[other.b, pb_, dst[ri].b], writes=[dst[ri].b])
                                  src, dst = dst, src
                              Hh = src
                              if not sample:
                                  for s_ in range(nseq):
                                      cc = s_ * Mseq + (Mseq - 1 if d == 0 else 0)
                                      for ri in range(2):
                                          op("dve", lambda e: e.tensor_copy(out=s_fin[:, ri:ri + 1], in_=Hh[ri][:, cc:cc + 1]),
                                             reads=[Hh[ri].b, s_fin.b], writes=[s_fin.b])
                                      dma(o_s5[s_, l, d].rearrange("g p r -> (g p) r")[128 * st:128 * st + 128, :], s_fin[:, :], reads=[s_fin.b])
                              W1 = Mseq + 1
                              for ri in range(2):
                                  hv = s_Hs[ri][:, 0:nseq * W1].rearrange("p (s m) -> p s m", s=nseq)
                                  body = hv[:, :, 1:W1] if d == 0 else hv[:, :, 0:Mseq]
                                  edge = hv[:, :, 0:1] if d == 0 else hv[:, :, Mseq:W1]
                                  op("act", lambda e: e.activation(out=body, in_=v3(Hh[ri]), func=AF.Copy), reads=[Hh[ri].b, s_Hs[ri].b], writes=[s_Hs[ri].b])
                                  if sample:
                                      op("dve", lambda e: e.tensor_copy(out=edge, in_=s_h0[:, ri:ri + 1].unsqueeze(1)), reads=[s_h0.b, s_Hs[ri].b], writes=[s_Hs[ri].b])
                                  else:
                                      op("dve", lambda e: e.memset(edge, 0.0), reads=[s_Hs[ri].b], writes=[s_Hs[ri].b])
                              off = 0 if d == 0 else 1
                              for bk in range(nb):
                                  for j in range(8):
                                      kk = j + 1 if d == 0 else 8 - j
                                      for ri in range(2):
                                          hv = s_Hs[ri][:, 0:nseq * W1].rearrange("p (s m) -> p s m", s=nseq)
                                          if nseq == 1:
                                              rhs_ = hv[:, 0, off + 64 * bk:off + 64 * bk + 64]
                                              out_ = ybank[bk][:, :].rearrange("p (m j) -> p m j", j=8)[:, :, j]
                                          else:
                                              rhs_ = hv[:, 2 * bk:2 * bk + 2, off:off + Mseq]
                                              out_ = ybank[bk][:, :].rearrange("p (s m j) -> p s m j", s=2, j=8)[:, :, :, j]
                                          op("pe", lambda e: e.matmul(out_, lhsT=s_Wc[kk][ri][:, :], rhs=rhs_, start=False, stop=False),
                                             reads=[s_Wc[kk][ri].b, s_Hs[ri].b], writes=[ybank[bk].b])
                      if reuse:
                          dma(s_Kbd[:, :, :].rearrange("p a c -> p (a c)"), s5k[hf], reads=[s5k_b], writes=[s_Kbd.b], q="pool")
                      else:
                          for a_ in range(16):
                              op("act", lambda e: e.activation(out=s_Kbd[:, a_, :], in_=s_Kacc[:, a_, :], func=AF.Copy), reads=[s_Kacc.b], writes=[s_Kbd.b])
                          dma(s5k[hf], s_Kbd[:, :, :].rearrange("p a c -> p (a c)"), reads=[s_Kbd.b], writes=[s5k_b])
                      for bk in range(nb):
                          u3 = s_uT[:, hf, bk * 512:(bk + 1) * 512].rearrange("p (m j) -> p m j", j=8)
                          y3 = ybank[bk][:, :].rearrange("p (m j) -> p m j", j=8)
                          for dl in range(8):
                              op("pe", lambda e: e.matmul(y3[:, :, dl:8], lhsT=s_Kbd[:, dl, :], rhs=u3[:, :, 0:8 - dl], start=False, stop=False),
                                 reads=[s_Kbd.b, s_uT.b], writes=[ybank[bk].b])
                              op("pe", lambda e: e.matmul(y3[:, :, 0:8 - dl], lhsT=s_Kbd[:, 8 + dl, :], rhs=u3[:, :, dl:8], start=False, stop=(dl == 7)),
                                 reads=[s_Kbd.b, s_uT.b], writes=[ybank[bk].b])
                      for bk in range(nb):
                          yb_ = ybank[bk]
                          op("act", lambda e: e.activation(out=s_ya[:], in_=yb_[:, :], func=AF.Square), reads=[yb_.b], writes=[s_ya.b])
                          op("dve", lambda e: e.tensor_scalar(out=s_ya[:], in0=s_ya[:], scalar1=0.044715, scalar2=1.0, op0=ALU.mult, op1=ALU.add),
                             reads=[s_ya.b], writes=[s_ya.b])
                          op("dve", lambda e: e.tensor_tensor(out=s_ya[:], in0=s_ya[:], in1=yb_[:, :], op=ALU.mult), reads=[s_ya.b, yb_.b], writes=[s_ya.b])
                          op("act", lambda e: e.activation(out=s_yb[:], in_=s_ya[:], func=AF.Sigmoid, scale=1.5957691216057308),
                             reads=[s_ya.b], writes=[s_yb.b])
                          op("dve", lambda e: e.tensor_tensor(out=s_gT[:, hf, bk * 512:(bk + 1) * 512], in0=s_yb[:], in1=yb_[:, :], op=ALU.mult),
                             reads=[s_yb.b, yb_.b], writes=[s_gT.b])
                  for oh in range(2):
                      for bk in range(nb):
                          tok = slice(bk * 512, (bk + 1) * 512)
                          for ih in range(2):
                              op("pe", lambda e: e.matmul(ptmp[:, :], lhsT=s_glu[:, ih, oh * 128:(oh + 1) * 128], rhs=s_gT[:, ih, tok],
                                                          start=(ih == 0), stop=(ih == 1)), reads=[s_glu.b, s_gT.b], writes=[ptmp.b])
                          op("act", lambda e: e.activation(out=s_ya[:], in_=ptmp[:, :], func=AF.Sigmoid, bias=s_gbcol[:, oh:oh + 1]),
                             reads=[ptmp.b, s_gbcol.b], writes=[s_ya.b])
                          for k in range(8):
                              op("pe", lambda e: e.matmul(p_acc[:, :], lhsT=s_wmix[:, k, 256 + oh * 128:256 + (oh + 1) * 128], rhs=hT[:, k, tok],
                                                          start=(k == 0), stop=(k == 7)), reads=[s_wmix.b, hT.b], writes=[p_acc.b])
                          op("act", lambda e: e.activation(out=s_yb[:], in_=p_acc[:, :], func=AF.Silu), reads=[p_acc.b], writes=[s_yb.b])
                          op("dve", lambda e: e.tensor_tensor(out=s_ya[:], in0=s_ya[:], in1=s_gT[:, oh, tok], op=ALU.mult),
                             reads=[s_ya.b, s_gT.b], writes=[s_ya.b])
                          op("dve", lambda e: e.tensor_tensor(out=brT[:, 4 + oh, tok], in0=s_ya[:], in1=s_yb[:], op=ALU.mult),
                             reads=[s_ya.b, s_yb.b], writes=[brT.b])

              def hy_group(l, g):
                  c, T, L, sample = g["c"], g["T"], g["L"], g["sample"]
                  tb = htab["s" if sample else "p"]
                  nseq, ntt = T // L, L // 128
                  nfb = ntt + 1
                  wbase = 136 if sample else 128
                  NB = min(512, L)
                  nbank = L // NB
                  cos_cb = tb["cos"].rearrange("(a p) j -> p a j", p=128)
                  sin_cb = tb["sin"].rearrange("(a p) j -> p a j", p=128)

                  def sin_layer(pp_, li, dst, w):
                      o4 = 4 * li
                      op("act", lambda e: e.activation(out=h_s2[:, 0:w], in_=pp_[0:64, 0:w], func=AF.Sin, scale=h_mlp[:, o4:o4 + 1],
                                                       bias=h_mlp[:, o4 + 1:o4 + 2]), reads=[pp_.b, h_mlp.b], writes=[h_s2.b])
                      op("act", lambda e: e.activation(out=h_s4[:, 0:w], in_=pp_[0:64, 0:w], func=AF.Sin, scale=h_mlp[:, o4 + 2:o4 + 3],
                                                       bias=h_mlp[:, o4 + 3:o4 + 4]), reads=[pp_.b, h_mlp.b], writes=[h_s4.b])
                      op("dve", lambda e: e.tensor_tensor(out=h_s4[:, 0:w], in0=h_s4[:, 0:w], in1=h_s4[:, 0:w], op=ALU.mult),
                         reads=[h_s4.b], writes=[h_s4.b])
                      op("dve", lambda e: e.tensor_scalar(out=h_s4[:, 0:w], in0=h_s4[:, 0:w], scalar1=-2.0, scalar2=1.0, op0=ALU.mult, op1=ALU.add),
                         reads=[h_s4.b], writes=[h_s4.b])
                      op("dve", lambda e: e.scalar_tensor_tensor(out=dst, in0=h_s2[:, 0:w], scalar=2.0, in1=h_s4[:, 0:w], op0=ALU.mult, op1=ALU.mult),
                         reads=[h_s2.b, h_s4.b], writes=[h_hid[li].b])

                  for c0 in range(0, L, 512):
                      w = min(512, L - c0)
                      dma(h_feat[:, 0:w], tb["feat"][:, c0:c0 + w], writes=[h_feat.b], q="pool")
                      pp_ = psf()
                      op("pe", lambda e: e.matmul(pp_[0:64, 0:w], lhsT=hw1_b[:, :], rhs=h_feat[:, 0:w], start=True, stop=True),
                         reads=[hw1_b.b, h_feat.b], writes=[pp_.b])
                      sin_layer(pp_, 0, h_hid[0][:, c0:c0 + w], w)
                  for c0 in range(0, L, 512):
                      w = min(512, L - c0)
                      pp_ = psf()
                      op("pe", lambda e: e.matmul(pp_[0:64, 0:w], lhsT=hw2_b[:, :], rhs=h_hid[0][:, c0:c0 + w], start=True, stop=True),
                         reads=[hw2_b.b, h_hid[0].b], writes=[pp_.b])
                      sin_layer(pp_, 1, h_hid[1][:, c0:c0 + w], w)

                  if not sample:
                      for b in range(ntt):
                          dma(h_tc[b][:, 0:ntt, :], cos_cb[:, :, b * 128:(b + 1) * 128], writes=[h_tc[b].b], q="pool")
                          dma(h_ts[b][:, 0:ntt, :], sin_cb[:, :, b * 128:(b + 1) * 128], writes=[h_ts[b].b], q="pool")
                          dma(h_cr[b][:, 0:NB], tb["cos"][b * 128:(b + 1) * 128, 0:NB], writes=[h_cr[b].b], q="pool")
                          dma(h_sr[b][:, 0:NB], tb["sin"][b * 128:(b + 1) * 128, 0:NB], writes=[h_sr[b].b], q="pool")
                  for hc in range(2):
                      r_ = 0
                      for k in range(3):
                          for wh in range(3):
                              dma(h_rows[r_:r_ + 1, :], hy_cw[l, k:k + 1, wh * 256 + hc * 128:wh * 256 + hc * 128 + 128], reads=[h_rows.b], writes=[h_rows.b])
                              r_ += 1
                      for wh in range(3):
                          dma(h_rows[r_:r_ + 1, :], hy_cb[l:l + 1, wh * 256 + hc * 128:wh * 256 + hc * 128 + 128], reads=[h_rows.b], writes=[h_rows.b])
                          r_ += 1
                      for o in range(2):
                          dma(h_rows[r_:r_ + 1, :], hy_bi[l, o:o + 1, hc * 128:hc * 128 + 128], reads=[h_rows.b], writes=[h_rows.b])
                          r_ += 1
                      pp_ = psf()
                      op("pe", lambda e: e.transpose(pp_[:, 0:14], h_rows[:, :], ident_f[0:14, 0:14]), reads=[h_rows.b, ident_f.b], writes=[pp_.b])
                      op("dve", lambda e: e.tensor_copy(out=h_colp[:], in_=pp_[:, 0:14]), reads=[pp_.b], writes=[h_colp.b])
                      if sample:
                          dma(h_wmix[:, :, :].rearrange("p k c -> p (k c)"), wc["hy%d" % hc][0][:, :], reads=[wc["hy%d" % hc][1]], writes=[h_wmix.b], q="pool")
                      else:
                          for k in range(8):
                              st_ = wstage[0]
                              for j in range(4):
                                  dma(st_[:, j * 128:(j + 1) * 128], w_in[l, k * 128:(k + 1) * 128, HY0 + j * 256 + hc * 128:HY0 + j * 256 + hc * 128 + 128],
                                      reads=[st_.b], writes=[st_.b])
                              op("dve" if k % 2 == 0 else "pool", lambda e: e.tensor_copy(out=h_wmix[:, k, :], in_=st_[:, 0:512]),
                                 reads=[st_.b], writes=[h_wmix.b])
                          dma(wc["hy%d" % hc][0][:, :], h_wmix[:, :, :].rearrange("p k c -> p (k c)"), reads=[h_wmix.b], writes=[wc["hy%d" % hc][1]])
                      if sample:
                          dma(h_w3[:].rearrange("p j c -> p (j c)"), wc["w3%d" % hc][0][:, :], reads=[wc["w3%d" % hc][1]], writes=[h_w3.b], q="pool")
                      else:
                          st_ = wstage[0]
                          for j in range(4):
                              dma(st_[0:64, j * 128:(j + 1) * 128], hy_w3[l, :, j * 256 + hc * 128:j * 256 + hc * 128 + 128], reads=[st_.b], writes=[st_.b], q="pool")
                          op("dve", lambda e: e.tensor_copy(out=h_w3[:].rearrange("p j c -> p (j c)"), in_=st_[0:64, 0:512]), reads=[st_.b], writes=[h_w3.b])
                          dma(wc["w3%d" % hc][0][:, :], h_w3[:].rearrange("p j c -> p (j c)"), reads=[h_w3.b], writes=[wc["w3%d" % hc][1]])
                      zT = h_sig[:, 3, 0:T]
                      z3 = zT.rearrange("p (s t) -> p s t", s=nseq)
                      for wh in range(3):
                          for c0 in range(0, T, 512):
                              pp_ = psf()
                              for k in range(8):
                                  op("pe", lambda e: e.matmul(pp_[:, :], lhsT=h_wmix[:, k, wh * 128:(wh + 1) * 128], rhs=hT[:, k, c0:c0 + 512],
                                                              start=(k == 0), stop=(k == 7)), reads=[h_wmix.b, hT.b], writes=[pp_.b])
                              op("act", lambda e: e.activation(out=h_sig[:, 3, c0:c0 + 512], in_=pp_[:, :], func=AF.Copy), reads=[pp_.b], writes=[h_sig.b])
                          dst = h_sig[:, wh, 0:T]
                          d3 = dst.rearrange("p (s t) -> p s t", s=nseq)
                          op("dve", lambda e: e.tensor_scalar(out=dst, in0=zT, scalar1=h_colp[:, 3 + wh:4 + wh], scalar2=h_colp[:, 9 + wh:10 + wh],
                                                              op0=ALU.mult, op1=ALU.add), reads=[h_sig.b, h_colp.b], writes=[h_sig.b])
                          op("dve", lambda e: e.scalar_tensor_tensor(out=d3[:, :, 1:L], in0=z3[:, :, 0:L - 1], scalar=h_colp[:, wh:wh + 1], in1=d3[:, :, 1:L],
                                                                     op0=ALU.mult, op1=ALU.add), reads=[h_sig.b, h_colp.b], writes=[h_sig.b])
                          op("dve", lambda e: e.scalar_tensor_tensor(out=d3[:, :, 0:L - 1], in0=z3[:, :, 1:L], scalar=h_colp[:, 6 + wh:7 + wh], in1=d3[:, :, 0:L - 1],
                                                                     op0=ALU.mult, op1=ALU.add), reads=[h_sig.b, h_colp.b], writes=[h_sig.b])
                      for lt in range(ntt):
                          pp_ = psf()
                          op("pe", lambda e: e.matmul(pp_[:, :], lhsT=h_hid[1][:, lt * 128:(lt + 1) * 128], rhs=h_w3[:].rearrange("p j c -> p (j c)"),
                                                      start=True, stop=True), reads=[h_hid[1].b, h_w3.b], writes=[pp_.b])
                          dma(h_win[:, :], tb["win"][lt * 128:(lt + 1) * 128, hc * 128:(hc + 1) * 128], writes=[h_win.b], q="pool")
                          f4 = h_fw[:].rearrange("p (j c) -> p j c", j=4)
                          op("dve", lambda e: e.tensor_tensor(out=f4, in0=pp_[:, :].rearrange("p (j c) -> p j c", j=4),
                                                              in1=h_win[:, :].unsqueeze(1).to_broadcast([128, 4, 128]), op=ALU.mult),
                             reads=[pp_.b, h_win.b], writes=[h_fw.b])
                          if lt == 0:
                              op("dve", lambda e: e.tensor_scalar(out=h_fw[:, 256:512], in0=h_fw[:, 256:512], scalar1=hcon_t[:, 160:161], scalar2=None,
                                                                  op0=ALU.mult), reads=[h_fw.b, hcon_t.b], writes=[h_fw.b])
                          op("act", lambda e: e.activation(out=h_abs[:], in_=h_fw[:], func=AF.Abs), reads=[h_fw.b], writes=[h_abs.b])
                          op("pe", lambda e: e.matmul(p_acc[:, :], lhsT=ones_sq[:, :], rhs=h_abs[:, :], start=(lt == 0), stop=(lt == ntt - 1)),
                             reads=[ones_sq.b, h_abs.b], writes=[p_acc.b])
                          op("dve", lambda e: e.tensor_tensor(out=h_ksd[:, lt, 0:2, :], in0=f4[:, 0:2, :], in1=f4[:, 2:4, :], op=ALU.add),
                             reads=[h_fw.b], writes=[h_ksd.b])
                          op("dve", lambda e: e.tensor_tensor(out=h_ksd[:, lt, 2:4, :], in0=f4[:, 2:4, :], in1=f4[:, 0:2, :], op=ALU.subtract),
                             reads=[h_fw.b, h_ksd.b], writes=[h_ksd.b])
                      n4 = p_acc[:, :].rearrange("p (j c) -> p j c", j=4)
                      op("dve", lambda e: e.tensor_copy(out=h_rn[:], in_=n4[:, 0:2, :]), reads=[p_acc.b], writes=[h_rn.b])
                      op("dve", lambda e: e.tensor_tensor(out=h_rn[:], in0=h_rn[:], in1=n4[:, 2:4, :], op=ALU.add), reads=[p_acc.b, h_rn.b], writes=[h_rn.b])
                      op("dve", lambda e: e.reciprocal(out=h_rn[:], in_=h_rn[:]), reads=[h_rn.b], writes=[h_rn.b])

                      def long_conv(s_, o, src_idx, combine):
                          t0 = s_ * L
                          for a in range(ntt):
                              pt_ = psb()
                              op("pe", lambda e: e.transpose(pt_[:, 0:128], h_sig[:, src_idx, t0 + a * 128:t0 + (a + 1) * 128], ident_b[:, :]),
                                 reads=[h_sig.b, ident_b.b], writes=[pt_.b])
                              op("act", lambda e: e.activation(out=h_utm[:, a, :], in_=pt_[:, 0:128], func=AF.Copy), reads=[pt_.b], writes=[h_utm.b])
                          for b in range(nfb):
                              nyq = (b == ntt)
                              tcb, tsb = h_tc[b % 2], h_ts[b % 2]
                              if not nyq and sample:
                                  dma(tcb[:, 0:ntt, :], cos_cb[:, :, b * 128:(b + 1) * 128], writes=[tcb.b], q="pool")
                                  dma(tsb[:, 0:ntt, :], sin_cb[:, :, b * 128:(b + 1) * 128], writes=[tsb.b], q="pool")
                              pu, pk = psf(), psf()
                              for dst_, col, tab, rhs_of in ((pu, 0, "c", lambda a: h_utm[:, a, :]), (pu, 128, "s", lambda a: h_utm[:, a, :]),
                                                             (pk, 0, "c", lambda a: h_ksd[:, a, o, :]), (pk, 128, "s", lambda a: h_ksd[:, a, 2 + o, :])):
                                  if nyq and tab == "s":
                                      continue
                                  for a in range(ntt):
                                      lt_ = h_alt[:, :] if nyq else (tcb if tab == "c" else tsb)[:, a, :]
                                      rd_ = [h_alt.b] if nyq else [(tcb if tab == "c" else tsb).b]
                                      op("pe", lambda e: e.matmul(dst_[:, col:col + 128], lhsT=lt_, rhs=rhs_of(a), start=(a == 0), stop=(a == ntt - 1)),
                                         reads=rd_ + [h_utm.b, h_ksd.b], writes=[dst_.b])
                              wc = hcon_t[:, wbase + b:wbase + b + 1]
                              op("dve", lambda e: e.scalar_tensor_tensor(out=h_k[0][:], in0=pk[:, 0:128], scalar=wc, in1=h_rn[:, o, :], op0=ALU.mult, op1=ALU.mult),
                                 reads=[pk.b, hcon_t.b, h_rn.b], writes=[h_k[0].b])
                              if nyq:
                                  op("dve", lambda e: e.tensor_tensor(out=h_Y[:, b, 0, :], in0=pu[:, 0:128], in1=h_k[0][:], op=ALU.mult),
                                     reads=[pu.b, h_k[0].b], writes=[h_Y.b])
                                  continue
                              op("dve", lambda e: e.scalar_tensor_tensor(out=h_k[1][:], in0=pk[:, 128:256], scalar=wc, in1=h_rn[:, o, :], op0=ALU.mult, op1=ALU.mult),
                                 reads=[pk.b, hcon_t.b, h_rn.b], writes=[h_k[1].b])
                              op("dve", lambda e: e.tensor_tensor(out=h_t[0][:], in0=pu[:, 0:128], in1=h_k[0][:], op=ALU.mult), reads=[pu.b, h_k[0].b], writes=[h_t[0].b])
                              op("dve", lambda e: e.tensor_tensor(out=h_t[1][:], in0=pu[:, 128:256], in1=h_k[1][:], op=ALU.mult), reads=[pu.b, h_k[1].b], writes=[h_t[1].b])
                              op("dve", lambda e: e.tensor_tensor(out=h_t[2][:], in0=pu[:, 128:256], in1=h_k[0][:], op=ALU.mult), reads=[pu.b, h_k[0].b], writes=[h_t[2].b])
                              op("dve", lambda e: e.tensor_tensor(out=h_t[3][:], in0=pu[:, 0:128], in1=h_k[1][:], op=ALU.mult), reads=[pu.b, h_k[1].b], writes=[h_t[3].b])
                              op("dve", lambda e: e.tensor_tensor(out=h_Y[:, b, 0, :], in0=h_t[0][:], in1=h_t[1][:], op=ALU.add),
                                 reads=[h_t[0].b, h_t[1].b], writes=[h_Y.b])
                              op("dve", lambda e: e.tensor_tensor(out=h_Y[:, b, 1, :], in0=h_t[2][:], in1=h_t[3][:], op=ALU.subtract),
                                 reads=[h_t[2].b, h_t[3].b, h_Y.b], writes=[h_Y.b])
                          for bank in range(nbank):
                              c0 = bank * NB
                              for b in range(ntt):
                                  crb, srb = h_cr[b % 2], h_sr[b % 2]
                                  if sample:
                                      dma(crb[:, 0:NB], tb["cos"][b * 128:(b + 1) * 128, c0:c0 + NB], writes=[crb.b], q="pool")
                                      dma(srb[:, 0:NB], tb["sin"][b * 128:(b + 1) * 128, c0:c0 + NB], writes=[srb.b], q="pool")
                                  op("pe", lambda e: e.matmul(p_acc[:, 0:NB], lhsT=h_Y[:, b, 0, :], rhs=crb[:, 0:NB], start=(b == 0), stop=False),
                                     reads=[h_Y.b, crb.b], writes=[p_acc.b])
                                  op("pe", lambda e: e.matmul(p_acc[:, 0:NB], lhsT=h_Y[:, b, 1, :], rhs=srb[:, 0:NB], start=False, stop=False),
                                     reads=[h_Y.b, srb.b], writes=[p_acc.b])
                              op("pe", lambda e: e.matmul(p_acc[:, 0:NB], lhsT=h_Y[0:1, ntt, 0, :], rhs=haltr_t[0:1, 0:NB], start=False, stop=True),
                                 reads=[h_Y.b, haltr_t.b], writes=[p_acc.b])
                              combine(t0 + c0)

                      def comb1(tk):
                          op("dve", lambda e: e.scalar_tensor_tensor(out=h_fw[:, 0:NB], in0=h_sig[:, 0, tk:tk + NB], scalar=h_colp[:, 12:13], in1=p_acc[:, 0:NB],
                                                                     op0=ALU.mult, op1=ALU.add), reads=[h_sig.b, h_colp.b, p_acc.b], writes=[h_fw.b])
                          op("dve", lambda e: e.tensor_tensor(out=h_sig[:, 3, tk:tk + NB], in0=h_fw[:, 0:NB], in1=h_sig[:, 1, tk:tk + NB], op=ALU.mult),
                             reads=[h_fw.b, h_sig.b], writes=[h_sig.b])

                      def comb2(tk):
                          op("dve", lambda e: e.scalar_tensor_tensor(out=h_fw[:, 0:NB], in0=h_sig[:, 3, tk:tk + NB], scalar=h_colp[:, 13:14], in1=p_acc[:, 0:NB],
                                                                     op0=ALU.mult, op1=ALU.add), reads=[h_sig.b, h_colp.b, p_acc.b], writes=[h_fw.b])
                          op("dve", lambda e: e.tensor_tensor(out=h_fw[:, 0:NB], in0=h_fw[:, 0:NB], in1=h_sig[:, 2, tk:tk + NB], op=ALU.mult),
                             reads=[h_fw.b, h_sig.b], writes=[h_fw.b])
                          pg_ = psf()
                          for k in range(8):
                              op("pe", lambda e: e.matmul(pg_[:, 0:NB], lhsT=h_wmix[:, k, 384:512], rhs=hT[:, k, tk:tk + NB], start=(k == 0), stop=(k == 7)),
                                 reads=[h_wmix.b, hT.b], writes=[pg_.b])
                          op("act", lambda e: e.activation(out=h_abs[:, 0:NB], in_=pg_[:, 0:NB], func=AF.Silu), reads=[pg_.b], writes=[h_abs.b])
                          op("dve", lambda e: e.tensor_tensor(out=brT[:, 2 + hc, tk:tk + NB], in0=h_fw[:, 0:NB], in1=h_abs[:, 0:NB], op=ALU.mult),
                             reads=[h_fw.b, h_abs.b], writes=[brT.b])

                      for s_ in range(nseq):
                          long_conv(s_, 0, 0, comb1)
                      for s_ in range(nseq):
                          long_conv(s_, 1, 3, comb2)

              groups = [
                  dict(c=0, src=(xp if l == 0 else xmid_p), srcb=xmid_p_b, T=TP, L=SEQ, sample=False),
                  dict(c=1, src=(xs if l == 0 else xmid_s), srcb=xmid_s_b, T=LS, L=LS, sample=True),
              ]
              for g in groups:
                  c, T, L, sample = g["c"], g["T"], g["L"], g["sample"]
                  ntile = T // 128
                  nseq = T // L
                  kb.barrier()
                  if sample:
                      dma(w_in_b[:, :, :].rearrange("p k c -> p (k c)"), wc["in"][0][:, :], reads=[wc["in"][1]], writes=[w_in_b.b], q="pool")
                  else:
                      for k in range(8):
                          st = wstage[k % 2]
                          dma(st[:, 0:NINU], w_in[l, k * 128:(k + 1) * 128, 0:NINU], writes=[st.b], q="pool")
                          eng = "dve" if k % 2 == 0 else "pool"
                          op(eng, lambda e: e.tensor_copy(out=w_in_b[:, k, :], in_=st[:, 0:NINU]), reads=[st.b], writes=[w_in_b.b])
                      dma(wc["in"][0][:, :], w_in_b[:, :, :].rearrange("p k c -> p (k c)"), reads=[w_in_b.b], writes=[wc["in"][1]])

                  def k_path(src_ckv_f32, src_kro_f32, rope_tile, kcol, vt, rd):
                      op("dve", lambda e: e.tensor_copy(out=ckvn_b[:], in_=src_ckv_f32), reads=rd, writes=[ckvn_b.b])
                      p = psb()
                      op("pe", lambda e: e.transpose(p[:, 0:128], ckvn_b[:, :], ident_b[:, :]),
                         reads=[ckvn_b.b, ident_b.b], writes=[p.b])
                      op("act", lambda e: e.activation(out=ckvT[:], in_=p[:, 0:128], func=AF.Copy),
                         reads=[p.b], writes=[ckvT.b])
                      pkv = psf()
                      op("pe", lambda e: e.matmul(pkv[:, :], lhsT=ckvT[:, :], rhs=w_ukv_b[:, :], start=True, stop=True),
                         reads=[ckvT.b, w_ukv_b.b], writes=[pkv.b])
                      chk('k1')
                      kv3 = pkv[:, :].rearrange("p (h d) -> p h d", h=H)
                      op("dve", lambda e: e.tensor_copy(out=kcat[:, :, 0:64], in_=kv3[:, :, 0:64]),
                         reads=[pkv.b], writes=[kcat.b])
                      op("pool", lambda e: e.tensor_copy(out=kcat[:, :, 64:96],
                                                         in_=src_kro_f32.unsqueeze(1).to_broadcast([128, H, 32])),
                         reads=rd + [kcat.b], writes=[kcat.b])
                      chk('k2')
                      op("act", lambda e: e.activation(out=vaug[:, vt, :, 0:64], in_=kv3[:, :, 64:128], func=AF.Copy),
                         reads=[pkv.b], writes=[vaug.b])
                      op("pool", lambda e: e.memset(vaug[:, vt, :, 64:65], 1.0), reads=[vaug.b], writes=[vaug.b])
                      chk('k3')
                      head_norm(kcat, bc_kn, rope_tile)
                      chk('k4')
                      pq = psb()
                      for h in range(H):
                          op("pe", lambda e: e.transpose(pq[0:QK, h * 128:(h + 1) * 128], qb[:, h, :], ident_b[:, :]),
                             reads=[qb.b, ident_b.b], writes=[pq.b])
                      op("act", lambda e: e.activation(out=kT[0:QK, :, kcol:kcol + 128],
                                                       in_=pq[0:QK, 0:512].rearrange("p (h t) -> p h t", h=H),
                                                       func=AF.Copy), reads=[pq.b], writes=[kT.b])

                  def head_norm(src, wbc, rope_tile):
                      op("act", lambda e: e.activation(out=sqh[:], in_=src[:], func=AF.Square),
                         reads=[src.b], writes=[sqh.b])
                      op("dve", lambda e: e.tensor_reduce(out=ssh[:], in_=sqh[:], axis=AX.X, op=ALU.add),
                         reads=[sqh.b], writes=[ssh.b])
                      rstd_from_ss(rsh, ssh, QK)
                      op("dve", lambda e: e.tensor_tensor(out=qf[:], in0=src[:],
                                                          in1=rsh[:, :].unsqueeze(2).to_broadcast([128, H, QK]),
                                                          op=ALU.mult), reads=[src.b, rsh.b], writes=[qf.b])
                      op("dve", lambda e: e.tensor_tensor(out=qf[:], in0=qf[:],
                                                          in1=wbc[:, :].unsqueeze(1).to_broadcast([128, H, QK]),
                                                          op=ALU.mult), reads=[qf.b, wbc.b], writes=[qf.b])
                      if rope_tile is not None:
                          r5 = qf[:, :, 64:96].rearrange("p h (a f j) -> p h a f j", a=2, f=2)
                          t5 = rtmp[:].rearrange("p h (a f j) -> p h a f j", a=2, f=2)
                          op("dve", lambda e: e.tensor_copy(out=t5[:, :, :, 0, :], in_=r5[:, :, :, 1, :]),
                             reads=[qf.b], writes=[rtmp.b])
                          op("dve", lambda e: e.tensor_copy(out=t5[:, :, :, 1, :], in_=r5[:, :, :, 0, :]),
                             reads=[qf.b, rtmp.b], writes=[rtmp.b])
                          op("dve", lambda e: e.tensor_tensor(out=rtmp[:], in0=rtmp[:],
                                                              in1=sinT[:, :].unsqueeze(1).to_broadcast([128, H, 32]),
                                                              op=ALU.mult), reads=[rtmp.b, sinT.b], writes=[rtmp.b])
                          op("dve", lambda e: e.tensor_tensor(out=rtmp2[:], in0=qf[:, :, 64:96],
                                                              in1=cosT[:, :].unsqueeze(1).to_broadcast([128, H, 32]),
                                                              op=ALU.mult), reads=[qf.b, cosT.b], writes=[rtmp2.b])
                          op("dve", lambda e: e.tensor_tensor(out=qf[:, :, 64:96], in0=rtmp2[:], in1=rtmp[:], op=ALU.add),
                             reads=[rtmp.b, rtmp2.b, qf.b], writes=[qf.b])
                      op("dve", lambda e: e.tensor_copy(out=qb[:], in_=qf[:]), reads=[qf.b], writes=[qb.b])

                  def q_path(tt, rope_tile):
                      pc = psb()
                      op("pe", lambda e: e.transpose(pc[:, 0:128], cqn[:, 0:128], ident_b[:, :]),
                         reads=[cqn.b, ident_b.b], writes=[pc.b])
                      op("pe", lambda e: e.transpose(pc[0:64, 128:256], cqn[:, 128:192], ident_b[:, :]),
                         reads=[cqn.b, ident_b.b], writes=[pc.b])
                      op("act", lambda e: e.activation(out=cqT[:, 0, :], in_=pc[:, 0:128], func=AF.Copy),
                         reads=[pc.b], writes=[cqT.b])
                      op("act", lambda e: e.activation(out=cqT[0:64, 1, :], in_=pc[0:64, 128:256], func=AF.Copy),
                         reads=[pc.b, cqT.b], writes=[cqT.b])
                      pqp = psf()
                      op("pe", lambda e: e.matmul(pqp[:, 0:384], lhsT=cqT[:, 0, :], rhs=w_uq_a[:, :], start=True, stop=False),
                         reads=[cqT.b, w_uq_a.b], writes=[pqp.b])
                      op("pe", lambda e: e.matmul(pqp[:, 0:384], lhsT=cqT[0:64, 1, :], rhs=w_uq_c[:, :], start=False, stop=True),
                         reads=[cqT.b, w_uq_c.b], writes=[pqp.b])
                      op("act", lambda e: e.activation(out=qpre[:].rearrange("p h d -> p (h d)"), in_=pqp[:, 0:384],
                                                       func=AF.Copy), reads=[pqp.b], writes=[qpre.b])
                      head_norm(qpre, bc_qn, rope_tile)
                      pq = psb()
                      for h in range(H):
                          op("pe", lambda e: e.transpose(pq[0:QK, h * 128:(h + 1) * 128], qb[:, h, :], ident_b[:, :]),
                             reads=[qb.b, ident_b.b], writes=[pq.b])
                      op("act", lambda e: e.activation(out=qT[0:QK, :, tt * 128:(tt + 1) * 128],
                                                       in_=pq[0:QK, 0:512].rearrange("p (h t) -> p h t", h=H),
                                                       func=AF.Copy), reads=[pq.b], writes=[qT.b])

                  qsplit = sample and l == DEPTH - 1
                  if qsplit:
                      cq_all = Tile(brT[:, 4:8, 0:768].rearrange("p j (i c) -> p j i c", c=192))

                  for s in range(nseq):
                      tps = L // 128
                      nctx = 0
                      if sample:
                          nctx = PAST // 128
                          for t in range(nctx):
                              dma(ckvn[:, :], cckv[l, t * 128:(t + 1) * 128, :], writes=[ckvn.b], q="pool")
                              dma(kro[:, :], ckro[l, t * 128:(t + 1) * 128, :], writes=[kro.b], q="pool")
                              k_path(ckvn[:, :], kro[:, :], None, t * 128, t, [ckvn.b, kro.b])
                      for tt in range(tps):
                          ti = s * tps + tt
                          r0 = ti * 128
                          dma(xt[:, :], g["src"][r0:r0 + 128, :], reads=[g["srcb"]], writes=[xt.b], q="pool")
                          op("act", lambda e: e.activation(out=junk[:], in_=xt[:], func=AF.Square, accum_out=ss1[:, 0:1]),
                             reads=[xt.b], writes=[xn.b, ss1.b])
                          rstd_from_ss(rs1, ss1, D)
                          op("dve", lambda e: e.tensor_scalar(out=xn[:], in0=xt[:], scalar1=rs1[:, 0:1], scalar2=None,
                                                              op0=ALU.mult), reads=[xt.b, rs1.b], writes=[xn.b])
                          pt = psb()
                          for k in range(8):
                              op("pe", lambda e: e.transpose(pt[:, k * 128:(k + 1) * 128], xn[:, k * 128:(k + 1) * 128],
                                                             ident_b[:, :]), reads=[xn.b, ident_b.b], writes=[pt.b])
                          for k in range(8):
                              op("act", lambda e: e.activation(out=hT[:, k, r0:r0 + 128], in_=pt[:, k * 128:(k + 1) * 128],
                                                               func=AF.Identity, scale=effsc[:, c, k:k + 1],
                                                               bias=shift[:, c, k:k + 1]),
                                 reads=[pt.b, effsc.b, shift.b], writes=[hT.b])
                          chk('normT')
                          pl = psf()
                          for k in range(8):
                              op("pe", lambda e: e.matmul(pl[:, 0:352], lhsT=hT[:, k, r0:r0 + 128], rhs=w_in_b[:, k, 0:352],
                                                          start=(k == 0), stop=(k == 7)),
                                 reads=[hT.b, w_in_b.b], writes=[pl.b])
                          op("act", lambda e: e.activation(out=lat[:], in_=pl[:, 0:352], func=AF.Copy),
                             reads=[pl.b], writes=[lat.b])
                          op("act", lambda e: e.activation(out=junk2[:, 0:192], in_=lat[:, 0:192], func=AF.Square,
                                                           accum_out=ssq[:, 0:1]), reads=[lat.b], writes=[junk2.b, ssq.b])
                          op("act", lambda e: e.activation(out=junk2[:, 0:128], in_=lat[:, 192:320], func=AF.Square,
                                                           accum_out=ssq[:, 1:2]), reads=[lat.b, ssq.b], writes=[junk2.b, ssq.b])
                          op("act", lambda e: e.activation(out=rsq[:, 0:1], in_=ssq[:, 0:1], func=AF.Sqrt, scale=1.0 / 192,
                                                           bias=eps_c[:, 0:1]), reads=[ssq.b, eps_c.b], writes=[rsq.b])
                          op("act", lambda e: e.activation(out=rsq[:, 1:2], in_=ssq[:, 1:2], func=AF.Sqrt, scale=1.0 / 128,
                                                           bias=eps_c[:, 0:1]), reads=[ssq.b, eps_c.b, rsq.b], writes=[rsq.b])
                          op("dve", lambda e: e.reciprocal(out=rsq[:], in_=rsq[:]), reads=[rsq.b], writes=[rsq.b])
                          if qsplit:
                              op("dve", lambda e: e.scalar_tensor_tensor(out=cq_all[:, tt // 4, tt % 4, :], in0=lat[:, 0:192], scalar=rsq[:, 0:1],
                                                                         in1=bc_qa[:, :], op0=ALU.mult, op1=ALU.mult),
                                 reads=[lat.b, rsq.b, bc_qa.b], writes=[cq_all.b])
                          else:
                              op("dve", lambda e: e.scalar_tensor_tensor(out=cqn[:], in0=lat[:, 0:192], scalar=rsq[:, 0:1],
                                                                         in1=bc_qa[:, :], op0=ALU.mult, op1=ALU.mult),
                                 reads=[lat.b, rsq.b, bc_qa.b], writes=[cqn.b])
                          op("dve", lambda e: e.scalar_tensor_tensor(out=ckvn[:], in0=lat[:, 192:320], scalar=rsq[:, 1:2],
                                                                     in1=bc_kva[:, :], op0=ALU.mult, op1=ALU.mult),
                             reads=[lat.b, rsq.b, bc_kva.b], writes=[ckvn.b])
                          if not sample:
                              dma(o_ckv[s, l, tt * 128:(tt + 1) * 128, :], ckvn[:, :], reads=[ckvn.b])
                              dma(o_kro[s, l, tt * 128:(tt + 1) * 128, :], lat[:, 320:352], reads=[lat.b])
                          rope_tile = None
                          if sample:
                              dma(cosT[:, :], ropec[r0:r0 + 128, :], writes=[cosT.b], q="pool")
                              dma(sinT[:, :], ropes[r0:r0 + 128, :], writes=[sinT.b], q="pool")
                              rope_tile = True
                          chk('latent')
                          if not qsplit:
                              q_path(tt, rope_tile)
                          chk('queries')
                          k_path(ckvn[:, :], lat[:, 320:352], rope_tile, (nctx + tt) * 128, nctx + tt, [ckvn.b, lat.b])

                      chk('tile0') if False else None
                      chk('phaseA')
                      nk = nctx + tps
                      QB = min(512, L)
                      nqs = QB // 128
                      po_b = [pf[5], pf[4], pf[3], pf[2]][:nqs]
                      if qsplit:
                          for i_ in range(4):
                              dma(cosT[:, :], ropecq[i_ * 128:(i_ + 1) * 128, :], writes=[cosT.b], q="pool")
                              dma(sinT[:, :], ropesq[i_ * 128:(i_ + 1) * 128, :], writes=[sinT.b], q="pool")
                              op("dve", lambda e: e.tensor_scalar(out=cqn[:], in0=cq_all[:, 0, i_, :], scalar1=qm[:, 0:1], scalar2=None,
                                                                  op0=ALU.mult), reads=[cq_all.b, qm.b], writes=[cqn.b])
                              for j in range(1, 4):
                                  op("dve", lambda e: e.scalar_tensor_tensor(out=cqn[:], in0=cq_all[:, j, i_, :], scalar=qm[:, j:j + 1],
                                                                             in1=cqn[:], op0=ALU.mult, op1=ALU.add),
                                     reads=[cq_all.b, qm.b, cqn.b], writes=[cqn.b])
                              q_path(i_, True)
                          hsel = Tile(brT[:, 2:4, :].rearrange("p r (b t) -> p (r b) t", t=512))
                          op("dve", lambda e: e.tensor_scalar(out=hsel[:, :, :], in0=hT[:, :, 0:512], scalar1=qm[:, 0:1],
                                                              scalar2=None, op0=ALU.mult), reads=[hT.b, qm.b], writes=[hsel.b])
                          for j in range(1, 4):
                              op("dve", lambda e: e.scalar_tensor_tensor(out=hsel[:, :, :], in0=hT[:, :, j * 512:(j + 1) * 512],
                                                                         scalar=qm[:, j:j + 1], in1=hsel[:, :, :],
                                                                         op0=ALU.mult, op1=ALU.add),
                                 reads=[hT.b, qm.b, hsel.b], writes=[hsel.b])
                          qsel = qT
                      for qb_ in range(1 if qsplit else L // QB):
                          q0 = qb_ * QB
                          qsrc, hsrc = (qsel, hsel) if qsplit else (qT, hT)
                          for h in range(H):
                              for kt in range(nk):
                                  pss = pf[kt % 2]
                                  pTb = pT2[kt % 2]
                                  op("pe", lambda e: e.matmul(pss[:, 0:QB], lhsT=kT[0:QK, h, kt * 128:(kt + 1) * 128],
                                                              rhs=qsrc[0:QK, h, q0:q0 + QB], start=True, stop=True),
                                     reads=[kT.b, qsrc.b], writes=[pss.b])
                                  op("act", lambda e: e.activation(out=pTb[:, 0:QB], in_=pss[:, 0:QB], func=AF.Exp, scale=QK ** -0.5),
                                     reads=[pss.b], writes=[pTb.b])
                                  for qs in range(nqs):
                                      op("pe", lambda e: e.matmul(po_b[qs][:, 0:65], lhsT=pTb[:, qs * 128:(qs + 1) * 128], rhs=vaug[:, kt, h, 0:65],
                                                                  start=(kt == 0), stop=(kt == nk - 1)),
                                         reads=[pTb.b, vaug.b], writes=[po_b[qs].b])
                              for qs in range(nqs):
                                  op("dve", lambda e: e.reciprocal(out=den[:, 0:1], in_=po_b[qs][:, 64:65]), reads=[po_b[qs].b], writes=[den.b])
                                  op("dve", lambda e: e.tensor_scalar(out=omla4[:, qs, h, :], in0=po_b[qs][:, 0:64], scalar1=den[:, 0:1], scalar2=None,
                                                                      op0=ALU.mult), reads=[po_b[qs].b, den.b], writes=[omla4.b])
                          for qs in range(nqs):
                              r0 = s * L + q0 + qs * 128
                              pgm = pf[0]
                              for k in range(8):
                                  op("pe", lambda e: e.matmul(pgm[:, 0:256], lhsT=hsrc[:, k, r0:r0 + 128], rhs=w_in_b[:, k, 352:608],
                                                              start=(k == 0), stop=(k == 7)),
                                     reads=[hsrc.b, w_in_b.b], writes=[pgm.b])
                              op("act", lambda e: e.activation(out=silg[:], in_=pgm[:, 0:256], func=AF.Silu),
                                 reads=[pgm.b], writes=[silg.b])
                              op("dve", lambda e: e.tensor_tensor(out=omg[:], in0=omla4[:, qs, :, :].rearrange("p h d -> p (h d)"),
                                                                  in1=silg[:], op=ALU.mult), reads=[omla4.b, silg.b], writes=[omg.b])
                              pbr = psb()
                              for k2 in range(2):
                                  op("pe", lambda e: e.transpose(pbr[:, k2 * 128:(k2 + 1) * 128], omg[:, k2 * 128:(k2 + 1) * 128],
                                                                 ident_b[:, :]), reads=[omg.b, ident_b.b], writes=[pbr.b])
                              if qsplit:
                                  for j in range(4):
                                      op("dve", lambda e: e.tensor_scalar(out=brT[:, 0:2, j * 512 + r0:j * 512 + r0 + 128],
                                                                          in0=pbr[:, 0:256].rearrange("p (k t) -> p k t", k=2),
                                                                          scalar1=qm[:, j:j + 1], scalar2=None, op0=ALU.mult),
                                         reads=[pbr.b, qm.b], writes=[brT.b])
                              else:
                                  op("act", lambda e: e.activation(out=brT[:, 0:2, r0:r0 + 128],
                                                                   in_=pbr[:, 0:256].rearrange("p (k t) -> p k t", k=2),
                                                                   func=AF.Copy), reads=[pbr.b], writes=[brT.b])

                  chk('attn%d' % c)
                  kb.barrier()
                  gla_group(l, g)
                  chk('gla%d' % c)
                  kb.barrier()
                  s5_group(l, g)
                  chk('s5%d' % c)
                  kb.barrier()
                  hy_group(l, g)
                  chk('hy%d' % c)
                  kb.barrier()
                  if sample:
                      dma(w_out_b[:, :, :].rearrange("p k c -> p (k c)"), wc["out"][0][:, :], reads=[wc["out"][1]], writes=[w_out_b.b], q="pool")
                  else:
                      for k in range(8):
                          st = wstage[k % 2]
                          dma(st[:, 0:D], w_out[l, k * 128:(k + 1) * 128, :], writes=[st.b], q="pool")
                          eng = "dve" if k % 2 == 0 else "pool"
                          op(eng, lambda e: e.tensor_copy(out=w_out_b[:, k, :], in_=st[:, 0:D]), reads=[st.b], writes=[w_out_b.b])
                      dma(wc["out"][0][:, :], w_out_b[:, :, :].rearrange("p k c -> p (k c)"), reads=[w_out_b.b], writes=[wc["out"][1]])
                  for ti in range(ntile):
                      r0 = ti * 128
                      last = (l == DEPTH - 1)
                      dma(xt[:, :], g["src"][r0:r0 + 128, :], reads=[g["srcb"]], writes=[xt.b], q="pool")
                      for hf in range(2):
                          p = psf()
                          for k in range(8):
                              op("pe", lambda e: e.matmul(p[:, :], lhsT=brT[:, k, r0:r0 + 128],
                                                          rhs=w_out_b[:, k, hf * 512:(hf + 1) * 512],
                                                          start=(k == 0), stop=(k == 7)),
                                 reads=[brT.b, w_out_b.b], writes=[p.b])
                          op("dve", lambda e: e.tensor_tensor(out=ot[:, hf * 512:(hf + 1) * 512], in0=p[:, :],
                                                              in1=gate_bc[c][:, hf * 512:(hf + 1) * 512], op=ALU.mult),
                             reads=[p.b, gate_bc[c].b], writes=[ot.b])
                      op("dve", lambda e: e.tensor_tensor(out=ot[:], in0=ot[:], in1=xt[:], op=ALU.add),
                         reads=[ot.b, xt.b], writes=[ot.b])
                      if not last:
                          dst = (xmid_s if sample else xmid_p)
                          dma(dst[r0:r0 + 128, :], ot[:, :], reads=[ot.b], writes=[xmid_s_b if sample else xmid_p_b])
                      elif not sample:
                          dma(yp[r0:r0 + 128, :], ot[:, :], reads=[ot.b])
                      else:
                          dma(ys[r0:r0 + 128, :], ot[:, :], reads=[ot.b])

              chk('layer0')
        except _Stop:
            if stop == 'layer0':
                dma(yp[:, :], xmid_p[:, :], reads=[xmid_p_b])
                dma(ys[:, :], xmid_s[:, :], reads=[xmid_s_b])
            kb.barrier()
            kb.phase("DBG")
            dbt = sb([128, 2048], F32)
            kb.phase(None)
            op("dve", lambda e: e.memset(dbt[:], 0.0), writes=[dbt.b])
            op("dve", lambda e: e.tensor_copy(out=dbt[:, 0:16], in_=effsc[:].rearrange("p c k -> p (c k)")), reads=[effsc.b, dbt.b], writes=[dbt.b])
            op("dve", lambda e: e.tensor_copy(out=dbt[:, 16:32], in_=shift[:].rearrange("p c k -> p (c k)")), reads=[shift.b, dbt.b], writes=[dbt.b])
            op("dve", lambda e: e.tensor_copy(out=dbt[:, 32:33], in_=rs1[:, 0:1]), reads=[rs1.b, dbt.b], writes=[dbt.b])
            op("dve", lambda e: e.tensor_copy(out=dbt[:, 64:416], in_=lat[:, :]), reads=[lat.b, dbt.b], writes=[dbt.b])
            op("dve", lambda e: e.tensor_copy(out=dbt[:, 512:1536].rearrange("p (k t) -> p k t", k=8), in_=hT[:, :, 0:128]), reads=[hT.b, dbt.b], writes=[dbt.b])
            op("dve", lambda e: e.tensor_copy(out=dbt[:, 1536:2048], in_=gate_bc[0][:, 0:512]), reads=[gate_bc[0].b, dbt.b], writes=[dbt.b])
            if stop == 'weights':
                for i_, t_ in enumerate((s_par["ar1"][:], s_par["ai1"][:], s_par["kr"][:], s_par["ki"][:], s_pwr[:, 0, :], s_pwi[:, 0, :],
                                         s_pwr[:, 7, :], s_pwi[:, 7, :])):
                    op("dve", lambda e: e.tensor_copy(out=dbt[:, 512 + 16 * i_:528 + 16 * i_], in_=t_), reads=[s_par["are"].b, dbt.b], writes=[dbt.b])
            dma(dbg[:, :], dbt[:, :], reads=[dbt.b])
        kb.finish()
    return nc


_CACHE = {}


def _rope_tables():
    half = 16
    inv = (10000.0 ** (-np.arange(0, half, 2, dtype=np.float32) / half)).astype(np.float32)
    t = np.arange(LS)
    row = (t // 64).astype(np.float32)
    col = (t % 64).astype(np.float32)
    cos = np.zeros((LS, 32), np.float32)
    sin = np.zeros((LS, 32), np.float32)
    for a, pos in enumerate((row, col)):
        ang = pos[:, None] * inv[None, :]
        ang = np.concatenate([ang, ang], axis=-1)
        cos[:, 16 * a:16 * a + 16] = np.cos(ang)
        s = np.sin(ang)
        s[:, 0:8] *= -1.0
        sin[:, 16 * a:16 * a + 16] = s
    return cos, sin


def _gla_consts():
    p = np.arange(128)[:, None]
    q = np.arange(128)[None, :]
    same = (p // 64) == (q // 64)
    c = np.zeros((128, GC), np.float32)
    c[:, 0:128] = same & (p <= q)
    c[:, 128:256] = same & (p >= q)
    c[:, 256:260] = (p // 32) == np.arange(4)[None, :]
    c[:, 260] = (p[:, 0] < 64)
    c[:, 261] = (p[:, 0] >= 64)
    c[:, 262:518] = (p // 32) == (np.arange(256)[None, :] // 64)
    c[:, 518:646] = 1.0
    return c


def _s5_consts():
    c = np.zeros((128, SC), np.float32)
    r = np.arange(128)[:, None]
    q = np.arange(128)[None, :]
    for k in range(4):
        c[:, 128 * k:128 * k + 128] = (q // 16) == (2 * k + r // 64)
        c[:, 512 + 128 * k:512 + 128 * k + 128] = (r // 16) == (2 * k + q // 64)
    c[:, 1024] = math.pi / 2
    return c


def _hy_tables(L):
    i = np.arange(L, dtype=np.float64)
    ang = (2.0 * np.pi / (2 * L)) * np.outer(i, i)
    cos = np.cos(ang).astype(ml_dtypes.bfloat16)
    sin = np.sin(ang).astype(ml_dtypes.bfloat16)
    pos = np.arange(L, dtype=np.float32)
    t = pos / L
    w = (2.0 * math.pi * pos / L).astype(np.float32)
    bands = np.linspace(1e-4, 15, 16, dtype=np.float32)
    feat = np.concatenate([t[:, None], np.cos(w[:, None] * bands), np.sin(w[:, None] * bands)], axis=-1)
    deltas = np.linspace(math.log(100.0) / 0.3, math.log(100.0) / 1.5, 256, dtype=np.float32)
    win = (np.exp(-t[:, None] * deltas[None, :]) + 0.05).astype(np.float32)
    return cos, sin, np.ascontiguousarray(feat.T).astype(ml_dtypes.bfloat16), win


def _hy_consts():
    c = np.zeros((128, HC), np.float32)
    p = np.arange(128)
    c[:, 0] = 1.0 - 2.0 * (p % 2)
    for base, L in ((128, SEQ), (136, LS)):
        n = 2 * L
        nt = L // 128
        c[:, base:base + nt] = 2.0 / n
        c[0, base] = 1.0 / n
        c[:, base + nt] = 1.0 / n
    c[:, 160] = 1.0
    c[0, 160] = 0.0
    altr = (1.0 - 2.0 * (np.arange(512) % 2)).astype(ml_dtypes.bfloat16)[None, :]
    return c, altr


W_NAMES = ["norm_w", "ada_w", "ada_b", "w_in", "w_out", "mla_qa_norm", "mla_kva_norm", "mla_w_uq",
           "mla_w_ukv", "mla_q_norm", "mla_k_norm", "gla_gw", "gla_gb", "gla_norm",
           "s5_a_re", "s5_a_im", "s5_log_dt", "s5_b_re", "s5_b_im", "s5_c_re", "s5_c_im", "s5_d", "s5_glu_w", "s5_glu_b",
           "hy_conv_w", "hy_conv_b", "hy_w1", "hy_b1", "hy_freq1", "hy_w2", "hy_b2", "hy_freq2", "hy_w3", "hy_bias"]


def _f32(a):
    return np.ascontiguousarray(np.asarray(a, dtype=np.float32))


def core_inputs(inp, i):
    b = i // 4
    cos, sin = _rope_tables()
    m = {
        "xp": _f32(inp["x_prompt"])[4 * i:4 * i + 4].reshape(TP, D),
        "xs": _f32(inp["x_sample"])[b],
        "cond": np.stack([_f32(inp["c_ctx"]), _f32(inp["c"])[b]]),
        "cckv": _f32(inp["cache_mla_ckv"])[b],
        "ckro": _f32(inp["cache_mla_krope"])[b],
        "ropec": cos, "ropes": sin,
        "sgla": _f32(inp["state_gla"])[b].reshape(DEPTH, 2, 128, 64),
        "gcon": _gla_consts(),
        "ss5": _f32(inp["state_s5"])[b].reshape(DEPTH, 2, 1024, 2),
        "scon": _s5_consts(),
        "qmask": np.ascontiguousarray(np.tile(np.eye(4, dtype=np.float32)[i % 4], (128, 1))),
        "ropecq": np.ascontiguousarray(cos[512 * (i % 4):512 * (i % 4 + 1)]),
        "ropesq": np.ascontiguousarray(sin[512 * (i % 4):512 * (i % 4 + 1)]),
    }
    if "hy" not in _CACHE:
        _CACHE["hy"] = {"p": _hy_tables(SEQ), "s": _hy_tables(LS), "c": _hy_consts()}
    for nm_ in ("p", "s"):
        cos_, sin_, feat_, win_ = _CACHE["hy"][nm_]
        m["hcos_" + nm_], m["hsin_" + nm_], m["hfeat_" + nm_], m["hwin_" + nm_] = cos_, sin_, feat_, win_
    m["hcon"], m["haltr"] = _CACHE["hy"]["c"]
    for n in W_NAMES:
        m[n] = _f32(inp[n])
    return m


def kernel(**inp):
    if "nc" not in _CACHE:
        _CACHE["nc"] = build_program()
    nc = _CACHE["nc"]
    in_maps = [core_inputs(inp, i) for i in range(8)]
    res = run_bass_kernel_spmd(nc, in_maps, core_ids=list(range(8))).results
    y_prompt = np.concatenate([r["yp"].reshape(NPS, SEQ, D) for r in res], axis=0)
    y_sample = np.stack([np.concatenate([res[4 * b + q]["ys"][512 * q:512 * (q + 1)] for q in range(4)], axis=0)
                         for b in range(2)], axis=0)
    ckv = np.concatenate([r["o_ckv"] for r in res], axis=0)
    kro = np.concatenate([r["o_kro"] for r in res], axis=0)
    s5 = np.concatenate([r["o_s5"] for r in res], axis=0)
    gla = np.concatenate([r["o_gla"] for r in res], axis=0)
    return (y_prompt.astype(np.float32), y_sample.astype(np.float32), ckv.astype(np.float32),
            kro.astype(np.float32), s5.astype(np.float32), gla.astype(np.float32))
```

```python
import contextlib
import math

import numpy as np
import ml_dtypes

import concourse.bass as bass
import concourse.mybir as mybir
from concourse.bass_utils import run_bass_kernel_spmd

F32 = mybir.dt.float32
BF16 = mybir.dt.bfloat16
AF = mybir.ActivationFunctionType
ALU = mybir.AluOpType
AX = mybir.AxisListType

D = 1024
DEPTH = 2
NIN = 2944
NINU = 608
EPS = 1e-6
SEQ = 256
NPS = 4
TP = NPS * SEQ
LS = 2048
PAST = 512
H = 4
QK = 96
GLA_DK = 32
GC = 646
SC = 1025
HC = 192
HY0 = 608
S50 = 1632
GLA0 = 2144
SAFE_SAME_ENGINE = True


class Buf:
    __slots__ = ("w", "r", "x")

    def __init__(self, exclusive=False):
        self.w = None
        self.r = {}
        self.x = exclusive


class Tile:
    def __init__(self, t, n=1):
        self.t = t
        self.b = Buf()
        self.bs = [Buf() for _ in range(n)]

    def __getitem__(self, k):
        return self.t[k]


class KB:
    def __init__(self, nc, es):
        self.nc, self.es = nc, es
        self.E = {"pe": nc.tensor, "act": nc.scalar, "dve": nc.vector, "pool": nc.gpsimd, "sp": nc.sync}
        self.sem = {k: es.enter_context(nc.semaphore("s_" + k)) for k in ("pe", "act", "dve", "pool")}
        self.NDS = 16
        for i in range(self.NDS):
            self.sem["d%d" % i] = es.enter_context(nc.semaphore("s_d%d" % i))
        self.cnt = dict.fromkeys(self.sem, 0)
        self.ndma = 0
        self.ndma_pool = 0
        self.seen = {}
        self.n = 0

    def sb(self, shape, dt, n=1):
        self.n += 1
        if getattr(self, "arena", None) is not None and self.in_arena:
            return self._carve(list(shape), dt)
        return Tile(self.es.enter_context(self.nc.sbuf_tensor("sb%d" % self.n, list(shape), dt)), n)

    def make_arena(self, ncol_f32):
        self.arena = self.es.enter_context(self.nc.sbuf_tensor("arena", [128, ncol_f32], F32))
        self.arena_cols = ncol_f32
        self.in_arena = False
        self.aoff = 0

    def phase(self, name):
        self.in_arena = name is not None
        self.aoff = 0

    def _carve(self, shape, dt):
        nel = 1
        for d_ in shape[1:]:
            nel *= d_
        ncol = nel if dt == F32 else (nel + 1) // 2
        ncol = (ncol + 7) // 8 * 8
        assert self.aoff + ncol <= self.arena_cols, ("arena overflow", self.aoff, ncol, shape)
        ap = self.arena[0:shape[0], self.aoff:self.aoff + ncol]
        self.aoff += ncol
        if dt != F32:
            ap = ap.bitcast(dt)
        ap = ap[:, 0:nel]
        if len(shape) == 3:
            ap = ap.rearrange("p (a b) -> p a b", a=shape[1])
        elif len(shape) == 4:
            ap = ap.rearrange("p (a b c) -> p a b c", a=shape[1], b=shape[2])
        return Tile(ap)

    def barrier(self):
        for eng, E in self.E.items():
            for k, c in self.cnt.items():
                if c and k != eng and self.seen.get((eng, k), 0) < c:
                    E.wait_ge(self.sem[k], c)
                    self.seen[(eng, k)] = c

    def ps(self, shape, dt):
        self.n += 1
        t = Tile(self.es.enter_context(self.nc.psum_tensor("ps%d" % self.n, list(shape), dt)))
        t.b.x = True
        return t

    def op(self, eng, fn, reads=(), writes=(), dma=False):
        deps = {}
        for b in reads:
            if b.w:
                deps[b.w[0]] = max(deps.get(b.w[0], 0), b.w[1])
            if b.x:
                for k, c in b.r.items():
                    if k != eng:
                        deps[k] = max(deps.get(k, 0), c)
        for b in writes:
            if b.w:
                deps[b.w[0]] = max(deps.get(b.w[0], 0), b.w[1])
            for k, c in b.r.items():
                deps[k] = max(deps.get(k, 0), c)
        E = self.E[eng]
        if dma:
            if eng == "pool":
                key = "d%d" % (12 + self.ndma_pool % 4)
                self.ndma_pool += 1
            else:
                key = "d%d" % (self.ndma % 12)
                self.ndma += 1
            inc = 16
            if self.cnt[key] and self.seen.get((eng, key), 0) < self.cnt[key]:
                E.wait_ge(self.sem[key], self.cnt[key])
                self.seen[(eng, key)] = self.cnt[key]
        else:
            key, inc = eng, 1
        for pk, c in deps.items():
            if pk == key and (eng == "pe" or not SAFE_SAME_ENGINE):
                continue
            if self.seen.get((eng, pk), 0) >= c:
                continue
            E.wait_ge(self.sem[pk], c)
            self.seen[(eng, pk)] = c
        inst = fn(E)
        self.cnt[key] += inc
        inst.then_inc(self.sem[key], inc)
        c = self.cnt[key]
        for b in reads:
            b.r[key] = c
        for b in writes:
            b.w = (key, c)
            b.r = {}

    def dma(self, out, in_, reads=(), writes=(), q="sp", **kw):
        self.op(q, lambda e: e.dma_start(out=out, in_=in_, **kw), reads, writes, dma=True)

    def finish(self):
        sp = self.E["sp"]
        for k in self.sem:
            if self.cnt[k]:
                sp.wait_ge(self.sem[k], self.cnt[k])


def _b(x):
    return x.b if isinstance(x, Tile) else x


class _Stop(Exception):
    pass


def build_program(stop=None):
    def chk(name):
        if stop == name:
            raise _Stop(name)

    nc = bass.Bass("TRN2", target_bir_lowering=False)
    es = contextlib.ExitStack()

    def din(name, shape, dt=F32):
        return nc.dram_tensor(name, list(shape), dt, kind="ExternalInput").ap()

    def dout(name, shape):
        return nc.dram_tensor(name, list(shape), F32, kind="ExternalOutput").ap()

    xp = din("xp", [TP, D])
    xs = din("xs", [LS, D])
    cond = din("cond", [2, D])
    cckv = din("cckv", [DEPTH, PAST, 128])
    ckro = din("ckro", [DEPTH, PAST, 32])
    ropec = din("ropec", [LS, 32])
    ropes = din("ropes", [LS, 32])
    norm_w = din("norm_w", [DEPTH, D])
    ada_w = din("ada_w", [DEPTH, D, 3 * D])
    ada_b = din("ada_b", [DEPTH, 3 * D])
    w_in = din("w_in", [DEPTH, D, NIN])
    w_out = din("w_out", [DEPTH, D, D])
    qa_n = din("mla_qa_norm", [DEPTH, 192])
    kva_n = din("mla_kva_norm", [DEPTH, 128])
    w_uq = din("mla_w_uq", [DEPTH, 192, 384])
    w_ukv = din("mla_w_ukv", [DEPTH, 128, 512])
    q_n = din("mla_q_norm", [DEPTH, 96])
    k_n = din("mla_k_norm", [DEPTH, 96])
    gla_gw = din("gla_gw", [DEPTH, 2, 16, 128])
    gla_gb = din("gla_gb", [DEPTH, 2, 128])
    gla_nw = din("gla_norm", [DEPTH, 64])
    sgla = din("sgla", [DEPTH, 2, 128, 64])
    gcon = din("gcon", [128, GC])
    s5_are = din("s5_a_re", [DEPTH, 2, 16, 64])
    s5_aim = din("s5_a_im", [DEPTH, 2, 16, 64])
    s5_ldt = din("s5_log_dt", [DEPTH, 2, 16])
    s5_bre = din("s5_b_re", [DEPTH, 2, 16, 64, 16])
    s5_bim = din("s5_b_im", [DEPTH, 2, 16, 64, 16])
    s5_cre = din("s5_c_re", [DEPTH, 2, 16, 16, 64])
    s5_cim = din("s5_c_im", [DEPTH, 2, 16, 16, 64])
    s5_dd = din("s5_d", [DEPTH, 256])
    s5_gw = din("s5_glu_w", [DEPTH, 256, 256])
    s5_gb = din("s5_glu_b", [DEPTH, 256])
    ss5 = din("ss5", [DEPTH, 2, 1024, 2])
    scon = din("scon", [128, SC])
    qmask = din("qmask", [128, 4])
    hcosq = din("hcosq", [LS, 512], BF16)
    hsinq = din("hsinq", [LS, 512], BF16)
    hy_cw = din("hy_conv_w", [DEPTH, 3, 768])
    hy_cb = din("hy_conv_b", [DEPTH, 768])
    hy_w1 = din("hy_w1", [DEPTH, 33, 64])
    hy_b1 = din("hy_b1", [DEPTH, 64])
    hy_f1 = din("hy_freq1", [DEPTH, 64])
    hy_w2 = din("hy_w2", [DEPTH, 64, 64])
    hy_b2 = din("hy_b2", [DEPTH, 64])
    hy_f2 = din("hy_freq2", [DEPTH, 64])
    hy_w3 = din("hy_w3", [DEPTH, 64, 1024])
    hy_bi = din("hy_bias", [DEPTH, 2, 256])
    htab = {}
    for nm_, L_ in (("p", SEQ), ("s", LS)):
        htab[nm_] = dict(cos=din("hcos_" + nm_, [L_, L_], BF16), sin=din("hsin_" + nm_, [L_, L_], BF16),
                         feat=din("hfeat_" + nm_, [33, L_], BF16), win=din("hwin_" + nm_, [L_, 256]))
    hcon = din("hcon", [128, HC])
    haltr = din("haltr", [1, 512], BF16)

    yp = dout("yp", [TP, D])
    ys = dout("ys", [LS, D])
    o_ckv = dout("o_ckv", [NPS, DEPTH, SEQ, 128])
    o_kro = dout("o_kro", [NPS, DEPTH, SEQ, 32])
    o_s5 = dout("o_s5", [NPS, DEPTH, 2, 16, 64, 2])
    o_gla = dout("o_gla", [NPS, DEPTH, 2, 4, 32, 64])
    dbg = dout("dbg", [128, 2048]) if stop else None

    xmid_p = nc.dram_tensor("xmid_p", [TP, D], F32, kind="Internal").ap()
    xmid_s = nc.dram_tensor("xmid_s", [LS, D], F32, kind="Internal").ap()
    xmid_p_b, xmid_s_b = Buf(), Buf()
    s5w = nc.dram_tensor("s5w", [16, 128, 34 * 128], BF16, kind="Internal").ap()
    s5k = nc.dram_tensor("s5k", [2, 128, 16 * 128], BF16, kind="Internal").ap()
    s5w_b, s5k_b = Buf(), Buf()
    wc = {n_: (nc.dram_tensor("wc_" + n_, sh_, BF16, kind="Internal").ap(), Buf()) for n_, sh_ in (
        ("in", [128, 8 * NINU]), ("gla", [128, 8 * 800]), ("s5", [128, 8 * 512]), ("hy0", [128, 8 * 512]), ("hy1", [128, 8 * 512]),
        ("w30", [64, 512]), ("w31", [64, 512]), ("out", [128, 8 * D]))}

    with es:
        kb = KB(nc, es)
        sb, ps, op, dma = kb.sb, kb.ps, kb.op, kb.dma

        ident_f = sb([128, 128], F32)
        ident_b = sb([128, 128], BF16)
        ones_f = sb([1, 128], F32)
        eps_c = sb([128, 1], F32)
        zero_f = sb([128, 512], BF16)
        op("pool", lambda e: e.memset(ident_f[:], 1.0), writes=[ident_f.b])
        op("pool", lambda e: e.affine_select(out=ident_f[:], in_=ident_f[:], pattern=[[-1, 128]],
                                             compare_op=ALU.is_equal, fill=0.0, base=0,
                                             channel_multiplier=1), reads=[ident_f.b], writes=[ident_f.b])
        op("pool", lambda e: e.tensor_copy(out=ident_b[:], in_=ident_f[:]), reads=[ident_f.b], writes=[ident_b.b])
        op("pool", lambda e: e.memset(ones_f[:], 1.0), writes=[ones_f.b])
        op("pool", lambda e: e.memset(eps_c[:], EPS), writes=[eps_c.b])
        op("pool", lambda e: e.memset(zero_f[:], 0.0), writes=[zero_f.b])
        qm = sb([128, 4], F32)
        dma(qm[:, :], qmask[:, :], writes=[qm.b], q="pool")

        pf = [ps([128, 512], F32) for _ in range(6)]
        pb = [ps([128, 1024], BF16) for _ in range(2)]
        rr = {"f": 0, "b": 0}

        def psf():
            rr["f"] = (rr["f"] + 1) % 5
            return pf[rr["f"]]

        p_acc = pf[5]

        def psb():
            rr["b"] = (rr["b"] + 1) % len(pb)
            return pb[rr["b"]]

        qT = sb([128, H, LS], BF16)
        kT = sb([128, H, LS + PAST], BF16)
        vaug = sb([128, (LS + PAST) // 128, H, 66], BF16)
        hT = sb([128, 8, LS], BF16)
        brT = sb([128, 8, LS], BF16)
        w_uq_a = sb([128, 384], BF16)
        w_uq_c = sb([64, 384], BF16)
        w_ukv_b = sb([128, 512], BF16)
        bc_qa = sb([128, 192], F32)
        bc_kva = sb([128, 128], F32)
        bc_qn = sb([128, 96], F32)
        bc_kn = sb([128, 96], F32)
        gate_bc = [sb([128, D], F32) for _ in range(2)]
        effsc = sb([128, 2, 8], F32)
        shift = sb([128, 2, 8], F32)
        rowst = sb([1, D], F32)
        scond = sb([8, 2, 128], F32)
        scondT = sb([128, 2, 8], F32)
        screp = sb([128, 128], F32)
        colst = sb([24, 128], F32)
        colsT = sb([128, 3, 8], F32)
        nwst = sb([8, 128], F32)
        nwT = sb([128, 8], F32)
        modacc = sb([128, 32], F32)

        class _Alias:
            def __init__(self, ap, owner):
                self.ap, self.b = ap, owner.b

            def __getitem__(self, k):
                return self.ap[k]

        wstage = [_Alias(qT.t[:].bitcast(F32).rearrange("p a c -> p (a c)")[:, 0:3 * D], qT)] * 2
        adast = wstage
        w_out_b = _Alias(kT.t[:].rearrange("p h t -> p (h t)")[:, 0:8 * D].rearrange("p (k c) -> p k c", k=8), kT)
        wmix = _Alias(kT.t[:].rearrange("p h t -> p (h t)")[:, 0:6400].rearrange("p (k c) -> p k c", k=8), kT)
        gcon_t = sb([128, GC], F32)
        dma(gcon_t[:, :], gcon[:, :], writes=[gcon_t.b])
        maskF, maskB_ = gcon_t[:, 0:128], gcon_t[:, 128:256]
        headm, halfm, blockm, ones64 = gcon_t[:, 256:260], gcon_t[:, 260:262], gcon_t[:, 262:518], gcon_t[:, 518:582]
        gwp_f = sb([32, 2, 128], F32)
        gwp = sb([32, 2, 128], BF16)
        gbst = sb([2, 128], F32)
        ngb = sb([128, 2], F32)
        bc_gn = sb([128, 64], F32)
        kb.make_arena(10496)
        kb.phase("GLA")
        o_bwd = _Alias(qT.t[:].bitcast(F32).rearrange("p a (b c) -> p (a b) c", c=256), qT)
        g_qf, g_kf, g_sp, g_bcp, g_e1, g_e2, g_e3 = [sb([128, 128], F32) for _ in range(7)]
        g_glr = sb([32, 128], BF16)
        g_vt = sb([128, 256], BF16)
        g_qt, g_kt, g_khT, g_khA, g_khB, g_attm = [sb([128, 128], BF16) for _ in range(6)]
        g_km = [sb([128, 128], BF16) for _ in range(4)]
        g_QA = [sb([128, 128], BF16) for _ in range(4)]
        g_QB = [sb([128, 128], BF16) for _ in range(4)]
        g_nbl = sb([128, 2], F32)
        g_tot = sb([128, 2], F32)
        g_eb = sb([128, 2], F32)
        g_dS = sb([128, 256], F32)
        g_dSr = [sb([128, 64], F32) for _ in range(2)]
        g_S = [sb([128, 64], F32) for _ in range(3)]
        g_Sb = [sb([128, 64], BF16) for _ in range(2)]
        g_os = sb([128, 256], F32)
        g_sq = sb([128, 256], F32)
        g_ss = sb([128, 4], F32)
        g_rs = sb([128, 4], F32)
        g_sil = sb([128, 256], F32)
        g_og = sb([128, 256], BF16)

        kflat = kT.t[:].rearrange("p h t -> p (h t)")
        vflat = vaug.t[:].rearrange("p a h d -> p (a h d)")
        s_wmix = _Alias(kflat[:, 0:4096].rearrange("p (k c) -> p k c", k=8), kT)
        s_uT = _Alias(kflat[:, 4096:8192].rearrange("p (h t) -> p h t", h=2), kT)
        s_Kbd = _Alias(kflat[:, 8192:10240].rearrange("p (a c) -> p a c", a=16), kT)
        s_gT = _Alias(vflat[:, 0:4096].rearrange("p (h t) -> p h t", h=2), vaug)
        s_Kacc = _Alias(qT.t[:].bitcast(F32).rearrange("p a c -> p (a c)")[:, 0:2048].rearrange("p (a c) -> p a c", a=16), qT)
        kb.phase(None)
        scon_t = sb([128, SC], F32)
        dma(scon_t[:, :], scon[:, :], writes=[scon_t.b])
        s_prst = sb([16, 128], F32)
        s_ldst = sb([16, 2], F32)
        s_par = {n_: sb([128, 16], F32) for n_ in ("are", "aim", "dt", "lr", "th", "er", "ei", "t1", "t2", "ar1", "ai1", "nai1",
                                                    "kr", "ki", "nki", "dr", "den")}
        s_pwr = sb([128, 8, 16], F32)
        s_pwi = sb([128, 8, 16], F32)
        s_npwi = sb([128, 8, 16], F32)
        s_dcol = sb([128, 2], F32)
        s_gbcol = sb([128, 2], F32)
        s_st2 = sb([2, 128], F32)
        s_Dd = [sb([128, 128], BF16) for _ in range(2)]
        s_glu = sb([128, 2, 256], BF16)
        kb.phase("S5")
        s_B = [sb([128, 16], F32) for _ in range(2)]
        s_bb = [sb([128, 16], F32) for _ in range(2)]
        s_Xm = [[sb([128, 128], F32) for _ in range(2)] for _ in range(2)]
        s_Xb = [[sb([128, 128], BF16) for _ in range(2)] for _ in range(8)]
        s_W = sb([128, 34, 128], BF16)

        class _View:
            def __init__(self, ap, b):
                self.ap, self.b = ap, b

            def __getitem__(self, k):
                return self.ap[k]

        s_Wb = [[_View(s_W[:, 2 * k_ + r_, :], s_W.b) for r_ in range(2)] for k_ in range(8)]
        s_C = [sb([128, 64], F32) for _ in range(2)]
        s_Cp = sb([128, 128], F32)
        s_Cm = [[sb([128, 128], F32) for _ in range(2)] for _ in range(2)]
        s_Wc = [[_View(s_W[:, 16 + 2 * k_ + r_, :], s_W.b) for r_ in range(2)] for k_ in range(9)]
        s_XA = [sb([128, 256], F32) for _ in range(2)]
        s_XB = [sb([128, 256], F32) for _ in range(2)]
        s_Hs = [sb([128, 4 * 33 + 257], BF16) for _ in range(2)]
        s_h0 = sb([128, 2], F32)
        s_fin = sb([128, 2], F32)
        s_tmp = sb([128, 4], F32)
        s_ya = sb([128, 512], F32)
        s_yb = sb([128, 512], F32)
        kb.phase(None)

        def bcast_row(dst, src_row_ap, n):
            dma(rowst[0:1, 0:n], src_row_ap, writes=[rowst.b])
            for c0 in range(0, n, 512):
                w = min(512, n - c0)
                p = psf()
                op("pe", lambda e: e.matmul(p[:, 0:w], lhsT=ones_f[0:1, :], rhs=rowst[0:1, c0:c0 + w],
                                            start=True, stop=True),
                   reads=[ones_f.b, rowst.b], writes=[p.b])
                op("dve", lambda e: e.tensor_copy(out=dst[:, c0:c0 + w], in_=p[:, 0:w]),
                   reads=[p.b], writes=[dst.b])

        def rstd_from_ss(dst, ss, n):
            op("act", lambda e: e.activation(out=dst[:], in_=ss[:], func=AF.Sqrt, scale=1.0 / n,
                                             bias=eps_c[:, 0:1]), reads=[ss.b, eps_c.b], writes=[dst.b])
            op("dve", lambda e: e.reciprocal(out=dst[:], in_=dst[:]), reads=[dst.b], writes=[dst.b])

        hcon_t = sb([128, HC], F32)
        dma(hcon_t[:, :], hcon[:, :], writes=[hcon_t.b])
        haltr_t = sb([1, 512], BF16)
        dma(haltr_t[:, :], haltr[:, :], writes=[haltr_t.b])
        h_alt = sb([128, 128], BF16)
        op("dve", lambda e: e.tensor_copy(out=h_alt[:], in_=hcon_t[:, 0:128]), reads=[hcon_t.b], writes=[h_alt.b])
        ones_sq = sb([128, 128], F32)
        op("pool", lambda e: e.memset(ones_sq[:], 1.0), writes=[ones_sq.b])
        hw1_b = sb([33, 64], BF16)
        hw2_b = sb([64, 64], BF16)
        h_mrow = sb([4, 64], F32)
        h_mcol = sb([64, 4], F32)
        h_mlp = sb([64, 8], F32)
        h_rows = sb([14, 128], F32)
        h_colp = sb([128, 14], F32)
        h_w3 = sb([64, 4, 128], BF16)
        h_win = sb([128, 128], F32)
        h_feat = sb([33, 512], BF16)
        h_cr = [sb([128, 512], BF16) for _ in range(2)]
        h_sr = [sb([128, 512], BF16) for _ in range(2)]
        h_sig = _Alias(qT.t[:], qT)
        class _Sub(_Alias):
            def __init__(self, ap):
                self.ap, self.b = ap, Buf()

        h_tc = [_Sub(kflat[:, 2048 * i_:2048 * i_ + 2048].rearrange("p (a j) -> p a j", a=16)) for i_ in (0, 1)]
        h_ts = [_Sub(kflat[:, 2048 * i_:2048 * i_ + 2048].rearrange("p (a j) -> p a j", a=16)) for i_ in (2, 3)]
        h_utm = _Sub(kflat[:, 8192:10240].rearrange("p (a j) -> p a j", a=16))
        h_wmix = _Alias(vflat[:, 0:4096].rearrange("p (k c) -> p k c", k=8), vaug)
        kb.phase("HY")
        h_hid = [sb([64, LS], BF16) for _ in range(2)]
        h_ksd = sb([128, 16, 4, 128], BF16)
        h_Y = sb([128, 17, 2, 128], BF16)
        h_k = [sb([128, 128], F32) for _ in range(2)]
        h_t = [sb([128, 128], F32) for _ in range(4)]
        h_rn = sb([128, 2, 128], F32)
        kb.phase(None)
        h_fw = sb([128, 512], F32)
        h_abs = sb([128, 512], F32)
        h_s2 = sb([64, 512], F32)
        h_s4 = sb([64, 512], F32)

        kb.phase("PA")
        w_in_b = sb([128, 8, NINU], BF16)
        xt = sb([128, D], F32)
        xn = sb([128, D], BF16)
        junk = xn
        ss1 = sb([128, 1], F32)
        rs1 = sb([128, 1], F32)
        lat = sb([128, 352], F32)
        junk2 = sb([128, 192], BF16)
        gm = sb([128, 256], F32)
        ssq = sb([128, 2], F32)
        rsq = sb([128, 2], F32)
        cqn = sb([128, 192], BF16)
        ckvn = sb([128, 128], F32)
        ckvn_b = sb([128, 128], BF16)
        kro = sb([128, 32], F32)
        cqT = sb([128, 2, 128], BF16)
        ckvT = sb([128, 128], BF16)
        qpre = sb([128, H, QK], F32)
        kcat = sb([128, H, QK], F32)
        sqh = sb([128, H, QK], F32)
        ssh = sb([128, H], F32)
        rsh = sb([128, H], F32)
        qf = sb([128, H, QK], F32)
        qb = sb([128, H, QK], BF16)
        rtmp = sb([128, H, 32], F32)
        rtmp2 = sb([128, H, 32], F32)
        cosT = sb([128, 32], F32)
        sinT = sb([128, 32], F32)
        pT2 = [sb([128, 512], BF16) for _ in range(2)]
        omla4 = sb([128, 4, H, 64], F32)
        den = sb([128, H], F32)
        omla = sb([128, H, 64], F32)
        omg = sb([128, 256], BF16)
        silg = sb([128, 256], F32)
        ot = sb([128, D], F32)
        kb.phase(None)

        try:
          for l in range(DEPTH):
              st = wstage[0]
              dma(st[:, 0:384], w_uq[l, 0:128, :], writes=[st.b], q="pool")
              op("dve", lambda e: e.tensor_copy(out=w_uq_a[:], in_=st[:, 0:384]), reads=[st.b], writes=[w_uq_a.b])
              st = wstage[1]
              dma(st[0:64, 0:384], w_uq[l, 128:192, :], writes=[st.b], q="pool")
              op("dve", lambda e: e.tensor_copy(out=w_uq_c[:], in_=st[0:64, 0:384]), reads=[st.b], writes=[w_uq_c.b])
              st = wstage[0]
              dma(st[:, 0:512], w_ukv[l, :, :], writes=[st.b], q="pool")
              op("dve", lambda e: e.tensor_copy(out=w_ukv_b[:], in_=st[:, 0:512]), reads=[st.b], writes=[w_ukv_b.b])
              bcast_row(bc_qa, qa_n[l:l + 1, :], 192)
              bcast_row(bc_kva, kva_n[l:l + 1, :], 128)
              bcast_row(bc_qn, q_n[l:l + 1, :], 96)
              bcast_row(bc_kn, k_n[l:l + 1, :], 96)
              bcast_row(bc_gn, gla_nw[l:l + 1, :], 64)
              op("dve", lambda e: e.memset(gwp_f[:], 0.0), writes=[gwp_f.b])
              dma(gwp_f[0:16, 0, :], gla_gw[l, 0, :, :], reads=[gwp_f.b], writes=[gwp_f.b])
              dma(gwp_f[16:32, 1, :], gla_gw[l, 1, :, :], reads=[gwp_f.b], writes=[gwp_f.b])
              op("dve", lambda e: e.tensor_copy(out=gwp[:], in_=gwp_f[:]), reads=[gwp_f.b], writes=[gwp.b])
              dma(gbst[:, :], gla_gb[l, :, :], writes=[gbst.b])
              p = psf()
              op("pe", lambda e: e.transpose(p[:, 0:2], gbst[:, :], ident_f[0:2, 0:2]), reads=[gbst.b, ident_f.b], writes=[p.b])
              op("dve", lambda e: e.tensor_scalar(out=ngb[:], in0=p[:, 0:2], scalar1=-1.0, scalar2=None, op0=ALU.mult),
                 reads=[p.b], writes=[ngb.b])

              P_ = s_par

              def colparam(dst, rows_ap, nrow=16):
                  dma(s_prst[0:nrow, :], rows_ap, writes=[s_prst.b])
                  p_ = psf()
                  op("pe", lambda e: e.transpose(p_[:, 0:nrow], s_prst[0:nrow, :], ident_f[0:nrow, 0:nrow]),
                     reads=[s_prst.b, ident_f.b], writes=[p_.b])
                  op("dve", lambda e: e.tensor_copy(out=dst, in_=p_[:, 0:nrow]), reads=[p_.b], writes=[s_par_b])

              s_par_b = P_["are"].b
              for t_ in P_.values():
                  t_.b = s_par_b
              s_pwr.b = s_pwi.b = s_npwi.b = s_par_b
              colparam(P_["are"][:], s5_are[l].rearrange("d (st gl) p -> (d st) (gl p)", gl=2))
              colparam(P_["aim"][:], s5_aim[l].rearrange("d (st gl) p -> (d st) (gl p)", gl=2))
              dma(s_ldst[:, :], s5_ldt[l].rearrange("d (st gl) -> (d st) gl", gl=2), writes=[s_ldst.b])
              op("dve", lambda e: e.tensor_copy(out=s_prst[:, :].rearrange("r (gl p) -> r gl p", gl=2),
                                                in_=s_ldst[:, :].unsqueeze(2).to_broadcast([16, 2, 64])),
                 reads=[s_ldst.b], writes=[s_prst.b])
              p_ = psf()
              op("pe", lambda e: e.transpose(p_[:, 0:16], s_prst[:, :], ident_f[0:16, 0:16]), reads=[s_prst.b, ident_f.b], writes=[p_.b])
              op("act", lambda e: e.activation(out=P_["dt"][:], in_=p_[:, 0:16], func=AF.Exp), reads=[p_.b], writes=[s_par_b])

              def pp(eng, fn):
                  op(eng, fn, reads=[s_par_b, scon_t.b], writes=[s_par_b])

              are, aim, dt_, lr, th, er, ei, t1, t2 = (P_[n_] for n_ in ("are", "aim", "dt", "lr", "th", "er", "ei", "t1", "t2"))
              pp("dve", lambda e: e.tensor_scalar(out=are[:], in0=are[:], scalar1=-1e-4, scalar2=None, op0=ALU.min))
              pp("dve", lambda e: e.tensor_tensor(out=lr[:], in0=are[:], in1=dt_[:], op=ALU.mult))
              pp("dve", lambda e: e.tensor_tensor(out=th[:], in0=aim[:], in1=dt_[:], op=ALU.mult))
              pp("act", lambda e: e.activation(out=t1[:], in_=lr[:], func=AF.Exp, scale=1.0 / 16))
              pp("act", lambda e: e.activation(out=er[:], in_=th[:], func=AF.Sin, scale=1.0 / 16, bias=scon_t[:, 1024:1025]))
              pp("act", lambda e: e.activation(out=ei[:], in_=th[:], func=AF.Sin, scale=1.0 / 16))
              pp("dve", lambda e: e.tensor_tensor(out=er[:], in0=er[:], in1=t1[:], op=ALU.mult))
              pp("dve", lambda e: e.tensor_tensor(out=ei[:], in0=ei[:], in1=t1[:], op=ALU.mult))

              def csq():
                  pp("dve", lambda e: e.tensor_tensor(out=t1[:], in0=er[:], in1=er[:], op=ALU.mult))
                  pp("dve", lambda e: e.tensor_tensor(out=t2[:], in0=ei[:], in1=ei[:], op=ALU.mult))
                  pp("dve", lambda e: e.scalar_tensor_tensor(out=ei[:], in0=er[:], scalar=2.0, in1=ei[:], op0=ALU.mult, op1=ALU.mult))
                  pp("dve", lambda e: e.tensor_tensor(out=er[:], in0=t1[:], in1=t2[:], op=ALU.subtract))

              for _ in range(4):
                  csq()
              ar1, ai1, nai1, kr, ki, nki, dr, dn_ = (P_[n_] for n_ in ("ar1", "ai1", "nai1", "kr", "ki", "nki", "dr", "den"))
              pp("dve", lambda e: e.tensor_copy(out=ar1[:], in_=er[:]))
              pp("dve", lambda e: e.tensor_copy(out=ai1[:], in_=ei[:]))
              pp("dve", lambda e: e.tensor_scalar(out=nai1[:], in0=ei[:], scalar1=-1.0, scalar2=None, op0=ALU.mult))
              pp("dve", lambda e: e.tensor_scalar(out=dr[:], in0=ar1[:], scalar1=-1.0, scalar2=None, op0=ALU.add))
              pp("dve", lambda e: e.tensor_tensor(out=t1[:], in0=are[:], in1=are[:], op=ALU.mult))
              pp("dve", lambda e: e.tensor_tensor(out=t2[:], in0=aim[:], in1=aim[:], op=ALU.mult))
              pp("dve", lambda e: e.tensor_tensor(out=dn_[:], in0=t1[:], in1=t2[:], op=ALU.add))
              pp("dve", lambda e: e.reciprocal(out=dn_[:], in_=dn_[:]))
              pp("dve", lambda e: e.tensor_tensor(out=t1[:], in0=dr[:], in1=are[:], op=ALU.mult))
              pp("dve", lambda e: e.tensor_tensor(out=t2[:], in0=ai1[:], in1=aim[:], op=ALU.mult))
              pp("dve", lambda e: e.tensor_tensor(out=t1[:], in0=t1[:], in1=t2[:], op=ALU.add))
              pp("dve", lambda e: e.tensor_tensor(out=kr[:], in0=t1[:], in1=dn_[:], op=ALU.mult))
              pp("dve", lambda e: e.tensor_tensor(out=t1[:], in0=ai1[:], in1=are[:], op=ALU.mult))
              pp("dve", lambda e: e.tensor_tensor(out=t2[:], in0=dr[:], in1=aim[:], op=ALU.mult))
              pp("dve", lambda e: e.tensor_tensor(out=t1[:], in0=t1[:], in1=t2[:], op=ALU.subtract))
              pp("dve", lambda e: e.tensor_tensor(out=ki[:], in0=t1[:], in1=dn_[:], op=ALU.mult))
              pp("dve", lambda e: e.tensor_scalar(out=nki[:], in0=ki[:], scalar1=-1.0, scalar2=None, op0=ALU.mult))
              for _ in range(3):
                  csq()
              for k in range(8):
                  pp("dve", lambda e: e.tensor_copy(out=s_pwr[:, k, :], in_=er[:]))
                  pp("dve", lambda e: e.tensor_copy(out=s_pwi[:, k, :], in_=ei[:]))
                  pp("dve", lambda e: e.tensor_scalar(out=s_npwi[:, k, :], in0=ei[:], scalar1=-1.0, scalar2=None, op0=ALU.mult))
                  if k < 7:
                      csq()
              for dst_, src_ in ((s_dcol, s5_dd), (s_gbcol, s5_gb)):
                  dma(s_st2[:, :], src_[l].rearrange("(h p) -> h p", p=128), writes=[s_st2.b])
                  p_ = psf()
                  op("pe", lambda e: e.transpose(p_[:, 0:2], s_st2[:, :], ident_f[0:2, 0:2]), reads=[s_st2.b, ident_f.b], writes=[p_.b])
                  op("dve", lambda e: e.tensor_copy(out=dst_[:], in_=p_[:, 0:2]), reads=[p_.b], writes=[dst_.b])
              for hf in range(2):
                  op("dve", lambda e: e.tensor_scalar(out=s_Dd[hf][:], in0=ident_f[:], scalar1=s_dcol[:, hf:hf + 1], scalar2=None, op0=ALU.mult),
                     reads=[ident_f.b, s_dcol.b], writes=[s_Dd[hf].b])
                  st = wstage[0]
                  dma(st[:, 0:256], s5_gw[l, hf * 128:(hf + 1) * 128, :], writes=[st.b], q="pool")
                  op("dve", lambda e: e.tensor_copy(out=s_glu[:, hf, :], in_=st[:, 0:256]), reads=[st.b], writes=[s_glu.b])
              for r_, src_ in enumerate((hy_f1, hy_b1, hy_f2, hy_b2)):
                  dma(h_mrow[r_:r_ + 1, :], src_[l:l + 1, :], reads=[h_mrow.b], writes=[h_mrow.b])
              p_ = psf()
              op("pe", lambda e: e.transpose(p_[0:64, 0:4], h_mrow[:, :], ident_f[0:4, 0:4]), reads=[h_mrow.b, ident_f.b], writes=[p_.b])
              op("dve", lambda e: e.tensor_copy(out=h_mcol[:], in_=p_[0:64, 0:4]), reads=[p_.b], writes=[h_mcol.b])
              for li in range(2):
                  fcol, bcol = h_mcol[:, 2 * li:2 * li + 1], h_mcol[:, 2 * li + 1:2 * li + 2]
                  o4 = 4 * li
                  op("dve", lambda e: e.tensor_scalar(out=h_mlp[:, o4:o4 + 1], in0=fcol, scalar1=0.5, scalar2=None, op0=ALU.mult),
                     reads=[h_mcol.b, h_mlp.b], writes=[h_mlp.b])
                  op("dve", lambda e: e.scalar_tensor_tensor(out=h_mlp[:, o4 + 1:o4 + 2], in0=fcol, scalar=0.5, in1=bcol, op0=ALU.mult, op1=ALU.mult),
                     reads=[h_mcol.b, h_mlp.b], writes=[h_mlp.b])
                  op("dve", lambda e: e.tensor_scalar(out=h_mlp[:, o4 + 2:o4 + 3], in0=fcol, scalar1=0.25, scalar2=None, op0=ALU.mult),
                     reads=[h_mcol.b, h_mlp.b], writes=[h_mlp.b])
                  op("dve", lambda e: e.scalar_tensor_tensor(out=h_mlp[:, o4 + 3:o4 + 4], in0=fcol, scalar=0.25, in1=bcol, op0=ALU.mult, op1=ALU.mult),
                     reads=[h_mcol.b, h_mlp.b], writes=[h_mlp.b])
              st = wstage[0]
              dma(st[0:33, 0:64], hy_w1[l, :, :], writes=[st.b], q="pool")
              op("dve", lambda e: e.tensor_copy(out=hw1_b[:], in_=st[0:33, 0:64]), reads=[st.b], writes=[hw1_b.b])
              dma(st[0:64, 0:64], hy_w2[l, :, :], writes=[st.b], q="pool")
              op("dve", lambda e: e.tensor_copy(out=hw2_b[:], in_=st[0:64, 0:64]), reads=[st.b], writes=[hw2_b.b])
              chk('weights')
              if l == 0:
                  for c in range(2):
                      dma(scond[:, c, :], cond[c].rearrange("(k p) -> k p", p=128), writes=[scond.b])
                  op("act", lambda e: e.activation(out=scond[:], in_=scond[:], func=AF.Silu),
                     reads=[scond.b], writes=[scond.b])
                  for c in range(2):
                      p = psf()
                      op("pe", lambda e: e.transpose(p[:, 0:8], scond[:, c, :], ident_f[0:8, 0:8]),
                         reads=[scond.b, ident_f.b], writes=[p.b])
                      op("dve", lambda e: e.tensor_copy(out=scondT[:, c, :], in_=p[:, 0:8]),
                         reads=[p.b], writes=[scondT.b])
              dma(colst[:, :], ada_b[l].rearrange("(j p) -> j p", p=128), writes=[colst.b])
              p = psf()
              op("pe", lambda e: e.transpose(p[:, 0:24], colst[:, :], ident_f[0:24, 0:24]),
                 reads=[colst.b, ident_f.b], writes=[p.b])
              op("dve", lambda e: e.tensor_copy(out=colsT[:].rearrange("p a k -> p (a k)"), in_=p[:, 0:24]),
                 reads=[p.b], writes=[colsT.b])
              dma(nwst[:, :], norm_w[l].rearrange("(k p) -> k p", p=128), writes=[nwst.b])
              p = psf()
              op("pe", lambda e: e.transpose(p[:, 0:8], nwst[:, :], ident_f[0:8, 0:8]),
                 reads=[nwst.b, ident_f.b], writes=[p.b])
              op("dve", lambda e: e.tensor_copy(out=nwT[:], in_=p[:, 0:8]), reads=[p.b], writes=[nwT.b])

              pmod = modacc
              op("dve", lambda e: e.memset(modacc[:], 0.0), writes=[modacc.b])
              pk = psf()
              pg = [psf() for _ in range(4)]
              for k in range(8):
                  a = adast[k % 2]
                  dma(a[:, :], ada_w[l, k * 128:(k + 1) * 128, :], writes=[a.b])
                  for j in range(16):
                      op("pe", lambda e: e.matmul(pk[:, 2 * j:2 * j + 2], lhsT=a[:, j * 128:(j + 1) * 128],
                                                  rhs=scondT[:, :, k], start=True, stop=True),
                         reads=[a.b, scondT.b], writes=[pk.b])
                  op("dve", lambda e: e.tensor_tensor(out=modacc[:, 0:32], in0=modacc[:, 0:32], in1=pk[:, 0:32], op=ALU.add),
                     reads=[pk.b, modacc.b], writes=[modacc.b])
                  for c in range(2):
                      op("dve", lambda e: e.tensor_copy(out=screp[:], in_=scondT[:, c, k:k + 1].to_broadcast([128, 128])),
                         reads=[scondT.b], writes=[screp.b])
                      for hf in range(2):
                          pgt = pg[2 * c + hf]
                          op("pe", lambda e: e.matmul(pgt[:, :], lhsT=screp[:, :],
                                                      rhs=a[:, 2048 + hf * 512:2048 + (hf + 1) * 512],
                                                      start=(k == 0), stop=False),
                             reads=[a.b, screp.b], writes=[pgt.b])
              dma(rowst[0:1, 0:D], ada_b[l:l + 1, 2048:3072], writes=[rowst.b])
              for c in range(2):
                  for hf in range(2):
                      pgt = pg[2 * c + hf]
                      op("pe", lambda e: e.matmul(pgt[:, :], lhsT=ones_f[0:1, :], rhs=rowst[0:1, hf * 512:(hf + 1) * 512],
                                                  start=False, stop=True),
                         reads=[ones_f.b, rowst.b], writes=[pgt.b])
                      op("dve", lambda e: e.tensor_copy(out=gate_bc[c][:, hf * 512:(hf + 1) * 512], in_=pgt[:, :]),
                         reads=[pgt.b], writes=[gate_bc[c].b])
              pm = pmod[:, 0:32].rearrange("p (j c) -> p c j", c=2)
              for c in range(2):
                  op("dve", lambda e: e.tensor_tensor(out=shift[:, c, :], in0=pm[:, c, 0:8], in1=colsT[:, 0, :], op=ALU.add),
                     reads=[pmod.b, colsT.b], writes=[shift.b])
                  op("dve", lambda e: e.tensor_tensor(out=effsc[:, c, :], in0=pm[:, c, 8:16], in1=colsT[:, 1, :], op=ALU.add),
                     reads=[pmod.b, colsT.b], writes=[effsc.b])
                  op("dve", lambda e: e.scalar_tensor_tensor(out=effsc[:, c, :], in0=effsc[:, c, :], scalar=1.0,
                                                             in1=nwT[:, :], op0=ALU.add, op1=ALU.mult),
                     reads=[effsc.b, nwT.b], writes=[effsc.b])

              chk('ada')
              def gla_group(l, g):
                  c, T, L, sample = g["c"], g["T"], g["L"], g["sample"]
                  for t_ in g_QA + g_QB:
                      op("pool", lambda e: e.memset(t_[:], 0.0), writes=[t_.b])
                  tps = L // 128
                  nseq = T // L
                  if sample:
                      dma(wmix[:, :, :].rearrange("p k c -> p (k c)"), wc["gla"][0][:, :], reads=[wc["gla"][1]], writes=[wmix.b], q="pool")
                  else:
                      for k in range(8):
                          st = wstage[0]
                          dma(st[:, 0:800], w_in[l, k * 128:(k + 1) * 128, GLA0:GLA0 + 800], writes=[st.b], q="pool")
                          op("dve" if k % 2 == 0 else "pool", lambda e: e.tensor_copy(out=wmix[:, k, :], in_=st[:, 0:800]),
                             reads=[st.b], writes=[wmix.b])
                      dma(wc["gla"][0][:, :], wmix[:, :, :].rearrange("p k c -> p (k c)"), reads=[wmix.b], writes=[wc["gla"][1]])
                  for d in (1, 0):
                      mask_d = maskF if d == 0 else maskB_
                      for s_ in range(nseq):
                          S0, S1, S2 = g_S
                          if sample:
                              dma(S0[:, :], sgla[l, d, :, :], writes=[S0.b])
                          else:
                              op("dve", lambda e: e.memset(S0[:], 0.0), writes=[S0.b])
                          order = range(tps) if d == 0 else range(tps - 1, -1, -1)
                          for tt in order:
                              ti = s_ * tps + tt
                              r0 = ti * 128
                              pq_, pk_, pg_, pv_ = psf(), psf(), psf(), psf()
                              for k in range(8):
                                  op("pe", lambda e: e.matmul(pq_[:, 0:128], lhsT=wmix[:, k, 0:128], rhs=hT[:, k, r0:r0 + 128],
                                                              start=(k == 0), stop=(k == 7)), reads=[wmix.b, hT.b], writes=[pq_.b])
                              for k in range(8):
                                  op("pe", lambda e: e.matmul(pk_[:, 0:128], lhsT=wmix[:, k, 128:256], rhs=hT[:, k, r0:r0 + 128],
                                                              start=(k == 0), stop=(k == 7)), reads=[wmix.b, hT.b], writes=[pk_.b])
                              for k in range(8):
                                  op("pe", lambda e: e.matmul(pg_[0:32, 0:128], lhsT=wmix[:, k, 512:544], rhs=hT[:, k, r0:r0 + 128],
                                                              start=(k == 0), stop=(k == 7)), reads=[wmix.b, hT.b], writes=[pg_.b])
                              for k in range(8):
                                  op("pe", lambda e: e.matmul(pv_[:, 0:256], lhsT=hT[:, k, r0:r0 + 128], rhs=wmix[:, k, 256:512],
                                                              start=(k == 0), stop=(k == 7)), reads=[wmix.b, hT.b], writes=[pv_.b])
                              op("act", lambda e: e.activation(out=g_qf[:], in_=pq_[:, 0:128], func=AF.Copy), reads=[pq_.b], writes=[g_qf.b])
                              op("act", lambda e: e.activation(out=g_kf[:], in_=pk_[:, 0:128], func=AF.Copy), reads=[pk_.b], writes=[g_kf.b])
                              op("act", lambda e: e.activation(out=g_glr[:], in_=pg_[0:32, 0:128], func=AF.Copy), reads=[pg_.b], writes=[g_glr.b])
                              op("dve", lambda e: e.tensor_copy(out=g_vt[:], in_=pv_[:, 0:256]), reads=[pv_.b], writes=[g_vt.b])
                              pl_ = psf()
                              op("pe", lambda e: e.matmul(pl_[:, 0:128], lhsT=gwp[:, d, :], rhs=g_glr[:, :], start=True, stop=True),
                                 reads=[gwp.b, g_glr.b], writes=[pl_.b])
                              op("act", lambda e: e.activation(out=g_sp[:], in_=pl_[:, 0:128], func=AF.Exp, scale=-1.0, bias=ngb[:, d:d + 1]),
                                 reads=[pl_.b, ngb.b], writes=[g_sp.b])
                              op("act", lambda e: e.activation(out=g_sp[:], in_=g_sp[:], func=AF.Ln, bias=ones64[:, 0:1]),
                                 reads=[g_sp.b, gcon_t.b], writes=[g_sp.b])
                              for ch in range(2):
                                  c0 = 64 * ch
                                  op("dve", lambda e: e.tensor_tensor_scan(out=g_bcp[:, c0:c0 + 64], data0=ones64, data1=g_sp[:, c0:c0 + 64],
                                                                           initial=0.0, op0=ALU.mult, op1=ALU.add),
                                     reads=[g_sp.b, gcon_t.b, g_bcp.b], writes=[g_bcp.b])
                              if d == 1:
                                  for ch in range(2):
                                      c0 = 64 * ch
                                      op("dve", lambda e: e.tensor_copy(out=g_tot[:, ch:ch + 1], in_=g_bcp[:, c0 + 63:c0 + 64]),
                                         reads=[g_bcp.b, g_tot.b], writes=[g_tot.b])
                                  op("dve", lambda e: e.tensor_tensor(out=g_bcp[:], in0=g_sp[:], in1=g_bcp[:], op=ALU.subtract),
                                     reads=[g_sp.b, g_bcp.b], writes=[g_bcp.b])
                                  for ch in range(2):
                                      c0 = 64 * ch
                                      op("dve", lambda e: e.tensor_scalar(out=g_bcp[:, c0:c0 + 64], in0=g_bcp[:, c0:c0 + 64],
                                                                          scalar1=g_tot[:, ch:ch + 1], scalar2=None, op0=ALU.add),
                                         reads=[g_bcp.b, g_tot.b], writes=[g_bcp.b])
                              for ch in range(2):
                                  col = 64 * ch + (63 if d == 0 else 0)
                                  op("dve", lambda e: e.tensor_scalar(out=g_nbl[:, ch:ch + 1], in0=g_bcp[:, col:col + 1], scalar1=-1.0 / 16,
                                                                      scalar2=None, op0=ALU.mult), reads=[g_bcp.b, g_nbl.b], writes=[g_nbl.b])
                              op("act", lambda e: e.activation(out=g_eb[:], in_=g_nbl[:], func=AF.Exp), reads=[g_nbl.b], writes=[g_eb.b])
                              op("act", lambda e: e.activation(out=g_e1[:], in_=g_bcp[:], func=AF.Exp, scale=-1.0 / 16), reads=[g_bcp.b], writes=[g_e1.b])
                              op("act", lambda e: e.activation(out=g_e2[:], in_=g_bcp[:], func=AF.Exp, scale=1.0 / 16), reads=[g_bcp.b], writes=[g_e2.b])
                              for ch in range(2):
                                  c0 = 64 * ch
                                  op("act", lambda e: e.activation(out=g_e3[:, c0:c0 + 64], in_=g_bcp[:, c0:c0 + 64], func=AF.Exp, scale=1.0 / 16,
                                                                   bias=g_nbl[:, ch:ch + 1]), reads=[g_bcp.b, g_nbl.b, g_e3.b], writes=[g_e3.b])
                              op("dve", lambda e: e.scalar_tensor_tensor(out=g_qt[:], in0=g_qf[:], scalar=GLA_DK ** -0.5, in1=g_e1[:],
                                                                         op0=ALU.mult, op1=ALU.mult), reads=[g_qf.b, g_e1.b], writes=[g_qt.b])
                              op("dve", lambda e: e.tensor_tensor(out=g_kt[:], in0=g_kf[:], in1=g_e2[:], op=ALU.mult),
                                 reads=[g_kf.b, g_e2.b], writes=[g_kt.b])
                              op("dve", lambda e: e.tensor_tensor(out=g_khT[:], in0=g_kf[:], in1=g_e3[:], op=ALU.mult),
                                 reads=[g_kf.b, g_e3.b], writes=[g_khT.b])
                              pt_ = psb()
                              op("pe", lambda e: e.transpose(pt_[:, 0:128], g_khT[:, :], ident_b[:, :]), reads=[g_khT.b, ident_b.b], writes=[pt_.b])
                              op("dve", lambda e: e.tensor_scalar(out=g_khA[:], in0=pt_[:, 0:128], scalar1=halfm[:, 0:1], scalar2=None, op0=ALU.mult),
                                 reads=[pt_.b, gcon_t.b], writes=[g_khA.b])
                              op("dve", lambda e: e.tensor_scalar(out=g_khB[:], in0=pt_[:, 0:128], scalar1=halfm[:, 1:2], scalar2=None, op0=ALU.mult),
                                 reads=[pt_.b, gcon_t.b], writes=[g_khB.b])
                              for h in range(4):
                                  op("pool", lambda e: e.tensor_scalar(out=g_km[h][:], in0=g_kt[:], scalar1=headm[:, h:h + 1], scalar2=None, op0=ALU.mult),
                                     reads=[g_kt.b, gcon_t.b], writes=[g_km[h].b])
                                  op("pool", lambda e: e.tensor_scalar(out=g_QA[h][:, 0:64], in0=g_qt[:, 0:64], scalar1=headm[:, h:h + 1], scalar2=None,
                                                                       op0=ALU.mult), reads=[g_qt.b, gcon_t.b], writes=[g_QA[h].b])
                                  op("pool", lambda e: e.tensor_scalar(out=g_QB[h][:, 64:128], in0=g_qt[:, 64:128], scalar1=headm[:, h:h + 1], scalar2=None,
                                                                       op0=ALU.mult), reads=[g_qt.b, gcon_t.b], writes=[g_QB[h].b])
                              for ch, kh in ((0, g_khA), (1, g_khB)):
                                  pd_ = psf()
                                  op("pe", lambda e: e.matmul(pd_[:, 0:256], lhsT=kh[:, :], rhs=g_vt[:, :], start=True, stop=True),
                                     reads=[kh.b, g_vt.b], writes=[pd_.b])
                                  op("dve", lambda e: e.tensor_tensor(out=g_dS[:], in0=pd_[:, 0:256], in1=blockm, op=ALU.mult),
                                     reads=[pd_.b, gcon_t.b], writes=[g_dS.b])
                                  op("dve", lambda e: e.tensor_reduce(out=g_dSr[ch][:], in_=g_dS[:].rearrange("p (h v) -> p v h", h=4),
                                                                      axis=AX.X, op=ALU.add), reads=[g_dS.b], writes=[g_dSr[ch].b])
                              first, second = (0, 1) if d == 0 else (1, 0)
                              op("dve", lambda e: e.tensor_copy(out=g_Sb[0][:], in_=S0[:]), reads=[S0.b], writes=[g_Sb[0].b])
                              op("dve", lambda e: e.scalar_tensor_tensor(out=S1[:], in0=S0[:], scalar=g_eb[:, first:first + 1], in1=g_dSr[first][:],
                                                                         op0=ALU.mult, op1=ALU.add), reads=[S0.b, g_eb.b, g_dSr[first].b], writes=[S1.b])
                              op("dve", lambda e: e.tensor_copy(out=g_Sb[1][:], in_=S1[:]), reads=[S1.b], writes=[g_Sb[1].b])
                              op("dve", lambda e: e.scalar_tensor_tensor(out=S2[:], in0=S1[:], scalar=g_eb[:, second:second + 1], in1=g_dSr[second][:],
                                                                         op0=ALU.mult, op1=ALU.add), reads=[S1.b, g_eb.b, g_dSr[second].b], writes=[S2.b])
                              SbA, SbB = (g_Sb[0], g_Sb[1]) if d == 0 else (g_Sb[1], g_Sb[0])
                              po = p_acc
                              for h in range(4):
                                  pa_ = psf()
                                  op("pe", lambda e: e.matmul(pa_[:, 0:128], lhsT=g_km[h][:, :], rhs=g_qt[:, :], start=True, stop=True),
                                     reads=[g_km[h].b, g_qt.b], writes=[pa_.b])
                                  op("dve", lambda e: e.tensor_tensor(out=g_attm[:], in0=pa_[:, 0:128], in1=mask_d, op=ALU.mult),
                                     reads=[pa_.b, gcon_t.b], writes=[g_attm.b])
                                  op("pe", lambda e: e.matmul(po[:, 64 * h:64 * h + 64], lhsT=g_attm[:, :], rhs=g_vt[:, 64 * h:64 * h + 64],
                                                              start=True, stop=False), reads=[g_attm.b, g_vt.b], writes=[po.b])
                                  op("pe", lambda e: e.matmul(po[:, 64 * h:64 * h + 64], lhsT=g_QA[h][:, :], rhs=SbA[:, :], start=False, stop=False),
                                     reads=[g_QA[h].b, SbA.b], writes=[po.b])
                                  op("pe", lambda e: e.matmul(po[:, 64 * h:64 * h + 64], lhsT=g_QB[h][:, :], rhs=SbB[:, :], start=False, stop=True),
                                     reads=[g_QB[h].b, SbB.b], writes=[po.b])
                              if d == 1:
                                  op("act", lambda e: e.activation(out=o_bwd[:, ti, :], in_=po[:, 0:256], func=AF.Copy), reads=[po.b], writes=[o_bwd.b])
                              else:
                                  op("dve", lambda e: e.tensor_tensor(out=g_os[:], in0=po[:, 0:256], in1=o_bwd[:, ti, :], op=ALU.add),
                                     reads=[po.b, o_bwd.b], writes=[g_os.b])
                                  op("act", lambda e: e.activation(out=g_sq[:], in_=g_os[:], func=AF.Square), reads=[g_os.b], writes=[g_sq.b])
                                  op("dve", lambda e: e.tensor_reduce(out=g_ss[:], in_=g_sq[:].rearrange("p (h v) -> p h v", h=4), axis=AX.X, op=ALU.add),
                                     reads=[g_sq.b], writes=[g_ss.b])
                                  rstd_from_ss(g_rs, g_ss, 64)
                                  os3 = g_os[:].rearrange("p (h v) -> p h v", h=4)
                                  op("dve", lambda e: e.tensor_tensor(out=os3, in0=os3, in1=g_rs[:, :].unsqueeze(2).to_broadcast([128, 4, 64]), op=ALU.mult),
                                     reads=[g_os.b, g_rs.b], writes=[g_os.b])
                                  op("dve", lambda e: e.tensor_tensor(out=os3, in0=os3, in1=bc_gn[:, :].unsqueeze(1).to_broadcast([128, 4, 64]), op=ALU.mult),
                                     reads=[g_os.b, bc_gn.b], writes=[g_os.b])
                                  pgg = psf()
                                  for k in range(8):
                                      op("pe", lambda e: e.matmul(pgg[:, 0:256], lhsT=hT[:, k, r0:r0 + 128], rhs=wmix[:, k, 544:800],
                                                                  start=(k == 0), stop=(k == 7)), reads=[hT.b, wmix.b], writes=[pgg.b])
                                  op("act", lambda e: e.activation(out=g_sil[:], in_=pgg[:, 0:256], func=AF.Silu), reads=[pgg.b], writes=[g_sil.b])
                                  op("dve", lambda e: e.tensor_tensor(out=g_og[:], in0=g_os[:], in1=g_sil[:], op=ALU.mult),
                                     reads=[g_os.b, g_sil.b], writes=[g_og.b])
                                  pbr_ = psb()
                                  for k2 in range(2):
                                      op("pe", lambda e: e.transpose(pbr_[:, k2 * 128:(k2 + 1) * 128], g_og[:, k2 * 128:(k2 + 1) * 128], ident_b[:, :]),
                                         reads=[g_og.b, ident_b.b], writes=[pbr_.b])
                                  op("act", lambda e: e.activation(out=brT[:, 6:8, r0:r0 + 128], in_=pbr_[:, 0:256].rearrange("p (k t) -> p k t", k=2),
                                                                   func=AF.Copy), reads=[pbr_.b], writes=[brT.b])
                              S0, S1, S2 = S2, S0, S1
                          if not sample:
                              dma(o_gla[s_, l, d].rearrange("h k v -> (h k) v"), S0[:, :], reads=[S0.b])

              def s5_group(l, g):
                  c, T, L, sample = g["c"], g["T"], g["L"], g["sample"]
                  nseq, Mseq, M, nb = T // L, L // 8, T // 8, T // 512
                  nsteps = Mseq.bit_length() - 1
                  reuse = sample
                  P_ = s_par
                  pb_ = s_par["are"].b
                  ybank = pf[1:1 + nb]
                  ptmp = pf[0]
                  if sample:
                      dma(s_wmix[:, :, :].rearrange("p k c -> p (k c)"), wc["s5"][0][:, :], reads=[wc["s5"][1]], writes=[s_wmix.b], q="pool")
                  else:
                      for k in range(8):
                          st_ = wstage[0]
                          dma(st_[:, 0:512], w_in[l, k * 128:(k + 1) * 128, S50:S50 + 512], writes=[st_.b], q="pool")
                          op("dve" if k % 2 == 0 else "pool", lambda e: e.tensor_copy(out=s_wmix[:, k, :], in_=st_[:, 0:512]),
                             reads=[st_.b], writes=[s_wmix.b])
                      dma(wc["s5"][0][:, :], s_wmix[:, :, :].rearrange("p k c -> p (k c)"), reads=[s_wmix.b], writes=[wc["s5"][1]])
                  for hf in range(2):
                      for bk in range(nb):
                          for k in range(8):
                              op("pe", lambda e: e.matmul(ptmp[:, :], lhsT=s_wmix[:, k, hf * 128:(hf + 1) * 128], rhs=hT[:, k, bk * 512:(bk + 1) * 512],
                                                          start=(k == 0), stop=(k == 7)), reads=[s_wmix.b, hT.b], writes=[ptmp.b])
                          op("act", lambda e: e.activation(out=s_uT[:, hf, bk * 512:(bk + 1) * 512], in_=ptmp[:, :], func=AF.Copy),
                             reads=[ptmp.b], writes=[s_uT.b])
                  for hf in range(2):
                      for bk in range(nb):
                          op("pe", lambda e: e.matmul(ybank[bk][:, :], lhsT=s_Dd[hf][:, :], rhs=s_uT[:, hf, bk * 512:(bk + 1) * 512],
                                                      start=True, stop=False), reads=[s_Dd[hf].b, s_uT.b], writes=[ybank[bk].b])
                      if not reuse:
                          op("dve", lambda e: e.memset(s_Kacc[:], 0.0), writes=[s_Kacc.b])
                      for d in range(2):
                          for ri, src_ in ((0, s5_cre), (1, s5_cim)):
                              if not reuse:
                                  dma(s_C[ri][:, :], src_[l, d, 8 * hf:8 * hf + 8].rearrange("g i p -> (g i) p"), writes=[s_C[ri].b], q="pool")
                          for k4 in range(4):
                              st = 4 * hf + k4
                              td = d * 8 + st
                              ar, ai, nai = P_["ar1"][:, td:td + 1], P_["ai1"][:, td:td + 1], P_["nai1"][:, td:td + 1]
                              kr_, ki_, nki_ = P_["kr"][:, td:td + 1], P_["ki"][:, td:td + 1], P_["nki"][:, td:td + 1]
                              if reuse:
                                  dma(s_W[:, :, :].rearrange("p a c -> p (a c)"), s5w[td], reads=[s5w_b], writes=[s_W.b], q="pool")
                              else:
                                  for ri, src_ in ((0, s5_bre), (1, s5_bim)):
                                      dma(s_B[ri][:, :], src_[l, d, 2 * st:2 * st + 2].rearrange("g p i -> (g p) i"), writes=[s_B[ri].b], q="pool")
                                  op("dve", lambda e: e.tensor_scalar(out=s_bb[0][:], in0=s_B[0][:], scalar1=kr_, scalar2=None, op0=ALU.mult),
                                     reads=[s_B[0].b, pb_], writes=[s_bb[0].b])
                                  op("dve", lambda e: e.tensor_scalar(out=s_bb[1][:], in0=s_B[0][:], scalar1=ki_, scalar2=None, op0=ALU.mult),
                                     reads=[s_B[0].b, pb_], writes=[s_bb[1].b])
                                  op("dve", lambda e: e.scalar_tensor_tensor(out=s_bb[0][:], in0=s_B[1][:], scalar=nki_, in1=s_bb[0][:], op0=ALU.mult, op1=ALU.add),
                                     reads=[s_B[1].b, pb_, s_bb[0].b], writes=[s_bb[0].b])
                                  op("dve", lambda e: e.scalar_tensor_tensor(out=s_bb[1][:], in0=s_B[1][:], scalar=kr_, in1=s_bb[1][:], op0=ALU.mult, op1=ALU.add),
                                     reads=[s_B[1].b, pb_, s_bb[1].b], writes=[s_bb[1].b])
                                  mB3 = scon_t[:, 128 * k4:128 * k4 + 128].rearrange("p (g i) -> p g i", g=8)
                                  for ri in range(2):
                                      op("dve", lambda e: e.tensor_tensor(out=s_Xm[0][ri][:].rearrange("p (g i) -> p g i", g=8), in0=mB3,
                                                                          in1=s_bb[ri][:, :].unsqueeze(1).to_broadcast([128, 8, 16]), op=ALU.mult),
                                         reads=[scon_t.b, s_bb[ri].b], writes=[s_Xm[0][ri].b])
                                  mC3 = scon_t[:, 512 + 128 * k4:512 + 128 * k4 + 128].rearrange("p (a q) -> p a q", a=2)
                                  for ri in range(2):
                                      op("dve", lambda e: e.tensor_tensor(out=s_Cp[:].rearrange("p (a q) -> p a q", a=2), in0=mC3,
                                                                          in1=s_C[ri][:, :].unsqueeze(1).to_broadcast([128, 2, 64]), op=ALU.mult),
                                         reads=[scon_t.b, s_C[ri].b], writes=[s_Cp.b])
                                      op("pe", lambda e: e.transpose(ptmp[:, 0:128], s_Cp[:, :], ident_f[:, :]), reads=[s_Cp.b, ident_f.b], writes=[ptmp.b])
                                      op("dve", lambda e: e.tensor_copy(out=s_Cm[0][ri][:], in_=ptmp[:, 0:128]), reads=[ptmp.b], writes=[s_Cm[0][ri].b])

                                  def half1(cur, nxt):
                                      op("dve", lambda e: e.tensor_scalar(out=nxt[0][:], in0=cur[0][:], scalar1=ar, scalar2=None, op0=ALU.mult),
                                         reads=[cur[0].b, pb_], writes=[nxt[0].b])
                                      op("dve", lambda e: e.tensor_scalar(out=nxt[1][:], in0=cur[0][:], scalar1=ai, scalar2=None, op0=ALU.mult),
                                         reads=[cur[0].b, pb_], writes=[nxt[1].b])

                                  def half2(cur, nxt):
                                      op("dve", lambda e: e.scalar_tensor_tensor(out=nxt[0][:], in0=cur[1][:], scalar=nai, in1=nxt[0][:], op0=ALU.mult, op1=ALU.add),
                                         reads=[cur[1].b, pb_, nxt[0].b], writes=[nxt[0].b])
                                      op("dve", lambda e: e.scalar_tensor_tensor(out=nxt[1][:], in0=cur[1][:], scalar=ar, in1=nxt[1][:], op0=ALU.mult, op1=ALU.add),
                                         reads=[cur[1].b, pb_, nxt[1].b], writes=[nxt[1].b])

                                  for k in range(9):
                                      Xc, Xn = s_Xm[k % 2], s_Xm[(k + 1) % 2]
                                      Cc, Cn = s_Cm[k % 2], s_Cm[(k + 1) % 2]
                                      if k < 8:
                                          for ri in range(2):
                                              op("act", lambda e: e.activation(out=s_Xb[k][ri][:], in_=Xc[ri][:], func=AF.Copy),
                                                 reads=[Xc[ri].b], writes=[s_Xb[k][ri].b])
                                      op("act", lambda e: e.activation(out=s_Wc[k][0][:], in_=Cc[0][:], func=AF.Copy), reads=[Cc[0].b], writes=[s_Wc[k][0].b])
                                      op("act", lambda e: e.activation(out=s_Wc[k][1][:], in_=Cc[1][:], func=AF.Copy, scale=-1.0),
                                         reads=[Cc[1].b], writes=[s_Wc[k][1].b])
                                      if k < 7:
                                          half1(Xc, Xn)
                                      if k < 8:
                                          half1(Cc, Cn)
                                      if k < 7:
                                          half2(Xc, Xn)
                                      if k < 8:
                                          half2(Cc, Cn)
                                      if k < 8:
                                          pt_ = psb()
                                          for ri in range(2):
                                              op("pe", lambda e: e.transpose(pt_[:, ri * 128:(ri + 1) * 128], s_Xb[k][ri][:, :], ident_b[:, :]),
                                                 reads=[s_Xb[k][ri].b, ident_b.b], writes=[pt_.b])
                                          for ri in range(2):
                                              op("act", lambda e: e.activation(out=s_Wb[k][ri][:], in_=pt_[:, ri * 128:(ri + 1) * 128], func=AF.Copy),
                                                 reads=[pt_.b], writes=[s_Wb[k][ri].b])
                                  for q4 in range(2):
                                      for dd in range(4):
                                          dl = 4 * q4 + dd
                                          op("pe", lambda e: e.matmul(ptmp[:, dd * 128:(dd + 1) * 128], lhsT=s_Xb[dl][0][:, :], rhs=s_Wc[0][0][:, :],
                                                                      start=True, stop=False), reads=[s_Xb[dl][0].b, s_Wc[0][0].b], writes=[ptmp.b])
                                          op("pe", lambda e: e.matmul(ptmp[:, dd * 128:(dd + 1) * 128], lhsT=s_Xb[dl][1][:, :], rhs=s_Wc[0][1][:, :],
                                                                      start=False, stop=True), reads=[s_Xb[dl][1].b, s_Wc[0][1].b], writes=[ptmp.b])
                                      ka = s_Kacc[:, d * 8 + 4 * q4:d * 8 + 4 * q4 + 4, :]
                                      op("dve", lambda e: e.tensor_tensor(out=ka, in0=ka, in1=ptmp[:, :].rearrange("p (a c) -> p a c", a=4), op=ALU.add),
                                         reads=[ptmp.b, s_Kacc.b], writes=[s_Kacc.b])
                                  dma(s5w[td], s_W[:, :, :].rearrange("p a c -> p (a c)"), reads=[s_W.b], writes=[s5w_b])
                              ub = s_uT[:, hf, 0:T].rearrange("p (m j) -> p m j", j=8)
                              for ri in range(2):
                                  for s_ in range(8):
                                      kk = 7 - s_ if d == 0 else s_
                                      op("pe", lambda e: e.matmul(p_acc[:, ri * 256:ri * 256 + M], lhsT=s_Wb[kk][ri][:, :], rhs=ub[:, :, s_],
                                                                  start=(s_ == 0), stop=(s_ == 7)), reads=[s_Wb[kk][ri].b, s_uT.b], writes=[p_acc.b])
                              for ri in range(2):
                                  op("dve", lambda e: e.tensor_copy(out=s_XA[ri][:, 0:M], in_=p_acc[:, ri * 256:ri * 256 + M]),
                                     reads=[p_acc.b], writes=[s_XA[ri].b])
                              if sample:
                                  dma(s_h0[:, :], ss5[l, d, 128 * st:128 * st + 128, :], writes=[s_h0.b], q="pool")
                                  a8r, a8i, na8i = s_pwr[:, 0, td:td + 1], s_pwi[:, 0, td:td + 1], s_npwi[:, 0, td:td + 1]
                                  cc = 0 if d == 0 else M - 1
                                  for ri, (sa, sb_) in enumerate(((a8r, na8i), (a8i, a8r))):
                                      xc = s_XA[ri][:, cc:cc + 1]
                                      op("dve", lambda e: e.scalar_tensor_tensor(out=xc, in0=s_h0[:, 0:1], scalar=sa, in1=xc, op0=ALU.mult, op1=ALU.add),
                                         reads=[s_h0.b, pb_, s_XA[ri].b], writes=[s_XA[ri].b])
                                      op("dve", lambda e: e.scalar_tensor_tensor(out=xc, in0=s_h0[:, 1:2], scalar=sb_, in1=xc, op0=ALU.mult, op1=ALU.add),
                                         reads=[s_h0.b, pb_, s_XA[ri].b], writes=[s_XA[ri].b])
                              src, dst = s_XA, s_XB
                              v3 = lambda t_: t_[:, 0:M].rearrange("p (s m) -> p s m", s=nseq)
                              for k in range(nsteps):
                                  sh = 1 << k
                                  pr, pi, npi = s_pwr[:, k, td:td + 1], s_pwi[:, k, td:td + 1], s_npwi[:, k, td:td + 1]
                                  if d == 0:
                                      lo, ls, kp = slice(sh, Mseq), slice(0, Mseq - sh), slice(0, sh)
                                  else:
                                      lo, ls, kp = slice(0, Mseq - sh), slice(sh, Mseq), slice(Mseq - sh, Mseq)
                                  for ri in range(2):
                                      op("act", lambda e: e.activation(out=v3(dst[ri])[:, :, kp], in_=v3(src[ri])[:, :, kp], func=AF.Copy),
                                         reads=[src[ri].b, dst[ri].b], writes=[dst[ri].b])
                                  for ri, m1 in enumerate((pr, pr)):
                                      op("dve", lambda e: e.scalar_tensor_tensor(out=v3(dst[ri])[:, :, lo], in0=v3(src[ri])[:, :, ls], scalar=m1,
                                                                                 in1=v3(src[ri])[:, :, lo], op0=ALU.mult, op1=ALU.add),
                                         reads=[src[ri].b, pb_, dst[ri].b], writes=[dst[ri].b])
                                  for ri, m2 in enumerate((npi, pi)):
                                      other = src[1 - ri]
                                      op("dve", lambda e: e.scalar_tensor_tensor(out=v3(dst[ri])[:, :, lo], in0=v3(other)[:, :, ls], scalar=m2,
                                                                                 in1=v3(dst[ri])[:, :, lo], op0=ALU.mult, op1=ALU.add),
                                         reads=[other.b, pb_, dst[ri].b], writes=[dst[ri].b])
                                  src, dst = dst, src
                              Hh = src
                              if not sample:
                                  for s_ in range(nseq):
                                      cc = s_ * Mseq + (Mseq - 1 if d == 0 else 0)
                                      for ri in range(2):
                                          op("dve", lambda e: e.tensor_copy(out=s_fin[:, ri:ri + 1], in_=Hh[ri][:, cc:cc + 1]),
                                             reads=[Hh[ri].b, s_fin.b], writes=[s_fin.b])
                                      dma(o_s5[s_, l, d].rearrange("g p r -> (g p) r")[128 * st:128 * st + 128, :], s_fin[:, :], reads=[s_fin.b])
                              W1 = Mseq + 1
                              for ri in range(2):
                                  hv = s_Hs[ri][:, 0:nseq * W1].rearrange("p (s m) -> p s m", s=nseq)
                                  body = hv[:, :, 1:W1] if d == 0 else hv[:, :, 0:Mseq]
                                  edge = hv[:, :, 0:1] if d == 0 else hv[:, :, Mseq:W1]
                                  op("act", lambda e: e.activation(out=body, in_=v3(Hh[ri]), func=AF.Copy), reads=[Hh[ri].b, s_Hs[ri].b], writes=[s_Hs[ri].b])
                                  if sample:
                                      op("dve", lambda e: e.tensor_copy(out=edge, in_=s_h0[:, ri:ri + 1].unsqueeze(1)), reads=[s_h0.b, s_Hs[ri].b], writes=[s_Hs[ri].b])
                                  else:
                                      op("dve", lambda e: e.memset(edge, 0.0), reads=[s_Hs[ri].b], writes=[s_Hs[ri].b])
                              off = 0 if d == 0 else 1
                              for bk in range(nb):
                                  for j in range(8):
                                      kk = j + 1 if d == 0 else 8 - j
                                      for ri in range(2):
                                          hv = s_Hs[ri][:, 0:nseq * W1].rearrange("p (s m) -> p s m", s=nseq)
                                          if nseq == 1:
                                              rhs_ = hv[:, 0, off + 64 * bk:off + 64 * bk + 64]
                                              out_ = ybank[bk][:, :].rearrange("p (m j) -> p m j", j=8)[:, :, j]
                                          else:
                                              rhs_ = hv[:, 2 * bk:2 * bk + 2, off:off + Mseq]
                                              out_ = ybank[bk][:, :].rearrange("p (s m j) -> p s m j", s=2, j=8)[:, :, :, j]
                                          op("pe", lambda e: e.matmul(out_, lhsT=s_Wc[kk][ri][:, :], rhs=rhs_, start=False, stop=False),
                                             reads=[s_Wc[kk][ri].b, s_Hs[ri].b], writes=[ybank[bk].b])
                      if reuse:
                          dma(s_Kbd[:, :, :].rearrange("p a c -> p (a c)"), s5k[hf], reads=[s5k_b], writes=[s_Kbd.b], q="pool")
                      else:
                          for a_ in range(16):
                              op("act", lambda e: e.activation(out=s_Kbd[:, a_, :], in_=s_Kacc[:, a_, :], func=AF.Copy), reads=[s_Kacc.b], writes=[s_Kbd.b])
                          dma(s5k[hf], s_Kbd[:, :, :].rearrange("p a c -> p (a c)"), reads=[s_Kbd.b], writes=[s5k_b])
                      for bk in range(nb):
                          u3 = s_uT[:, hf, bk * 512:(bk + 1) * 512].rearrange("p (m j) -> p m j", j=8)
                          y3 = ybank[bk][:, :].rearrange("p (m j) -> p m j", j=8)
                          for dl in range(8):
                              op("pe", lambda e: e.matmul(y3[:, :, dl:8], lhsT=s_Kbd[:, dl, :], rhs=u3[:, :, 0:8 - dl], start=False, stop=False),
                                 reads=[s_Kbd.b, s_uT.b], writes=[ybank[bk].b])
                              op("pe", lambda e: e.matmul(y3[:, :, 0:8 - dl], lhsT=s_Kbd[:, 8 + dl, :], rhs=u3[:, :, dl:8], start=False, stop=(dl == 7)),
                                 reads=[s_Kbd.b, s_uT.b], writes=[ybank[bk].b])
                      for bk in range(nb):
                          yb_ = ybank[bk]
                          op("act", lambda e: e.activation(out=s_ya[:], in_=yb_[:, :], func=AF.Square), reads=[yb_.b], writes=[s_ya.b])
                          op("dve", lambda e: e.tensor_scalar(out=s_ya[:], in0=s_ya[:], scalar1=0.044715, scalar2=1.0, op0=ALU.mult, op1=ALU.add),
                             reads=[s_ya.b], writes=[s_ya.b])
                          op("dve", lambda e: e.tensor_tensor(out=s_ya[:], in0=s_ya[:], in1=yb_[:, :], op=ALU.mult), reads=[s_ya.b, yb_.b], writes=[s_ya.b])
                          op("act", lambda e: e.activation(out=s_yb[:], in_=s_ya[:], func=AF.Sigmoid, scale=1.5957691216057308),
                             reads=[s_ya.b], writes=[s_yb.b])
                          op("dve", lambda e: e.tensor_tensor(out=s_gT[:, hf, bk * 512:(bk + 1) * 512], in0=s_yb[:], in1=yb_[:, :], op=ALU.mult),
                             reads=[s_yb.b, yb_.b], writes=[s_gT.b])
                  for oh in range(2):
                      for bk in range(nb):
                          tok = slice(bk * 512, (bk + 1) * 512)
                          for ih in range(2):
                              op("pe", lambda e: e.matmul(ptmp[:, :], lhsT=s_glu[:, ih, oh * 128:(oh + 1) * 128], rhs=s_gT[:, ih, tok],
                                                          start=(ih == 0), stop=(ih == 1)), reads=[s_glu.b, s_gT.b], writes=[ptmp.b])
                          op("act", lambda e: e.activation(out=s_ya[:], in_=ptmp[:, :], func=AF.Sigmoid, bias=s_gbcol[:, oh:oh + 1]),
                             reads=[ptmp.b, s_gbcol.b], writes=[s_ya.b])
                          for k in range(8):
                              op("pe", lambda e: e.matmul(p_acc[:, :], lhsT=s_wmix[:, k, 256 + oh * 128:256 + (oh + 1) * 128], rhs=hT[:, k, tok],
                                                          start=(k == 0), stop=(k == 7)), reads=[s_wmix.b, hT.b], writes=[p_acc.b])
                          op("act", lambda e: e.activation(out=s_yb[:], in_=p_acc[:, :], func=AF.Silu), reads=[p_acc.b], writes=[s_yb.b])
                          op("dve", lambda e: e.tensor_tensor(out=s_ya[:], in0=s_ya[:], in1=s_gT[:, oh, tok], op=ALU.mult),
                             reads=[s_ya.b, s_gT.b], writes=[s_ya.b])
                          op("dve", lambda e: e.tensor_tensor(out=brT[:, 4 + oh, tok], in0=s_ya[:], in1=s_yb[:], op=ALU.mult),
                             reads=[s_ya.b, s_yb.b], writes=[brT.b])

              def hy_group(l, g):
                  c, T, L, sample = g["c"], g["T"], g["L"], g["sample"]
                  tb = htab["s" if sample else "p"]
                  nseq, ntt = T // L, L // 128
                  nfb = ntt + 1
                  wbase = 136 if sample else 128
                  NB = min(512, L)
                  nbank = L // NB
                  cos_cb = tb["cos"].rearrange("(a p) j -> p a j", p=128)
                  sin_cb = tb["sin"].rearrange("(a p) j -> p a j", p=128)

                  def sin_layer(pp_, li, dst, w):
                      o4 = 4 * li
                      op("act", lambda e: e.activation(out=h_s2[:, 0:w], in_=pp_[0:64, 0:w], func=AF.Sin, scale=h_mlp[:, o4:o4 + 1],
                                                       bias=h_mlp[:, o4 + 1:o4 + 2]), reads=[pp_.b, h_mlp.b], writes=[h_s2.b])
                      op("act", lambda e: e.activation(out=h_s4[:, 0:w], in_=pp_[0:64, 0:w], func=AF.Sin, scale=h_mlp[:, o4 + 2:o4 + 3],
                                                       bias=h_mlp[:, o4 + 3:o4 + 4]), reads=[pp_.b, h_mlp.b], writes=[h_s4.b])
                      op("dve", lambda e: e.tensor_tensor(out=h_s4[:, 0:w], in0=h_s4[:, 0:w], in1=h_s4[:, 0:w], op=ALU.mult),
                         reads=[h_s4.b], writes=[h_s4.b])
                      op("dve", lambda e: e.tensor_scalar(out=h_s4[:, 0:w], in0=h_s4[:, 0:w], scalar1=-2.0, scalar2=1.0, op0=ALU.mult, op1=ALU.add),
                         reads=[h_s4.b], writes=[h_s4.b])
                      op("dve", lambda e: e.scalar_tensor_tensor(out=dst, in0=h_s2[:, 0:w], scalar=2.0, in1=h_s4[:, 0:w], op0=ALU.mult, op1=ALU.mult),
                         reads=[h_s2.b, h_s4.b], writes=[h_hid[li].b])

                  for c0 in range(0, L, 512):
                      w = min(512, L - c0)
                      dma(h_feat[:, 0:w], tb["feat"][:, c0:c0 + w], writes=[h_feat.b], q="pool")
                      pp_ = psf()
                      op("pe", lambda e: e.matmul(pp_[0:64, 0:w], lhsT=hw1_b[:, :], rhs=h_feat[:, 0:w], start=True, stop=True),
                         reads=[hw1_b.b, h_feat.b], writes=[pp_.b])
                      sin_layer(pp_, 0, h_hid[0][:, c0:c0 + w], w)
                  for c0 in range(0, L, 512):
                      w = min(512, L - c0)
                      pp_ = psf()
                      op("pe", lambda e: e.matmul(pp_[0:64, 0:w], lhsT=hw2_b[:, :], rhs=h_hid[0][:, c0:c0 + w], start=True, stop=True),
                         reads=[hw2_b.b, h_hid[0].b], writes=[pp_.b])
                      sin_layer(pp_, 1, h_hid[1][:, c0:c0 + w], w)

                  if not sample:
                      for b in range(ntt):
                          dma(h_tc[b][:, 0:ntt, :], cos_cb[:, :, b * 128:(b + 1) * 128], writes=[h_tc[b].b], q="pool")
                          dma(h_ts[b][:, 0:ntt, :], sin_cb[:, :, b * 128:(b + 1) * 128], writes=[h_ts[b].b], q="pool")
                          dma(h_cr[b][:, 0:NB], tb["cos"][b * 128:(b + 1) * 128, 0:NB], writes=[h_cr[b].b], q="pool")
                          dma(h_sr[b][:, 0:NB], tb["sin"][b * 128:(b + 1) * 128, 0:NB], writes=[h_sr[b].b], q="pool")
                  hy_own = sample and l == DEPTH - 1
                  for hc in range(2):
                      r_ = 0
                      for k in range(3):
                          for wh in range(3):
                              dma(h_rows[r_:r_ + 1, :], hy_cw[l, k:k + 1, wh * 256 + hc * 128:wh * 256 + hc * 128 + 128], reads=[h_rows.b], writes=[h_rows.b])
                              r_ += 1
                      for wh in range(3):
                          dma(h_rows[r_:r_ + 1, :], hy_cb[l:l + 1, wh * 256 + hc * 128:wh * 256 + hc * 128 + 128], reads=[h_rows.b], writes=[h_rows.b])
                          r_ += 1
                      for o in range(2):
                          dma(h_rows[r_:r_ + 1, :], hy_bi[l, o:o + 1, hc * 128:hc * 128 + 128], reads=[h_rows.b], writes=[h_rows.b])
                          r_ += 1
                      pp_ = psf()
                      op("pe", lambda e: e.transpose(pp_[:, 0:14], h_rows[:, :], ident_f[0:14, 0:14]), reads=[h_rows.b, ident_f.b], writes=[pp_.b])
                      op("dve", lambda e: e.tensor_copy(out=h_colp[:], in_=pp_[:, 0:14]), reads=[pp_.b], writes=[h_colp.b])
                      if sample:
                          dma(h_wmix[:, :, :].rearrange("p k c -> p (k c)"), wc["hy%d" % hc][0][:, :], reads=[wc["hy%d" % hc][1]], writes=[h_wmix.b], q="pool")
                      else:
                          for k in range(8):
                              st_ = wstage[0]
                              for j in range(4):
                                  dma(st_[:, j * 128:(j + 1) * 128], w_in[l, k * 128:(k + 1) * 128, HY0 + j * 256 + hc * 128:HY0 + j * 256 + hc * 128 + 128],
                                      reads=[st_.b], writes=[st_.b])
                              op("dve" if k % 2 == 0 else "pool", lambda e: e.tensor_copy(out=h_wmix[:, k, :], in_=st_[:, 0:512]),
                                 reads=[st_.b], writes=[h_wmix.b])
                          dma(wc["hy%d" % hc][0][:, :], h_wmix[:, :, :].rearrange("p k c -> p (k c)"), reads=[h_wmix.b], writes=[wc["hy%d" % hc][1]])
                      if sample:
                          dma(h_w3[:].rearrange("p j c -> p (j c)"), wc["w3%d" % hc][0][:, :], reads=[wc["w3%d" % hc][1]], writes=[h_w3.b], q="pool")
                      else:
                          st_ = wstage[0]
                          for j in range(4):
                              dma(st_[0:64, j * 128:(j + 1) * 128], hy_w3[l, :, j * 256 + hc * 128:j * 256 + hc * 128 + 128], reads=[st_.b], writes=[st_.b], q="pool")
                          op("dve", lambda e: e.tensor_copy(out=h_w3[:].rearrange("p j c -> p (j c)"), in_=st_[0:64, 0:512]), reads=[st_.b], writes=[h_w3.b])
                          dma(wc["w3%d" % hc][0][:, :], h_w3[:].rearrange("p j c -> p (j c)"), reads=[h_w3.b], writes=[wc["w3%d" % hc][1]])
                      zT = h_sig[:, 3, 0:T]
                      z3 = zT.rearrange("p (s t) -> p s t", s=nseq)
                      for wh in range(3):
                          for c0 in range(0, T, 512):
                              pp_ = psf()
                              for k in range(8):
                                  op("pe", lambda e: e.matmul(pp_[:, :], lhsT=h_wmix[:, k, wh * 128:(wh + 1) * 128], rhs=hT[:, k, c0:c0 + 512],
                                                              start=(k == 0), stop=(k == 7)), reads=[h_wmix.b, hT.b], writes=[pp_.b])
                              op("act", lambda e: e.activation(out=h_sig[:, 3, c0:c0 + 512], in_=pp_[:, :], func=AF.Copy), reads=[pp_.b], writes=[h_sig.b])
                          dst = h_sig[:, wh, 0:T]
                          d3 = dst.rearrange("p (s t) -> p s t", s=nseq)
                          op("dve", lambda e: e.tensor_scalar(out=dst, in0=zT, scalar1=h_colp[:, 3 + wh:4 + wh], scalar2=h_colp[:, 9 + wh:10 + wh],
                                                              op0=ALU.mult, op1=ALU.add), reads=[h_sig.b, h_colp.b], writes=[h_sig.b])
                          op("dve", lambda e: e.scalar_tensor_tensor(out=d3[:, :, 1:L], in0=z3[:, :, 0:L - 1], scalar=h_colp[:, wh:wh + 1], in1=d3[:, :, 1:L],
                                                                     op0=ALU.mult, op1=ALU.add), reads=[h_sig.b, h_colp.b], writes=[h_sig.b])
                          op("dve", lambda e: e.scalar_tensor_tensor(out=d3[:, :, 0:L - 1], in0=z3[:, :, 1:L], scalar=h_colp[:, 6 + wh:7 + wh], in1=d3[:, :, 0:L - 1],
                                                                     op0=ALU.mult, op1=ALU.add), reads=[h_sig.b, h_colp.b], writes=[h_sig.b])
                      for lt in range(ntt):
                          pp_ = psf()
                          op("pe", lambda e: e.matmul(pp_[:, :], lhsT=h_hid[1][:, lt * 128:(lt + 1) * 128], rhs=h_w3[:].rearrange("p j c -> p (j c)"),
                                                      start=True, stop=True), reads=[h_hid[1].b, h_w3.b], writes=[pp_.b])
                          dma(h_win[:, :], tb["win"][lt * 128:(lt + 1) * 128, hc * 128:(hc + 1) * 128], writes=[h_win.b], q="pool")
                          f4 = h_fw[:].rearrange("p (j c) -> p j c", j=4)
                          op("dve", lambda e: e.tensor_tensor(out=f4, in0=pp_[:, :].rearrange("p (j c) -> p j c", j=4),
                                                              in1=h_win[:, :].unsqueeze(1).to_broadcast([128, 4, 128]), op=ALU.mult),
                             reads=[pp_.b, h_win.b], writes=[h_fw.b])
                          if lt == 0:
                              op("dve", lambda e: e.tensor_scalar(out=h_fw[:, 256:512], in0=h_fw[:, 256:512], scalar1=hcon_t[:, 160:161], scalar2=None,
                                                                  op0=ALU.mult), reads=[h_fw.b, hcon_t.b], writes=[h_fw.b])
                          op("act", lambda e: e.activation(out=h_abs[:], in_=h_fw[:], func=AF.Abs), reads=[h_fw.b], writes=[h_abs.b])
                          op("pe", lambda e: e.matmul(p_acc[:, :], lhsT=ones_sq[:, :], rhs=h_abs[:, :], start=(lt == 0), stop=(lt == ntt - 1)),
                             reads=[ones_sq.b, h_abs.b], writes=[p_acc.b])
                          op("dve", lambda e: e.tensor_tensor(out=h_ksd[:, lt, 0:2, :], in0=f4[:, 0:2, :], in1=f4[:, 2:4, :], op=ALU.add),
                             reads=[h_fw.b], writes=[h_ksd.b])
                          op("dve", lambda e: e.tensor_tensor(out=h_ksd[:, lt, 2:4, :], in0=f4[:, 2:4, :], in1=f4[:, 0:2, :], op=ALU.subtract),
                             reads=[h_fw.b, h_ksd.b], writes=[h_ksd.b])
                      n4 = p_acc[:, :].rearrange("p (j c) -> p j c", j=4)
                      op("dve", lambda e: e.tensor_copy(out=h_rn[:], in_=n4[:, 0:2, :]), reads=[p_acc.b], writes=[h_rn.b])
                      op("dve", lambda e: e.tensor_tensor(out=h_rn[:], in0=h_rn[:], in1=n4[:, 2:4, :], op=ALU.add), reads=[p_acc.b, h_rn.b], writes=[h_rn.b])
                      op("dve", lambda e: e.reciprocal(out=h_rn[:], in_=h_rn[:]), reads=[h_rn.b], writes=[h_rn.b])

                      def long_conv(s_, o, src_idx, combine, own=False):
                          t0 = s_ * L
                          for a in range(ntt):
                              pt_ = psb()
                              op("pe", lambda e: e.transpose(pt_[:, 0:128], h_sig[:, src_idx, t0 + a * 128:t0 + (a + 1) * 128], ident_b[:, :]),
                                 reads=[h_sig.b, ident_b.b], writes=[pt_.b])
                              op("act", lambda e: e.activation(out=h_utm[:, a, :], in_=pt_[:, 0:128], func=AF.Copy), reads=[pt_.b], writes=[h_utm.b])
                          for b in range(nfb):
                              nyq = (b == ntt)
                              tcb, tsb = h_tc[b % 2], h_ts[b % 2]
                              if not nyq and sample:
                                  dma(tcb[:, 0:ntt, :], cos_cb[:, :, b * 128:(b + 1) * 128], writes=[tcb.b], q="pool")
                                  dma(tsb[:, 0:ntt, :], sin_cb[:, :, b * 128:(b + 1) * 128], writes=[tsb.b], q="pool")
                              pu, pk = psf(), psf()
                              for dst_, col, tab, rhs_of in ((pu, 0, "c", lambda a: h_utm[:, a, :]), (pu, 128, "s", lambda a: h_utm[:, a, :]),
                                                             (pk, 0, "c", lambda a: h_ksd[:, a, o, :]), (pk, 128, "s", lambda a: h_ksd[:, a, 2 + o, :])):
                                  if nyq and tab == "s":
                                      continue
                                  for a in range(ntt):
                                      lt_ = h_alt[:, :] if nyq else (tcb if tab == "c" else tsb)[:, a, :]
                                      rd_ = [h_alt.b] if nyq else [(tcb if tab == "c" else tsb).b]
                                      op("pe", lambda e: e.matmul(dst_[:, col:col + 128], lhsT=lt_, rhs=rhs_of(a), start=(a == 0), stop=(a == ntt - 1)),
                                         reads=rd_ + [h_utm.b, h_ksd.b], writes=[dst_.b])
                              wc = hcon_t[:, wbase + b:wbase + b + 1]
                              op("dve", lambda e: e.scalar_tensor_tensor(out=h_k[0][:], in0=pk[:, 0:128], scalar=wc, in1=h_rn[:, o, :], op0=ALU.mult, op1=ALU.mult),
                                 reads=[pk.b, hcon_t.b, h_rn.b], writes=[h_k[0].b])
                              if nyq:
                                  op("dve", lambda e: e.tensor_tensor(out=h_Y[:, b, 0, :], in0=pu[:, 0:128], in1=h_k[0][:], op=ALU.mult),
                                     reads=[pu.b, h_k[0].b], writes=[h_Y.b])
                                  continue
                              op("dve", lambda e: e.scalar_tensor_tensor(out=h_k[1][:], in0=pk[:, 128:256], scalar=wc, in1=h_rn[:, o, :], op0=ALU.mult, op1=ALU.mult),
                                 reads=[pk.b, hcon_t.b, h_rn.b], writes=[h_k[1].b])
                              op("dve", lambda e: e.tensor_tensor(out=h_t[0][:], in0=pu[:, 0:128], in1=h_k[0][:], op=ALU.mult), reads=[pu.b, h_k[0].b], writes=[h_t[0].b])
                              op("dve", lambda e: e.tensor_tensor(out=h_t[1][:], in0=pu[:, 128:256], in1=h_k[1][:], op=ALU.mult), reads=[pu.b, h_k[1].b], writes=[h_t[1].b])
                              op("dve", lambda e: e.tensor_tensor(out=h_t[2][:], in0=pu[:, 128:256], in1=h_k[0][:], op=ALU.mult), reads=[pu.b, h_k[0].b], writes=[h_t[2].b])
                              op("dve", lambda e: e.tensor_tensor(out=h_t[3][:], in0=pu[:, 0:128], in1=h_k[1][:], op=ALU.mult), reads=[pu.b, h_k[1].b], writes=[h_t[3].b])
                              op("dve", lambda e: e.tensor_tensor(out=h_Y[:, b, 0, :], in0=h_t[0][:], in1=h_t[1][:], op=ALU.add),
                                 reads=[h_t[0].b, h_t[1].b], writes=[h_Y.b])
                              op("dve", lambda e: e.tensor_tensor(out=h_Y[:, b, 1, :], in0=h_t[2][:], in1=h_t[3][:], op=ALU.subtract),
                                 reads=[h_t[2].b, h_t[3].b, h_Y.b], writes=[h_Y.b])
                          if own:
                              op("dve", lambda e: e.tensor_scalar(out=h_sig[:, 2:4, 0:512], in0=h_sig[:, 2:4, 0:512], scalar1=qm[:, 0:1],
                                                                  scalar2=None, op0=ALU.mult), reads=[h_sig.b, qm.b], writes=[h_sig.b])
                              for j in range(1, 4):
                                  op("dve", lambda e: e.scalar_tensor_tensor(out=h_sig[:, 2:4, 0:512], in0=h_sig[:, 2:4, j * 512:(j + 1) * 512],
                                                                             scalar=qm[:, j:j + 1], in1=h_sig[:, 2:4, 0:512],
                                                                             op0=ALU.mult, op1=ALU.add),
                                     reads=[h_sig.b, qm.b], writes=[h_sig.b])
                          for bank in range(1 if own else nbank):
                              c0 = bank * NB
                              for b in range(ntt):
                                  crb, srb = h_cr[b % 2], h_sr[b % 2]
                                  if own:
                                      dma(crb[:, 0:NB], hcosq[b * 128:(b + 1) * 128, 0:NB], writes=[crb.b], q="pool")
                                      dma(srb[:, 0:NB], hsinq[b * 128:(b + 1) * 128, 0:NB], writes=[srb.b], q="pool")
                                  elif sample:
                                      dma(crb[:, 0:NB], tb["cos"][b * 128:(b + 1) * 128, c0:c0 + NB], writes=[crb.b], q="pool")
                                      dma(srb[:, 0:NB], tb["sin"][b * 128:(b + 1) * 128, c0:c0 + NB], writes=[srb.b], q="pool")
                                  op("pe", lambda e: e.matmul(p_acc[:, 0:NB], lhsT=h_Y[:, b, 0, :], rhs=crb[:, 0:NB], start=(b == 0), stop=False),
                                     reads=[h_Y.b, crb.b], writes=[p_acc.b])
                                  op("pe", lambda e: e.matmul(p_acc[:, 0:NB], lhsT=h_Y[:, b, 1, :], rhs=srb[:, 0:NB], start=False, stop=False),
                                     reads=[h_Y.b, srb.b], writes=[p_acc.b])
                              op("pe", lambda e: e.matmul(p_acc[:, 0:NB], lhsT=h_Y[0:1, ntt, 0, :], rhs=haltr_t[0:1, 0:NB], start=False, stop=True),
                                 reads=[h_Y.b, haltr_t.b], writes=[p_acc.b])
                              combine(t0 + c0)

                      def comb1(tk):
                          op("dve", lambda e: e.scalar_tensor_tensor(out=h_fw[:, 0:NB], in0=h_sig[:, 0, tk:tk + NB], scalar=h_colp[:, 12:13], in1=p_acc[:, 0:NB],
                                                                     op0=ALU.mult, op1=ALU.add), reads=[h_sig.b, h_colp.b, p_acc.b], writes=[h_fw.b])
                          op("dve", lambda e: e.tensor_tensor(out=h_sig[:, 3, tk:tk + NB], in0=h_fw[:, 0:NB], in1=h_sig[:, 1, tk:tk + NB], op=ALU.mult),
                             reads=[h_fw.b, h_sig.b], writes=[h_sig.b])

                      def comb2(tk):
                          op("dve", lambda e: e.scalar_tensor_tensor(out=h_fw[:, 0:NB], in0=h_sig[:, 3, tk:tk + NB], scalar=h_colp[:, 13:14], in1=p_acc[:, 0:NB],
                                                                     op0=ALU.mult, op1=ALU.add), reads=[h_sig.b, h_colp.b, p_acc.b], writes=[h_fw.b])
                          op("dve", lambda e: e.tensor_tensor(out=h_fw[:, 0:NB], in0=h_fw[:, 0:NB], in1=h_sig[:, 2, tk:tk + NB], op=ALU.mult),
                             reads=[h_fw.b, h_sig.b], writes=[h_fw.b])
                          if hy_own:
                              gacc = gate_bc[0]
                              for j in range(4):
                                  pg_ = psf()
                                  for k in range(8):
                                      op("pe", lambda e: e.matmul(pg_[:, 0:NB], lhsT=h_wmix[:, k, 384:512], rhs=hT[:, k, j * 512:(j + 1) * 512],
                                                                  start=(k == 0), stop=(k == 7)), reads=[h_wmix.b, hT.b], writes=[pg_.b])
                                  op("act", lambda e: e.activation(out=h_abs[:, 0:NB], in_=pg_[:, 0:NB], func=AF.Silu), reads=[pg_.b], writes=[h_abs.b])
                                  if j == 0:
                                      op("dve", lambda e: e.tensor_scalar(out=gacc[:, 0:NB], in0=h_abs[:, 0:NB], scalar1=qm[:, 0:1], scalar2=None,
                                                                          op0=ALU.mult), reads=[h_abs.b, qm.b], writes=[gacc.b])
                                  else:
                                      op("dve", lambda e: e.scalar_tensor_tensor(out=gacc[:, 0:NB], in0=h_abs[:, 0:NB], scalar=qm[:, j:j + 1],
                                                                                 in1=gacc[:, 0:NB], op0=ALU.mult, op1=ALU.add),
                                         reads=[h_abs.b, qm.b, gacc.b], writes=[gacc.b])
                              for j in range(4):
                                  op("dve", lambda e: e.scalar_tensor_tensor(out=brT[:, 2 + hc, j * 512:(j + 1) * 512], in0=h_fw[:, 0:NB],
                                                                             scalar=qm[:, j:j + 1], in1=gacc[:, 0:NB], op0=ALU.mult, op1=ALU.mult),
                                     reads=[h_fw.b, gacc.b, qm.b], writes=[brT.b])
                          else:
                              pg_ = psf()
                              for k in range(8):
                                  op("pe", lambda e: e.matmul(pg_[:, 0:NB], lhsT=h_wmix[:, k, 384:512], rhs=hT[:, k, tk:tk + NB], start=(k == 0), stop=(k == 7)),
                                     reads=[h_wmix.b, hT.b], writes=[pg_.b])
                              op("act", lambda e: e.activation(out=h_abs[:, 0:NB], in_=pg_[:, 0:NB], func=AF.Silu), reads=[pg_.b], writes=[h_abs.b])
                              op("dve", lambda e: e.tensor_tensor(out=brT[:, 2 + hc, tk:tk + NB], in0=h_fw[:, 0:NB], in1=h_abs[:, 0:NB], op=ALU.mult),
                                 reads=[h_fw.b, h_abs.b], writes=[brT.b])

                      for s_ in range(nseq):
                          long_conv(s_, 0, 0, comb1)
                      for s_ in range(nseq):
                          long_conv(s_, 1, 3, comb2, own=hy_own)

              groups = [
                  dict(c=0, src=(xp if l == 0 else xmid_p), srcb=xmid_p_b, T=TP, L=SEQ, sample=False),
                  dict(c=1, src=(xs if l == 0 else xmid_s), srcb=xmid_s_b, T=LS, L=LS, sample=True),
              ]
              for g in groups:
                  c, T, L, sample = g["c"], g["T"], g["L"], g["sample"]
                  ntile = T // 128
                  nseq = T // L
                  kb.barrier()
                  if sample:
                      dma(w_in_b[:, :, :].rearrange("p k c -> p (k c)"), wc["in"][0][:, :], reads=[wc["in"][1]], writes=[w_in_b.b], q="pool")
                  else:
                      for k in range(8):
                          st = wstage[k % 2]
                          dma(st[:, 0:NINU], w_in[l, k * 128:(k + 1) * 128, 0:NINU], writes=[st.b], q="pool")
                          eng = "dve" if k % 2 == 0 else "pool"
                          op(eng, lambda e: e.tensor_copy(out=w_in_b[:, k, :], in_=st[:, 0:NINU]), reads=[st.b], writes=[w_in_b.b])
                      dma(wc["in"][0][:, :], w_in_b[:, :, :].rearrange("p k c -> p (k c)"), reads=[w_in_b.b], writes=[wc["in"][1]])

                  def k_path(src_ckv_f32, src_kro_f32, rope_tile, kcol, vt, rd):
                      op("dve", lambda e: e.tensor_copy(out=ckvn_b[:], in_=src_ckv_f32), reads=rd, writes=[ckvn_b.b])
                      p = psb()
                      op("pe", lambda e: e.transpose(p[:, 0:128], ckvn_b[:, :], ident_b[:, :]),
                         reads=[ckvn_b.b, ident_b.b], writes=[p.b])
                      op("act", lambda e: e.activation(out=ckvT[:], in_=p[:, 0:128], func=AF.Copy),
                         reads=[p.b], writes=[ckvT.b])
                      pkv = psf()
                      op("pe", lambda e: e.matmul(pkv[:, :], lhsT=ckvT[:, :], rhs=w_ukv_b[:, :], start=True, stop=True),
                         reads=[ckvT.b, w_ukv_b.b], writes=[pkv.b])
                      chk('k1')
                      kv3 = pkv[:, :].rearrange("p (h d) -> p h d", h=H)
                      op("dve", lambda e: e.tensor_copy(out=kcat[:, :, 0:64], in_=kv3[:, :, 0:64]),
                         reads=[pkv.b], writes=[kcat.b])
                      op("pool", lambda e: e.tensor_copy(out=kcat[:, :, 64:96],
                                                         in_=src_kro_f32.unsqueeze(1).to_broadcast([128, H, 32])),
                         reads=rd + [kcat.b], writes=[kcat.b])
                      chk('k2')
                      op("act", lambda e: e.activation(out=vaug[:, vt, :, 0:64], in_=kv3[:, :, 64:128], func=AF.Copy),
                         reads=[pkv.b], writes=[vaug.b])
                      op("pool", lambda e: e.memset(vaug[:, vt, :, 64:65], 1.0), reads=[vaug.b], writes=[vaug.b])
                      chk('k3')
                      head_norm(kcat, bc_kn, rope_tile)
                      chk('k4')
                      pq = psb()
                      for h in range(H):
                          op("pe", lambda e: e.transpose(pq[0:QK, h * 128:(h + 1) * 128], qb[:, h, :], ident_b[:, :]),
                             reads=[qb.b, ident_b.b], writes=[pq.b])
                      op("act", lambda e: e.activation(out=kT[0:QK, :, kcol:kcol + 128],
                                                       in_=pq[0:QK, 0:512].rearrange("p (h t) -> p h t", h=H),
                                                       func=AF.Copy), reads=[pq.b], writes=[kT.b])

                  def head_norm(src, wbc, rope_tile):
                      op("act", lambda e: e.activation(out=sqh[:], in_=src[:], func=AF.Square),
                         reads=[src.b], writes=[sqh.b])
                      op("dve", lambda e: e.tensor_reduce(out=ssh[:], in_=sqh[:], axis=AX.X, op=ALU.add),
                         reads=[sqh.b], writes=[ssh.b])
                      rstd_from_ss(rsh, ssh, QK)
                      op("dve", lambda e: e.tensor_tensor(out=qf[:], in0=src[:],
                                                          in1=rsh[:, :].unsqueeze(2).to_broadcast([128, H, QK]),
                                                          op=ALU.mult), reads=[src.b, rsh.b], writes=[qf.b])
                      op("dve", lambda e: e.tensor_tensor(out=qf[:], in0=qf[:],
                                                          in1=wbc[:, :].unsqueeze(1).to_broadcast([128, H, QK]),
                                                          op=ALU.mult), reads=[qf.b, wbc.b], writes=[qf.b])
                      if rope_tile is not None:
                          r5 = qf[:, :, 64:96].rearrange("p h (a f j) -> p h a f j", a=2, f=2)
                          t5 = rtmp[:].rearrange("p h (a f j) -> p h a f j", a=2, f=2)
                          op("dve", lambda e: e.tensor_copy(out=t5[:, :, :, 0, :], in_=r5[:, :, :, 1, :]),
                             reads=[qf.b], writes=[rtmp.b])
                          op("dve", lambda e: e.tensor_copy(out=t5[:, :, :, 1, :], in_=r5[:, :, :, 0, :]),
                             reads=[qf.b, rtmp.b], writes=[rtmp.b])
                          op("dve", lambda e: e.tensor_tensor(out=rtmp[:], in0=rtmp[:],
                                                              in1=sinT[:, :].unsqueeze(1).to_broadcast([128, H, 32]),
                                                              op=ALU.mult), reads=[rtmp.b, sinT.b], writes=[rtmp.b])
                          op("dve", lambda e: e.tensor_tensor(out=rtmp2[:], in0=qf[:, :, 64:96],
                                                              in1=cosT[:, :].unsqueeze(1).to_broadcast([128, H, 32]),
                                                              op=ALU.mult), reads=[qf.b, cosT.b], writes=[rtmp2.b])
                          op("dve", lambda e: e.tensor_tensor(out=qf[:, :, 64:96], in0=rtmp2[:], in1=rtmp[:], op=ALU.add),
                             reads=[rtmp.b, rtmp2.b, qf.b], writes=[qf.b])
                      op("dve", lambda e: e.tensor_copy(out=qb[:], in_=qf[:]), reads=[qf.b], writes=[qb.b])

                  for s in range(nseq):
                      tps = L // 128
                      nctx = 0
                      if sample:
                          nctx = PAST // 128
                          for t in range(nctx):
                              dma(ckvn[:, :], cckv[l, t * 128:(t + 1) * 128, :], writes=[ckvn.b], q="pool")
                              dma(kro[:, :], ckro[l, t * 128:(t + 1) * 128, :], writes=[kro.b], q="pool")
                              k_path(ckvn[:, :], kro[:, :], None, t * 128, t, [ckvn.b, kro.b])
                      for tt in range(tps):
                          ti = s * tps + tt
                          r0 = ti * 128
                          dma(xt[:, :], g["src"][r0:r0 + 128, :], reads=[g["srcb"]], writes=[xt.b], q="pool")
                          op("act", lambda e: e.activation(out=junk[:], in_=xt[:], func=AF.Square, accum_out=ss1[:, 0:1]),
                             reads=[xt.b], writes=[xn.b, ss1.b])
                          rstd_from_ss(rs1, ss1, D)
                          op("dve", lambda e: e.tensor_scalar(out=xn[:], in0=xt[:], scalar1=rs1[:, 0:1], scalar2=None,
                                                              op0=ALU.mult), reads=[xt.b, rs1.b], writes=[xn.b])
                          pt = psb()
                          for k in range(8):
                              op("pe", lambda e: e.transpose(pt[:, k * 128:(k + 1) * 128], xn[:, k * 128:(k + 1) * 128],
                                                             ident_b[:, :]), reads=[xn.b, ident_b.b], writes=[pt.b])
                          for k in range(8):
                              op("act", lambda e: e.activation(out=hT[:, k, r0:r0 + 128], in_=pt[:, k * 128:(k + 1) * 128],
                                                               func=AF.Identity, scale=effsc[:, c, k:k + 1],
                                                               bias=shift[:, c, k:k + 1]),
                                 reads=[pt.b, effsc.b, shift.b], writes=[hT.b])
                          chk('normT')
                          pl = psf()
                          for k in range(8):
                              op("pe", lambda e: e.matmul(pl[:, 0:352], lhsT=hT[:, k, r0:r0 + 128], rhs=w_in_b[:, k, 0:352],
                                                          start=(k == 0), stop=(k == 7)),
                                 reads=[hT.b, w_in_b.b], writes=[pl.b])
                          op("act", lambda e: e.activation(out=lat[:], in_=pl[:, 0:352], func=AF.Copy),
                             reads=[pl.b], writes=[lat.b])
                          op("act", lambda e: e.activation(out=junk2[:, 0:192], in_=lat[:, 0:192], func=AF.Square,
                                                           accum_out=ssq[:, 0:1]), reads=[lat.b], writes=[junk2.b, ssq.b])
                          op("act", lambda e: e.activation(out=junk2[:, 0:128], in_=lat[:, 192:320], func=AF.Square,
                                                           accum_out=ssq[:, 1:2]), reads=[lat.b, ssq.b], writes=[junk2.b, ssq.b])
                          op("act", lambda e: e.activation(out=rsq[:, 0:1], in_=ssq[:, 0:1], func=AF.Sqrt, scale=1.0 / 192,
                                                           bias=eps_c[:, 0:1]), reads=[ssq.b, eps_c.b], writes=[rsq.b])
                          op("act", lambda e: e.activation(out=rsq[:, 1:2], in_=ssq[:, 1:2], func=AF.Sqrt, scale=1.0 / 128,
                                                           bias=eps_c[:, 0:1]), reads=[ssq.b, eps_c.b, rsq.b], writes=[rsq.b])
                          op("dve", lambda e: e.reciprocal(out=rsq[:], in_=rsq[:]), reads=[rsq.b], writes=[rsq.b])
                          op("dve", lambda e: e.scalar_tensor_tensor(out=cqn[:], in0=lat[:, 0:192], scalar=rsq[:, 0:1],
                                                                     in1=bc_qa[:, :], op0=ALU.mult, op1=ALU.mult),
                             reads=[lat.b, rsq.b, bc_qa.b], writes=[cqn.b])
                          op("dve", lambda e: e.scalar_tensor_tensor(out=ckvn[:], in0=lat[:, 192:320], scalar=rsq[:, 1:2],
                                                                     in1=bc_kva[:, :], op0=ALU.mult, op1=ALU.mult),
                             reads=[lat.b, rsq.b, bc_kva.b], writes=[ckvn.b])
                          if not sample:
                              dma(o_ckv[s, l, tt * 128:(tt + 1) * 128, :], ckvn[:, :], reads=[ckvn.b])
                              dma(o_kro[s, l, tt * 128:(tt + 1) * 128, :], lat[:, 320:352], reads=[lat.b])
                          rope_tile = None
                          if sample:
                              dma(cosT[:, :], ropec[r0:r0 + 128, :], writes=[cosT.b], q="pool")
                              dma(sinT[:, :], ropes[r0:r0 + 128, :], writes=[sinT.b], q="pool")
                              rope_tile = True
                          chk('latent')
                          pc = psb()
                          op("pe", lambda e: e.transpose(pc[:, 0:128], cqn[:, 0:128], ident_b[:, :]),
                             reads=[cqn.b, ident_b.b], writes=[pc.b])
                          op("pe", lambda e: e.transpose(pc[0:64, 128:256], cqn[:, 128:192], ident_b[:, :]),
                             reads=[cqn.b, ident_b.b], writes=[pc.b])
                          op("act", lambda e: e.activation(out=cqT[:, 0, :], in_=pc[:, 0:128], func=AF.Copy),
                             reads=[pc.b], writes=[cqT.b])
                          op("act", lambda e: e.activation(out=cqT[0:64, 1, :], in_=pc[0:64, 128:256], func=AF.Copy),
                             reads=[pc.b, cqT.b], writes=[cqT.b])
                          pqp = psf()
                          op("pe", lambda e: e.matmul(pqp[:, 0:384], lhsT=cqT[:, 0, :], rhs=w_uq_a[:, :], start=True, stop=False),
                             reads=[cqT.b, w_uq_a.b], writes=[pqp.b])
                          op("pe", lambda e: e.matmul(pqp[:, 0:384], lhsT=cqT[0:64, 1, :], rhs=w_uq_c[:, :], start=False, stop=True),
                             reads=[cqT.b, w_uq_c.b], writes=[pqp.b])
                          op("act", lambda e: e.activation(out=qpre[:].rearrange("p h d -> p (h d)"), in_=pqp[:, 0:384],
                                                           func=AF.Copy), reads=[pqp.b], writes=[qpre.b])
                          head_norm(qpre, bc_qn, rope_tile)
                          pq = psb()
                          for h in range(H):
                              op("pe", lambda e: e.transpose(pq[0:QK, h * 128:(h + 1) * 128], qb[:, h, :], ident_b[:, :]),
                                 reads=[qb.b, ident_b.b], writes=[pq.b])
                          op("act", lambda e: e.activation(out=qT[0:QK, :, tt * 128:(tt + 1) * 128],
                                                           in_=pq[0:QK, 0:512].rearrange("p (h t) -> p h t", h=H),
                                                           func=AF.Copy), reads=[pq.b], writes=[qT.b])
                          chk('queries')
                          k_path(ckvn[:, :], lat[:, 320:352], rope_tile, (nctx + tt) * 128, nctx + tt, [ckvn.b, lat.b])

                      chk('tile0') if False else None
                      chk('phaseA')
                      nk = nctx + tps
                      QB = min(512, L)
                      nqs = QB // 128
                      po_b = [pf[5], pf[4], pf[3], pf[2]][:nqs]
                      qsplit = sample and l == DEPTH - 1
                      if qsplit:
                          qsel = qT
                          hsel = Tile(brT[:, 2:4, :].rearrange("p r (b t) -> p (r b) t", t=512))
                          for dst_, src_, np_ in ((qsel, qT, QK), (hsel, hT, 128)):
                              op("dve", lambda e: e.tensor_scalar(out=dst_[0:np_, :, 0:512], in0=src_[0:np_, :, 0:512], scalar1=qm[0:np_, 0:1],
                                                                  scalar2=None, op0=ALU.mult), reads=[src_.b, qm.b], writes=[dst_.b])
                              for j in range(1, 4):
                                  op("dve", lambda e: e.scalar_tensor_tensor(out=dst_[0:np_, :, 0:512], in0=src_[0:np_, :, j * 512:(j + 1) * 512],
                                                                             scalar=qm[0:np_, j:j + 1], in1=dst_[0:np_, :, 0:512],
                                                                             op0=ALU.mult, op1=ALU.add),
                                     reads=[src_.b, qm.b, dst_.b], writes=[dst_.b])
                      for qb_ in range(1 if qsplit else L // QB):
                          q0 = qb_ * QB
                          qsrc, hsrc = (qsel, hsel) if qsplit else (qT, hT)
                          for h in range(H):
                              for kt in range(nk):
                                  pss = pf[kt % 2]
                                  pTb = pT2[kt % 2]
                                  op("pe", lambda e: e.matmul(pss[:, 0:QB], lhsT=kT[0:QK, h, kt * 128:(kt + 1) * 128],
                                                              rhs=qsrc[0:QK, h, q0:q0 + QB], start=True, stop=True),
                                     reads=[kT.b, qsrc.b], writes=[pss.b])
                                  op("act", lambda e: e.activation(out=pTb[:, 0:QB], in_=pss[:, 0:QB], func=AF.Exp, scale=QK ** -0.5),
                                     reads=[pss.b], writes=[pTb.b])
                                  for qs in range(nqs):
                                      op("pe", lambda e: e.matmul(po_b[qs][:, 0:65], lhsT=pTb[:, qs * 128:(qs + 1) * 128], rhs=vaug[:, kt, h, 0:65],
                                                                  start=(kt == 0), stop=(kt == nk - 1)),
                                         reads=[pTb.b, vaug.b], writes=[po_b[qs].b])
                              for qs in range(nqs):
                                  op("dve", lambda e: e.reciprocal(out=den[:, 0:1], in_=po_b[qs][:, 64:65]), reads=[po_b[qs].b], writes=[den.b])
                                  op("dve", lambda e: e.tensor_scalar(out=omla4[:, qs, h, :], in0=po_b[qs][:, 0:64], scalar1=den[:, 0:1], scalar2=None,
                                                                      op0=ALU.mult), reads=[po_b[qs].b, den.b], writes=[omla4.b])
                          for qs in range(nqs):
                              r0 = s * L + q0 + qs * 128
                              pgm = pf[0]
                              for k in range(8):
                                  op("pe", lambda e: e.matmul(pgm[:, 0:256], lhsT=hsrc[:, k, r0:r0 + 128], rhs=w_in_b[:, k, 352:608],
                                                              start=(k == 0), stop=(k == 7)),
                                     reads=[hsrc.b, w_in_b.b], writes=[pgm.b])
                              op("act", lambda e: e.activation(out=silg[:], in_=pgm[:, 0:256], func=AF.Silu),
                                 reads=[pgm.b], writes=[silg.b])
                              op("dve", lambda e: e.tensor_tensor(out=omg[:], in0=omla4[:, qs, :, :].rearrange("p h d -> p (h d)"),
                                                                  in1=silg[:], op=ALU.mult), reads=[omla4.b, silg.b], writes=[omg.b])
                              pbr = psb()
                              for k2 in range(2):
                                  op("pe", lambda e: e.transpose(pbr[:, k2 * 128:(k2 + 1) * 128], omg[:, k2 * 128:(k2 + 1) * 128],
                                                                 ident_b[:, :]), reads=[omg.b, ident_b.b], writes=[pbr.b])
                              if qsplit:
                                  for j in range(4):
                                      op("dve", lambda e: e.tensor_scalar(out=brT[:, 0:2, j * 512 + r0:j * 512 + r0 + 128],
                                                                          in0=pbr[:, 0:256].rearrange("p (k t) -> p k t", k=2),
                                                                          scalar1=qm[:, j:j + 1], scalar2=None, op0=ALU.mult),
                                         reads=[pbr.b, qm.b], writes=[brT.b])
                              else:
                                  op("act", lambda e: e.activation(out=brT[:, 0:2, r0:r0 + 128],
                                                                   in_=pbr[:, 0:256].rearrange("p (k t) -> p k t", k=2),
                                                                   func=AF.Copy), reads=[pbr.b], writes=[brT.b])

                  chk('attn%d' % c)
                  kb.barrier()
                  gla_group(l, g)
                  chk('gla%d' % c)
                  kb.barrier()
                  s5_group(l, g)
                  chk('s5%d' % c)
                  kb.barrier()
                  hy_group(l, g)
                  chk('hy%d' % c)
                  kb.barrier()
                  if sample:
                      dma(w_out_b[:, :, :].rearrange("p k c -> p (k c)"), wc["out"][0][:, :], reads=[wc["out"][1]], writes=[w_out_b.b], q="pool")
                  else:
                      for k in range(8):
                          st = wstage[k % 2]
                          dma(st[:, 0:D], w_out[l, k * 128:(k + 1) * 128, :], writes=[st.b], q="pool")
                          eng = "dve" if k % 2 == 0 else "pool"
                          op(eng, lambda e: e.tensor_copy(out=w_out_b[:, k, :], in_=st[:, 0:D]), reads=[st.b], writes=[w_out_b.b])
                      dma(wc["out"][0][:, :], w_out_b[:, :, :].rearrange("p k c -> p (k c)"), reads=[w_out_b.b], writes=[wc["out"][1]])
                  for ti in range(ntile):
                      r0 = ti * 128
                      last = (l == DEPTH - 1)
                      dma(xt[:, :], g["src"][r0:r0 + 128, :], reads=[g["srcb"]], writes=[xt.b], q="pool")
                      for hf in range(2):
                          p = psf()
                          for k in range(8):
                              op("pe", lambda e: e.matmul(p[:, :], lhsT=brT[:, k, r0:r0 + 128],
                                                          rhs=w_out_b[:, k, hf * 512:(hf + 1) * 512],
                                                          start=(k == 0), stop=(k == 7)),
                                 reads=[brT.b, w_out_b.b], writes=[p.b])
                          op("dve", lambda e: e.tensor_tensor(out=ot[:, hf * 512:(hf + 1) * 512], in0=p[:, :],
                                                              in1=gate_bc[c][:, hf * 512:(hf + 1) * 512], op=ALU.mult),
                             reads=[p.b, gate_bc[c].b], writes=[ot.b])
                      op("dve", lambda e: e.tensor_tensor(out=ot[:], in0=ot[:], in1=xt[:], op=ALU.add),
                         reads=[ot.b, xt.b], writes=[ot.b])
                      if not last:
                          dst = (xmid_s if sample else xmid_p)
                          dma(dst[r0:r0 + 128, :], ot[:, :], reads=[ot.b], writes=[xmid_s_b if sample else xmid_p_b])
                      elif not sample:
                          dma(yp[r0:r0 + 128, :], ot[:, :], reads=[ot.b])
                      else:
                          dma(ys[r0:r0 + 128, :], ot[:, :], reads=[ot.b])

              chk('layer0')
        except _Stop:
            if stop == 'layer0':
                dma(yp[:, :], xmid_p[:, :], reads=[xmid_p_b])
                dma(ys[:, :], xmid_s[:, :], reads=[xmid_s_b])
            kb.barrier()
            kb.phase("DBG")
            dbt = sb([128, 2048], F32)
            kb.phase(None)
            op("dve", lambda e: e.memset(dbt[:], 0.0), writes=[dbt.b])
            op("dve", lambda e: e.tensor_copy(out=dbt[:, 0:16], in_=effsc[:].rearrange("p c k -> p (c k)")), reads=[effsc.b, dbt.b], writes=[dbt.b])
            op("dve", lambda e: e.tensor_copy(out=dbt[:, 16:32], in_=shift[:].rearrange("p c k -> p (c k)")), reads=[shift.b, dbt.b], writes=[dbt.b])
            op("dve", lambda e: e.tensor_copy(out=dbt[:, 32:33], in_=rs1[:, 0:1]), reads=[rs1.b, dbt.b], writes=[dbt.b])
            op("dve", lambda e: e.tensor_copy(out=dbt[:, 64:416], in_=lat[:, :]), reads=[lat.b, dbt.b], writes=[dbt.b])
            op("dve", lambda e: e.tensor_copy(out=dbt[:, 512:1536].rearrange("p (k t) -> p k t", k=8), in_=hT[:, :, 0:128]), reads=[hT.b, dbt.b], writes=[dbt.b])
            op("dve", lambda e: e.tensor_copy(out=dbt[:, 1536:2048], in_=gate_bc[0][:, 0:512]), reads=[gate_bc[0].b, dbt.b], writes=[dbt.b])
            if stop == 'weights':
                for i_, t_ in enumerate((s_par["ar1"][:], s_par["ai1"][:], s_par["kr"][:], s_par["ki"][:], s_pwr[:, 0, :], s_pwi[:, 0, :],
                                         s_pwr[:, 7, :], s_pwi[:, 7, :])):
                    op("dve", lambda e: e.tensor_copy(out=dbt[:, 512 + 16 * i_:528 + 16 * i_], in_=t_), reads=[s_par["are"].b, dbt.b], writes=[dbt.b])
            dma(dbg[:, :], dbt[:, :], reads=[dbt.b])
        kb.finish()
    return nc


_CACHE = {}


def _rope_tables():
    half = 16
    inv = (10000.0 ** (-np.arange(0, half, 2, dtype=np.float32) / half)).astype(np.float32)
    t = np.arange(LS)
    row = (t // 64).astype(np.float32)
    col = (t % 64).astype(np.float32)
    cos = np.zeros((LS, 32), np.float32)
    sin = np.zeros((LS, 32), np.float32)
    for a, pos in enumerate((row, col)):
        ang = pos[:, None] * inv[None, :]
        ang = np.concatenate([ang, ang], axis=-1)
        cos[:, 16 * a:16 * a + 16] = np.cos(ang)
        s = np.sin(ang)
        s[:, 0:8] *= -1.0
        sin[:, 16 * a:16 * a + 16] = s
    return cos, sin


def _gla_consts():
    p = np.arange(128)[:, None]
    q = np.arange(128)[None, :]
    same = (p // 64) == (q // 64)
    c = np.zeros((128, GC), np.float32)
    c[:, 0:128] = same & (p <= q)
    c[:, 128:256] = same & (p >= q)
    c[:, 256:260] = (p // 32) == np.arange(4)[None, :]
    c[:, 260] = (p[:, 0] < 64)
    c[:, 261] = (p[:, 0] >= 64)
    c[:, 262:518] = (p // 32) == (np.arange(256)[None, :] // 64)
    c[:, 518:646] = 1.0
    return c


def _s5_consts():
    c = np.zeros((128, SC), np.float32)
    r = np.arange(128)[:, None]
    q = np.arange(128)[None, :]
    for k in range(4):
        c[:, 128 * k:128 * k + 128] = (q // 16) == (2 * k + r // 64)
        c[:, 512 + 128 * k:512 + 128 * k + 128] = (r // 16) == (2 * k + q // 64)
    c[:, 1024] = math.pi / 2
    return c


def _hy_tables(L):
    i = np.arange(L, dtype=np.float64)
    ang = (2.0 * np.pi / (2 * L)) * np.outer(i, i)
    cos = np.cos(ang).astype(ml_dtypes.bfloat16)
    sin = np.sin(ang).astype(ml_dtypes.bfloat16)
    pos = np.arange(L, dtype=np.float32)
    t = pos / L
    w = (2.0 * math.pi * pos / L).astype(np.float32)
    bands = np.linspace(1e-4, 15, 16, dtype=np.float32)
    feat = np.concatenate([t[:, None], np.cos(w[:, None] * bands), np.sin(w[:, None] * bands)], axis=-1)
    deltas = np.linspace(math.log(100.0) / 0.3, math.log(100.0) / 1.5, 256, dtype=np.float32)
    win = (np.exp(-t[:, None] * deltas[None, :]) + 0.05).astype(np.float32)
    return cos, sin, np.ascontiguousarray(feat.T).astype(ml_dtypes.bfloat16), win


def _hy_consts():
    c = np.zeros((128, HC), np.float32)
    p = np.arange(128)
    c[:, 0] = 1.0 - 2.0 * (p % 2)
    for base, L in ((128, SEQ), (136, LS)):
        n = 2 * L
        nt = L // 128
        c[:, base:base + nt] = 2.0 / n
        c[0, base] = 1.0 / n
        c[:, base + nt] = 1.0 / n
    c[:, 160] = 1.0
    c[0, 160] = 0.0
    altr = (1.0 - 2.0 * (np.arange(512) % 2)).astype(ml_dtypes.bfloat16)[None, :]
    return c, altr


W_NAMES = ["norm_w", "ada_w", "ada_b", "w_in", "w_out", "mla_qa_norm", "mla_kva_norm", "mla_w_uq",
           "mla_w_ukv", "mla_q_norm", "mla_k_norm", "gla_gw", "gla_gb", "gla_norm",
           "s5_a_re", "s5_a_im", "s5_log_dt", "s5_b_re", "s5_b_im", "s5_c_re", "s5_c_im", "s5_d", "s5_glu_w", "s5_glu_b",
           "hy_conv_w", "hy_conv_b", "hy_w1", "hy_b1", "hy_freq1", "hy_w2", "hy_b2", "hy_freq2", "hy_w3", "hy_bias"]


def _f32(a):
    return np.ascontiguousarray(np.asarray(a, dtype=np.float32))


def core_inputs(inp, i):
    b = i // 4
    cos, sin = _rope_tables()
    m = {
        "xp": _f32(inp["x_prompt"])[4 * i:4 * i + 4].reshape(TP, D),
        "xs": _f32(inp["x_sample"])[b],
        "cond": np.stack([_f32(inp["c_ctx"]), _f32(inp["c"])[b]]),
        "cckv": _f32(inp["cache_mla_ckv"])[b],
        "ckro": _f32(inp["cache_mla_krope"])[b],
        "ropec": cos, "ropes": sin,
        "sgla": _f32(inp["state_gla"])[b].reshape(DEPTH, 2, 128, 64),
        "gcon": _gla_consts(),
        "ss5": _f32(inp["state_s5"])[b].reshape(DEPTH, 2, 1024, 2),
        "scon": _s5_consts(),
        "qmask": np.ascontiguousarray(np.tile(np.eye(4, dtype=np.float32)[i % 4], (128, 1))),
    }
    q_ = i % 4
    if "hy" not in _CACHE:
        _CACHE["hy"] = {"p": _hy_tables(SEQ), "s": _hy_tables(LS), "c": _hy_consts()}
    for nm_ in ("p", "s"):
        cos_, sin_, feat_, win_ = _CACHE["hy"][nm_]
        m["hcos_" + nm_], m["hsin_" + nm_], m["hfeat_" + nm_], m["hwin_" + nm_] = cos_, sin_, feat_, win_
    m["hcon"], m["haltr"] = _CACHE["hy"]["c"]
    m["hcosq"] = np.ascontiguousarray(_CACHE["hy"]["s"][0][:, 512 * q_:512 * (q_ + 1)])
    m["hsinq"] = np.ascontiguousarray(_CACHE["hy"]["s"][1][:, 512 * q_:512 * (q_ + 1)])
    for n in W_NAMES:
        m[n] = _f32(inp[n])
    return m


def kernel(**inp):
    if "nc" not in _CACHE:
        _CACHE["nc"] = build_program()
    nc = _CACHE["nc"]
    in_maps = [core_inputs(inp, i) for i in range(8)]
    res = run_bass_kernel_spmd(nc, in_maps, core_ids=list(range(8))).results
    y_prompt = np.concatenate([r["yp"].reshape(NPS, SEQ, D) for r in res], axis=0)
    y_sample = np.stack([np.concatenate([res[4 * b + q]["ys"][512 * q:512 * (q + 1)] for q in range(4)], axis=0)
                         for b in range(2)], axis=0)
    ckv = np.concatenate([r["o_ckv"] for r in res], axis=0)
    kro = np.concatenate([r["o_kro"] for r in res], axis=0)
    s5 = np.concatenate([r["o_s5"] for r in res], axis=0)
    gla = np.concatenate([r["o_gla"] for r in res], axis=0)
    return (y_prompt.astype(np.float32), y_sample.astype(np.float32), ckv.astype(np.float32),
            kro.astype(np.float32), s5.astype(np.float32), gla.astype(np.float32))
```

```python
import contextlib
import math

import numpy as np
import ml_dtypes

import concourse.bass as bass
import concourse.mybir as mybir
from concourse.bass_utils import run_bass_kernel_spmd

F32 = mybir.dt.float32
BF16 = mybir.dt.bfloat16
AF = mybir.ActivationFunctionType
ALU = mybir.AluOpType
AX = mybir.AxisListType

D = 1024
DEPTH = 2
NIN = 2944
NINU = 608
EPS = 1e-6
SEQ = 256
NPS = 4
TP = NPS * SEQ
LS = 2048
PAST = 512
H = 4
QK = 96
GLA_DK = 32
GC = 646
SC = 1025
HC = 192
HY0 = 608
S50 = 1632
GLA0 = 2144
SAFE_SAME_ENGINE = True


class Buf:
    __slots__ = ("w", "r", "x")

    def __init__(self, exclusive=False):
        self.w = None
        self.r = {}
        self.x = exclusive


class Tile:
    def __init__(self, t, n=1):
        self.t = t
        self.b = Buf()
        self.bs = [Buf() for _ in range(n)]

    def __getitem__(self, k):
        return self.t[k]


class KB:
    def __init__(self, nc, es):
        self.nc, self.es = nc, es
        self.E = {"pe": nc.tensor, "act": nc.scalar, "dve": nc.vector, "pool": nc.gpsimd, "sp": nc.sync}
        self.sem = {k: es.enter_context(nc.semaphore("s_" + k)) for k in ("pe", "act", "dve", "pool")}
        self.NDS = 16
        for i in range(self.NDS):
            self.sem["d%d" % i] = es.enter_context(nc.semaphore("s_d%d" % i))
        self.cnt = dict.fromkeys(self.sem, 0)
        self.ndma = 0
        self.ndma_pool = 0
        self.seen = {}
        self.n = 0

    def sb(self, shape, dt, n=1):
        self.n += 1
        if getattr(self, "arena", None) is not None and self.in_arena:
            return self._carve(list(shape), dt)
        return Tile(self.es.enter_context(self.nc.sbuf_tensor("sb%d" % self.n, list(shape), dt)), n)

    def make_arena(self, ncol_f32):
        self.arena = self.es.enter_context(self.nc.sbuf_tensor("arena", [128, ncol_f32], F32))
        self.arena_cols = ncol_f32
        self.in_arena = False
        self.aoff = 0

    def phase(self, name):
        self.in_arena = name is not None
        self.aoff = 0

    def _carve(self, shape, dt):
        nel = 1
        for d_ in shape[1:]:
            nel *= d_
        ncol = nel if dt == F32 else (nel + 1) // 2
        ncol = (ncol + 7) // 8 * 8
        assert self.aoff + ncol <= self.arena_cols, ("arena overflow", self.aoff, ncol, shape)
        ap = self.arena[0:shape[0], self.aoff:self.aoff + ncol]
        self.aoff += ncol
        if dt != F32:
            ap = ap.bitcast(dt)
        ap = ap[:, 0:nel]
        if len(shape) == 3:
            ap = ap.rearrange("p (a b) -> p a b", a=shape[1])
        elif len(shape) == 4:
            ap = ap.rearrange("p (a b c) -> p a b c", a=shape[1], b=shape[2])
        return Tile(ap)

    def barrier(self):
        for eng, E in self.E.items():
            for k, c in self.cnt.items():
                if c and k != eng and self.seen.get((eng, k), 0) < c:
                    E.wait_ge(self.sem[k], c)
                    self.seen[(eng, k)] = c

    def ps(self, shape, dt):
        self.n += 1
        t = Tile(self.es.enter_context(self.nc.psum_tensor("ps%d" % self.n, list(shape), dt)))
        t.b.x = True
        return t

    def op(self, eng, fn, reads=(), writes=(), dma=False):
        deps = {}
        for b in reads:
            if b.w:
                deps[b.w[0]] = max(deps.get(b.w[0], 0), b.w[1])
            if b.x:
                for k, c in b.r.items():
                    if k != eng:
                        deps[k] = max(deps.get(k, 0), c)
        for b in writes:
            if b.w:
                deps[b.w[0]] = max(deps.get(b.w[0], 0), b.w[1])
            for k, c in b.r.items():
                deps[k] = max(deps.get(k, 0), c)
        E = self.E[eng]
        if dma:
            if eng == "pool":
                key = "d%d" % (12 + self.ndma_pool % 4)
                self.ndma_pool += 1
            else:
                key = "d%d" % (self.ndma % 12)
                self.ndma += 1
            inc = 16
            if self.cnt[key] and self.seen.get((eng, key), 0) < self.cnt[key]:
                E.wait_ge(self.sem[key], self.cnt[key])
                self.seen[(eng, key)] = self.cnt[key]
        else:
            key, inc = eng, 1
        for pk, c in deps.items():
            if pk == key and (eng == "pe" or not SAFE_SAME_ENGINE):
                continue
            if self.seen.get((eng, pk), 0) >= c:
                continue
            E.wait_ge(self.sem[pk], c)
            self.seen[(eng, pk)] = c
        inst = fn(E)
        self.cnt[key] += inc
        inst.then_inc(self.sem[key], inc)
        c = self.cnt[key]
        for b in reads:
            b.r[key] = c
        for b in writes:
            b.w = (key, c)
            b.r = {}

    def dma(self, out, in_, reads=(), writes=(), q="sp", **kw):
        self.op(q, lambda e: e.dma_start(out=out, in_=in_, **kw), reads, writes, dma=True)

    def finish(self):
        sp = self.E["sp"]
        for k in self.sem:
            if self.cnt[k]:
                sp.wait_ge(self.sem[k], self.cnt[k])


def _b(x):
    return x.b if isinstance(x, Tile) else x


class _Stop(Exception):
    pass


def build_program(stop=None):
    def chk(name):
        if stop == name:
            raise _Stop(name)

    nc = bass.Bass("TRN2", target_bir_lowering=False)
    es = contextlib.ExitStack()

    def din(name, shape, dt=F32):
        return nc.dram_tensor(name, list(shape), dt, kind="ExternalInput").ap()

    def dout(name, shape):
        return nc.dram_tensor(name, list(shape), F32, kind="ExternalOutput").ap()

    xp = din("xp", [TP, D])
    xs = din("xs", [LS, D])
    cond = din("cond", [2, D])
    cckv = din("cckv", [DEPTH, PAST, 128])
    ckro = din("ckro", [DEPTH, PAST, 32])
    ropec = din("ropec", [LS, 32])
    ropes = din("ropes", [LS, 32])
    norm_w = din("norm_w", [DEPTH, D])
    ada_w = din("ada_w", [DEPTH, D, 3 * D])
    ada_b = din("ada_b", [DEPTH, 3 * D])
    w_in = din("w_in", [DEPTH, D, NIN])
    w_out = din("w_out", [DEPTH, D, D])
    qa_n = din("mla_qa_norm", [DEPTH, 192])
    kva_n = din("mla_kva_norm", [DEPTH, 128])
    w_uq = din("mla_w_uq", [DEPTH, 192, 384])
    w_ukv = din("mla_w_ukv", [DEPTH, 128, 512])
    q_n = din("mla_q_norm", [DEPTH, 96])
    k_n = din("mla_k_norm", [DEPTH, 96])
    gla_gw = din("gla_gw", [DEPTH, 2, 16, 128])
    gla_gb = din("gla_gb", [DEPTH, 2, 128])
    gla_nw = din("gla_norm", [DEPTH, 64])
    sgla = din("sgla", [DEPTH, 2, 128, 64])
    gcon = din("gcon", [128, GC])
    s5_are = din("s5_a_re", [DEPTH, 2, 16, 64])
    s5_aim = din("s5_a_im", [DEPTH, 2, 16, 64])
    s5_ldt = din("s5_log_dt", [DEPTH, 2, 16])
    s5_bre = din("s5_b_re", [DEPTH, 2, 16, 64, 16])
    s5_bim = din("s5_b_im", [DEPTH, 2, 16, 64, 16])
    s5_cre = din("s5_c_re", [DEPTH, 2, 16, 16, 64])
    s5_cim = din("s5_c_im", [DEPTH, 2, 16, 16, 64])
    s5_dd = din("s5_d", [DEPTH, 256])
    s5_gw = din("s5_glu_w", [DEPTH, 256, 256])
    s5_gb = din("s5_glu_b", [DEPTH, 256])
    ss5 = din("ss5", [DEPTH, 2, 1024, 2])
    scon = din("scon", [128, SC])
    qmask = din("qmask", [128, 4])
    hcosf = din("hcosf", [LS // 128, 128, LS], BF16)
    hsinf = din("hsinf", [LS // 128, 128, LS], BF16)
    hcosq = din("hcosq", [LS, 512], BF16)
    hsinq = din("hsinq", [LS, 512], BF16)
    hy_cw = din("hy_conv_w", [DEPTH, 3, 768])
    hy_cb = din("hy_conv_b", [DEPTH, 768])
    hy_w1 = din("hy_w1", [DEPTH, 33, 64])
    hy_b1 = din("hy_b1", [DEPTH, 64])
    hy_f1 = din("hy_freq1", [DEPTH, 64])
    hy_w2 = din("hy_w2", [DEPTH, 64, 64])
    hy_b2 = din("hy_b2", [DEPTH, 64])
    hy_f2 = din("hy_freq2", [DEPTH, 64])
    hy_w3 = din("hy_w3", [DEPTH, 64, 1024])
    hy_bi = din("hy_bias", [DEPTH, 2, 256])
    htab = {}
    for nm_, L_ in (("p", SEQ), ("s", LS)):
        htab[nm_] = dict(cos=din("hcos_" + nm_, [L_, L_], BF16), sin=din("hsin_" + nm_, [L_, L_], BF16),
                         feat=din("hfeat_" + nm_, [33, L_], BF16), win=din("hwin_" + nm_, [L_, 256]))
    hcon = din("hcon", [128, HC])
    haltr = din("haltr", [1, 512], BF16)

    yp = dout("yp", [TP, D])
    ys = dout("ys", [LS, D])
    o_ckv = dout("o_ckv", [NPS, DEPTH, SEQ, 128])
    o_kro = dout("o_kro", [NPS, DEPTH, SEQ, 32])
    o_s5 = dout("o_s5", [NPS, DEPTH, 2, 16, 64, 2])
    o_gla = dout("o_gla", [NPS, DEPTH, 2, 4, 32, 64])
    dbg = dout("dbg", [128, 2048]) if stop else None

    xmid_p = nc.dram_tensor("xmid_p", [TP, D], F32, kind="Internal").ap()
    xmid_s = nc.dram_tensor("xmid_s", [LS, D], F32, kind="Internal").ap()
    xmid_p_b, xmid_s_b = Buf(), Buf()
    s5w = nc.dram_tensor("s5w", [16, 128, 34 * 128], BF16, kind="Internal").ap()
    s5k = nc.dram_tensor("s5k", [2, 128, 16 * 128], BF16, kind="Internal").ap()
    s5w_b, s5k_b = Buf(), Buf()
    wc = {n_: (nc.dram_tensor("wc_" + n_, sh_, BF16, kind="Internal").ap(), Buf()) for n_, sh_ in (
        ("in", [128, 8 * NINU]), ("gla", [128, 8 * 800]), ("s5", [128, 8 * 512]), ("hy0", [128, 8 * 512]), ("hy1", [128, 8 * 512]),
        ("w30", [64, 512]), ("w31", [64, 512]), ("out", [128, 8 * D]))}

    with es:
        kb = KB(nc, es)
        sb, ps, op, dma = kb.sb, kb.ps, kb.op, kb.dma

        ident_f = sb([128, 128], F32)
        ident_b = sb([128, 128], BF16)
        ones_f = sb([1, 128], F32)
        eps_c = sb([128, 1], F32)
        zero_f = sb([128, 512], BF16)
        op("pool", lambda e: e.memset(ident_f[:], 1.0), writes=[ident_f.b])
        op("pool", lambda e: e.affine_select(out=ident_f[:], in_=ident_f[:], pattern=[[-1, 128]],
                                             compare_op=ALU.is_equal, fill=0.0, base=0,
                                             channel_multiplier=1), reads=[ident_f.b], writes=[ident_f.b])
        op("pool", lambda e: e.tensor_copy(out=ident_b[:], in_=ident_f[:]), reads=[ident_f.b], writes=[ident_b.b])
        op("pool", lambda e: e.memset(ones_f[:], 1.0), writes=[ones_f.b])
        op("pool", lambda e: e.memset(eps_c[:], EPS), writes=[eps_c.b])
        op("pool", lambda e: e.memset(zero_f[:], 0.0), writes=[zero_f.b])
        qm = sb([128, 4], F32)
        dma(qm[:, :], qmask[:, :], writes=[qm.b], q="pool")

        pf = [ps([128, 512], F32) for _ in range(6)]
        pb = [ps([128, 1024], BF16) for _ in range(2)]
        rr = {"f": 0, "b": 0}

        def psf():
            rr["f"] = (rr["f"] + 1) % 5
            return pf[rr["f"]]

        p_acc = pf[5]

        def psb():
            rr["b"] = (rr["b"] + 1) % len(pb)
            return pb[rr["b"]]

        qT = sb([128, H, LS], BF16)
        kT = sb([128, H, LS + PAST], BF16)
        vaug = sb([128, (LS + PAST) // 128, H, 66], BF16)
        hT = sb([128, 8, LS], BF16)
        brT = sb([128, 8, LS], BF16)
        w_uq_a = sb([128, 384], BF16)
        w_uq_c = sb([64, 384], BF16)
        w_ukv_b = sb([128, 512], BF16)
        bc_qa = sb([128, 192], F32)
        bc_kva = sb([128, 128], F32)
        bc_qn = sb([128, 96], F32)
        bc_kn = sb([128, 96], F32)
        gate_bc = [sb([128, D], F32) for _ in range(2)]
        effsc = sb([128, 2, 8], F32)
        shift = sb([128, 2, 8], F32)
        rowst = sb([1, D], F32)
        scond = sb([8, 2, 128], F32)
        scondT = sb([128, 2, 8], F32)
        screp = sb([128, 128], F32)
        colst = sb([24, 128], F32)
        colsT = sb([128, 3, 8], F32)
        nwst = sb([8, 128], F32)
        nwT = sb([128, 8], F32)
        modacc = sb([128, 32], F32)

        class _Alias:
            def __init__(self, ap, owner):
                self.ap, self.b = ap, owner.b

            def __getitem__(self, k):
                return self.ap[k]

        wstage = [_Alias(qT.t[:].bitcast(F32).rearrange("p a c -> p (a c)")[:, 0:3 * D], qT)] * 2
        adast = wstage
        w_out_b = _Alias(kT.t[:].rearrange("p h t -> p (h t)")[:, 0:8 * D].rearrange("p (k c) -> p k c", k=8), kT)
        wmix = _Alias(kT.t[:].rearrange("p h t -> p (h t)")[:, 0:6400].rearrange("p (k c) -> p k c", k=8), kT)
        gcon_t = sb([128, GC], F32)
        dma(gcon_t[:, :], gcon[:, :], writes=[gcon_t.b])
        maskF, maskB_ = gcon_t[:, 0:128], gcon_t[:, 128:256]
        headm, halfm, blockm, ones64 = gcon_t[:, 256:260], gcon_t[:, 260:262], gcon_t[:, 262:518], gcon_t[:, 518:582]
        gwp_f = sb([32, 2, 128], F32)
        gwp = sb([32, 2, 128], BF16)
        gbst = sb([2, 128], F32)
        ngb = sb([128, 2], F32)
        bc_gn = sb([128, 64], F32)
        kb.make_arena(10496)
        kb.phase("GLA")
        o_bwd = _Alias(qT.t[:].bitcast(F32).rearrange("p a (b c) -> p (a b) c", c=256), qT)
        g_qf, g_kf, g_sp, g_bcp, g_e1, g_e2, g_e3 = [sb([128, 128], F32) for _ in range(7)]
        g_glr = sb([32, 128], BF16)
        g_vt = sb([128, 256], BF16)
        g_qt, g_kt, g_khT, g_khA, g_khB, g_attm = [sb([128, 128], BF16) for _ in range(6)]
        g_km = [sb([128, 128], BF16) for _ in range(4)]
        g_QA = [sb([128, 128], BF16) for _ in range(4)]
        g_QB = [sb([128, 128], BF16) for _ in range(4)]
        g_nbl = sb([128, 2], F32)
        g_tot = sb([128, 2], F32)
        g_eb = sb([128, 2], F32)
        g_dS = sb([128, 256], F32)
        g_dSr = [sb([128, 64], F32) for _ in range(2)]
        g_S = [sb([128, 64], F32) for _ in range(3)]
        g_Sb = [sb([128, 64], BF16) for _ in range(2)]
        g_os = sb([128, 256], F32)
        g_sq = sb([128, 256], F32)
        g_ss = sb([128, 4], F32)
        g_rs = sb([128, 4], F32)
        g_sil = sb([128, 256], F32)
        g_og = sb([128, 256], BF16)

        kflat = kT.t[:].rearrange("p h t -> p (h t)")
        vflat = vaug.t[:].rearrange("p a h d -> p (a h d)")
        s_wmix = _Alias(kflat[:, 0:4096].rearrange("p (k c) -> p k c", k=8), kT)
        s_uT = _Alias(kflat[:, 4096:8192].rearrange("p (h t) -> p h t", h=2), kT)
        s_Kbd = _Alias(kflat[:, 8192:10240].rearrange("p (a c) -> p a c", a=16), kT)
        s_gT = _Alias(vflat[:, 0:4096].rearrange("p (h t) -> p h t", h=2), vaug)
        s_Kacc = _Alias(qT.t[:].bitcast(F32).rearrange("p a c -> p (a c)")[:, 0:2048].rearrange("p (a c) -> p a c", a=16), qT)
        kb.phase(None)
        scon_t = sb([128, SC], F32)
        dma(scon_t[:, :], scon[:, :], writes=[scon_t.b])
        s_prst = sb([16, 128], F32)
        s_ldst = sb([16, 2], F32)
        s_par = {n_: sb([128, 16], F32) for n_ in ("are", "aim", "dt", "lr", "th", "er", "ei", "t1", "t2", "ar1", "ai1", "nai1",
                                                    "kr", "ki", "nki", "dr", "den")}
        s_pwr = sb([128, 8, 16], F32)
        s_pwi = sb([128, 8, 16], F32)
        s_npwi = sb([128, 8, 16], F32)
        s_dcol = sb([128, 2], F32)
        s_gbcol = sb([128, 2], F32)
        s_st2 = sb([2, 128], F32)
        s_Dd = [sb([128, 128], BF16) for _ in range(2)]
        s_glu = sb([128, 2, 256], BF16)
        kb.phase("S5")
        s_B = [sb([128, 16], F32) for _ in range(2)]
        s_bb = [sb([128, 16], F32) for _ in range(2)]
        s_Xm = [[sb([128, 128], F32) for _ in range(2)] for _ in range(2)]
        s_Xb = [[sb([128, 128], BF16) for _ in range(2)] for _ in range(8)]
        s_W = sb([128, 34, 128], BF16)

        class _View:
            def __init__(self, ap, b):
                self.ap, self.b = ap, b

            def __getitem__(self, k):
                return self.ap[k]

        s_Wb = [[_View(s_W[:, 2 * k_ + r_, :], s_W.b) for r_ in range(2)] for k_ in range(8)]
        s_C = [sb([128, 64], F32) for _ in range(2)]
        s_Cp = sb([128, 128], F32)
        s_Cm = [[sb([128, 128], F32) for _ in range(2)] for _ in range(2)]
        s_Wc = [[_View(s_W[:, 16 + 2 * k_ + r_, :], s_W.b) for r_ in range(2)] for k_ in range(9)]
        s_XA = [sb([128, 256], F32) for _ in range(2)]
        s_XB = [sb([128, 256], F32) for _ in range(2)]
        s_Hs = [sb([128, 4 * 33 + 257], BF16) for _ in range(2)]
        s_h0 = sb([128, 2], F32)
        s_fin = sb([128, 2], F32)
        s_tmp = sb([128, 4], F32)
        s_ya = sb([128, 512], F32)
        s_yb = sb([128, 512], F32)
        kb.phase(None)

        def bcast_row(dst, src_row_ap, n):
            dma(rowst[0:1, 0:n], src_row_ap, writes=[rowst.b])
            for c0 in range(0, n, 512):
                w = min(512, n - c0)
                p = psf()
                op("pe", lambda e: e.matmul(p[:, 0:w], lhsT=ones_f[0:1, :], rhs=rowst[0:1, c0:c0 + w],
                                            start=True, stop=True),
                   reads=[ones_f.b, rowst.b], writes=[p.b])
                op("dve", lambda e: e.tensor_copy(out=dst[:, c0:c0 + w], in_=p[:, 0:w]),
                   reads=[p.b], writes=[dst.b])

        def rstd_from_ss(dst, ss, n):
            op("act", lambda e: e.activation(out=dst[:], in_=ss[:], func=AF.Sqrt, scale=1.0 / n,
                                             bias=eps_c[:, 0:1]), reads=[ss.b, eps_c.b], writes=[dst.b])
            op("dve", lambda e: e.reciprocal(out=dst[:], in_=dst[:]), reads=[dst.b], writes=[dst.b])

        hcon_t = sb([128, HC], F32)
        dma(hcon_t[:, :], hcon[:, :], writes=[hcon_t.b])
        haltr_t = sb([1, 512], BF16)
        dma(haltr_t[:, :], haltr[:, :], writes=[haltr_t.b])
        h_alt = sb([128, 128], BF16)
        op("dve", lambda e: e.tensor_copy(out=h_alt[:], in_=hcon_t[:, 0:128]), reads=[hcon_t.b], writes=[h_alt.b])
        ones_sq = sb([128, 128], F32)
        op("pool", lambda e: e.memset(ones_sq[:], 1.0), writes=[ones_sq.b])
        hw1_b = sb([33, 64], BF16)
        hw2_b = sb([64, 64], BF16)
        h_mrow = sb([4, 64], F32)
        h_mcol = sb([64, 4], F32)
        h_mlp = sb([64, 8], F32)
        h_rows = sb([14, 128], F32)
        h_colp = sb([128, 14], F32)
        h_w3 = sb([64, 4, 128], BF16)
        h_win = sb([128, 128], F32)
        h_feat = sb([33, 512], BF16)
        h_cr = [sb([128, 512], BF16) for _ in range(2)]
        h_sr = [sb([128, 512], BF16) for _ in range(2)]
        h_sig = _Alias(qT.t[:], qT)
        class _Sub(_Alias):
            def __init__(self, ap):
                self.ap, self.b = ap, Buf()

        h_tc = [_Sub(kflat[:, 2048 * i_:2048 * i_ + 2048].rearrange("p (a j) -> p a j", a=16)) for i_ in (0, 1)]
        h_ts = [_Sub(kflat[:, 2048 * i_:2048 * i_ + 2048].rearrange("p (a j) -> p a j", a=16)) for i_ in (2, 3)]
        h_utm = _Sub(kflat[:, 8192:10240].rearrange("p (a j) -> p a j", a=16))
        h_wmix = _Alias(vflat[:, 0:4096].rearrange("p (k c) -> p k c", k=8), vaug)
        kb.phase("HY")
        h_hid = [sb([64, LS], BF16) for _ in range(2)]
        h_ksd = sb([128, 16, 4, 128], BF16)
        h_Y = sb([128, 17, 2, 128], BF16)
        h_k = [sb([128, 128], F32) for _ in range(2)]
        h_t = [sb([128, 128], F32) for _ in range(4)]
        h_rn = sb([128, 2, 128], F32)
        kb.phase(None)
        h_fw = sb([128, 512], F32)
        h_abs = sb([128, 512], F32)
        h_s2 = sb([64, 512], F32)
        h_s4 = sb([64, 512], F32)

        kb.phase("PA")
        w_in_b = sb([128, 8, NINU], BF16)
        xt = sb([128, D], F32)
        xn = sb([128, D], BF16)
        junk = xn
        ss1 = sb([128, 1], F32)
        rs1 = sb([128, 1], F32)
        lat = sb([128, 352], F32)
        junk2 = sb([128, 192], BF16)
        gm = sb([128, 256], F32)
        ssq = sb([128, 2], F32)
        rsq = sb([128, 2], F32)
        cqn = sb([128, 192], BF16)
        ckvn = sb([128, 128], F32)
        ckvn_b = sb([128, 128], BF16)
        kro = sb([128, 32], F32)
        cqT = sb([128, 2, 128], BF16)
        ckvT = sb([128, 128], BF16)
        qpre = sb([128, H, QK], F32)
        kcat = sb([128, H, QK], F32)
        sqh = sb([128, H, QK], F32)
        ssh = sb([128, H], F32)
        rsh = sb([128, H], F32)
        qf = sb([128, H, QK], F32)
        qb = sb([128, H, QK], BF16)
        rtmp = sb([128, H, 32], F32)
        rtmp2 = sb([128, H, 32], F32)
        cosT = sb([128, 32], F32)
        sinT = sb([128, 32], F32)
        pT2 = [sb([128, 512], BF16) for _ in range(2)]
        omla4 = sb([128, 4, H, 64], F32)
        den = sb([128, H], F32)
        omla = sb([128, H, 64], F32)
        omg = sb([128, 256], BF16)
        silg = sb([128, 256], F32)
        ot = sb([128, D], F32)
        kb.phase(None)

        try:
          for l in range(DEPTH):
              st = wstage[0]
              dma(st[:, 0:384], w_uq[l, 0:128, :], writes=[st.b], q="pool")
              op("dve", lambda e: e.tensor_copy(out=w_uq_a[:], in_=st[:, 0:384]), reads=[st.b], writes=[w_uq_a.b])
              st = wstage[1]
              dma(st[0:64, 0:384], w_uq[l, 128:192, :], writes=[st.b], q="pool")
              op("dve", lambda e: e.tensor_copy(out=w_uq_c[:], in_=st[0:64, 0:384]), reads=[st.b], writes=[w_uq_c.b])
              st = wstage[0]
              dma(st[:, 0:512], w_ukv[l, :, :], writes=[st.b], q="pool")
              op("dve", lambda e: e.tensor_copy(out=w_ukv_b[:], in_=st[:, 0:512]), reads=[st.b], writes=[w_ukv_b.b])
              bcast_row(bc_qa, qa_n[l:l + 1, :], 192)
              bcast_row(bc_kva, kva_n[l:l + 1, :], 128)
              bcast_row(bc_qn, q_n[l:l + 1, :], 96)
              bcast_row(bc_kn, k_n[l:l + 1, :], 96)
              bcast_row(bc_gn, gla_nw[l:l + 1, :], 64)
              op("dve", lambda e: e.memset(gwp_f[:], 0.0), writes=[gwp_f.b])
              dma(gwp_f[0:16, 0, :], gla_gw[l, 0, :, :], reads=[gwp_f.b], writes=[gwp_f.b])
              dma(gwp_f[16:32, 1, :], gla_gw[l, 1, :, :], reads=[gwp_f.b], writes=[gwp_f.b])
              op("dve", lambda e: e.tensor_copy(out=gwp[:], in_=gwp_f[:]), reads=[gwp_f.b], writes=[gwp.b])
              dma(gbst[:, :], gla_gb[l, :, :], writes=[gbst.b])
              p = psf()
              op("pe", lambda e: e.transpose(p[:, 0:2], gbst[:, :], ident_f[0:2, 0:2]), reads=[gbst.b, ident_f.b], writes=[p.b])
              op("dve", lambda e: e.tensor_scalar(out=ngb[:], in0=p[:, 0:2], scalar1=-1.0, scalar2=None, op0=ALU.mult),
                 reads=[p.b], writes=[ngb.b])

              P_ = s_par

              def colparam(dst, rows_ap, nrow=16):
                  dma(s_prst[0:nrow, :], rows_ap, writes=[s_prst.b])
                  p_ = psf()
                  op("pe", lambda e: e.transpose(p_[:, 0:nrow], s_prst[0:nrow, :], ident_f[0:nrow, 0:nrow]),
                     reads=[s_prst.b, ident_f.b], writes=[p_.b])
                  op("dve", lambda e: e.tensor_copy(out=dst, in_=p_[:, 0:nrow]), reads=[p_.b], writes=[s_par_b])

              s_par_b = P_["are"].b
              for t_ in P_.values():
                  t_.b = s_par_b
              s_pwr.b = s_pwi.b = s_npwi.b = s_par_b
              colparam(P_["are"][:], s5_are[l].rearrange("d (st gl) p -> (d st) (gl p)", gl=2))
              colparam(P_["aim"][:], s5_aim[l].rearrange("d (st gl) p -> (d st) (gl p)", gl=2))
              dma(s_ldst[:, :], s5_ldt[l].rearrange("d (st gl) -> (d st) gl", gl=2), writes=[s_ldst.b])
              op("dve", lambda e: e.tensor_copy(out=s_prst[:, :].rearrange("r (gl p) -> r gl p", gl=2),
                                                in_=s_ldst[:, :].unsqueeze(2).to_broadcast([16, 2, 64])),
                 reads=[s_ldst.b], writes=[s_prst.b])
              p_ = psf()
              op("pe", lambda e: e.transpose(p_[:, 0:16], s_prst[:, :], ident_f[0:16, 0:16]), reads=[s_prst.b, ident_f.b], writes=[p_.b])
              op("act", lambda e: e.activation(out=P_["dt"][:], in_=p_[:, 0:16], func=AF.Exp), reads=[p_.b], writes=[s_par_b])

              def pp(eng, fn):
                  op(eng, fn, reads=[s_par_b, scon_t.b], writes=[s_par_b])

              are, aim, dt_, lr, th, er, ei, t1, t2 = (P_[n_] for n_ in ("are", "aim", "dt", "lr", "th", "er", "ei", "t1", "t2"))
              pp("dve", lambda e: e.tensor_scalar(out=are[:], in0=are[:], scalar1=-1e-4, scalar2=None, op0=ALU.min))
              pp("dve", lambda e: e.tensor_tensor(out=lr[:], in0=are[:], in1=dt_[:], op=ALU.mult))
              pp("dve", lambda e: e.tensor_tensor(out=th[:], in0=aim[:], in1=dt_[:], op=ALU.mult))
              pp("act", lambda e: e.activation(out=t1[:], in_=lr[:], func=AF.Exp, scale=1.0 / 16))
              pp("act", lambda e: e.activation(out=er[:], in_=th[:], func=AF.Sin, scale=1.0 / 16, bias=scon_t[:, 1024:1025]))
              pp("act", lambda e: e.activation(out=ei[:], in_=th[:], func=AF.Sin, scale=1.0 / 16))
              pp("dve", lambda e: e.tensor_tensor(out=er[:], in0=er[:], in1=t1[:], op=ALU.mult))
              pp("dve", lambda e: e.tensor_tensor(out=ei[:], in0=ei[:], in1=t1[:], op=ALU.mult))

              def csq():
                  pp("dve", lambda e: e.tensor_tensor(out=t1[:], in0=er[:], in1=er[:], op=ALU.mult))
                  pp("dve", lambda e: e.tensor_tensor(out=t2[:], in0=ei[:], in1=ei[:], op=ALU.mult))
                  pp("dve", lambda e: e.scalar_tensor_tensor(out=ei[:], in0=er[:], scalar=2.0, in1=ei[:], op0=ALU.mult, op1=ALU.mult))
                  pp("dve", lambda e: e.tensor_tensor(out=er[:], in0=t1[:], in1=t2[:], op=ALU.subtract))

              for _ in range(4):
                  csq()
              ar1, ai1, nai1, kr, ki, nki, dr, dn_ = (P_[n_] for n_ in ("ar1", "ai1", "nai1", "kr", "ki", "nki", "dr", "den"))
              pp("dve", lambda e: e.tensor_copy(out=ar1[:], in_=er[:]))
              pp("dve", lambda e: e.tensor_copy(out=ai1[:], in_=ei[:]))
              pp("dve", lambda e: e.tensor_scalar(out=nai1[:], in0=ei[:], scalar1=-1.0, scalar2=None, op0=ALU.mult))
              pp("dve", lambda e: e.tensor_scalar(out=dr[:], in0=ar1[:], scalar1=-1.0, scalar2=None, op0=ALU.add))
              pp("dve", lambda e: e.tensor_tensor(out=t1[:], in0=are[:], in1=are[:], op=ALU.mult))
              pp("dve", lambda e: e.tensor_tensor(out=t2[:], in0=aim[:], in1=aim[:], op=ALU.mult))
              pp("dve", lambda e: e.tensor_tensor(out=dn_[:], in0=t1[:], in1=t2[:], op=ALU.add))
              pp("dve", lambda e: e.reciprocal(out=dn_[:], in_=dn_[:]))
              pp("dve", lambda e: e.tensor_tensor(out=t1[:], in0=dr[:], in1=are[:], op=ALU.mult))
              pp("dve", lambda e: e.tensor_tensor(out=t2[:], in0=ai1[:], in1=aim[:], op=ALU.mult))
              pp("dve", lambda e: e.tensor_tensor(out=t1[:], in0=t1[:], in1=t2[:], op=ALU.add))
              pp("dve", lambda e: e.tensor_tensor(out=kr[:], in0=t1[:], in1=dn_[:], op=ALU.mult))
              pp("dve", lambda e: e.tensor_tensor(out=t1[:], in0=ai1[:], in1=are[:], op=ALU.mult))
              pp("dve", lambda e: e.tensor_tensor(out=t2[:], in0=dr[:], in1=aim[:], op=ALU.mult))
              pp("dve", lambda e: e.tensor_tensor(out=t1[:], in0=t1[:], in1=t2[:], op=ALU.subtract))
              pp("dve", lambda e: e.tensor_tensor(out=ki[:], in0=t1[:], in1=dn_[:], op=ALU.mult))
              pp("dve", lambda e: e.tensor_scalar(out=nki[:], in0=ki[:], scalar1=-1.0, scalar2=None, op0=ALU.mult))
              for _ in range(3):
                  csq()
              for k in range(8):
                  pp("dve", lambda e: e.tensor_copy(out=s_pwr[:, k, :], in_=er[:]))
                  pp("dve", lambda e: e.tensor_copy(out=s_pwi[:, k, :], in_=ei[:]))
                  pp("dve", lambda e: e.tensor_scalar(out=s_npwi[:, k, :], in0=ei[:], scalar1=-1.0, scalar2=None, op0=ALU.mult))
                  if k < 7:
                      csq()
              for dst_, src_ in ((s_dcol, s5_dd), (s_gbcol, s5_gb)):
                  dma(s_st2[:, :], src_[l].rearrange("(h p) -> h p", p=128), writes=[s_st2.b])
                  p_ = psf()
                  op("pe", lambda e: e.transpose(p_[:, 0:2], s_st2[:, :], ident_f[0:2, 0:2]), reads=[s_st2.b, ident_f.b], writes=[p_.b])
                  op("dve", lambda e: e.tensor_copy(out=dst_[:], in_=p_[:, 0:2]), reads=[p_.b], writes=[dst_.b])
              for hf in range(2):
                  op("dve", lambda e: e.tensor_scalar(out=s_Dd[hf][:], in0=ident_f[:], scalar1=s_dcol[:, hf:hf + 1], scalar2=None, op0=ALU.mult),
                     reads=[ident_f.b, s_dcol.b], writes=[s_Dd[hf].b])
                  st = wstage[0]
                  dma(st[:, 0:256], s5_gw[l, hf * 128:(hf + 1) * 128, :], writes=[st.b], q="pool")
                  op("dve", lambda e: e.tensor_copy(out=s_glu[:, hf, :], in_=st[:, 0:256]), reads=[st.b], writes=[s_glu.b])
              for r_, src_ in enumerate((hy_f1, hy_b1, hy_f2, hy_b2)):
                  dma(h_mrow[r_:r_ + 1, :], src_[l:l + 1, :], reads=[h_mrow.b], writes=[h_mrow.b])
              p_ = psf()
              op("pe", lambda e: e.transpose(p_[0:64, 0:4], h_mrow[:, :], ident_f[0:4, 0:4]), reads=[h_mrow.b, ident_f.b], writes=[p_.b])
              op("dve", lambda e: e.tensor_copy(out=h_mcol[:], in_=p_[0:64, 0:4]), reads=[p_.b], writes=[h_mcol.b])
              for li in range(2):
                  fcol, bcol = h_mcol[:, 2 * li:2 * li + 1], h_mcol[:, 2 * li + 1:2 * li + 2]
                  o4 = 4 * li
                  op("dve", lambda e: e.tensor_scalar(out=h_mlp[:, o4:o4 + 1], in0=fcol, scalar1=0.5, scalar2=None, op0=ALU.mult),
                     reads=[h_mcol.b, h_mlp.b], writes=[h_mlp.b])
                  op("dve", lambda e: e.scalar_tensor_tensor(out=h_mlp[:, o4 + 1:o4 + 2], in0=fcol, scalar=0.5, in1=bcol, op0=ALU.mult, op1=ALU.mult),
                     reads=[h_mcol.b, h_mlp.b], writes=[h_mlp.b])
                  op("dve", lambda e: e.tensor_scalar(out=h_mlp[:, o4 + 2:o4 + 3], in0=fcol, scalar1=0.25, scalar2=None, op0=ALU.mult),
                     reads=[h_mcol.b, h_mlp.b], writes=[h_mlp.b])
                  op("dve", lambda e: e.scalar_tensor_tensor(out=h_mlp[:, o4 + 3:o4 + 4], in0=fcol, scalar=0.25, in1=bcol, op0=ALU.mult, op1=ALU.mult),
                     reads=[h_mcol.b, h_mlp.b], writes=[h_mlp.b])
              st = wstage[0]
              dma(st[0:33, 0:64], hy_w1[l, :, :], writes=[st.b], q="pool")
              op("dve", lambda e: e.tensor_copy(out=hw1_b[:], in_=st[0:33, 0:64]), reads=[st.b], writes=[hw1_b.b])
              dma(st[0:64, 0:64], hy_w2[l, :, :], writes=[st.b], q="pool")
              op("dve", lambda e: e.tensor_copy(out=hw2_b[:], in_=st[0:64, 0:64]), reads=[st.b], writes=[hw2_b.b])
              chk('weights')
              if l == 0:
                  for c in range(2):
                      dma(scond[:, c, :], cond[c].rearrange("(k p) -> k p", p=128), writes=[scond.b])
                  op("act", lambda e: e.activation(out=scond[:], in_=scond[:], func=AF.Silu),
                     reads=[scond.b], writes=[scond.b])
                  for c in range(2):
                      p = psf()
                      op("pe", lambda e: e.transpose(p[:, 0:8], scond[:, c, :], ident_f[0:8, 0:8]),
                         reads=[scond.b, ident_f.b], writes=[p.b])
                      op("dve", lambda e: e.tensor_copy(out=scondT[:, c, :], in_=p[:, 0:8]),
                         reads=[p.b], writes=[scondT.b])
              dma(colst[:, :], ada_b[l].rearrange("(j p) -> j p", p=128), writes=[colst.b])
              p = psf()
              op("pe", lambda e: e.transpose(p[:, 0:24], colst[:, :], ident_f[0:24, 0:24]),
                 reads=[colst.b, ident_f.b], writes=[p.b])
              op("dve", lambda e: e.tensor_copy(out=colsT[:].rearrange("p a k -> p (a k)"), in_=p[:, 0:24]),
                 reads=[p.b], writes=[colsT.b])
              dma(nwst[:, :], norm_w[l].rearrange("(k p) -> k p", p=128), writes=[nwst.b])
              p = psf()
              op("pe", lambda e: e.transpose(p[:, 0:8], nwst[:, :], ident_f[0:8, 0:8]),
                 reads=[nwst.b, ident_f.b], writes=[p.b])
              op("dve", lambda e: e.tensor_copy(out=nwT[:], in_=p[:, 0:8]), reads=[p.b], writes=[nwT.b])

              pmod = modacc
              op("dve", lambda e: e.memset(modacc[:], 0.0), writes=[modacc.b])
              pk = psf()
              pg = [psf() for _ in range(4)]
              for k in range(8):
                  a = adast[k % 2]
                  dma(a[:, :], ada_w[l, k * 128:(k + 1) * 128, :], writes=[a.b])
                  for j in range(16):
                      op("pe", lambda e: e.matmul(pk[:, 2 * j:2 * j + 2], lhsT=a[:, j * 128:(j + 1) * 128],
                                                  rhs=scondT[:, :, k], start=True, stop=True),
                         reads=[a.b, scondT.b], writes=[pk.b])
                  op("dve", lambda e: e.tensor_tensor(out=modacc[:, 0:32], in0=modacc[:, 0:32], in1=pk[:, 0:32], op=ALU.add),
                     reads=[pk.b, modacc.b], writes=[modacc.b])
                  for c in range(2):
                      op("dve", lambda e: e.tensor_copy(out=screp[:], in_=scondT[:, c, k:k + 1].to_broadcast([128, 128])),
                         reads=[scondT.b], writes=[screp.b])
                      for hf in range(2):
                          pgt = pg[2 * c + hf]
                          op("pe", lambda e: e.matmul(pgt[:, :], lhsT=screp[:, :],
                                                      rhs=a[:, 2048 + hf * 512:2048 + (hf + 1) * 512],
                                                      start=(k == 0), stop=False),
                             reads=[a.b, screp.b], writes=[pgt.b])
              dma(rowst[0:1, 0:D], ada_b[l:l + 1, 2048:3072], writes=[rowst.b])
              for c in range(2):
                  for hf in range(2):
                      pgt = pg[2 * c + hf]
                      op("pe", lambda e: e.matmul(pgt[:, :], lhsT=ones_f[0:1, :], rhs=rowst[0:1, hf * 512:(hf + 1) * 512],
                                                  start=False, stop=True),
                         reads=[ones_f.b, rowst.b], writes=[pgt.b])
                      op("dve", lambda e: e.tensor_copy(out=gate_bc[c][:, hf * 512:(hf + 1) * 512], in_=pgt[:, :]),
                         reads=[pgt.b], writes=[gate_bc[c].b])
              pm = pmod[:, 0:32].rearrange("p (j c) -> p c j", c=2)
              for c in range(2):
                  op("dve", lambda e: e.tensor_tensor(out=shift[:, c, :], in0=pm[:, c, 0:8], in1=colsT[:, 0, :], op=ALU.add),
                     reads=[pmod.b, colsT.b], writes=[shift.b])
                  op("dve", lambda e: e.tensor_tensor(out=effsc[:, c, :], in0=pm[:, c, 8:16], in1=colsT[:, 1, :], op=ALU.add),
                     reads=[pmod.b, colsT.b], writes=[effsc.b])
                  op("dve", lambda e: e.scalar_tensor_tensor(out=effsc[:, c, :], in0=effsc[:, c, :], scalar=1.0,
                                                             in1=nwT[:, :], op0=ALU.add, op1=ALU.mult),
                     reads=[effsc.b, nwT.b], writes=[effsc.b])

              chk('ada')
              def gla_group(l, g):
                  c, T, L, sample = g["c"], g["T"], g["L"], g["sample"]
                  for t_ in g_QA + g_QB:
                      op("pool", lambda e: e.memset(t_[:], 0.0), writes=[t_.b])
                  tps = L // 128
                  nseq = T // L
                  if sample:
                      dma(wmix[:, :, :].rearrange("p k c -> p (k c)"), wc["gla"][0][:, :], reads=[wc["gla"][1]], writes=[wmix.b], q="pool")
                  else:
                      for k in range(8):
                          st = wstage[0]
                          dma(st[:, 0:800], w_in[l, k * 128:(k + 1) * 128, GLA0:GLA0 + 800], writes=[st.b], q="pool")
                          op("dve" if k % 2 == 0 else "pool", lambda e: e.tensor_copy(out=wmix[:, k, :], in_=st[:, 0:800]),
                             reads=[st.b], writes=[wmix.b])
                      dma(wc["gla"][0][:, :], wmix[:, :, :].rearrange("p k c -> p (k c)"), reads=[wmix.b], writes=[wc["gla"][1]])
                  for d in (1, 0):
                      mask_d = maskF if d == 0 else maskB_
                      for s_ in range(nseq):
                          S0, S1, S2 = g_S
                          if sample:
                              dma(S0[:, :], sgla[l, d, :, :], writes=[S0.b])
                          else:
                              op("dve", lambda e: e.memset(S0[:], 0.0), writes=[S0.b])
                          order = range(tps) if d == 0 else range(tps - 1, -1, -1)
                          for tt in order:
                              ti = s_ * tps + tt
                              r0 = ti * 128
                              pq_, pk_, pg_, pv_ = psf(), psf(), psf(), psf()
                              for k in range(8):
                                  op("pe", lambda e: e.matmul(pq_[:, 0:128], lhsT=wmix[:, k, 0:128], rhs=hT[:, k, r0:r0 + 128],
                                                              start=(k == 0), stop=(k == 7)), reads=[wmix.b, hT.b], writes=[pq_.b])
                              for k in range(8):
                                  op("pe", lambda e: e.matmul(pk_[:, 0:128], lhsT=wmix[:, k, 128:256], rhs=hT[:, k, r0:r0 + 128],
                                                              start=(k == 0), stop=(k == 7)), reads=[wmix.b, hT.b], writes=[pk_.b])
                              for k in range(8):
                                  op("pe", lambda e: e.matmul(pg_[0:32, 0:128], lhsT=wmix[:, k, 512:544], rhs=hT[:, k, r0:r0 + 128],
                                                              start=(k == 0), stop=(k == 7)), reads=[wmix.b, hT.b], writes=[pg_.b])
                              for k in range(8):
                                  op("pe", lambda e: e.matmul(pv_[:, 0:256], lhsT=hT[:, k, r0:r0 + 128], rhs=wmix[:, k, 256:512],
                                                              start=(k == 0), stop=(k == 7)), reads=[wmix.b, hT.b], writes=[pv_.b])
                              op("act", lambda e: e.activation(out=g_qf[:], in_=pq_[:, 0:128], func=AF.Copy), reads=[pq_.b], writes=[g_qf.b])
                              op("act", lambda e: e.activation(out=g_kf[:], in_=pk_[:, 0:128], func=AF.Copy), reads=[pk_.b], writes=[g_kf.b])
                              op("act", lambda e: e.activation(out=g_glr[:], in_=pg_[0:32, 0:128], func=AF.Copy), reads=[pg_.b], writes=[g_glr.b])
                              op("dve", lambda e: e.tensor_copy(out=g_vt[:], in_=pv_[:, 0:256]), reads=[pv_.b], writes=[g_vt.b])
                              pl_ = psf()
                              op("pe", lambda e: e.matmul(pl_[:, 0:128], lhsT=gwp[:, d, :], rhs=g_glr[:, :], start=True, stop=True),
                                 reads=[gwp.b, g_glr.b], writes=[pl_.b])
                              op("act", lambda e: e.activation(out=g_sp[:], in_=pl_[:, 0:128], func=AF.Exp, scale=-1.0, bias=ngb[:, d:d + 1]),
                                 reads=[pl_.b, ngb.b], writes=[g_sp.b])
                              op("act", lambda e: e.activation(out=g_sp[:], in_=g_sp[:], func=AF.Ln, bias=ones64[:, 0:1]),
                                 reads=[g_sp.b, gcon_t.b], writes=[g_sp.b])
                              for ch in range(2):
                                  c0 = 64 * ch
                                  op("dve", lambda e: e.tensor_tensor_scan(out=g_bcp[:, c0:c0 + 64], data0=ones64, data1=g_sp[:, c0:c0 + 64],
                                                                           initial=0.0, op0=ALU.mult, op1=ALU.add),
                                     reads=[g_sp.b, gcon_t.b, g_bcp.b], writes=[g_bcp.b])
                              if d == 1:
                                  for ch in range(2):
                                      c0 = 64 * ch
                                      op("dve", lambda e: e.tensor_copy(out=g_tot[:, ch:ch + 1], in_=g_bcp[:, c0 + 63:c0 + 64]),
                                         reads=[g_bcp.b, g_tot.b], writes=[g_tot.b])
                                  op("dve", lambda e: e.tensor_tensor(out=g_bcp[:], in0=g_sp[:], in1=g_bcp[:], op=ALU.subtract),
                                     reads=[g_sp.b, g_bcp.b], writes=[g_bcp.b])
                                  for ch in range(2):
                                      c0 = 64 * ch
                                      op("dve", lambda e: e.tensor_scalar(out=g_bcp[:, c0:c0 + 64], in0=g_bcp[:, c0:c0 + 64],
                                                                          scalar1=g_tot[:, ch:ch + 1], scalar2=None, op0=ALU.add),
                                         reads=[g_bcp.b, g_tot.b], writes=[g_bcp.b])
                              for ch in range(2):
                                  col = 64 * ch + (63 if d == 0 else 0)
                                  op("dve", lambda e: e.tensor_scalar(out=g_nbl[:, ch:ch + 1], in0=g_bcp[:, col:col + 1], scalar1=-1.0 / 16,
                                                                      scalar2=None, op0=ALU.mult), reads=[g_bcp.b, g_nbl.b], writes=[g_nbl.b])
                              op("act", lambda e: e.activation(out=g_eb[:], in_=g_nbl[:], func=AF.Exp), reads=[g_nbl.b], writes=[g_eb.b])
                              op("act", lambda e: e.activation(out=g_e1[:], in_=g_bcp[:], func=AF.Exp, scale=-1.0 / 16), reads=[g_bcp.b], writes=[g_e1.b])
                              op("act", lambda e: e.activation(out=g_e2[:], in_=g_bcp[:], func=AF.Exp, scale=1.0 / 16), reads=[g_bcp.b], writes=[g_e2.b])
                              for ch in range(2):
                                  c0 = 64 * ch
                                  op("act", lambda e: e.activation(out=g_e3[:, c0:c0 + 64], in_=g_bcp[:, c0:c0 + 64], func=AF.Exp, scale=1.0 / 16,
                                                                   bias=g_nbl[:, ch:ch + 1]), reads=[g_bcp.b, g_nbl.b, g_e3.b], writes=[g_e3.b])
                              op("dve", lambda e: e.scalar_tensor_tensor(out=g_qt[:], in0=g_qf[:], scalar=GLA_DK ** -0.5, in1=g_e1[:],
                                                                         op0=ALU.mult, op1=ALU.mult), reads=[g_qf.b, g_e1.b], writes=[g_qt.b])
                              op("dve", lambda e: e.tensor_tensor(out=g_kt[:], in0=g_kf[:], in1=g_e2[:], op=ALU.mult),
                                 reads=[g_kf.b, g_e2.b], writes=[g_kt.b])
                              op("dve", lambda e: e.tensor_tensor(out=g_khT[:], in0=g_kf[:], in1=g_e3[:], op=ALU.mult),
                                 reads=[g_kf.b, g_e3.b], writes=[g_khT.b])
                              pt_ = psb()
                              op("pe", lambda e: e.transpose(pt_[:, 0:128], g_khT[:, :], ident_b[:, :]), reads=[g_khT.b, ident_b.b], writes=[pt_.b])
                              op("dve", lambda e: e.tensor_scalar(out=g_khA[:], in0=pt_[:, 0:128], scalar1=halfm[:, 0:1], scalar2=None, op0=ALU.mult),
                                 reads=[pt_.b, gcon_t.b], writes=[g_khA.b])
                              op("dve", lambda e: e.tensor_scalar(out=g_khB[:], in0=pt_[:, 0:128], scalar1=halfm[:, 1:2], scalar2=None, op0=ALU.mult),
                                 reads=[pt_.b, gcon_t.b], writes=[g_khB.b])
                              for h in range(4):
                                  op("pool", lambda e: e.tensor_scalar(out=g_km[h][:], in0=g_kt[:], scalar1=headm[:, h:h + 1], scalar2=None, op0=ALU.mult),
                                     reads=[g_kt.b, gcon_t.b], writes=[g_km[h].b])
                                  op("pool", lambda e: e.tensor_scalar(out=g_QA[h][:, 0:64], in0=g_qt[:, 0:64], scalar1=headm[:, h:h + 1], scalar2=None,
                                                                       op0=ALU.mult), reads=[g_qt.b, gcon_t.b], writes=[g_QA[h].b])
                                  op("pool", lambda e: e.tensor_scalar(out=g_QB[h][:, 64:128], in0=g_qt[:, 64:128], scalar1=headm[:, h:h + 1], scalar2=None,
                                                                       op0=ALU.mult), reads=[g_qt.b, gcon_t.b], writes=[g_QB[h].b])
                              for ch, kh in ((0, g_khA), (1, g_khB)):
                                  pd_ = psf()
                                  op("pe", lambda e: e.matmul(pd_[:, 0:256], lhsT=kh[:, :], rhs=g_vt[:, :], start=True, stop=True),
                                     reads=[kh.b, g_vt.b], writes=[pd_.b])
                                  op("dve", lambda e: e.tensor_tensor(out=g_dS[:], in0=pd_[:, 0:256], in1=blockm, op=ALU.mult),
                                     reads=[pd_.b, gcon_t.b], writes=[g_dS.b])
                                  op("dve", lambda e: e.tensor_reduce(out=g_dSr[ch][:], in_=g_dS[:].rearrange("p (h v) -> p v h", h=4),
                                                                      axis=AX.X, op=ALU.add), reads=[g_dS.b], writes=[g_dSr[ch].b])
                              first, second = (0, 1) if d == 0 else (1, 0)
                              op("dve", lambda e: e.tensor_copy(out=g_Sb[0][:], in_=S0[:]), reads=[S0.b], writes=[g_Sb[0].b])
                              op("dve", lambda e: e.scalar_tensor_tensor(out=S1[:], in0=S0[:], scalar=g_eb[:, first:first + 1], in1=g_dSr[first][:],
                                                                         op0=ALU.mult, op1=ALU.add), reads=[S0.b, g_eb.b, g_dSr[first].b], writes=[S1.b])
                              op("dve", lambda e: e.tensor_copy(out=g_Sb[1][:], in_=S1[:]), reads=[S1.b], writes=[g_Sb[1].b])
                              op("dve", lambda e: e.scalar_tensor_tensor(out=S2[:], in0=S1[:], scalar=g_eb[:, second:second + 1], in1=g_dSr[second][:],
                                                                         op0=ALU.mult, op1=ALU.add), reads=[S1.b, g_eb.b, g_dSr[second].b], writes=[S2.b])
                              SbA, SbB = (g_Sb[0], g_Sb[1]) if d == 0 else (g_Sb[1], g_Sb[0])
                              po = p_acc
                              for h in range(4):
                                  pa_ = psf()
                                  op("pe", lambda e: e.matmul(pa_[:, 0:128], lhsT=g_km[h][:, :], rhs=g_qt[:, :], start=True, stop=True),
                                     reads=[g_km[h].b, g_qt.b], writes=[pa_.b])
                                  op("dve", lambda e: e.tensor_tensor(out=g_attm[:], in0=pa_[:, 0:128], in1=mask_d, op=ALU.mult),
                                     reads=[pa_.b, gcon_t.b], writes=[g_attm.b])
                                  op("pe", lambda e: e.matmul(po[:, 64 * h:64 * h + 64], lhsT=g_attm[:, :], rhs=g_vt[:, 64 * h:64 * h + 64],
                                                              start=True, stop=False), reads=[g_attm.b, g_vt.b], writes=[po.b])
                                  op("pe", lambda e: e.matmul(po[:, 64 * h:64 * h + 64], lhsT=g_QA[h][:, :], rhs=SbA[:, :], start=False, stop=False),
                                     reads=[g_QA[h].b, SbA.b], writes=[po.b])
                                  op("pe", lambda e: e.matmul(po[:, 64 * h:64 * h + 64], lhsT=g_QB[h][:, :], rhs=SbB[:, :], start=False, stop=True),
                                     reads=[g_QB[h].b, SbB.b], writes=[po.b])
                              if d == 1:
                                  op("act", lambda e: e.activation(out=o_bwd[:, ti, :], in_=po[:, 0:256], func=AF.Copy), reads=[po.b], writes=[o_bwd.b])
                              else:
                                  op("dve", lambda e: e.tensor_tensor(out=g_os[:], in0=po[:, 0:256], in1=o_bwd[:, ti, :], op=ALU.add),
                                     reads=[po.b, o_bwd.b], writes=[g_os.b])
                                  op("act", lambda e: e.activation(out=g_sq[:], in_=g_os[:], func=AF.Square), reads=[g_os.b], writes=[g_sq.b])
                                  op("dve", lambda e: e.tensor_reduce(out=g_ss[:], in_=g_sq[:].rearrange("p (h v) -> p h v", h=4), axis=AX.X, op=ALU.add),
                                     reads=[g_sq.b], writes=[g_ss.b])
                                  rstd_from_ss(g_rs, g_ss, 64)
                                  os3 = g_os[:].rearrange("p (h v) -> p h v", h=4)
                                  op("dve", lambda e: e.tensor_tensor(out=os3, in0=os3, in1=g_rs[:, :].unsqueeze(2).to_broadcast([128, 4, 64]), op=ALU.mult),
                                     reads=[g_os.b, g_rs.b], writes=[g_os.b])
                                  op("dve", lambda e: e.tensor_tensor(out=os3, in0=os3, in1=bc_gn[:, :].unsqueeze(1).to_broadcast([128, 4, 64]), op=ALU.mult),
                                     reads=[g_os.b, bc_gn.b], writes=[g_os.b])
                                  pgg = psf()
                                  for k in range(8):
                                      op("pe", lambda e: e.matmul(pgg[:, 0:256], lhsT=hT[:, k, r0:r0 + 128], rhs=wmix[:, k, 544:800],
                                                                  start=(k == 0), stop=(k == 7)), reads=[hT.b, wmix.b], writes=[pgg.b])
                                  op("act", lambda e: e.activation(out=g_sil[:], in_=pgg[:, 0:256], func=AF.Silu), reads=[pgg.b], writes=[g_sil.b])
                                  op("dve", lambda e: e.tensor_tensor(out=g_og[:], in0=g_os[:], in1=g_sil[:], op=ALU.mult),
                                     reads=[g_os.b, g_sil.b], writes=[g_og.b])
                                  pbr_ = psb()
                                  for k2 in range(2):
                                      op("pe", lambda e: e.transpose(pbr_[:, k2 * 128:(k2 + 1) * 128], g_og[:, k2 * 128:(k2 + 1) * 128], ident_b[:, :]),
                                         reads=[g_og.b, ident_b.b], writes=[pbr_.b])
                                  op("act", lambda e: e.activation(out=brT[:, 6:8, r0:r0 + 128], in_=pbr_[:, 0:256].rearrange("p (k t) -> p k t", k=2),
                                                                   func=AF.Copy), reads=[pbr_.b], writes=[brT.b])
                              S0, S1, S2 = S2, S0, S1
                          if not sample:
                              dma(o_gla[s_, l, d].rearrange("h k v -> (h k) v"), S0[:, :], reads=[S0.b])

              def s5_group(l, g):
                  c, T, L, sample = g["c"], g["T"], g["L"], g["sample"]
                  nseq, Mseq, M, nb = T // L, L // 8, T // 8, T // 512
                  nsteps = Mseq.bit_length() - 1
                  reuse = sample
                  P_ = s_par
                  pb_ = s_par["are"].b
                  ybank = pf[1:1 + nb]
                  ptmp = pf[0]
                  if sample:
                      dma(s_wmix[:, :, :].rearrange("p k c -> p (k c)"), wc["s5"][0][:, :], reads=[wc["s5"][1]], writes=[s_wmix.b], q="pool")
                  else:
                      for k in range(8):
                          st_ = wstage[0]
                          dma(st_[:, 0:512], w_in[l, k * 128:(k + 1) * 128, S50:S50 + 512], writes=[st_.b], q="pool")
                          op("dve" if k % 2 == 0 else "pool", lambda e: e.tensor_copy(out=s_wmix[:, k, :], in_=st_[:, 0:512]),
                             reads=[st_.b], writes=[s_wmix.b])
                      dma(wc["s5"][0][:, :], s_wmix[:, :, :].rearrange("p k c -> p (k c)"), reads=[s_wmix.b], writes=[wc["s5"][1]])
                  for hf in range(2):
                      for bk in range(nb):
                          for k in range(8):
                              op("pe", lambda e: e.matmul(ptmp[:, :], lhsT=s_wmix[:, k, hf * 128:(hf + 1) * 128], rhs=hT[:, k, bk * 512:(bk + 1) * 512],
                                                          start=(k == 0), stop=(k == 7)), reads=[s_wmix.b, hT.b], writes=[ptmp.b])
                          op("act", lambda e: e.activation(out=s_uT[:, hf, bk * 512:(bk + 1) * 512], in_=ptmp[:, :], func=AF.Copy),
                             reads=[ptmp.b], writes=[s_uT.b])
                  for hf in range(2):
                      for bk in range(nb):
                          op("pe", lambda e: e.matmul(ybank[bk][:, :], lhsT=s_Dd[hf][:, :], rhs=s_uT[:, hf, bk * 512:(bk + 1) * 512],
                                                      start=True, stop=False), reads=[s_Dd[hf].b, s_uT.b], writes=[ybank[bk].b])
                      if not reuse:
                          op("dve", lambda e: e.memset(s_Kacc[:], 0.0), writes=[s_Kacc.b])
                      for d in range(2):
                          for ri, src_ in ((0, s5_cre), (1, s5_cim)):
                              if not reuse:
                                  dma(s_C[ri][:, :], src_[l, d, 8 * hf:8 * hf + 8].rearrange("g i p -> (g i) p"), writes=[s_C[ri].b], q="pool")
                          for k4 in range(4):
                              st = 4 * hf + k4
                              td = d * 8 + st
                              ar, ai, nai = P_["ar1"][:, td:td + 1], P_["ai1"][:, td:td + 1], P_["nai1"][:, td:td + 1]
                              kr_, ki_, nki_ = P_["kr"][:, td:td + 1], P_["ki"][:, td:td + 1], P_["nki"][:, td:td + 1]
                              if reuse:
                                  dma(s_W[:, :, :].rearrange("p a c -> p (a c)"), s5w[td], reads=[s5w_b], writes=[s_W.b], q="pool")
                              else:
                                  for ri, src_ in ((0, s5_bre), (1, s5_bim)):
                                      dma(s_B[ri][:, :], src_[l, d, 2 * st:2 * st + 2].rearrange("g p i -> (g p) i"), writes=[s_B[ri].b], q="pool")
                                  op("dve", lambda e: e.tensor_scalar(out=s_bb[0][:], in0=s_B[0][:], scalar1=kr_, scalar2=None, op0=ALU.mult),
                                     reads=[s_B[0].b, pb_], writes=[s_bb[0].b])
                                  op("dve", lambda e: e.tensor_scalar(out=s_bb[1][:], in0=s_B[0][:], scalar1=ki_, scalar2=None, op0=ALU.mult),
                                     reads=[s_B[0].b, pb_], writes=[s_bb[1].b])
                                  op("dve", lambda e: e.scalar_tensor_tensor(out=s_bb[0][:], in0=s_B[1][:], scalar=nki_, in1=s_bb[0][:], op0=ALU.mult, op1=ALU.add),
                                     reads=[s_B[1].b, pb_, s_bb[0].b], writes=[s_bb[0].b])
                                  op("dve", lambda e: e.scalar_tensor_tensor(out=s_bb[1][:], in0=s_B[1][:], scalar=kr_, in1=s_bb[1][:], op0=ALU.mult, op1=ALU.add),
                                     reads=[s_B[1].b, pb_, s_bb[1].b], writes=[s_bb[1].b])
                                  mB3 = scon_t[:, 128 * k4:128 * k4 + 128].rearrange("p (g i) -> p g i", g=8)
                                  for ri in range(2):
                                      op("dve", lambda e: e.tensor_tensor(out=s_Xm[0][ri][:].rearrange("p (g i) -> p g i", g=8), in0=mB3,
                                                                          in1=s_bb[ri][:, :].unsqueeze(1).to_broadcast([128, 8, 16]), op=ALU.mult),
                                         reads=[scon_t.b, s_bb[ri].b], writes=[s_Xm[0][ri].b])
                                  mC3 = scon_t[:, 512 + 128 * k4:512 + 128 * k4 + 128].rearrange("p (a q) -> p a q", a=2)
                                  for ri in range(2):
                                      op("dve", lambda e: e.tensor_tensor(out=s_Cp[:].rearrange("p (a q) -> p a q", a=2), in0=mC3,
                                                                          in1=s_C[ri][:, :].unsqueeze(1).to_broadcast([128, 2, 64]), op=ALU.mult),
                                         reads=[scon_t.b, s_C[ri].b], writes=[s_Cp.b])
                                      op("pe", lambda e: e.transpose(ptmp[:, 0:128], s_Cp[:, :], ident_f[:, :]), reads=[s_Cp.b, ident_f.b], writes=[ptmp.b])
                                      op("dve", lambda e: e.tensor_copy(out=s_Cm[0][ri][:], in_=ptmp[:, 0:128]), reads=[ptmp.b], writes=[s_Cm[0][ri].b])

                                  def half1(cur, nxt):
                                      op("dve", lambda e: e.tensor_scalar(out=nxt[0][:], in0=cur[0][:], scalar1=ar, scalar2=None, op0=ALU.mult),
                                         reads=[cur[0].b, pb_], writes=[nxt[0].b])
                                      op("dve", lambda e: e.tensor_scalar(out=nxt[1][:], in0=cur[0][:], scalar1=ai, scalar2=None, op0=ALU.mult),
                                         reads=[cur[0].b, pb_], writes=[nxt[1].b])

                                  def half2(cur, nxt):
                                      op("dve", lambda e: e.scalar_tensor_tensor(out=nxt[0][:], in0=cur[1][:], scalar=nai, in1=nxt[0][:], op0=ALU.mult, op1=ALU.add),
                                         reads=[cur[1].b, pb_, nxt[0].b], writes=[nxt[0].b])
                                      op("dve", lambda e: e.scalar_tensor_tensor(out=nxt[1][:], in0=cur[1][:], scalar=ar, in1=nxt[1][:], op0=ALU.mult, op1=ALU.add),
                                         reads=[cur[1].b, pb_, nxt[1].b], writes=[nxt[1].b])

                                  for k in range(9):
                                      Xc, Xn = s_Xm[k % 2], s_Xm[(k + 1) % 2]
                                      Cc, Cn = s_Cm[k % 2], s_Cm[(k + 1) % 2]
                                      if k < 8:
                                          for ri in range(2):
                                              op("act", lambda e: e.activation(out=s_Xb[k][ri][:], in_=Xc[ri][:], func=AF.Copy),
                                                 reads=[Xc[ri].b], writes=[s_Xb[k][ri].b])
                                      op("act", lambda e: e.activation(out=s_Wc[k][0][:], in_=Cc[0][:], func=AF.Copy), reads=[Cc[0].b], writes=[s_Wc[k][0].b])
                                      op("act", lambda e: e.activation(out=s_Wc[k][1][:], in_=Cc[1][:], func=AF.Copy, scale=-1.0),
                                         reads=[Cc[1].b], writes=[s_Wc[k][1].b])
                                      if k < 7:
                                          half1(Xc, Xn)
                                      if k < 8:
                                          half1(Cc, Cn)
                                      if k < 7:
                                          half2(Xc, Xn)
                                      if k < 8:
                                          half2(Cc, Cn)
                                      if k < 8:
                                          pt_ = psb()
                                          for ri in range(2):
                                              op("pe", lambda e: e.transpose(pt_[:, ri * 128:(ri + 1) * 128], s_Xb[k][ri][:, :], ident_b[:, :]),
                                                 reads=[s_Xb[k][ri].b, ident_b.b], writes=[pt_.b])
                                          for ri in range(2):
                                              op("act", lambda e: e.activation(out=s_Wb[k][ri][:], in_=pt_[:, ri * 128:(ri + 1) * 128], func=AF.Copy),
                                                 reads=[pt_.b], writes=[s_Wb[k][ri].b])
                                  for q4 in range(2):
                                      for dd in range(4):
                                          dl = 4 * q4 + dd
                                          op("pe", lambda e: e.matmul(ptmp[:, dd * 128:(dd + 1) * 128], lhsT=s_Xb[dl][0][:, :], rhs=s_Wc[0][0][:, :],
                                                                      start=True, stop=False), reads=[s_Xb[dl][0].b, s_Wc[0][0].b], writes=[ptmp.b])
                                          op("pe", lambda e: e.matmul(ptmp[:, dd * 128:(dd + 1) * 128], lhsT=s_Xb[dl][1][:, :], rhs=s_Wc[0][1][:, :],
                                                                      start=False, stop=True), reads=[s_Xb[dl][1].b, s_Wc[0][1].b], writes=[ptmp.b])
                                      ka = s_Kacc[:, d * 8 + 4 * q4:d * 8 + 4 * q4 + 4, :]
                                      op("dve", lambda e: e.tensor_tensor(out=ka, in0=ka, in1=ptmp[:, :].rearrange("p (a c) -> p a c", a=4), op=ALU.add),
                                         reads=[ptmp.b, s_Kacc.b], writes=[s_Kacc.b])
                                  dma(s5w[td], s_W[:, :, :].rearrange("p a c -> p (a c)"), reads=[s_W.b], writes=[s5w_b])
                              ub = s_uT[:, hf, 0:T].rearrange("p (m j) -> p m j", j=8)
                              for ri in range(2):
                                  for s_ in range(8):
                                      kk = 7 - s_ if d == 0 else s_
                                      op("pe", lambda e: e.matmul(p_acc[:, ri * 256:ri * 256 + M], lhsT=s_Wb[kk][ri][:, :], rhs=ub[:, :, s_],
                                                                  start=(s_ == 0), stop=(s_ == 7)), reads=[s_Wb[kk][ri].b, s_uT.b], writes=[p_acc.b])
                              for ri in range(2):
                                  op("dve", lambda e: e.tensor_copy(out=s_XA[ri][:, 0:M], in_=p_acc[:, ri * 256:ri * 256 + M]),
                                     reads=[p_acc.b], writes=[s_XA[ri].b])
                              if sample:
                                  dma(s_h0[:, :], ss5[l, d, 128 * st:128 * st + 128, :], writes=[s_h0.b], q="pool")
                                  a8r, a8i, na8i = s_pwr[:, 0, td:td + 1], s_pwi[:, 0, td:td + 1], s_npwi[:, 0, td:td + 1]
                                  cc = 0 if d == 0 else M - 1
                                  for ri, (sa, sb_) in enumerate(((a8r, na8i), (a8i, a8r))):
                                      xc = s_XA[ri][:, cc:cc + 1]
                                      op("dve", lambda e: e.scalar_tensor_tensor(out=xc, in0=s_h0[:, 0:1], scalar=sa, in1=xc, op0=ALU.mult, op1=ALU.add),
                                         reads=[s_h0.b, pb_, s_XA[ri].b], writes=[s_XA[ri].b])
                                      op("dve", lambda e: e.scalar_tensor_tensor(out=xc, in0=s_h0[:, 1:2], scalar=sb_, in1=xc, op0=ALU.mult, op1=ALU.add),
                                         reads=[s_h0.b, pb_, s_XA[ri].b], writes=[s_XA[ri].b])
                              src, dst = s_XA, s_XB
                              v3 = lambda t_: t_[:, 0:M].rearrange("p (s m) -> p s m", s=nseq)
                              for k in range(nsteps):
                                  sh = 1 << k
                                  pr, pi, npi = s_pwr[:, k, td:td + 1], s_pwi[:, k, td:td + 1], s_npwi[:, k, td:td + 1]
                                  if d == 0:
                                      lo, ls, kp = slice(sh, Mseq), slice(0, Mseq - sh), slice(0, sh)
                                  else:
                                      lo, ls, kp = slice(0, Mseq - sh), slice(sh, Mseq), slice(Mseq - sh, Mseq)
                                  for ri in range(2):
                                      op("act", lambda e: e.activation(out=v3(dst[ri])[:, :, kp], in_=v3(src[ri])[:, :, kp], func=AF.Copy),
                                         reads=[src[ri].b, dst[ri].b], writes=[dst[ri].b])
                                  for ri, m1 in enumerate((pr, pr)):
                                      op("dve", lambda e: e.scalar_tensor_tensor(out=v3(dst[ri])[:, :, lo], in0=v3(src[ri])[:, :, ls], scalar=m1,
                                                                                 in1=v3(src[ri])[:, :, lo], op0=ALU.mult, op1=ALU.add),
                                         reads=[src[ri].b, pb_, dst[ri].b], writes=[dst[ri].b])
                                  for ri, m2 in enumerate((npi, pi)):
                                      other = src[1 - ri]
                                      op("dve", lambda e: e.scalar_tensor_tensor(out=v3(dst[ri])[:, :, lo], in0=v3(other)[:, :, ls], scalar=m2,
                                                                                 in1=v3(dst[ri])[:, :, lo], op0=ALU.mult, op1=ALU.add),
                                         reads=[other.b, pb_, dst[ri].b], writes=[dst[ri].b])
                                  src, dst = dst, src
                              Hh = src
                              if not sample:
                                  for s_ in range(nseq):
                                      cc = s_ * Mseq + (Mseq - 1 if d == 0 else 0)
                                      for ri in range(2):
                                          op("dve", lambda e: e.tensor_copy(out=s_fin[:, ri:ri + 1], in_=Hh[ri][:, cc:cc + 1]),
                                             reads=[Hh[ri].b, s_fin.b], writes=[s_fin.b])
                                      dma(o_s5[s_, l, d].rearrange("g p r -> (g p) r")[128 * st:128 * st + 128, :], s_fin[:, :], reads=[s_fin.b])
                              W1 = Mseq + 1
                              for ri in range(2):
                                  hv = s_Hs[ri][:, 0:nseq * W1].rearrange("p (s m) -> p s m", s=nseq)
                                  body = hv[:, :, 1:W1] if d == 0 else hv[:, :, 0:Mseq]
                                  edge = hv[:, :, 0:1] if d == 0 else hv[:, :, Mseq:W1]
                                  op("act", lambda e: e.activation(out=body, in_=v3(Hh[ri]), func=AF.Copy), reads=[Hh[ri].b, s_Hs[ri].b], writes=[s_Hs[ri].b])
                                  if sample:
                                      op("dve", lambda e: e.tensor_copy(out=edge, in_=s_h0[:, ri:ri + 1].unsqueeze(1)), reads=[s_h0.b, s_Hs[ri].b], writes=[s_Hs[ri].b])
                                  else:
                                      op("dve", lambda e: e.memset(edge, 0.0), reads=[s_Hs[ri].b], writes=[s_Hs[ri].b])
                              off = 0 if d == 0 else 1
                              for bk in range(nb):
                                  for j in range(8):
                                      kk = j + 1 if d == 0 else 8 - j
                                      for ri in range(2):
                                          hv = s_Hs[ri][:, 0:nseq * W1].rearrange("p (s m) -> p s m", s=nseq)
                                          if nseq == 1:
                                              rhs_ = hv[:, 0, off + 64 * bk:off + 64 * bk + 64]
                                              out_ = ybank[bk][:, :].rearrange("p (m j) -> p m j", j=8)[:, :, j]
                                          else:
                                              rhs_ = hv[:, 2 * bk:2 * bk + 2, off:off + Mseq]
                                              out_ = ybank[bk][:, :].rearrange("p (s m j) -> p s m j", s=2, j=8)[:, :, :, j]
                                          op("pe", lambda e: e.matmul(out_, lhsT=s_Wc[kk][ri][:, :], rhs=rhs_, start=False, stop=False),
                                             reads=[s_Wc[kk][ri].b, s_Hs[ri].b], writes=[ybank[bk].b])
                      if reuse:
                          dma(s_Kbd[:, :, :].rearrange("p a c -> p (a c)"), s5k[hf], reads=[s5k_b], writes=[s_Kbd.b], q="pool")
                      else:
                          for a_ in range(16):
                              op("act", lambda e: e.activation(out=s_Kbd[:, a_, :], in_=s_Kacc[:, a_, :], func=AF.Copy), reads=[s_Kacc.b], writes=[s_Kbd.b])
                          dma(s5k[hf], s_Kbd[:, :, :].rearrange("p a c -> p (a c)"), reads=[s_Kbd.b], writes=[s5k_b])
                      for bk in range(nb):
                          u3 = s_uT[:, hf, bk * 512:(bk + 1) * 512].rearrange("p (m j) -> p m j", j=8)
                          y3 = ybank[bk][:, :].rearrange("p (m j) -> p m j", j=8)
                          for dl in range(8):
                              op("pe", lambda e: e.matmul(y3[:, :, dl:8], lhsT=s_Kbd[:, dl, :], rhs=u3[:, :, 0:8 - dl], start=False, stop=False),
                                 reads=[s_Kbd.b, s_uT.b], writes=[ybank[bk].b])
                              op("pe", lambda e: e.matmul(y3[:, :, 0:8 - dl], lhsT=s_Kbd[:, 8 + dl, :], rhs=u3[:, :, dl:8], start=False, stop=(dl == 7)),
                                 reads=[s_Kbd.b, s_uT.b], writes=[ybank[bk].b])
                      for bk in range(nb):
                          yb_ = ybank[bk]
                          op("act", lambda e: e.activation(out=s_ya[:], in_=yb_[:, :], func=AF.Square), reads=[yb_.b], writes=[s_ya.b])
                          op("dve", lambda e: e.tensor_scalar(out=s_ya[:], in0=s_ya[:], scalar1=0.044715, scalar2=1.0, op0=ALU.mult, op1=ALU.add),
                             reads=[s_ya.b], writes=[s_ya.b])
                          op("dve", lambda e: e.tensor_tensor(out=s_ya[:], in0=s_ya[:], in1=yb_[:, :], op=ALU.mult), reads=[s_ya.b, yb_.b], writes=[s_ya.b])
                          op("act", lambda e: e.activation(out=s_yb[:], in_=s_ya[:], func=AF.Sigmoid, scale=1.5957691216057308),
                             reads=[s_ya.b], writes=[s_yb.b])
                          op("dve", lambda e: e.tensor_tensor(out=s_gT[:, hf, bk * 512:(bk + 1) * 512], in0=s_yb[:], in1=yb_[:, :], op=ALU.mult),
                             reads=[s_yb.b, yb_.b], writes=[s_gT.b])
                  for oh in range(2):
                      for bk in range(nb):
                          tok = slice(bk * 512, (bk + 1) * 512)
                          for ih in range(2):
                              op("pe", lambda e: e.matmul(ptmp[:, :], lhsT=s_glu[:, ih, oh * 128:(oh + 1) * 128], rhs=s_gT[:, ih, tok],
                                                          start=(ih == 0), stop=(ih == 1)), reads=[s_glu.b, s_gT.b], writes=[ptmp.b])
                          op("act", lambda e: e.activation(out=s_ya[:], in_=ptmp[:, :], func=AF.Sigmoid, bias=s_gbcol[:, oh:oh + 1]),
                             reads=[ptmp.b, s_gbcol.b], writes=[s_ya.b])
                          for k in range(8):
                              op("pe", lambda e: e.matmul(p_acc[:, :], lhsT=s_wmix[:, k, 256 + oh * 128:256 + (oh + 1) * 128], rhs=hT[:, k, tok],
                                                          start=(k == 0), stop=(k == 7)), reads=[s_wmix.b, hT.b], writes=[p_acc.b])
                          op("act", lambda e: e.activation(out=s_yb[:], in_=p_acc[:, :], func=AF.Silu), reads=[p_acc.b], writes=[s_yb.b])
                          op("dve", lambda e: e.tensor_tensor(out=s_ya[:], in0=s_ya[:], in1=s_gT[:, oh, tok], op=ALU.mult),
                             reads=[s_ya.b, s_gT.b], writes=[s_ya.b])
                          op("dve", lambda e: e.tensor_tensor(out=brT[:, 4 + oh, tok], in0=s_ya[:], in1=s_yb[:], op=ALU.mult),
                             reads=[s_ya.b, s_yb.b], writes=[brT.b])

              def hy_group(l, g):
                  c, T, L, sample = g["c"], g["T"], g["L"], g["sample"]
                  tb = htab["s" if sample else "p"]
                  nseq, ntt = T // L, L // 128
                  nfb = ntt + 1
                  wbase = 136 if sample else 128
                  NB = min(512, L)
                  nbank = L // NB
                  cos_cb = tb["cos"].rearrange("(a p) j -> p a j", p=128)
                  sin_cb = tb["sin"].rearrange("(a p) j -> p a j", p=128)

                  def sin_layer(pp_, li, dst, w):
                      o4 = 4 * li
                      op("act", lambda e: e.activation(out=h_s2[:, 0:w], in_=pp_[0:64, 0:w], func=AF.Sin, scale=h_mlp[:, o4:o4 + 1],
                                                       bias=h_mlp[:, o4 + 1:o4 + 2]), reads=[pp_.b, h_mlp.b], writes=[h_s2.b])
                      op("act", lambda e: e.activation(out=h_s4[:, 0:w], in_=pp_[0:64, 0:w], func=AF.Sin, scale=h_mlp[:, o4 + 2:o4 + 3],
                                                       bias=h_mlp[:, o4 + 3:o4 + 4]), reads=[pp_.b, h_mlp.b], writes=[h_s4.b])
                      op("dve", lambda e: e.tensor_tensor(out=h_s4[:, 0:w], in0=h_s4[:, 0:w], in1=h_s4[:, 0:w], op=ALU.mult),
                         reads=[h_s4.b], writes=[h_s4.b])
                      op("dve", lambda e: e.tensor_scalar(out=h_s4[:, 0:w], in0=h_s4[:, 0:w], scalar1=-2.0, scalar2=1.0, op0=ALU.mult, op1=ALU.add),
                         reads=[h_s4.b], writes=[h_s4.b])
                      op("dve", lambda e: e.scalar_tensor_tensor(out=dst, in0=h_s2[:, 0:w], scalar=2.0, in1=h_s4[:, 0:w], op0=ALU.mult, op1=ALU.mult),
                         reads=[h_s2.b, h_s4.b], writes=[h_hid[li].b])

                  for c0 in range(0, L, 512):
                      w = min(512, L - c0)
                      dma(h_feat[:, 0:w], tb["feat"][:, c0:c0 + w], writes=[h_feat.b], q="pool")
                      pp_ = psf()
                      op("pe", lambda e: e.matmul(pp_[0:64, 0:w], lhsT=hw1_b[:, :], rhs=h_feat[:, 0:w], start=True, stop=True),
                         reads=[hw1_b.b, h_feat.b], writes=[pp_.b])
                      sin_layer(pp_, 0, h_hid[0][:, c0:c0 + w], w)
                  for c0 in range(0, L, 512):
                      w = min(512, L - c0)
                      pp_ = psf()
                      op("pe", lambda e: e.matmul(pp_[0:64, 0:w], lhsT=hw2_b[:, :], rhs=h_hid[0][:, c0:c0 + w], start=True, stop=True),
                         reads=[hw2_b.b, h_hid[0].b], writes=[pp_.b])
                      sin_layer(pp_, 1, h_hid[1][:, c0:c0 + w], w)

                  if not sample:
                      for b in range(ntt):
                          dma(h_tc[b][:, 0:ntt, :], cos_cb[:, :, b * 128:(b + 1) * 128], writes=[h_tc[b].b], q="pool")
                          dma(h_ts[b][:, 0:ntt, :], sin_cb[:, :, b * 128:(b + 1) * 128], writes=[h_ts[b].b], q="pool")
                          dma(h_cr[b][:, 0:NB], tb["cos"][b * 128:(b + 1) * 128, 0:NB], writes=[h_cr[b].b], q="pool")
                          dma(h_sr[b][:, 0:NB], tb["sin"][b * 128:(b + 1) * 128, 0:NB], writes=[h_sr[b].b], q="pool")
                  hy_own = sample and l == DEPTH - 1
                  for hc in range(2):
                      r_ = 0
                      for k in range(3):
                          for wh in range(3):
                              dma(h_rows[r_:r_ + 1, :], hy_cw[l, k:k + 1, wh * 256 + hc * 128:wh * 256 + hc * 128 + 128], reads=[h_rows.b], writes=[h_rows.b])
                              r_ += 1
                      for wh in range(3):
                          dma(h_rows[r_:r_ + 1, :], hy_cb[l:l + 1, wh * 256 + hc * 128:wh * 256 + hc * 128 + 128], reads=[h_rows.b], writes=[h_rows.b])
                          r_ += 1
                      for o in range(2):
                          dma(h_rows[r_:r_ + 1, :], hy_bi[l, o:o + 1, hc * 128:hc * 128 + 128], reads=[h_rows.b], writes=[h_rows.b])
                          r_ += 1
                      pp_ = psf()
                      op("pe", lambda e: e.transpose(pp_[:, 0:14], h_rows[:, :], ident_f[0:14, 0:14]), reads=[h_rows.b, ident_f.b], writes=[pp_.b])
                      op("dve", lambda e: e.tensor_copy(out=h_colp[:], in_=pp_[:, 0:14]), reads=[pp_.b], writes=[h_colp.b])
                      if sample:
                          dma(h_wmix[:, :, :].rearrange("p k c -> p (k c)"), wc["hy%d" % hc][0][:, :], reads=[wc["hy%d" % hc][1]], writes=[h_wmix.b], q="pool")
                      else:
                          for k in range(8):
                              st_ = wstage[0]
                              for j in range(4):
                                  dma(st_[:, j * 128:(j + 1) * 128], w_in[l, k * 128:(k + 1) * 128, HY0 + j * 256 + hc * 128:HY0 + j * 256 + hc * 128 + 128],
                                      reads=[st_.b], writes=[st_.b])
                              op("dve" if k % 2 == 0 else "pool", lambda e: e.tensor_copy(out=h_wmix[:, k, :], in_=st_[:, 0:512]),
                                 reads=[st_.b], writes=[h_wmix.b])
                          dma(wc["hy%d" % hc][0][:, :], h_wmix[:, :, :].rearrange("p k c -> p (k c)"), reads=[h_wmix.b], writes=[wc["hy%d" % hc][1]])
                      if sample:
                          dma(h_w3[:].rearrange("p j c -> p (j c)"), wc["w3%d" % hc][0][:, :], reads=[wc["w3%d" % hc][1]], writes=[h_w3.b], q="pool")
                      else:
                          st_ = wstage[0]
                          for j in range(4):
                              dma(st_[0:64, j * 128:(j + 1) * 128], hy_w3[l, :, j * 256 + hc * 128:j * 256 + hc * 128 + 128], reads=[st_.b], writes=[st_.b], q="pool")
                          op("dve", lambda e: e.tensor_copy(out=h_w3[:].rearrange("p j c -> p (j c)"), in_=st_[0:64, 0:512]), reads=[st_.b], writes=[h_w3.b])
                          dma(wc["w3%d" % hc][0][:, :], h_w3[:].rearrange("p j c -> p (j c)"), reads=[h_w3.b], writes=[wc["w3%d" % hc][1]])
                      zT = h_sig[:, 3, 0:T]
                      z3 = zT.rearrange("p (s t) -> p s t", s=nseq)
                      for wh in range(3):
                          for c0 in range(0, T, 512):
                              pp_ = psf()
                              for k in range(8):
                                  op("pe", lambda e: e.matmul(pp_[:, :], lhsT=h_wmix[:, k, wh * 128:(wh + 1) * 128], rhs=hT[:, k, c0:c0 + 512],
                                                              start=(k == 0), stop=(k == 7)), reads=[h_wmix.b, hT.b], writes=[pp_.b])
                              op("act", lambda e: e.activation(out=h_sig[:, 3, c0:c0 + 512], in_=pp_[:, :], func=AF.Copy), reads=[pp_.b], writes=[h_sig.b])
                          dst = h_sig[:, wh, 0:T]
                          d3 = dst.rearrange("p (s t) -> p s t", s=nseq)
                          op("dve", lambda e: e.tensor_scalar(out=dst, in0=zT, scalar1=h_colp[:, 3 + wh:4 + wh], scalar2=h_colp[:, 9 + wh:10 + wh],
                                                              op0=ALU.mult, op1=ALU.add), reads=[h_sig.b, h_colp.b], writes=[h_sig.b])
                          op("dve", lambda e: e.scalar_tensor_tensor(out=d3[:, :, 1:L], in0=z3[:, :, 0:L - 1], scalar=h_colp[:, wh:wh + 1], in1=d3[:, :, 1:L],
                                                                     op0=ALU.mult, op1=ALU.add), reads=[h_sig.b, h_colp.b], writes=[h_sig.b])
                          op("dve", lambda e: e.scalar_tensor_tensor(out=d3[:, :, 0:L - 1], in0=z3[:, :, 1:L], scalar=h_colp[:, 6 + wh:7 + wh], in1=d3[:, :, 0:L - 1],
                                                                     op0=ALU.mult, op1=ALU.add), reads=[h_sig.b, h_colp.b], writes=[h_sig.b])
                      for lt in range(ntt):
                          pp_ = psf()
                          op("pe", lambda e: e.matmul(pp_[:, :], lhsT=h_hid[1][:, lt * 128:(lt + 1) * 128], rhs=h_w3[:].rearrange("p j c -> p (j c)"),
                                                      start=True, stop=True), reads=[h_hid[1].b, h_w3.b], writes=[pp_.b])
                          dma(h_win[:, :], tb["win"][lt * 128:(lt + 1) * 128, hc * 128:(hc + 1) * 128], writes=[h_win.b], q="pool")
                          f4 = h_fw[:].rearrange("p (j c) -> p j c", j=4)
                          op("dve", lambda e: e.tensor_tensor(out=f4, in0=pp_[:, :].rearrange("p (j c) -> p j c", j=4),
                                                              in1=h_win[:, :].unsqueeze(1).to_broadcast([128, 4, 128]), op=ALU.mult),
                             reads=[pp_.b, h_win.b], writes=[h_fw.b])
                          if lt == 0:
                              op("dve", lambda e: e.tensor_scalar(out=h_fw[:, 256:512], in0=h_fw[:, 256:512], scalar1=hcon_t[:, 160:161], scalar2=None,
                                                                  op0=ALU.mult), reads=[h_fw.b, hcon_t.b], writes=[h_fw.b])
                          op("act", lambda e: e.activation(out=h_abs[:], in_=h_fw[:], func=AF.Abs), reads=[h_fw.b], writes=[h_abs.b])
                          op("pe", lambda e: e.matmul(p_acc[:, :], lhsT=ones_sq[:, :], rhs=h_abs[:, :], start=(lt == 0), stop=(lt == ntt - 1)),
                             reads=[ones_sq.b, h_abs.b], writes=[p_acc.b])
                          op("dve", lambda e: e.tensor_tensor(out=h_ksd[:, lt, 0:2, :], in0=f4[:, 0:2, :], in1=f4[:, 2:4, :], op=ALU.add),
                             reads=[h_fw.b], writes=[h_ksd.b])
                          op("dve", lambda e: e.tensor_tensor(out=h_ksd[:, lt, 2:4, :], in0=f4[:, 2:4, :], in1=f4[:, 0:2, :], op=ALU.subtract),
                             reads=[h_fw.b, h_ksd.b], writes=[h_ksd.b])
                      n4 = p_acc[:, :].rearrange("p (j c) -> p j c", j=4)
                      op("dve", lambda e: e.tensor_copy(out=h_rn[:], in_=n4[:, 0:2, :]), reads=[p_acc.b], writes=[h_rn.b])
                      op("dve", lambda e: e.tensor_tensor(out=h_rn[:], in0=h_rn[:], in1=n4[:, 2:4, :], op=ALU.add), reads=[p_acc.b, h_rn.b], writes=[h_rn.b])
                      op("dve", lambda e: e.reciprocal(out=h_rn[:], in_=h_rn[:]), reads=[h_rn.b], writes=[h_rn.b])

                      def long_conv(s_, o, src_idx, combine, own=False):
                          t0 = s_ * L
                          for a in range(ntt):
                              pt_ = psb()
                              op("pe", lambda e: e.transpose(pt_[:, 0:128], h_sig[:, src_idx, t0 + a * 128:t0 + (a + 1) * 128], ident_b[:, :]),
                                 reads=[h_sig.b, ident_b.b], writes=[pt_.b])
                              op("act", lambda e: e.activation(out=h_utm[:, a, :], in_=pt_[:, 0:128], func=AF.Copy), reads=[pt_.b], writes=[h_utm.b])
                          for b in range(nfb):
                              nyq = (b == ntt)
                              tcb, tsb = h_tc[b % 2], h_ts[b % 2]
                              if not nyq and sample:
                                  dma(tcb[:, 0:ntt, :], hcosf[b].rearrange("p (a j) -> p a j", a=ntt), writes=[tcb.b], q="pool")
                                  dma(tsb[:, 0:ntt, :], hsinf[b].rearrange("p (a j) -> p a j", a=ntt), writes=[tsb.b], q="pool")
                              pu, pk = psf(), psf()
                              for dst_, col, tab, rhs_of in ((pu, 0, "c", lambda a: h_utm[:, a, :]), (pu, 128, "s", lambda a: h_utm[:, a, :]),
                                                             (pk, 0, "c", lambda a: h_ksd[:, a, o, :]), (pk, 128, "s", lambda a: h_ksd[:, a, 2 + o, :])):
                                  if nyq and tab == "s":
                                      continue
                                  for a in range(ntt):
                                      lt_ = h_alt[:, :] if nyq else (tcb if tab == "c" else tsb)[:, a, :]
                                      rd_ = [h_alt.b] if nyq else [(tcb if tab == "c" else tsb).b]
                                      op("pe", lambda e: e.matmul(dst_[:, col:col + 128], lhsT=lt_, rhs=rhs_of(a), start=(a == 0), stop=(a == ntt - 1)),
                                         reads=rd_ + [h_utm.b, h_ksd.b], writes=[dst_.b])
                              wc = hcon_t[:, wbase + b:wbase + b + 1]
                              op("dve", lambda e: e.scalar_tensor_tensor(out=h_k[0][:], in0=pk[:, 0:128], scalar=wc, in1=h_rn[:, o, :], op0=ALU.mult, op1=ALU.mult),
                                 reads=[pk.b, hcon_t.b, h_rn.b], writes=[h_k[0].b])
                              if nyq:
                                  op("dve", lambda e: e.tensor_tensor(out=h_Y[:, b, 0, :], in0=pu[:, 0:128], in1=h_k[0][:], op=ALU.mult),
                                     reads=[pu.b, h_k[0].b], writes=[h_Y.b])
                                  continue
                              op("dve", lambda e: e.scalar_tensor_tensor(out=h_k[1][:], in0=pk[:, 128:256], scalar=wc, in1=h_rn[:, o, :], op0=ALU.mult, op1=ALU.mult),
                                 reads=[pk.b, hcon_t.b, h_rn.b], writes=[h_k[1].b])
                              op("dve", lambda e: e.tensor_tensor(out=h_t[0][:], in0=pu[:, 0:128], in1=h_k[0][:], op=ALU.mult), reads=[pu.b, h_k[0].b], writes=[h_t[0].b])
                              op("dve", lambda e: e.tensor_tensor(out=h_t[1][:], in0=pu[:, 128:256], in1=h_k[1][:], op=ALU.mult), reads=[pu.b, h_k[1].b], writes=[h_t[1].b])
                              op("dve", lambda e: e.tensor_tensor(out=h_t[2][:], in0=pu[:, 128:256], in1=h_k[0][:], op=ALU.mult), reads=[pu.b, h_k[0].b], writes=[h_t[2].b])
                              op("dve", lambda e: e.tensor_tensor(out=h_t[3][:], in0=pu[:, 0:128], in1=h_k[1][:], op=ALU.mult), reads=[pu.b, h_k[1].b], writes=[h_t[3].b])
                              op("dve", lambda e: e.tensor_tensor(out=h_Y[:, b, 0, :], in0=h_t[0][:], in1=h_t[1][:], op=ALU.add),
                                 reads=[h_t[0].b, h_t[1].b], writes=[h_Y.b])
                              op("dve", lambda e: e.tensor_tensor(out=h_Y[:, b, 1, :], in0=h_t[2][:], in1=h_t[3][:], op=ALU.subtract),
                                 reads=[h_t[2].b, h_t[3].b, h_Y.b], writes=[h_Y.b])
                          if own:
                              op("dve", lambda e: e.tensor_scalar(out=h_sig[:, 2:4, 0:512], in0=h_sig[:, 2:4, 0:512], scalar1=qm[:, 0:1],
                                                                  scalar2=None, op0=ALU.mult), reads=[h_sig.b, qm.b], writes=[h_sig.b])
                              for j in range(1, 4):
                                  op("dve", lambda e: e.scalar_tensor_tensor(out=h_sig[:, 2:4, 0:512], in0=h_sig[:, 2:4, j * 512:(j + 1) * 512],
                                                                             scalar=qm[:, j:j + 1], in1=h_sig[:, 2:4, 0:512],
                                                                             op0=ALU.mult, op1=ALU.add),
                                     reads=[h_sig.b, qm.b], writes=[h_sig.b])
                          for bank in range(1 if own else nbank):
                              c0 = bank * NB
                              for b in range(ntt):
                                  crb, srb = h_cr[b % 2], h_sr[b % 2]
                                  if own:
                                      dma(crb[:, 0:NB], hcosq[b * 128:(b + 1) * 128, 0:NB], writes=[crb.b], q="pool")
                                      dma(srb[:, 0:NB], hsinq[b * 128:(b + 1) * 128, 0:NB], writes=[srb.b], q="pool")
                                  elif sample:
                                      dma(crb[:, 0:NB], tb["cos"][b * 128:(b + 1) * 128, c0:c0 + NB], writes=[crb.b], q="pool")
                                      dma(srb[:, 0:NB], tb["sin"][b * 128:(b + 1) * 128, c0:c0 + NB], writes=[srb.b], q="pool")
                                  op("pe", lambda e: e.matmul(p_acc[:, 0:NB], lhsT=h_Y[:, b, 0, :], rhs=crb[:, 0:NB], start=(b == 0), stop=False),
                                     reads=[h_Y.b, crb.b], writes=[p_acc.b])
                                  op("pe", lambda e: e.matmul(p_acc[:, 0:NB], lhsT=h_Y[:, b, 1, :], rhs=srb[:, 0:NB], start=False, stop=False),
                                     reads=[h_Y.b, srb.b], writes=[p_acc.b])
                              op("pe", lambda e: e.matmul(p_acc[:, 0:NB], lhsT=h_Y[0:1, ntt, 0, :], rhs=haltr_t[0:1, 0:NB], start=False, stop=True),
                                 reads=[h_Y.b, haltr_t.b], writes=[p_acc.b])
                              combine(t0 + c0)

                      def comb1(tk):
                          op("dve", lambda e: e.scalar_tensor_tensor(out=h_fw[:, 0:NB], in0=h_sig[:, 0, tk:tk + NB], scalar=h_colp[:, 12:13], in1=p_acc[:, 0:NB],
                                                                     op0=ALU.mult, op1=ALU.add), reads=[h_sig.b, h_colp.b, p_acc.b], writes=[h_fw.b])
                          op("dve", lambda e: e.tensor_tensor(out=h_sig[:, 3, tk:tk + NB], in0=h_fw[:, 0:NB], in1=h_sig[:, 1, tk:tk + NB], op=ALU.mult),
                             reads=[h_fw.b, h_sig.b], writes=[h_sig.b])

                      def comb2(tk):
                          op("dve", lambda e: e.scalar_tensor_tensor(out=h_fw[:, 0:NB], in0=h_sig[:, 3, tk:tk + NB], scalar=h_colp[:, 13:14], in1=p_acc[:, 0:NB],
                                                                     op0=ALU.mult, op1=ALU.add), reads=[h_sig.b, h_colp.b, p_acc.b], writes=[h_fw.b])
                          op("dve", lambda e: e.tensor_tensor(out=h_fw[:, 0:NB], in0=h_fw[:, 0:NB], in1=h_sig[:, 2, tk:tk + NB], op=ALU.mult),
                             reads=[h_fw.b, h_sig.b], writes=[h_fw.b])
                          if hy_own:
                              gacc = gate_bc[0]
                              for j in range(4):
                                  pg_ = psf()
                                  for k in range(8):
                                      op("pe", lambda e: e.matmul(pg_[:, 0:NB], lhsT=h_wmix[:, k, 384:512], rhs=hT[:, k, j * 512:(j + 1) * 512],
                                                                  start=(k == 0), stop=(k == 7)), reads=[h_wmix.b, hT.b], writes=[pg_.b])
                                  op("act", lambda e: e.activation(out=h_abs[:, 0:NB], in_=pg_[:, 0:NB], func=AF.Silu), reads=[pg_.b], writes=[h_abs.b])
                                  if j == 0:
                                      op("dve", lambda e: e.tensor_scalar(out=gacc[:, 0:NB], in0=h_abs[:, 0:NB], scalar1=qm[:, 0:1], scalar2=None,
                                                                          op0=ALU.mult), reads=[h_abs.b, qm.b], writes=[gacc.b])
                                  else:
                                      op("dve", lambda e: e.scalar_tensor_tensor(out=gacc[:, 0:NB], in0=h_abs[:, 0:NB], scalar=qm[:, j:j + 1],
                                                                                 in1=gacc[:, 0:NB], op0=ALU.mult, op1=ALU.add),
                                         reads=[h_abs.b, qm.b, gacc.b], writes=[gacc.b])
                              for j in range(4):
                                  op("dve", lambda e: e.scalar_tensor_tensor(out=brT[:, 2 + hc, j * 512:(j + 1) * 512], in0=h_fw[:, 0:NB],
                                                                             scalar=qm[:, j:j + 1], in1=gacc[:, 0:NB], op0=ALU.mult, op1=ALU.mult),
                                     reads=[h_fw.b, gacc.b, qm.b], writes=[brT.b])
                          else:
                              pg_ = psf()
                              for k in range(8):
                                  op("pe", lambda e: e.matmul(pg_[:, 0:NB], lhsT=h_wmix[:, k, 384:512], rhs=hT[:, k, tk:tk + NB], start=(k == 0), stop=(k == 7)),
                                     reads=[h_wmix.b, hT.b], writes=[pg_.b])
                              op("act", lambda e: e.activation(out=h_abs[:, 0:NB], in_=pg_[:, 0:NB], func=AF.Silu), reads=[pg_.b], writes=[h_abs.b])
                              op("dve", lambda e: e.tensor_tensor(out=brT[:, 2 + hc, tk:tk + NB], in0=h_fw[:, 0:NB], in1=h_abs[:, 0:NB], op=ALU.mult),
                                 reads=[h_fw.b, h_abs.b], writes=[brT.b])

                      for s_ in range(nseq):
                          long_conv(s_, 0, 0, comb1)
                      for s_ in range(nseq):
                          long_conv(s_, 1, 3, comb2, own=hy_own)

              groups = [
                  dict(c=0, src=(xp if l == 0 else xmid_p), srcb=xmid_p_b, T=TP, L=SEQ, sample=False),
                  dict(c=1, src=(xs if l == 0 else xmid_s), srcb=xmid_s_b, T=LS, L=LS, sample=True),
              ]
              for g in groups:
                  c, T, L, sample = g["c"], g["T"], g["L"], g["sample"]
                  ntile = T // 128
                  nseq = T // L
                  kb.barrier()
                  if sample:
                      dma(w_in_b[:, :, :].rearrange("p k c -> p (k c)"), wc["in"][0][:, :], reads=[wc["in"][1]], writes=[w_in_b.b], q="pool")
                  else:
                      for k in range(8):
                          st = wstage[k % 2]
                          dma(st[:, 0:NINU], w_in[l, k * 128:(k + 1) * 128, 0:NINU], writes=[st.b], q="pool")
                          eng = "dve" if k % 2 == 0 else "pool"
                          op(eng, lambda e: e.tensor_copy(out=w_in_b[:, k, :], in_=st[:, 0:NINU]), reads=[st.b], writes=[w_in_b.b])
                      dma(wc["in"][0][:, :], w_in_b[:, :, :].rearrange("p k c -> p (k c)"), reads=[w_in_b.b], writes=[wc["in"][1]])

                  def k_path(src_ckv_f32, src_kro_f32, rope_tile, kcol, vt, rd):
                      op("dve", lambda e: e.tensor_copy(out=ckvn_b[:], in_=src_ckv_f32), reads=rd, writes=[ckvn_b.b])
                      p = psb()
                      op("pe", lambda e: e.transpose(p[:, 0:128], ckvn_b[:, :], ident_b[:, :]),
                         reads=[ckvn_b.b, ident_b.b], writes=[p.b])
                      op("act", lambda e: e.activation(out=ckvT[:], in_=p[:, 0:128], func=AF.Copy),
                         reads=[p.b], writes=[ckvT.b])
                      pkv = psf()
                      op("pe", lambda e: e.matmul(pkv[:, :], lhsT=ckvT[:, :], rhs=w_ukv_b[:, :], start=True, stop=True),
                         reads=[ckvT.b, w_ukv_b.b], writes=[pkv.b])
                      chk('k1')
                      kv3 = pkv[:, :].rearrange("p (h d) -> p h d", h=H)
                      op("dve", lambda e: e.tensor_copy(out=kcat[:, :, 0:64], in_=kv3[:, :, 0:64]),
                         reads=[pkv.b], writes=[kcat.b])
                      op("pool", lambda e: e.tensor_copy(out=kcat[:, :, 64:96],
                                                         in_=src_kro_f32.unsqueeze(1).to_broadcast([128, H, 32])),
                         reads=rd + [kcat.b], writes=[kcat.b])
                      chk('k2')
                      op("act", lambda e: e.activation(out=vaug[:, vt, :, 0:64], in_=kv3[:, :, 64:128], func=AF.Copy),
                         reads=[pkv.b], writes=[vaug.b])
                      op("pool", lambda e: e.memset(vaug[:, vt, :, 64:65], 1.0), reads=[vaug.b], writes=[vaug.b])
                      chk('k3')
                      head_norm(kcat, bc_kn, rope_tile)
                      chk('k4')
                      pq = psb()
                      for h in range(H):
                          op("pe", lambda e: e.transpose(pq[0:QK, h * 128:(h + 1) * 128], qb[:, h, :], ident_b[:, :]),
                             reads=[qb.b, ident_b.b], writes=[pq.b])
                      op("act", lambda e: e.activation(out=kT[0:QK, :, kcol:kcol + 128],
                                                       in_=pq[0:QK, 0:512].rearrange("p (h t) -> p h t", h=H),
                                                       func=AF.Copy), reads=[pq.b], writes=[kT.b])

                  def head_norm(src, wbc, rope_tile):
                      op("act", lambda e: e.activation(out=sqh[:], in_=src[:], func=AF.Square),
                         reads=[src.b], writes=[sqh.b])
                      op("dve", lambda e: e.tensor_reduce(out=ssh[:], in_=sqh[:], axis=AX.X, op=ALU.add),
                         reads=[sqh.b], writes=[ssh.b])
                      rstd_from_ss(rsh, ssh, QK)
                      op("dve", lambda e: e.tensor_tensor(out=qf[:], in0=src[:],
                                                          in1=rsh[:, :].unsqueeze(2).to_broadcast([128, H, QK]),
                                                          op=ALU.mult), reads=[src.b, rsh.b], writes=[qf.b])
                      op("dve", lambda e: e.tensor_tensor(out=qf[:], in0=qf[:],
                                                          in1=wbc[:, :].unsqueeze(1).to_broadcast([128, H, QK]),
                                                          op=ALU.mult), reads=[qf.b, wbc.b], writes=[qf.b])
                      if rope_tile is not None:
                          r5 = qf[:, :, 64:96].rearrange("p h (a f j) -> p h a f j", a=2, f=2)
                          t5 = rtmp[:].rearrange("p h (a f j) -> p h a f j", a=2, f=2)
                          op("dve", lambda e: e.tensor_copy(out=t5[:, :, :, 0, :], in_=r5[:, :, :, 1, :]),
                             reads=[qf.b], writes=[rtmp.b])
                          op("dve", lambda e: e.tensor_copy(out=t5[:, :, :, 1, :], in_=r5[:, :, :, 0, :]),
                             reads=[qf.b, rtmp.b], writes=[rtmp.b])
                          op("dve", lambda e: e.tensor_tensor(out=rtmp[:], in0=rtmp[:],
                                                              in1=sinT[:, :].unsqueeze(1).to_broadcast([128, H, 32]),
                                                              op=ALU.mult), reads=[rtmp.b, sinT.b], writes=[rtmp.b])
                          op("dve", lambda e: e.tensor_tensor(out=rtmp2[:], in0=qf[:, :, 64:96],
                                                              in1=cosT[:, :].unsqueeze(1).to_broadcast([128, H, 32]),
                                                              op=ALU.mult), reads=[qf.b, cosT.b], writes=[rtmp2.b])
                          op("dve", lambda e: e.tensor_tensor(out=qf[:, :, 64:96], in0=rtmp2[:], in1=rtmp[:], op=ALU.add),
                             reads=[rtmp.b, rtmp2.b, qf.b], writes=[qf.b])
                      op("dve", lambda e: e.tensor_copy(out=qb[:], in_=qf[:]), reads=[qf.b], writes=[qb.b])

                  for s in range(nseq):
                      tps = L // 128
                      nctx = 0
                      if sample:
                          nctx = PAST // 128
                          for t in range(nctx):
                              dma(ckvn[:, :], cckv[l, t * 128:(t + 1) * 128, :], writes=[ckvn.b], q="pool")
                              dma(kro[:, :], ckro[l, t * 128:(t + 1) * 128, :], writes=[kro.b], q="pool")
                              k_path(ckvn[:, :], kro[:, :], None, t * 128, t, [ckvn.b, kro.b])
                      for tt in range(tps):
                          ti = s * tps + tt
                          r0 = ti * 128
                          dma(xt[:, :], g["src"][r0:r0 + 128, :], reads=[g["srcb"]], writes=[xt.b], q="pool")
                          op("act", lambda e: e.activation(out=junk[:], in_=xt[:], func=AF.Square, accum_out=ss1[:, 0:1]),
                             reads=[xt.b], writes=[xn.b, ss1.b])
                          rstd_from_ss(rs1, ss1, D)
                          op("dve", lambda e: e.tensor_scalar(out=xn[:], in0=xt[:], scalar1=rs1[:, 0:1], scalar2=None,
                                                              op0=ALU.mult), reads=[xt.b, rs1.b], writes=[xn.b])
                          pt = psb()
                          for k in range(8):
                              op("pe", lambda e: e.transpose(pt[:, k * 128:(k + 1) * 128], xn[:, k * 128:(k + 1) * 128],
                                                             ident_b[:, :]), reads=[xn.b, ident_b.b], writes=[pt.b])
                          for k in range(8):
                              op("act", lambda e: e.activation(out=hT[:, k, r0:r0 + 128], in_=pt[:, k * 128:(k + 1) * 128],
                                                               func=AF.Identity, scale=effsc[:, c, k:k + 1],
                                                               bias=shift[:, c, k:k + 1]),
                                 reads=[pt.b, effsc.b, shift.b], writes=[hT.b])
                          chk('normT')
                          pl = psf()
                          for k in range(8):
                              op("pe", lambda e: e.matmul(pl[:, 0:352], lhsT=hT[:, k, r0:r0 + 128], rhs=w_in_b[:, k, 0:352],
                                                          start=(k == 0), stop=(k == 7)),
                                 reads=[hT.b, w_in_b.b], writes=[pl.b])
                          op("act", lambda e: e.activation(out=lat[:], in_=pl[:, 0:352], func=AF.Copy),
                             reads=[pl.b], writes=[lat.b])
                          op("act", lambda e: e.activation(out=junk2[:, 0:192], in_=lat[:, 0:192], func=AF.Square,
                                                           accum_out=ssq[:, 0:1]), reads=[lat.b], writes=[junk2.b, ssq.b])
                          op("act", lambda e: e.activation(out=junk2[:, 0:128], in_=lat[:, 192:320], func=AF.Square,
                                                           accum_out=ssq[:, 1:2]), reads=[lat.b, ssq.b], writes=[junk2.b, ssq.b])
                          op("act", lambda e: e.activation(out=rsq[:, 0:1], in_=ssq[:, 0:1], func=AF.Sqrt, scale=1.0 / 192,
                                                           bias=eps_c[:, 0:1]), reads=[ssq.b, eps_c.b], writes=[rsq.b])
                          op("act", lambda e: e.activation(out=rsq[:, 1:2], in_=ssq[:, 1:2], func=AF.Sqrt, scale=1.0 / 128,
                                                           bias=eps_c[:, 0:1]), reads=[ssq.b, eps_c.b, rsq.b], writes=[rsq.b])
                          op("dve", lambda e: e.reciprocal(out=rsq[:], in_=rsq[:]), reads=[rsq.b], writes=[rsq.b])
                          op("dve", lambda e: e.scalar_tensor_tensor(out=cqn[:], in0=lat[:, 0:192], scalar=rsq[:, 0:1],
                                                                     in1=bc_qa[:, :], op0=ALU.mult, op1=ALU.mult),
                             reads=[lat.b, rsq.b, bc_qa.b], writes=[cqn.b])
                          op("dve", lambda e: e.scalar_tensor_tensor(out=ckvn[:], in0=lat[:, 192:320], scalar=rsq[:, 1:2],
                                                                     in1=bc_kva[:, :], op0=ALU.mult, op1=ALU.mult),
                             reads=[lat.b, rsq.b, bc_kva.b], writes=[ckvn.b])
                          if not sample:
                              dma(o_ckv[s, l, tt * 128:(tt + 1) * 128, :], ckvn[:, :], reads=[ckvn.b])
                              dma(o_kro[s, l, tt * 128:(tt + 1) * 128, :], lat[:, 320:352], reads=[lat.b])
                          rope_tile = None
                          if sample:
                              dma(cosT[:, :], ropec[r0:r0 + 128, :], writes=[cosT.b], q="pool")
                              dma(sinT[:, :], ropes[r0:r0 + 128, :], writes=[sinT.b], q="pool")
                              rope_tile = True
                          chk('latent')
                          pc = psb()
                          op("pe", lambda e: e.transpose(pc[:, 0:128], cqn[:, 0:128], ident_b[:, :]),
                             reads=[cqn.b, ident_b.b], writes=[pc.b])
                          op("pe", lambda e: e.transpose(pc[0:64, 128:256], cqn[:, 128:192], ident_b[:, :]),
                             reads=[cqn.b, ident_b.b], writes=[pc.b])
                          op("act", lambda e: e.activation(out=cqT[:, 0, :], in_=pc[:, 0:128], func=AF.Copy),
                             reads=[pc.b], writes=[cqT.b])
                          op("act", lambda e: e.activation(out=cqT[0:64, 1, :], in_=pc[0:64, 128:256], func=AF.Copy),
                             reads=[pc.b, cqT.b], writes=[cqT.b])
                          pqp = psf()
                          op("pe", lambda e: e.matmul(pqp[:, 0:384], lhsT=cqT[:, 0, :], rhs=w_uq_a[:, :], start=True, stop=False),
                             reads=[cqT.b, w_uq_a.b], writes=[pqp.b])
                          op("pe", lambda e: e.matmul(pqp[:, 0:384], lhsT=cqT[0:64, 1, :], rhs=w_uq_c[:, :], start=False, stop=True),
                             reads=[cqT.b, w_uq_c.b], writes=[pqp.b])
                          op("act", lambda e: e.activation(out=qpre[:].rearrange("p h d -> p (h d)"), in_=pqp[:, 0:384],
                                                           func=AF.Copy), reads=[pqp.b], writes=[qpre.b])
                          head_norm(qpre, bc_qn, rope_tile)
                          pq = psb()
                          for h in range(H):
                              op("pe", lambda e: e.transpose(pq[0:QK, h * 128:(h + 1) * 128], qb[:, h, :], ident_b[:, :]),
                                 reads=[qb.b, ident_b.b], writes=[pq.b])
                          op("act", lambda e: e.activation(out=qT[0:QK, :, tt * 128:(tt + 1) * 128],
                                                           in_=pq[0:QK, 0:512].rearrange("p (h t) -> p h t", h=H),
                                                           func=AF.Copy), reads=[pq.b], writes=[qT.b])
                          chk('queries')
                          k_path(ckvn[:, :], lat[:, 320:352], rope_tile, (nctx + tt) * 128, nctx + tt, [ckvn.b, lat.b])

                      chk('tile0') if False else None
                      chk('phaseA')
                      nk = nctx + tps
                      QB = min(512, L)
                      nqs = QB // 128
                      po_b = [pf[5], pf[4], pf[3], pf[2]][:nqs]
                      qsplit = sample and l == DEPTH - 1
                      if qsplit:
                          qsel = qT
                          hsel = Tile(brT[:, 2:4, :].rearrange("p r (b t) -> p (r b) t", t=512))
                          for dst_, src_, np_ in ((qsel, qT, QK), (hsel, hT, 128)):
                              op("dve", lambda e: e.tensor_scalar(out=dst_[0:np_, :, 0:512], in0=src_[0:np_, :, 0:512], scalar1=qm[0:np_, 0:1],
                                                                  scalar2=None, op0=ALU.mult), reads=[src_.b, qm.b], writes=[dst_.b])
                              for j in range(1, 4):
                                  op("dve", lambda e: e.scalar_tensor_tensor(out=dst_[0:np_, :, 0:512], in0=src_[0:np_, :, j * 512:(j + 1) * 512],
                                                                             scalar=qm[0:np_, j:j + 1], in1=dst_[0:np_, :, 0:512],
                                                                             op0=ALU.mult, op1=ALU.add),
                                     reads=[src_.b, qm.b, dst_.b], writes=[dst_.b])
                      for qb_ in range(1 if qsplit else L // QB):
                          q0 = qb_ * QB
                          qsrc, hsrc = (qsel, hsel) if qsplit else (qT, hT)
                          for h in range(H):
                              for kt in range(nk):
                                  pss = pf[kt % 2]
                                  pTb = pT2[kt % 2]
                                  op("pe", lambda e: e.matmul(pss[:, 0:QB], lhsT=kT[0:QK, h, kt * 128:(kt + 1) * 128],
                                                              rhs=qsrc[0:QK, h, q0:q0 + QB], start=True, stop=True),
                                     reads=[kT.b, qsrc.b], writes=[pss.b])
                                  op("act", lambda e: e.activation(out=pTb[:, 0:QB], in_=pss[:, 0:QB], func=AF.Exp, scale=QK ** -0.5),
                                     reads=[pss.b], writes=[pTb.b])
                                  for qs in range(nqs):
                                      op("pe", lambda e: e.matmul(po_b[qs][:, 0:65], lhsT=pTb[:, qs * 128:(qs + 1) * 128], rhs=vaug[:, kt, h, 0:65],
                                                                  start=(kt == 0), stop=(kt == nk - 1)),
                                         reads=[pTb.b, vaug.b], writes=[po_b[qs].b])
                              for qs in range(nqs):
                                  op("dve", lambda e: e.reciprocal(out=den[:, 0:1], in_=po_b[qs][:, 64:65]), reads=[po_b[qs].b], writes=[den.b])
                                  op("dve", lambda e: e.tensor_scalar(out=omla4[:, qs, h, :], in0=po_b[qs][:, 0:64], scalar1=den[:, 0:1], scalar2=None,
                                                                      op0=ALU.mult), reads=[po_b[qs].b, den.b], writes=[omla4.b])
                          for qs in range(nqs):
                              r0 = s * L + q0 + qs * 128
                              pgm = pf[0]
                              for k in range(8):
                                  op("pe", lambda e: e.matmul(pgm[:, 0:256], lhsT=hsrc[:, k, r0:r0 + 128], rhs=w_in_b[:, k, 352:608],
                                                              start=(k == 0), stop=(k == 7)),
                                     reads=[hsrc.b, w_in_b.b], writes=[pgm.b])
                              op("act", lambda e: e.activation(out=silg[:], in_=pgm[:, 0:256], func=AF.Silu),
                                 reads=[pgm.b], writes=[silg.b])
                              op("dve", lambda e: e.tensor_tensor(out=omg[:], in0=omla4[:, qs, :, :].rearrange("p h d -> p (h d)"),
                                                                  in1=silg[:], op=ALU.mult), reads=[omla4.b, silg.b], writes=[omg.b])
                              pbr = psb()
                              for k2 in range(2):
                                  op("pe", lambda e: e.transpose(pbr[:, k2 * 128:(k2 + 1) * 128], omg[:, k2 * 128:(k2 + 1) * 128],
                                                                 ident_b[:, :]), reads=[omg.b, ident_b.b], writes=[pbr.b])
                              if qsplit:
                                  for j in range(4):
                                      op("dve", lambda e: e.tensor_scalar(out=brT[:, 0:2, j * 512 + r0:j * 512 + r0 + 128],
                                                                          in0=pbr[:, 0:256].rearrange("p (k t) -> p k t", k=2),
                                                                          scalar1=qm[:, j:j + 1], scalar2=None, op0=ALU.mult),
                                         reads=[pbr.b, qm.b], writes=[brT.b])
                              else:
                                  op("act", lambda e: e.activation(out=brT[:, 0:2, r0:r0 + 128],
                                                                   in_=pbr[:, 0:256].rearrange("p (k t) -> p k t", k=2),
                                                                   func=AF.Copy), reads=[pbr.b], writes=[brT.b])

                  chk('attn%d' % c)
                  kb.barrier()
                  gla_group(l, g)
                  chk('gla%d' % c)
                  kb.barrier()
                  s5_group(l, g)
                  chk('s5%d' % c)
                  kb.barrier()
                  hy_group(l, g)
                  chk('hy%d' % c)
                  kb.barrier()
                  if sample:
                      dma(w_out_b[:, :, :].rearrange("p k c -> p (k c)"), wc["out"][0][:, :], reads=[wc["out"][1]], writes=[w_out_b.b], q="pool")
                  else:
                      for k in range(8):
                          st = wstage[k % 2]
                          dma(st[:, 0:D], w_out[l, k * 128:(k + 1) * 128, :], writes=[st.b], q="pool")
                          eng = "dve" if k % 2 == 0 else "pool"
                          op(eng, lambda e: e.tensor_copy(out=w_out_b[:, k, :], in_=st[:, 0:D]), reads=[st.b], writes=[w_out_b.b])
                      dma(wc["out"][0][:, :], w_out_b[:, :, :].rearrange("p k c -> p (k c)"), reads=[w_out_b.b], writes=[wc["out"][1]])
                  for ti in range(ntile):
                      r0 = ti * 128
                      last = (l == DEPTH - 1)
                      dma(xt[:, :], g["src"][r0:r0 + 128, :], reads=[g["srcb"]], writes=[xt.b], q="pool")
                      for hf in range(2):
                          p = psf()
                          for k in range(8):
                              op("pe", lambda e: e.matmul(p[:, :], lhsT=brT[:, k, r0:r0 + 128],
                                                          rhs=w_out_b[:, k, hf * 512:(hf + 1) * 512],
                                                          start=(k == 0), stop=(k == 7)),
                                 reads=[brT.b, w_out_b.b], writes=[p.b])
                          op("dve", lambda e: e.tensor_tensor(out=ot[:, hf * 512:(hf + 1) * 512], in0=p[:, :],
                                                              in1=gate_bc[c][:, hf * 512:(hf + 1) * 512], op=ALU.mult),
                             reads=[p.b, gate_bc[c].b], writes=[ot.b])
                      op("dve", lambda e: e.tensor_tensor(out=ot[:], in0=ot[:], in1=xt[:], op=ALU.add),
                         reads=[ot.b, xt.b], writes=[ot.b])
                      if not last:
                          dst = (xmid_s if sample else xmid_p)
                          dma(dst[r0:r0 + 128, :], ot[:, :], reads=[ot.b], writes=[xmid_s_b if sample else xmid_p_b])
                      elif not sample:
                          dma(yp[r0:r0 + 128, :], ot[:, :], reads=[ot.b])
                      else:
                          dma(ys[r0:r0 + 128, :], ot[:, :], reads=[ot.b])

              chk('layer0')
        except _Stop:
            if stop == 'layer0':
                dma(yp[:, :], xmid_p[:, :], reads=[xmid_p_b])
                dma(ys[:, :], xmid_s[:, :], reads=[xmid_s_b])
            kb.barrier()
            kb.phase("DBG")
            dbt = sb([128, 2048], F32)
            kb.phase(None)
            op("dve", lambda e: e.memset(dbt[:], 0.0), writes=[dbt.b])
            op("dve", lambda e: e.tensor_copy(out=dbt[:, 0:16], in_=effsc[:].rearrange("p c k -> p (c k)")), reads=[effsc.b, dbt.b], writes=[dbt.b])
            op("dve", lambda e: e.tensor_copy(out=dbt[:, 16:32], in_=shift[:].rearrange("p c k -> p (c k)")), reads=[shift.b, dbt.b], writes=[dbt.b])
            op("dve", lambda e: e.tensor_copy(out=dbt[:, 32:33], in_=rs1[:, 0:1]), reads=[rs1.b, dbt.b], writes=[dbt.b])
            op("dve", lambda e: e.tensor_copy(out=dbt[:, 64:416], in_=lat[:, :]), reads=[lat.b, dbt.b], writes=[dbt.b])
            op("dve", lambda e: e.tensor_copy(out=dbt[:, 512:1536].rearrange("p (k t) -> p k t", k=8), in_=hT[:, :, 0:128]), reads=[hT.b, dbt.b], writes=[dbt.b])
            op("dve", lambda e: e.tensor_copy(out=dbt[:, 1536:2048], in_=gate_bc[0][:, 0:512]), reads=[gate_bc[0].b, dbt.b], writes=[dbt.b])
            if stop == 'weights':
                for i_, t_ in enumerate((s_par["ar1"][:], s_par["ai1"][:], s_par["kr"][:], s_par["ki"][:], s_pwr[:, 0, :], s_pwi[:, 0, :],
                                         s_pwr[:, 7, :], s_pwi[:, 7, :])):
                    op("dve", lambda e: e.tensor_copy(out=dbt[:, 512 + 16 * i_:528 + 16 * i_], in_=t_), reads=[s_par["are"].b, dbt.b], writes=[dbt.b])
            dma(dbg[:, :], dbt[:, :], reads=[dbt.b])
        kb.finish()
    return nc


_CACHE = {}


def _rope_tables():
    half = 16
    inv = (10000.0 ** (-np.arange(0, half, 2, dtype=np.float32) / half)).astype(np.float32)
    t = np.arange(LS)
    row = (t // 64).astype(np.float32)
    col = (t % 64).astype(np.float32)
    cos = np.zeros((LS, 32), np.float32)
    sin = np.zeros((LS, 32), np.float32)
    for a, pos in enumerate((row, col)):
        ang = pos[:, None] * inv[None, :]
        ang = np.concatenate([ang, ang], axis=-1)
        cos[:, 16 * a:16 * a + 16] = np.cos(ang)
        s = np.sin(ang)
        s[:, 0:8] *= -1.0
        sin[:, 16 * a:16 * a + 16] = s
    return cos, sin


def _gla_consts():
    p = np.arange(128)[:, None]
    q = np.arange(128)[None, :]
    same = (p // 64) == (q // 64)
    c = np.zeros((128, GC), np.float32)
    c[:, 0:128] = same & (p <= q)
    c[:, 128:256] = same & (p >= q)
    c[:, 256:260] = (p // 32) == np.arange(4)[None, :]
    c[:, 260] = (p[:, 0] < 64)
    c[:, 261] = (p[:, 0] >= 64)
    c[:, 262:518] = (p // 32) == (np.arange(256)[None, :] // 64)
    c[:, 518:646] = 1.0
    return c


def _s5_consts():
    c = np.zeros((128, SC), np.float32)
    r = np.arange(128)[:, None]
    q = np.arange(128)[None, :]
    for k in range(4):
        c[:, 128 * k:128 * k + 128] = (q // 16) == (2 * k + r // 64)
        c[:, 512 + 128 * k:512 + 128 * k + 128] = (r // 16) == (2 * k + q // 64)
    c[:, 1024] = math.pi / 2
    return c


def _hy_tables(L):
    i = np.arange(L, dtype=np.float64)
    ang = (2.0 * np.pi / (2 * L)) * np.outer(i, i)
    cos = np.cos(ang).astype(ml_dtypes.bfloat16)
    sin = np.sin(ang).astype(ml_dtypes.bfloat16)
    pos = np.arange(L, dtype=np.float32)
    t = pos / L
    w = (2.0 * math.pi * pos / L).astype(np.float32)
    bands = np.linspace(1e-4, 15, 16, dtype=np.float32)
    feat = np.concatenate([t[:, None], np.cos(w[:, None] * bands), np.sin(w[:, None] * bands)], axis=-1)
    deltas = np.linspace(math.log(100.0) / 0.3, math.log(100.0) / 1.5, 256, dtype=np.float32)
    win = (np.exp(-t[:, None] * deltas[None, :]) + 0.05).astype(np.float32)
    return cos, sin, np.ascontiguousarray(feat.T).astype(ml_dtypes.bfloat16), win


def _hy_consts():
    c = np.zeros((128, HC), np.float32)
    p = np.arange(128)
    c[:, 0] = 1.0 - 2.0 * (p % 2)
    for base, L in ((128, SEQ), (136, LS)):
        n = 2 * L
        nt = L // 128
        c[:, base:base + nt] = 2.0 / n
        c[0, base] = 1.0 / n
        c[:, base + nt] = 1.0 / n
    c[:, 160] = 1.0
    c[0, 160] = 0.0
    altr = (1.0 - 2.0 * (np.arange(512) % 2)).astype(ml_dtypes.bfloat16)[None, :]
    return c, altr


W_NAMES = ["norm_w", "ada_w", "ada_b", "w_in", "w_out", "mla_qa_norm", "mla_kva_norm", "mla_w_uq",
           "mla_w_ukv", "mla_q_norm", "mla_k_norm", "gla_gw", "gla_gb", "gla_norm",
           "s5_a_re", "s5_a_im", "s5_log_dt", "s5_b_re", "s5_b_im", "s5_c_re", "s5_c_im", "s5_d", "s5_glu_w", "s5_glu_b",
           "hy_conv_w", "hy_conv_b", "hy_w1", "hy_b1", "hy_freq1", "hy_w2", "hy_b2", "hy_freq2", "hy_w3", "hy_bias"]


def _f32(a):
    return np.ascontiguousarray(np.asarray(a, dtype=np.float32))


def core_inputs(inp, i):
    b = i // 4
    cos, sin = _rope_tables()
    m = {
        "xp": _f32(inp["x_prompt"])[4 * i:4 * i + 4].reshape(TP, D),
        "xs": _f32(inp["x_sample"])[b],
        "cond": np.stack([_f32(inp["c_ctx"]), _f32(inp["c"])[b]]),
        "cckv": _f32(inp["cache_mla_ckv"])[b],
        "ckro": _f32(inp["cache_mla_krope"])[b],
        "ropec": cos, "ropes": sin,
        "sgla": _f32(inp["state_gla"])[b].reshape(DEPTH, 2, 128, 64),
        "gcon": _gla_consts(),
        "ss5": _f32(inp["state_s5"])[b].reshape(DEPTH, 2, 1024, 2),
        "scon": _s5_consts(),
        "qmask": np.ascontiguousarray(np.tile(np.eye(4, dtype=np.float32)[i % 4], (128, 1))),
    }
    q_ = i % 4
    if "hy" not in _CACHE:
        _CACHE["hy"] = {"p": _hy_tables(SEQ), "s": _hy_tables(LS), "c": _hy_consts()}
    for nm_ in ("p", "s"):
        cos_, sin_, feat_, win_ = _CACHE["hy"][nm_]
        m["hcos_" + nm_], m["hsin_" + nm_], m["hfeat_" + nm_], m["hwin_" + nm_] = cos_, sin_, feat_, win_
    m["hcon"], m["haltr"] = _CACHE["hy"]["c"]
    if "hyf" not in _CACHE:
        _CACHE["hyf"] = [np.ascontiguousarray(t_.reshape(LS // 128, 128, LS // 128, 128).transpose(2, 1, 0, 3)).reshape(LS // 128, 128, LS)
                         for t_ in _CACHE["hy"]["s"][0:2]]
    m["hcosf"], m["hsinf"] = _CACHE["hyf"]
    m["hcosq"] = np.ascontiguousarray(_CACHE["hy"]["s"][0][:, 512 * q_:512 * (q_ + 1)])
    m["hsinq"] = np.ascontiguousarray(_CACHE["hy"]["s"][1][:, 512 * q_:512 * (q_ + 1)])
    for n in W_NAMES:
        m[n] = _f32(inp[n])
    return m


def kernel(**inp):
    if "nc" not in _CACHE:
        _CACHE["nc"] = build_program()
    nc = _CACHE["nc"]
    in_maps = [core_inputs(inp, i) for i in range(8)]
    res = run_bass_kernel_spmd(nc, in_maps, core_ids=list(range(8))).results
    y_prompt = np.concatenate([r["yp"].reshape(NPS, SEQ, D) for r in res], axis=0)
    y_sample = np.stack([np.concatenate([res[4 * b + q]["ys"][512 * q:512 * (q + 1)] for q in range(4)], axis=0)
                         for b in range(2)], axis=0)
    ckv = np.concatenate([r["o_ckv"] for r in res], axis=0)
    kro = np.concatenate([r["o_kro"] for r in res], axis=0)
    s5 = np.concatenate([r["o_s5"] for r in res], axis=0)
    gla = np.concatenate([r["o_gla"] for r in res], axis=0)
    return (y_prompt.astype(np.float32), y_sample.astype(np.float32), ckv.astype(np.float32),
            kro.astype(np.float32), s5.astype(np.float32), gla.astype(np.float32))
```
